# Optimizing a Trainium2 kernel written in Bass

```python
import jax, jax.numpy as jnp
from jax import lax
import numpy as np

D_MODEL = 1024
BATCH = 2
SEQ = 8192
DEPTH = 1

CHUNK = 64
POOL_GROUPS = 4
POOL_WINDOWS = (2, 4, 8, 16)
POOL_WIDTH = D_MODEL // 2
POOL_GROUP_DIM = POOL_WIDTH // POOL_GROUPS
SB_HEAD_DIM = 64
SB_HEADS = D_MODEL // 128
SB_WIDTH = SB_HEADS * SB_HEAD_DIM
SB_BLOCK = 128
MEM_LEN = 256
X_HEADS = 4
X_HEAD_DIM = D_MODEL // 16
X_WIDTH = X_HEADS * X_HEAD_DIM
N_BRANCHES = 3
D_FF = -(-8 * D_MODEL // (3 * 256)) * 256
RMS_EPS = 1e-6
IN_WIDTH = POOL_WIDTH + 3 * SB_WIDTH + X_WIDTH + N_BRANCHES * D_MODEL

kernel_name = "hybrid_pool_stickbreak_memx_block"


def rmsnorm(x, g):
    xf = x.astype(jnp.float32)
    y = xf * lax.rsqrt(jnp.mean(xf * xf, axis=-1, keepdims=True) + RMS_EPS) * g.astype(jnp.float32)
    return y.astype(x.dtype)


def pool_mixer(p, w_mix, scale):
    b, s, _ = p.shape
    pg = p.astype(jnp.float32).reshape(b, s, POOL_GROUPS, POOL_GROUP_DIM)
    c = jnp.concatenate([jnp.zeros((b, 1, POOL_GROUPS, POOL_GROUP_DIM), jnp.float32),
                         jnp.cumsum(pg, axis=1)], axis=1)
    windows = jnp.array(POOL_WINDOWS, jnp.int32)
    t1 = jnp.arange(1, s + 1, dtype=jnp.int32)[:, None]
    lo = jnp.maximum(t1 - windows[None, :], 0)
    g_idx = jnp.arange(POOL_GROUPS, dtype=jnp.int32)[None, :]
    c_lo = c[:, lo, g_idx]
    count = jnp.minimum(t1, windows[None, :]).astype(jnp.float32)
    mixed = (c[:, 1:] - c_lo) / count[None, :, :, None] - pg
    mixed = mixed.astype(p.dtype)
    y = jnp.einsum('bsgc,gcd->bsgd', mixed, w_mix)
    return y.reshape(b, s, POOL_WIDTH) * scale


def stick_breaking_attention(q, k, v):
    b, s, h, dh = q.shape
    nb = s // SB_BLOCK
    scale = 1.0 / float(np.sqrt(dh))
    kf = k.astype(jnp.float32).transpose(0, 2, 1, 3)
    vf = v.astype(jnp.float32).transpose(0, 2, 1, 3)
    qb = q.astype(jnp.float32).transpose(0, 2, 1, 3).reshape(b, h, nb, SB_BLOCK, dh)
    qb = qb.transpose(2, 0, 1, 3, 4)
    key_pos = jnp.arange(s, dtype=jnp.int32)[None, :]

    def one_block(args):
        q_blk, bi = args
        z = jnp.einsum('bhqd,bhkd->bhqk', q_blk, kf) * scale
        qpos = bi * SB_BLOCK + jnp.arange(SB_BLOCK, dtype=jnp.int32)[:, None]
        mask = key_pos < qpos
        log1m = jnp.where(mask, jax.nn.log_sigmoid(-z), 0.0)
        excl = lax.cumsum(log1m, axis=3, reverse=True) - log1m
        a = jnp.where(mask, jnp.exp(jax.nn.log_sigmoid(z) + excl), 0.0)
        return jnp.einsum('bhqk,bhkd->bhqd', a, vf)

    out = lax.map(one_block, (qb, jnp.arange(nb, dtype=jnp.int32)))
    out = out.transpose(1, 0, 3, 2, 4).reshape(b, s, h * dh)
    return out.astype(q.dtype)


def memory_cross_attention(q, mem_n, w_mem_kv):
    b, s = q.shape[:2]
    m = mem_n.shape[1]
    kv = (mem_n @ w_mem_kv).reshape(b, m, 2, X_HEADS, X_HEAD_DIM)
    mk, mv = kv[:, :, 0], kv[:, :, 1]
    scores = jnp.einsum('bshd,bmhd->bhsm', q.astype(jnp.float32), mk.astype(jnp.float32))
    p = jax.nn.softmax(scores / float(np.sqrt(X_HEAD_DIM)), axis=-1)
    out = jnp.einsum('bhsm,bmhd->bshd', p, mv.astype(jnp.float32))
    return out.reshape(b, s, X_WIDTH).astype(q.dtype)


def setup_inputs(seed: int = 0) -> dict:
    key = jax.random.key(seed)
    ks = jax.random.split(key, 20)
    f32 = jnp.float32

    def nrm(k, shape, fan_in):
        return jax.random.normal(k, shape, f32) * (fan_in ** -0.5)

    def gain(k, shape):
        return 1.0 + 0.02 * jax.random.normal(k, shape, f32)

    L = DEPTH
    return {
        "x": jax.random.normal(ks[0], (BATCH, SEQ, D_MODEL), f32),
        "mem": jax.random.normal(ks[1], (BATCH, MEM_LEN, D_MODEL), f32),
        "norm_mix_pre": gain(ks[2], (L, D_MODEL)),
        "w_in": nrm(ks[3], (L, D_MODEL, IN_WIDTH), D_MODEL),
        "w_pool_mix": nrm(ks[4], (L, POOL_GROUPS, POOL_GROUP_DIM, POOL_GROUP_DIM), POOL_GROUP_DIM),
        "pool_scale": gain(ks[5], (L, POOL_WIDTH)),
        "w_pool_o": nrm(ks[6], (L, POOL_WIDTH, D_MODEL), POOL_WIDTH),
        "w_sb_o": nrm(ks[7], (L, SB_WIDTH, D_MODEL), SB_WIDTH),
        "norm_mem": gain(ks[8], (L, D_MODEL)),
        "w_mem_kv": nrm(ks[9], (L, D_MODEL, 2 * X_WIDTH), D_MODEL),
        "w_x_o": nrm(ks[10], (L, X_WIDTH, D_MODEL), X_WIDTH),
        "w_out": nrm(ks[11], (L, D_MODEL, D_MODEL), D_MODEL),
        "norm_mix_post": gain(ks[12], (L, D_MODEL)),
        "norm_ffn_pre": gain(ks[13], (L, D_MODEL)),
        "w_ffn_in": nrm(ks[14], (L, D_MODEL, 2 * D_FF), D_MODEL),
        "w_ffn_out": nrm(ks[15], (L, D_FF, D_MODEL), D_FF),
        "norm_ffn_post": gain(ks[16], (L, D_MODEL)),
    }


def reference(x, mem, norm_mix_pre, w_in, w_pool_mix, pool_scale, w_pool_o, w_sb_o,
              norm_mem, w_mem_kv, w_x_o, w_out, norm_mix_post, norm_ffn_pre,
              w_ffn_in, w_ffn_out, norm_ffn_post):
    b, s, d = x.shape
    h = x
    for l in range(DEPTH):
        n = rmsnorm(h, norm_mix_pre[l])
        proj = n @ w_in[l]
        o0 = POOL_WIDTH
        o1 = o0 + 3 * SB_WIDTH
        o2 = o1 + X_WIDTH
        p_in = proj[..., :o0]
        qkv = proj[..., o0:o1].reshape(b, s, 3, SB_HEADS, SB_HEAD_DIM)
        xq = proj[..., o1:o2].reshape(b, s, X_HEADS, X_HEAD_DIM)
        gates = jax.nn.sigmoid(proj[..., o2:].astype(jnp.float32)).reshape(b, s, N_BRANCHES, d)

        y_pool = pool_mixer(p_in, w_pool_mix[l], pool_scale[l]) @ w_pool_o[l]
        y_sb = stick_breaking_attention(qkv[:, :, 0], qkv[:, :, 1], qkv[:, :, 2]) @ w_sb_o[l]
        mem_n = rmsnorm(mem, norm_mem[l])
        y_x = memory_cross_attention(xq, mem_n, w_mem_kv[l]) @ w_x_o[l]

        merged = (gates[:, :, 0] * y_pool.astype(jnp.float32)
                  + gates[:, :, 1] * y_sb.astype(jnp.float32)
                  + gates[:, :, 2] * y_x.astype(jnp.float32)).astype(h.dtype)
        h = h + rmsnorm(merged @ w_out[l], norm_mix_post[l])

        n2 = rmsnorm(h, norm_ffn_pre[l])
        gu = n2 @ w_ffn_in[l]
        ff = (jax.nn.silu(gu[..., :D_FF]) * gu[..., D_FF:]) @ w_ffn_out[l]
        h = h + rmsnorm(ff, norm_ffn_post[l])
    return h
```

```python
import contextlib
import numpy as np
import concourse.bass as bass
import concourse.mybir as mybir
from concourse.bass_utils import run_bass_kernel_spmd

F32 = mybir.dt.float32
BF16 = mybir.dt.bfloat16
I32 = mybir.dt.int32
AF = mybir.ActivationFunctionType
ALU = mybir.AluOpType
AX = mybir.AxisListType

ENGS = ("pe", "act", "dve", "pool", "sp")
SAME_ENG_RAW = True

D = 1024
DFF = 2816
NFC = DFF // 128
INW = 5376
O_Q, O_K, O_V, O_XQ, O_G = 512, 1024, 1536, 2048, 2304
EPS = 1e-6
NEG_BIG = -30000.0
V_PRE, V_MEM, V_POST, V_FPRE, V_FPOST, V_PSC, NVEC = 0, 8, 16, 24, 32, 40, 44


class Buf:
    __slots__ = ("name", "w", "r", "excl")

    def __init__(self, name):
        self.name = name
        self.w = {}
        self.r = {}
        self.excl = False


class Builder:
    def __init__(self, nc, es):
        self.nc = nc
        self.es = es
        self.q = {e: [] for e in ENGS}
        self.cnt = {}
        self.seen = {e: {} for e in ENGS}
        self.semh = {}
        self.nbuf = 0
        for e in ENGS:
            if e != "sp":
                self._newsem(e)

    def _newsem(self, key):
        self.semh[key] = self.es.enter_context(self.nc.semaphore("s%d" % len(self.semh)))
        self.cnt[key] = 0

    def buf(self, name=None):
        self.nbuf += 1
        return Buf("%s#%d" % (name or "b", self.nbuf))

    def _wait(self, eng, key, val):
        if self.seen[eng].get(key, 0) >= val:
            return
        self.seen[eng][key] = val
        sem = self.semh[key]
        self.q[eng].append(lambda e, sem=sem, val=val: e.wait_ge(sem, val))

    def _deps(self, eng, reads, writes, acc):
        raw = {}
        oth = {}

        def add(dst, d):
            for k, v in d.items():
                if dst.get(k, 0) < v:
                    dst[k] = v
        for b in reads:
            add(raw, b.w)
            if b.excl:
                add(oth, b.r)
        for b in writes:
            add(oth, b.w)
            add(oth, b.r)
        for b in acc:
            add(oth, b.r)
        for k, v in raw.items():
            if k == eng and (eng == "pe" or not SAME_ENG_RAW):
                continue
            self._wait(eng, k, v)
        for k, v in oth.items():
            if k == eng:
                continue
            self._wait(eng, k, v)

    def _record(self, key, val, reads, writes, acc):
        for b in reads:
            if b.r.get(key, 0) < val:
                b.r[key] = val
        for b in writes:
            b.w = {key: val}
            b.r = {}
        for b in acc:
            b.w[key] = val
            b.r = {}

    def op(self, eng, fn, reads=(), writes=(), acc=()):
        self._deps(eng, reads, writes, acc)
        self.cnt[eng] += 1
        c = self.cnt[eng]
        sem = self.semh[eng]
        self.q[eng].append(lambda e, fn=fn, sem=sem: fn(e).then_inc(sem, 1))
        self._record(eng, c, reads, writes, acc)

    def dma(self, qeng, out_ap, in_ap, sb, store=False, reads=(), writes=(), acc=(), **kw):
        self._deps(qeng, reads, writes, acc)
        key = ("st" if store else "ld", sb.name)
        if key not in self.semh:
            self._newsem(key)
        self.cnt[key] += 16
        c = self.cnt[key]
        sem = self.semh[key]
        self.q[qeng].append(
            lambda e, o=out_ap, i=in_ap, sem=sem, kw=kw: e.dma_start(out=o, in_=i, **kw).then_inc(sem, 16))
        self._record(key, c, reads, writes, acc)

    def barrier(self):
        for e in ENGS:
            for k, c in self.cnt.items():
                if k == e or c == 0:
                    continue
                self._wait(e, k, c)

    def emit(self):
        q = self.q
        with self.nc.Block() as block:
            @block.tensor
            def _(e):
                for t in q["pe"]:
                    t(e)

            @block.scalar
            def _(e):
                for t in q["act"]:
                    t(e)

            @block.vector
            def _(e):
                for t in q["dve"]:
                    t(e)

            @block.gpsimd
            def _(e):
                for t in q["pool"]:
                    t(e)

            @block.sync
            def _(e):
                for t in q["sp"]:
                    t(e)


def build_program(G, debug=False, stop_after=None):
    S = 1024 * G
    NBLK = S // 128
    NB = 2 * G
    NQ = 128 * NB
    BPT = 4 if G >= 2 else 2
    TO = 128 * BPT
    TOH = 144 * BPT
    NT = NB // BPT
    ST = S // 512

    nc = bass.Bass("TRN2", target_bir_lowering=False)

    def din(name, shape, dt=F32):
        return nc.dram_tensor(name, list(shape), dt, kind="ExternalInput").ap()

    xT = din("xT", [D, S])
    xoh = din("xoh", [D, NB * 144])
    pos = din("pos", [NQ])
    memT = din("memT", [D, 256])
    w_in = din("w_in", [D, INW])
    w_pmix = din("w_pmix", [512, 128])
    w_po = din("w_po", [512, D])
    w_sbo = din("w_sbo", [512, D])
    w_kv = din("w_kv", [D, 512])
    w_xo = din("w_xo", [256, D])
    w_out = din("w_out", [D, D])
    w_fi = din("w_fi", [D, 2 * DFF])
    w_fo = din("w_fo", [DFF, D])
    vecs = din("vecs", [128, NVEC])
    outT = nc.dram_tensor("outT", [D, NQ], F32, kind="ExternalOutput").ap()
    kT_s = nc.dram_tensor("kT_s", [4, 128, S], BF16).ap()
    v_s = nc.dram_tensor("v_s", [4, 128, NBLK, 128], BF16).ap()
    c0_s = nc.dram_tensor("c0_s", [D, NQ], F32).ap()
    c2_s = nc.dram_tensor("c2_s", [D, NQ], F32).ap()
    h1_s = nc.dram_tensor("h1_s", [D, NQ], F32).ap()
    n2_s = nc.dram_tensor("n2_s", [D, NQ], BF16).ap()
    dbg = {}
    if debug:
        dbg["qT"] = nc.dram_tensor("dbg_qT", [128, 4, 2 * NQ], BF16, kind="ExternalOutput").ap()
        dbg["sbo"] = nc.dram_tensor("dbg_sbo", [128, 4, NQ], BF16, kind="ExternalOutput").ap()
        dbg["kT"] = nc.dram_tensor("dbg_kT", [4, 128, S], BF16, kind="ExternalOutput").ap()
        dbg["v"] = nc.dram_tensor("dbg_v", [4, 128, NBLK, 128], BF16, kind="ExternalOutput").ap()
        dbg["c0"] = nc.dram_tensor("dbg_c0", [D, NQ], F32, kind="ExternalOutput").ap()
        dbg["c2"] = nc.dram_tensor("dbg_c2", [D, NQ], F32, kind="ExternalOutput").ap()
        dbg["h1"] = nc.dram_tensor("dbg_h1", [D, NQ], F32, kind="ExternalOutput").ap()

    with contextlib.ExitStack() as es:
        B = Builder(nc, es)

        def sbt(scope, name, shape, dt):
            return scope.enter_context(nc.sbuf_tensor(name, list(shape), dt))

        def mm(out, lhsT, rhs, start, stop, R, W=(), A=()):
            B.op("pe", lambda e: e.matmul(out, lhsT, rhs, start=start, stop=stop), reads=R, writes=W, acc=A)

        def act(out, in_, func, R, W=(), A=(), **kw):
            B.op("act", lambda e: e.activation(out=out, in_=in_, func=func, **kw), reads=R, writes=W, acc=A)

        def tt(eng, out, in0, in1, op, R, W=(), A=()):
            B.op(eng, lambda e: e.tensor_tensor(out=out, in0=in0, in1=in1, op=op), reads=R, writes=W, acc=A)

        def ts(eng, out, in0, s1, s2, op0, op1, R, W=(), A=()):
            if op1 is None:
                B.op(eng, lambda e: e.tensor_scalar(out=out, in0=in0, scalar1=s1, scalar2=s2, op0=op0),
                     reads=R, writes=W, acc=A)
            else:
                B.op(eng, lambda e: e.tensor_scalar(out=out, in0=in0, scalar1=s1, scalar2=s2, op0=op0, op1=op1),
                     reads=R, writes=W, acc=A)

        def cp(eng, out, in_, R, W=(), A=()):
            if eng == "act":
                act(out, in_, AF.Copy, R, W, A)
            else:
                B.op(eng, lambda e: e.tensor_copy(out=out, in_=in_), reads=R, writes=W, acc=A)

        PSALL = es.enter_context(nc.psum_tensor("psall", [128, 4096], F32))
        PS = [PSALL[:, i * 512:(i + 1) * 512] for i in range(8)]
        PB = [B.buf("ps%d" % i) for i in range(8)]
        for b_ in PB:
            b_.excl = True
        PST = PS[7].bitcast(BF16)
        PSTB = PB[7]

        gs = es
        vec = sbt(gs, "vec", [128, NVEC], F32); vecb = B.buf("vec")
        ones = sbt(gs, "ones", [128, 128], BF16); onesb = B.buf("ones")
        nTin = sbt(gs, "nTin", [128, 128], BF16); nTinb = B.buf("nTin")
        nOnes = sbt(gs, "nOnes", [128, 128], BF16); nOnesb = B.buf("nOnes")
        nBigI = sbt(gs, "nBigI", [128, 128], BF16); nBigIb = B.buf("nBigI")
        ident = sbt(gs, "ident", [128, 128], BF16); identb = B.buf("ident")
        dif_i = sbt(gs, "dif_i", [128, 128], I32); difib = B.buf("dif_i")
        dif_f = sbt(gs, "dif_f", [128, 128], F32); diffb = B.buf("dif_f")
        kp8_i = sbt(gs, "kp8_i", [128, 8], I32); kp8ib = B.buf("kp8i")
        kp8 = sbt(gs, "kp8", [128, 8], F32); kp8b = B.buf("kp8")
        qpos = sbt(gs, "qpos", [128, NQ], F32); qposb = B.buf("qpos")
        notM = sbt(gs, "notM", [128, 8, 2, 256], BF16); notMb = B.buf("notM")
        NSTG = 2
        st_state = {"i": 0, "e": 0, "n": 0}

        def new_stg(scope):
            st_state["n"] += 1
            st_state["stg"] = [sbt(scope, "stg%d_%d" % (st_state["n"], i), [128, 1536], F32) for i in range(NSTG)]
            st_state["stgb"] = [B.buf("stg%d" % i) for i in range(NSTG)]

        B.dma("sp", vec[:], vecs, vecb, writes=[vecb])
        B.dma("sp", qpos[:], pos.partition_broadcast(128), qposb, writes=[qposb])
        B.op("dve", lambda e: e.memset(ones[:], 1.0), writes=[onesb])
        B.op("dve", lambda e: e.memset(nOnes[:], -1.0), writes=[nOnesb])
        B.op("pool", lambda e: e.iota(dif_i[:], pattern=[[-1, 128]], base=0, channel_multiplier=1), writes=[difib])
        cp("dve", dif_f[:], dif_i[:], R=[difib], W=[diffb])
        ts("dve", nTin[:], dif_f[:], 0.0, -1.0, ALU.is_ge, ALU.mult, R=[diffb], W=[nTinb])
        ts("dve", nBigI[:], dif_f[:], 0.0, NEG_BIG, ALU.is_equal, ALU.mult, R=[diffb], W=[nBigIb])
        ts("dve", ident[:], dif_f[:], 0.0, None, ALU.is_equal, None, R=[diffb], W=[identb])
        B.op("pool", lambda e: e.iota(kp8_i[:], pattern=[[128, 8]], base=0, channel_multiplier=1), writes=[kp8ib])
        cp("dve", kp8[:], kp8_i[:], R=[kp8ib], W=[kp8b])
        for jj in range(8):
            for h in range(2):
                ts("dve", notM[:, jj, h, :], qpos[:, 0:256], kp8[:, jj:jj + 1], None, ALU.is_le, None,
                   R=[qposb, kp8b], **({"W": [notMb]} if (jj == 0 and h == 0) else {"A": [notMb]}))

        def load_w(dst, dstb, src, kc, ncols, gcol=None, first=True):
            kg = max(1, min(kc, 1536 // ncols))
            k0 = 0
            while k0 < kc:
                kn = min(kg, kc - k0)
                i = st_state["i"] % NSTG
                st_state["i"] += 1
                stg, stgb = st_state["stg"], st_state["stgb"]
                sv = stg[i][:, 0:kn * ncols].rearrange("p (k n) -> p k n", k=kn)
                B.dma("sp", sv, src[k0 * 128:(k0 + kn) * 128, :].rearrange("(k p) n -> p k n", p=128), stgb[i],
                      writes=[stgb[i]])
                eng = "dve" if st_state["e"] % 2 == 0 else "act"
                st_state["e"] += 1
                kw = {"W": [dstb]} if (first and k0 == 0) else {"A": [dstb]}
                cp(eng, dst[:, k0:k0 + kn, :], sv, R=[stgb[i]], **kw)
                k0 += kn

        def run(gen):
            for _ in gen:
                pass

        def interleave(*gens):
            gens = list(gens)
            while gens:
                for g_ in list(gens):
                    try:
                        next(g_)
                    except StopIteration:
                        gens.remove(g_)

        def stt(out, in0, scal, in1, R, W=(), A=()):
            B.op("dve", lambda e: e.scalar_tensor_tensor(out=out, in0=in0, scalar=scal, in1=in1,
                                                         op0=ALU.mult, op1=ALU.mult), reads=R, writes=W, acc=A)

        def rstd_from(ss_ps, ss_b, ncols, ln_t, ln_b, out_t, out_b, out_is_write=True):
            act(ln_t, ss_ps, AF.Ln, R=[ss_b], W=[ln_b], scale=1.0 / D, bias=EPS)
            act(out_t, ln_t, AF.Exp, R=[ln_b], W=[out_b], scale=-0.5)

        s13 = contextlib.ExitStack()
        es.enter_context(s13)
        NGT = BPT // 2
        Qz = sbt(s13, "Qz", [128, 4, G, 2, 256], BF16); Qzb = B.buf("Qz")
        B.op("pool", lambda e: e.memset(Qz[:].rearrange("p a g h q -> p (a g h q)"), 0.0), writes=[Qzb])
        wqkv = sbt(s13, "wqkv", [128, 8, 1536], BF16); wqkvb = B.buf("wqkv")

        xn_s = nc.dram_tensor("xn_s", [128, 8, NB * 144], BF16).ap()
        xnsb = B.buf("xn_s")
        with contextlib.ExitStack() as s1:
            new_stg(s1)
            load_w(wqkv, wqkvb, w_in[:, O_Q:O_Q + 1536], 8, 1536)
            xo_t = [sbt(s1, "xo_t%d" % i, [128, 8, TOH], F32) for i in range(2)]
            xo_b = [B.buf("xo_t%d" % i) for i in range(2)]
            sq1 = sbt(s1, "sq1", [128, 8, TOH], BF16); sq1b = B.buf("sq1")
            ln1 = sbt(s1, "ln1", [128, TOH], F32); ln1b = B.buf("ln1")
            rs1 = sbt(s1, "rs1", [128, TOH], F32); rs1b = B.buf("rs1")
            xn1 = [sbt(s1, "xn1_%d" % i, [128, 8, BPT, 144], BF16) for i in range(2)]
            xn1b = [B.buf("xn1_%d" % i) for i in range(2)]
            HB = TOH // 2
            for t in range(NT):
                sl = t % 2
                B.dma("sp", xo_t[sl][:], xoh[:, t * TOH:(t + 1) * TOH].rearrange("(c p) n -> p c n", p=128),
                      xo_b[sl], writes=[xo_b[sl]])
                act(sq1[:], xo_t[sl][:], AF.Square, R=[xo_b[sl]], W=[sq1b])
                for hf in range(2):
                    for k in range(8):
                        mm(PS[hf][:, 0:HB], ones[:], sq1[:, k, hf * HB:(hf + 1) * HB], start=(k == 0), stop=(k == 7),
                           R=[onesb, sq1b], **({"W": [PB[hf]]} if k == 0 else {"A": [PB[hf]]}))
                for hf in range(2):
                    act(ln1[:, hf * HB:(hf + 1) * HB], PS[hf][:, 0:HB], AF.Ln, R=[PB[hf]],
                        **({"W": [ln1b]} if hf == 0 else {"A": [ln1b]}), scale=1.0 / D, bias=EPS)
                act(rs1[:], ln1[:], AF.Exp, R=[ln1b], W=[rs1b], scale=-0.5)
                xnv = xn1[sl][:].rearrange("p c b t -> p c (b t)")
                for k in range(8):
                    stt(xnv[:, k, :], xo_t[sl][:, k, :], vec[:, V_PRE + k:V_PRE + k + 1], rs1[:],
                        R=[xo_b[sl], rs1b, vecb], **({"W": [xn1b[sl]]} if k == 0 else {"A": [xn1b[sl]]}))
                B.dma("sp", xn_s[:, :, t * TOH:(t + 1) * TOH], xnv, xn1b[sl], store=True,
                      reads=[xn1b[sl]], acc=[xnsb])
                for c4 in range(4):
                    pb = 2 + (c4 % 2)
                    for k in range(8):
                        mm(PS[pb][:, 0:TO].rearrange("p (b t) -> p b t", b=BPT),
                           wqkv[:, k, c4 * 128:(c4 + 1) * 128], xn1[sl][:, k, :, 16:144],
                           start=(k == 0), stop=(k == 7), R=[wqkvb, xn1b[sl]],
                           **({"W": [PB[pb]]} if k == 0 else {"A": [PB[pb]]}))
                    for h in range(2):
                        act(Qz[h * 64:(h + 1) * 64, c4, t * NGT:(t + 1) * NGT, h, :],
                            PS[pb][h * 64:(h + 1) * 64, 0:TO].rearrange("p (g q) -> p g q", g=NGT), AF.Copy,
                            R=[PB[pb]], A=[Qzb], scale=0.125)
        B.barrier()
        if debug:
            B.dma("sp", dbg["qT"], Qz[:].rearrange("p a g h q -> p a (g h q)"), Qzb, store=True, reads=[Qzb])
        if stop_after == "P1":
            B.barrier()
            B.emit()
            return nc

        kTsb = B.buf("kT_s"); vsb = B.buf("v_s")
        with contextlib.ExitStack() as s2:
            x_t = [sbt(s2, "x_t%d" % i, [128, 8, 512], F32) for i in range(3)]
            x_b = [B.buf("x_t%d" % i) for i in range(3)]
            sq2 = [sbt(s2, "sq2_%d" % i, [128, 8, 512], BF16) for i in range(2)]
            sq2b = [B.buf("sq2_%d" % i) for i in range(2)]
            ln2 = sbt(s2, "ln2", [128, 512], F32); ln2b = B.buf("ln2")
            rs2 = sbt(s2, "rs2", [128, 512], F32); rs2b = B.buf("rs2")
            xn2 = [sbt(s2, "xn2_%d" % i, [128, 8, 512], BF16) for i in range(2)]
            xn2b = [B.buf("xn2_%d" % i) for i in range(2)]
            kst = [sbt(s2, "kst%d" % i, [128, 4, 512], BF16) for i in range(2)]
            kstb = [B.buf("kst%d" % i) for i in range(2)]
            vst = [sbt(s2, "vst%d" % i, [128, 4, 512], BF16) for i in range(2)]
            vstb = [B.buf("vst%d" % i) for i in range(2)]

            def p2_load(t):
                B.dma("sp", x_t[t % 3][:], xT[:, t * 512:(t + 1) * 512].rearrange("(c p) n -> p c n", p=128),
                      x_b[t % 3], writes=[x_b[t % 3]])

            def p2_sq(t):
                sl = t % 2
                act(sq2[sl][:], x_t[t % 3][:], AF.Square, R=[x_b[t % 3]], W=[sq2b[sl]])

            def p2_norm(t):
                sl = t % 2
                for k in range(8):
                    mm(PS[0][:], ones[:], sq2[sl][:, k, :], start=(k == 0), stop=(k == 7), R=[onesb, sq2b[sl]],
                       **({"W": [PB[0]]} if k == 0 else {"A": [PB[0]]}))
                rstd_from(PS[0][:], PB[0], 512, ln2[:], ln2b, rs2[:], rs2b)
                for k in range(8):
                    stt(xn2[sl][:, k, :], x_t[t % 3][:, k, :], vec[:, V_PRE + k:V_PRE + k + 1], rs2[:],
                        R=[x_b[t % 3], rs2b, vecb], **({"W": [xn2b[sl]]} if k == 0 else {"A": [xn2b[sl]]}))

            def p2_kv(t):
                sl = t % 2
                for c4 in range(4):
                    pb = 1 + (c4 % 3)
                    for k in range(8):
                        mm(PS[pb][:], wqkv[:, k, 512 + c4 * 128:512 + (c4 + 1) * 128], xn2[sl][:, k, :],
                           start=(k == 0), stop=(k == 7), R=[wqkvb, xn2b[sl]],
                           **({"W": [PB[pb]]} if k == 0 else {"A": [PB[pb]]}))
                    cp("dve", kst[sl][:, c4, :], PS[pb][:], R=[PB[pb]],
                       **({"W": [kstb[sl]]} if c4 == 0 else {"A": [kstb[sl]]}))
                B.dma("pool", kT_s[:, :, t * 512:(t + 1) * 512].rearrange("c p s -> p c s"), kst[sl][:], kstb[sl],
                      store=True, reads=[kstb[sl]], acc=[kTsb])
                for bk in range(4):
                    pb = 4 + (bk % 3)
                    for k in range(8):
                        mm(PS[pb][:], xn2[sl][:, k, bk * 128:(bk + 1) * 128], wqkv[:, k, 1024:1536],
                           start=(k == 0), stop=(k == 7), R=[wqkvb, xn2b[sl]],
                           **({"W": [PB[pb]]} if k == 0 else {"A": [PB[pb]]}))
                    cp("dve", vst[sl][:, bk, :], PS[pb][:], R=[PB[pb]],
                       **({"W": [vstb[sl]]} if bk == 0 else {"A": [vstb[sl]]}))
                for c4 in range(4):
                    B.dma("pool", v_s[c4, :, 4 * t:4 * t + 4, :], vst[sl][:, :, c4 * 128:(c4 + 1) * 128], vstb[sl],
                          store=True, reads=[vstb[sl]], acc=[vsb])

            p2_load(0)
            p2_sq(0)
            p2_norm(0)
            if ST > 1:
                p2_load(1)
                p2_sq(1)
            for t in range(ST):
                if t + 2 < ST:
                    p2_load(t + 2)
                    p2_sq(t + 2)
                if t + 1 < ST:
                    p2_norm(t + 1)
                p2_kv(t)
        B.barrier()
        if debug:
            with contextlib.ExitStack() as sd:
                dk = sbt(sd, "dk", [128, S], BF16); dkb = B.buf("dk")
                dv = sbt(sd, "dv", [128, NBLK, 128], BF16); dvb = B.buf("dv")
                for c4 in range(4):
                    B.dma("sp", dk[:], kT_s[c4], dkb, reads=[kTsb], writes=[dkb])
                    B.dma("sp", dbg["kT"][c4], dk[:], dkb, store=True, reads=[dkb])
                    B.dma("sp", dv[:], v_s[c4], dvb, reads=[vsb], writes=[dvb])
                    B.dma("sp", dbg["v"][c4], dv[:], dvb, store=True, reads=[dvb])
                B.barrier()

        if stop_after == "P2":
            B.barrier()
            B.emit()
            return nc
        s34 = contextlib.ExitStack()
        with contextlib.ExitStack() as s3:
            sbo = sbt(s3, "sbo", [128, 4, NQ], BF16); sbob = B.buf("sbo")
            kt = [sbt(s3, "kt%d" % i, [128, S], BF16) for i in range(2)]
            ktb = [B.buf("kt%d" % i) for i in range(2)]
            vt = [sbt(s3, "vt%d" % i, [128, NBLK, 128], BF16) for i in range(2)]
            vtb = [B.buf("vt%d" % i) for i in range(2)]
            NE = 2
            e_t = [sbt(s3, "e_t%d" % i, [128, 1024], F32) for i in range(NE)]
            e_b = [B.buf("e_t%d" % i) for i in range(NE)]
            NSP = 3
            sp_t = [sbt(s3, "sp_t%d" % i, [128, 1024], BF16) for i in range(NSP)]
            sp_b = [B.buf("sp_t%d" % i) for i in range(NSP)]
            a_t = [sbt(s3, "a_t%d" % i, [128, 1024], BF16) for i in range(NSP)]
            a_b = [B.buf("a_t%d" % i) for i in range(NSP)]
            R_t = [sbt(s3, "R_t%d" % i, [128, 512], F32) for i in range(2)]
            R_b = [B.buf("R_t%d" % i) for i in range(2)]
            Rb_t = [sbt(s3, "Rb_t%d" % i, [128, 512], BF16) for i in range(4)]
            Rb_b = [B.buf("Rb_t%d" % i) for i in range(4)]
            NZP = 3

            def p3_load(c):
                B.dma("sp", kt[c % 2][:], kT_s[c], ktb[c % 2], reads=[kTsb], writes=[ktb[c % 2]])
                B.dma("sp", vt[c % 2][:], v_s[c], vtb[c % 2], reads=[vsb], writes=[vtb[c % 2]])

            pairs = []
            chain_id = 0
            for c in range(4):
                for g in range(G):
                    nj = 8 * g + 8
                    for pi in range(nj // 2):
                        jh = nj - 1 - 2 * pi
                        pairs.append(dict(c=c, g=g, jh=jh, jl=jh - 1, pi=pi, last=(jh == 1), chain=chain_id))
                    chain_id += 1
            for s_i, T in enumerate(pairs):
                T["zs"] = s_i % NZP
                T["sl3"] = s_i % NSP
                T["esl"] = s_i % NE
                T["ob"] = 6 + (T["chain"] % 2)
                T["rs"] = T["chain"] % 2

            def st_P1(T):
                c, g = T["c"], T["g"]
                qv = Qz[:, c, g, :, :].rearrange("p h q -> p (h q)")
                for i, j in enumerate((T["jh"], T["jl"])):
                    bk_ = 2 * T["zs"] + i
                    mm(PS[bk_][:], kt[c % 2][:, j * 128:(j + 1) * 128], qv, start=True, stop=False,
                       R=[ktb[c % 2], Qzb], W=[PB[bk_]])
                    jj = j - 8 * g
                    if jj >= 0:
                        mm(PS[bk_][:], nBigI[:], notM[:, jj, :, :].rearrange("p h q -> p (h q)"),
                           start=False, stop=False, R=[nBigIb, notMb], A=[PB[bk_]])

            def zpair(T):
                zs = T["zs"]
                return PSALL[:, zs * 1024:(zs + 1) * 1024], [PB[2 * zs], PB[2 * zs + 1]]

            def st_A1a(T):
                zap, zbufs = zpair(T)
                e_ = e_t[T["esl"]]; eb = e_b[T["esl"]]
                act(e_[:], zap, AF.Exp, R=zbufs, W=[eb])

            def st_A1b(T):
                e_ = e_t[T["esl"]]; eb = e_b[T["esl"]]
                sp_ = sp_t[T["sl3"]]; spb = sp_b[T["sl3"]]
                act(sp_[:], e_[:], AF.Ln, R=[eb], W=[spb], bias=1.0)

            def st_P2(T):
                b0 = 2 * T["zs"]; b1 = b0 + 1
                sp_ = sp_t[T["sl3"]]; spb = sp_b[T["sl3"]]
                first = (T["pi"] == 0)
                rbi = (T["chain"] % 2) * 2 + (T["pi"] % 2)
                mm(PS[b0][:], nTin[:], sp_[:, 0:512], start=False, stop=first, R=[nTinb, spb], A=[PB[b0]])
                if not first:
                    mm(PS[b0][:], nOnes[:], Rb_t[rbi][:], start=False, stop=True, R=[nOnesb, Rb_b[rbi]], A=[PB[b0]])
                mm(PS[b1][:], nTin[:], sp_[:, 512:1024], start=False, stop=False, R=[nTinb, spb], A=[PB[b1]])
                mm(PS[b1][:], nOnes[:], sp_[:, 0:512], start=False, stop=first, R=[nOnesb, spb], A=[PB[b1]])
                if not first:
                    mm(PS[b1][:], nOnes[:], Rb_t[rbi][:], start=False, stop=True, R=[nOnesb, Rb_b[rbi]], A=[PB[b1]])

            def st_G(T):
                if T["last"]:
                    return
                sp_ = sp_t[T["sl3"]]; spb = sp_b[T["sl3"]]
                R_ = R_t[T["rs"]]; Rbuf = R_b[T["rs"]]
                rbn = (T["chain"] % 2) * 2 + ((T["pi"] + 1) % 2)
                if T["pi"] == 0:
                    tt("dve", R_[:], sp_[:, 0:512], sp_[:, 512:1024], ALU.add, R=[spb], W=[Rbuf])
                else:
                    tt("dve", R_[:], R_[:], sp_[:, 0:512], ALU.add, R=[spb, Rbuf], A=[Rbuf])
                    tt("dve", R_[:], R_[:], sp_[:, 512:1024], ALU.add, R=[spb, Rbuf], A=[Rbuf])
                cp("dve", Rb_t[rbn][:], R_[:], R=[Rbuf], W=[Rb_b[rbn]])

            def st_A2(T):
                zap, zbufs = zpair(T)
                a_ = a_t[T["sl3"]]; ab = a_b[T["sl3"]]
                act(a_[:], zap, AF.Exp, R=zbufs, W=[ab])

            def st_P3(T):
                c, g = T["c"], T["g"]
                a_ = a_t[T["sl3"]]; ab = a_b[T["sl3"]]
                ob = T["ob"]
                first = (T["pi"] == 0)
                mm(PS[ob][:], vt[c % 2][:, T["jh"], :], a_[:, 0:512], start=first, stop=False, R=[vtb[c % 2], ab],
                   **({"W": [PB[ob]]} if first else {"A": [PB[ob]]}))
                mm(PS[ob][:], vt[c % 2][:, T["jl"], :], a_[:, 512:1024], start=False, stop=T["last"],
                   R=[vtb[c % 2], ab], A=[PB[ob]])
                if T["last"]:
                    q0 = g * 256
                    for h in range(2):
                        cp("dve", sbo[h * 64:(h + 1) * 64, c, q0:q0 + 256],
                           PS[ob][h * 64:(h + 1) * 64, h * 256:(h + 1) * 256], R=[PB[ob]], A=[sbob])

            p3_load(0)
            n_t = len(pairs)
            for s_i in range(n_t + 2):
                if 0 <= s_i - 2 < n_t:
                    T2 = pairs[s_i - 2]
                    if T2["g"] == 0 and T2["pi"] == 0 and T2["c"] + 1 < 4:
                        p3_load(T2["c"] + 1)
                if s_i < n_t:
                    T = pairs[s_i]
                    st_P1(T)
                    st_A1a(T)
                if 0 <= s_i - 1 < n_t:
                    T1 = pairs[s_i - 1]
                    st_P2(T1)
                    st_G(T1)
                    st_A2(T1)
                if s_i < n_t:
                    st_A1b(pairs[s_i])
                if 0 <= s_i - 2 < n_t:
                    st_P3(pairs[s_i - 2])
            B.barrier()
            if debug:
                B.dma("sp", dbg["sbo"], sbo[:], sbob, store=True, reads=[sbob])
                B.barrier()
            if stop_after == "P3":
                B.barrier()
                B.emit()
                return nc
            sbo_s = nc.dram_tensor("sbo_s", [128, 4, NQ], BF16).ap()
            sbosb = B.buf("sbo_s")
            B.dma("sp", sbo_s, sbo[:], sbob, store=True, reads=[sbob], writes=[sbosb])
            B.barrier()
        s13.close()

        with contextlib.ExitStack() as s4:
            xn = sbt(s4, "xn", [128, 8, NB, 144], BF16); xnb = B.buf("xn")
            B.dma("sp", xn[:].rearrange("p c b t -> p c (b t)"), xn_s, xnb, reads=[xnsb], writes=[xnb])
            NOUT = 3
            o_t = [sbt(s4, "o_t%d" % i, [128, TO], F32) for i in range(NOUT)]
            o_b = [B.buf("o_t%d" % i) for i in range(NOUT)]
            sg_t = [sbt(s4, "sg_t%d" % i, [128, TO], F32) for i in range(2)]
            sg_b = [B.buf("sg_t%d" % i) for i in range(2)]
            cnt4 = {"o": 0, "sg": 0}

            def gate_mm(pb, wg, wgb, dc, t):
                for k in range(8):
                    mm(PS[pb][:, 0:TO].rearrange("p (b t) -> p b t", b=BPT),
                       wg[:, k, dc * 128:(dc + 1) * 128], xn[:, k, t * BPT:(t + 1) * BPT, 16:144],
                       start=(k == 0), stop=(k == 7), R=[wgb, xnb],
                       **({"W": [PB[pb]]} if k == 0 else {"A": [PB[pb]]}))

            def gated(ypb, gpb):
                si = cnt4["sg"] % 2; cnt4["sg"] += 1
                oi = cnt4["o"] % NOUT; cnt4["o"] += 1
                act(sg_t[si][:], PS[gpb][:, 0:TO], AF.Sigmoid, R=[PB[gpb]], W=[sg_b[si]])
                tt("dve", o_t[oi][:], sg_t[si][:], PS[ypb][:, 0:TO], ALU.mult, R=[sg_b[si], PB[ypb]], W=[o_b[oi]])
                return o_t[oi], o_b[oi]

            c0sb = B.buf("c0_s")
            with contextlib.ExitStack() as sa:
                wpi = sbt(sa, "wpi", [128, 8, 512], BF16); wpib = B.buf("wpi")
                wmix = sbt(sa, "wmix", [128, 4, 128], BF16); wmixb = B.buf("wmix")
                wpo = sbt(sa, "wpo", [128, 4, D], BF16); wpob = B.buf("wpo")
                wg0 = sbt(sa, "wg0", [128, 8, D], BF16); wg0b = B.buf("wg0")
                new_stg(sa)
                load_w(wpi, wpib, w_in[:, 0:512], 8, 512, gcol=V_PRE)
                load_w(wmix, wmixb, w_pmix, 4, 128)
                load_w(wpo, wpob, w_po, 4, D)
                load_w(wg0, wg0b, w_in[:, O_G:O_G + D], 8, D, gcol=V_PRE)
                pa = [sbt(sa, "pa%d" % i, [128, BPT, 144], F32) for i in range(3)]
                pab = [B.buf("pa%d" % i) for i in range(3)]
                icn = sbt(sa, "icn", [128, 4, TO], F32); icnb = B.buf("icn")
                mixd = sbt(sa, "mixd", [128, 4, TO], BF16); mixdb = B.buf("mixd")
                pm = [sbt(sa, "pm%d" % i, [128, 4, TO], BF16) for i in range(2)]
                pmb = [B.buf("pm%d" % i) for i in range(2)]
                tmpa = sbt(sa, "tmpa", [128, BPT, 128], F32); tmpab = B.buf("tmpa")
                HB = TOH // 2
                HBK = BPT // 2

                def pa_pool(t):
                    for gi in range(4):
                        ts("dve", icn[:, gi, :], qpos[:, t * TO:(t + 1) * TO], 1.0, float(2 << gi), ALU.add, ALU.min,
                           R=[qposb], **({"W": [icnb]} if gi == 0 else {"A": [icnb]}))
                    B.op("dve", lambda e: e.reciprocal(out=icn[:], in_=icn[:]), reads=[icnb], writes=[icnb])
                    for gi in range(4):
                        p0, p0b = pa[0], pab[0]
                        for hf in range(2):
                            pb = hf
                            for k in range(8):
                                mm(PS[pb][:, 0:HB].rearrange("p (b t) -> p b t", b=HBK),
                                   wpi[:, k, gi * 128:(gi + 1) * 128],
                                   xn[:, k, t * BPT + hf * HBK:t * BPT + (hf + 1) * HBK, :],
                                   start=(k == 0), stop=(k == 7), R=[wpib, xnb],
                                   **({"W": [PB[pb]]} if k == 0 else {"A": [PB[pb]]}))
                            cp("act", p0[:, hf * HBK:(hf + 1) * HBK, :],
                               PS[pb][:, 0:HB].rearrange("p (b t) -> p b t", b=HBK), R=[PB[pb]],
                               **({"W": [p0b]} if hf == 0 else {"A": [p0b]}))
                        cur, curb = p0, p0b
                        pp = 1
                        dsh = 1
                        lo = 0
                        for step in range(gi + 1):
                            nxt, nxtb = pa[pp], pab[pp]
                            lo = lo + dsh
                            tt("dve" if step % 2 == 0 else "pool", nxt[:, :, lo:144], cur[:, :, lo:144],
                               cur[:, :, lo - dsh:144 - dsh], ALU.add, R=[curb], W=[nxtb])
                            cur, curb = nxt, nxtb
                            pp = 3 - pp
                            dsh *= 2
                        tt("dve", tmpa[:], cur[:, :, 16:144], icn[:, gi, :].rearrange("p (b t) -> p b t", b=BPT),
                           ALU.mult, R=[curb, icnb], W=[tmpab])
                        tt("dve", mixd[:, gi, :].rearrange("p (b t) -> p b t", b=BPT), tmpa[:], p0[:, :, 16:144],
                           ALU.subtract, R=[tmpab, p0b], **({"W": [mixdb]} if gi == 0 else {"A": [mixdb]}))
                        yield
                    for gi in range(4):
                        pb = 2 + (gi % 2)
                        mm(PS[pb][:, 0:TO], wmix[:, gi, :], mixd[:, gi, :], start=True, stop=True,
                           R=[wmixb, mixdb], W=[PB[pb]])
                        ts("dve", pm[t % 2][:, gi, :], PS[pb][:, 0:TO], vec[:, V_PSC + gi:V_PSC + gi + 1], None,
                           ALU.mult, None, R=[PB[pb], vecb], **({"W": [pmb[t % 2]]} if gi == 0 else {"A": [pmb[t % 2]]}))
                    yield

                def pa_proj(t):
                    for dc in range(8):
                        ypb = 4 + (dc % 2)
                        gpb = 6 + (dc % 2)
                        for gi in range(4):
                            mm(PS[ypb][:, 0:TO], wpo[:, gi, dc * 128:(dc + 1) * 128], pm[t % 2][:, gi, :],
                               start=(gi == 0), stop=(gi == 3), R=[wpob, pmb[t % 2]],
                               **({"W": [PB[ypb]]} if gi == 0 else {"A": [PB[ypb]]}))
                        gate_mm(gpb, wg0, wg0b, dc, t)
                        ot, otb = gated(ypb, gpb)
                        B.dma("sp", c0_s[dc * 128:(dc + 1) * 128, t * TO:(t + 1) * TO], ot[:], otb, store=True,
                              reads=[otb], acc=[c0sb])
                        yield

                run(pa_pool(0))
                for t in range(NT):
                    if t + 1 < NT:
                        interleave(pa_proj(t), pa_pool(t + 1))
                    else:
                        run(pa_proj(t))
                B.barrier()

            c2sb = B.buf("c2_s")
            with contextlib.ExitStack() as sc:
                wxq = sbt(sc, "wxq", [128, 8, 256], BF16); wxqb = B.buf("wxq")
                wkv = sbt(sc, "wkv", [128, 8, 512], BF16); wkvb = B.buf("wkv")
                wxo = sbt(sc, "wxo", [128, 2, D], BF16); wxob = B.buf("wxo")
                wg2 = sbt(sc, "wg2", [128, 8, D], BF16); wg2b = B.buf("wg2")
                new_stg(sc)
                load_w(wkv, wkvb, w_kv, 8, 512, gcol=V_MEM)
                load_w(wxq, wxqb, w_in[:, O_XQ:O_XQ + 256], 8, 256, gcol=V_PRE)
                load_w(wxo, wxob, w_xo, 2, D)
                load_w(wg2, wg2b, w_in[:, O_G + 2 * D:O_G + 3 * D], 8, D, gcol=V_PRE)
                m_t = sbt(sc, "m_t", [128, 8, 256], F32); m_b = B.buf("m_t")
                msq = sbt(sc, "msq", [128, 8, 256], BF16); msqb = B.buf("msq")
                mln = sbt(sc, "mln", [128, 256], F32); mlnb = B.buf("mln")
                mrs = sbt(sc, "mrs", [128, 256], F32); mrsb = B.buf("mrs")
                mn = sbt(sc, "mn", [128, 8, 256], BF16); mnb = B.buf("mn")
                mkT = sbt(sc, "mkT", [128, 2, 256], BF16); mkTb = B.buf("mkT")
                mvz = sbt(sc, "mvz", [128, 4, 2, 128], BF16); mvb = B.buf("mvz")
                xqz = sbt(sc, "xqz", [128, 4, TO], BF16); xqTb = B.buf("xqz")
                B.op("pool", lambda e: e.memset(mvz[:].rearrange("p a b c -> p (a b c)"), 0.0), writes=[mvb])
                B.op("pool", lambda e: e.memset(xqz[:].rearrange("p a b -> p (a b)"), 0.0), writes=[xqTb])
                nmx = sbt(sc, "nmx", [128, 4], F32); nmxb = B.buf("nmx")
                ssum = sbt(sc, "ssum", [128, 4], F32); ssumb = B.buf("ssum")
                rsm = sbt(sc, "rsm", [128, 4], F32); rsmb = B.buf("rsm")
                P_t = sbt(sc, "P_t", [128, 4, 256], F32); P_b = B.buf("P_t")
                Pn = sbt(sc, "Pn", [128, 4, 256], BF16); Pnb = B.buf("Pn")
                PT = sbt(sc, "PT", [128, 8, 128], BF16); PTb = B.buf("PT")
                xoT = sbt(sc, "xoT", [128, 2, TO], BF16); xoTb = B.buf("xoT")

                B.dma("sp", m_t[:], memT.rearrange("(c p) n -> p c n", p=128), m_b, writes=[m_b])
                act(msq[:], m_t[:], AF.Square, R=[m_b], W=[msqb])
                for k in range(8):
                    mm(PS[0][:, 0:256], ones[:], msq[:, k, :], start=(k == 0), stop=(k == 7), R=[onesb, msqb],
                       **({"W": [PB[0]]} if k == 0 else {"A": [PB[0]]}))
                rstd_from(PS[0][:, 0:256], PB[0], 256, mln[:], mlnb, mrs[:], mrsb)
                for k in range(8):
                    stt(mn[:, k, :], m_t[:, k, :], vec[:, V_MEM + k:V_MEM + k + 1], mrs[:], R=[m_b, mrsb, vecb],
                        **({"W": [mnb]} if k == 0 else {"A": [mnb]}))
                for ch in range(2):
                    for k in range(8):
                        mm(PS[1][:, 0:256], wkv[:, k, ch * 128:(ch + 1) * 128], mn[:, k, :], start=(k == 0), stop=(k == 7),
                           R=[wkvb, mnb], **({"W": [PB[1]]} if k == 0 else {"A": [PB[1]]}))
                    cp("dve", mkT[:, ch, :], PS[1][:, 0:256], R=[PB[1]], **({"W": [mkTb]} if ch == 0 else {"A": [mkTb]}))
                for mc in range(2):
                    for k in range(8):
                        mm(PS[2][:, 0:256], mn[:, k, mc * 128:(mc + 1) * 128], wkv[:, k, 256:512], start=(k == 0), stop=(k == 7),
                           R=[wkvb, mnb], **({"W": [PB[2]]} if k == 0 else {"A": [PB[2]]}))
                    for h in range(4):
                        cp("dve", mvz[:, h, mc, (h % 2) * 64:(h % 2 + 1) * 64], PS[2][:, h * 64:(h + 1) * 64],
                           R=[PB[2]], A=[mvb])

                for t in range(NT):
                    for ch in range(2):
                        for k in range(8):
                            mm(PS[3][:, 0:TO].rearrange("p (b t) -> p b t", b=BPT),
                               wxq[:, k, ch * 128:(ch + 1) * 128], xn[:, k, t * BPT:(t + 1) * BPT, 16:144],
                               start=(k == 0), stop=(k == 7), R=[wxqb, xnb],
                               **({"W": [PB[3]]} if k == 0 else {"A": [PB[3]]}))
                        for hp in range(2):
                            act(xqz[hp * 64:(hp + 1) * 64, ch * 2 + hp, :], PS[3][hp * 64:(hp + 1) * 64, 0:TO], AF.Copy,
                                R=[PB[3]], A=[xqTb], scale=0.125)
                    for bk in range(BPT):
                        for h in range(4):
                            pb = h // 2
                            hp = h % 2
                            mm(PS[pb][:, hp * 256:(hp + 1) * 256],
                               xqz[:, h, bk * 128:(bk + 1) * 128], mkT[:, h // 2, :],
                               start=True, stop=True, R=[xqTb, mkTb],
                               **({"W": [PB[pb]]} if hp == 0 else {"A": [PB[pb]]}))
                        for pb in range(2):
                            B.op("dve", lambda e, pb=pb: e.reduce_max(
                                out=nmx[:, 2 * pb:2 * pb + 2], in_=PS[pb][:].rearrange("p (h m) -> p h m", h=2),
                                axis=AX.X, negate=True), reads=[PB[pb]],
                                **({"writes": [nmxb]} if pb == 0 else {"acc": [nmxb]}))
                        for h in range(4):
                            pb = h // 2
                            hp = h % 2
                            act(P_t[:, h, :], PS[pb][:, hp * 256:(hp + 1) * 256], AF.Exp, R=[PB[pb], nmxb],
                                **({"W": [P_b, ssumb]} if h == 0 else {"A": [P_b, ssumb]}),
                                bias=nmx[:, h:h + 1], accum_out=ssum[:, h:h + 1])
                        B.op("dve", lambda e: e.reciprocal(out=rsm[:], in_=ssum[:]), reads=[ssumb], writes=[rsmb])
                        for h in range(4):
                            ts("dve", Pn[:, h, :], P_t[:, h, :], rsm[:, h:h + 1], None, ALU.mult, None,
                               R=[P_b, rsmb], **({"W": [Pnb]} if h == 0 else {"A": [Pnb]}))
                        for h in range(4):
                            for mc in range(2):
                                i8 = h * 2 + mc
                                B.op("pe", lambda e, h=h, mc=mc, i8=i8: e.transpose(
                                    PST[:, i8 * 128:(i8 + 1) * 128], Pn[:, h, mc * 128:(mc + 1) * 128], ident[:]),
                                    reads=[Pnb, identb], **({"writes": [PSTB]} if i8 == 0 else {"acc": [PSTB]}))
                        cp("act", PT[:].rearrange("p a b -> p (a b)"), PST[:], R=[PSTB], W=[PTb])
                        for ch in range(2):
                            pb = 2 + ch
                            n4 = 0
                            for h in (2 * ch, 2 * ch + 1):
                                for mc in range(2):
                                    mm(PS[pb][:, bk * 128:(bk + 1) * 128], mvz[:, h, mc, :], PT[:, h * 2 + mc, :],
                                       start=(n4 == 0), stop=(n4 == 3), R=[mvb, PTb],
                                       **({"W": [PB[pb]]} if (bk == 0 and n4 == 0) else {"A": [PB[pb]]}))
                                    n4 += 1
                    for ch in range(2):
                        cp("dve", xoT[:, ch, :], PS[2 + ch][:, 0:TO], R=[PB[2 + ch]],
                           **({"W": [xoTb]} if ch == 0 else {"A": [xoTb]}))
                    for dc in range(8):
                        ypb = 4 + (dc % 2)
                        gpb = 6
                        for ch in range(2):
                            mm(PS[ypb][:, 0:TO], wxo[:, ch, dc * 128:(dc + 1) * 128], xoT[:, ch, :],
                               start=(ch == 0), stop=(ch == 1), R=[wxob, xoTb],
                               **({"W": [PB[ypb]]} if ch == 0 else {"A": [PB[ypb]]}))
                        gate_mm(gpb, wg2, wg2b, dc, t)
                        ot, otb = gated(ypb, gpb)
                        B.dma("pool", c2_s[dc * 128:(dc + 1) * 128, t * TO:(t + 1) * TO], ot[:], otb, store=True,
                              reads=[otb], acc=[c2sb])
                B.barrier()
            if debug:
                with contextlib.ExitStack() as sd:
                    dd = sbt(sd, "dd", [128, 8, NQ], F32); ddb = B.buf("dd")
                    for nm, src, srcb in (("c0", c0_s, c0sb), ("c2", c2_s, c2sb)):
                        B.dma("sp", dd[:], src.rearrange("(c p) n -> p c n", p=128), ddb, reads=[srcb], writes=[ddb])
                        B.dma("sp", dbg[nm].rearrange("(c p) n -> p c n", p=128), dd[:], ddb, store=True, reads=[ddb])
                    B.barrier()

            if stop_after == "P4c":
                B.barrier()
                B.emit()
                return nc
            h1sb = B.buf("h1_s"); n2sb = B.buf("n2_s")
            with contextlib.ExitStack() as sd4:
                wsbo = sbt(sd4, "wsbo", [128, 4, D], BF16); wsbob = B.buf("wsbo")
                wg1 = sbt(sd4, "wg1", [128, 8, D], BF16); wg1b = B.buf("wg1")
                wout = sbt(sd4, "wout", [128, 8, D], BF16); woutb = B.buf("wout")
                new_stg(sd4)
                load_w(wsbo, wsbob, w_sbo, 4, D)
                load_w(wg1, wg1b, w_in[:, O_G + D:O_G + 2 * D], 8, D, gcol=V_PRE)
                load_w(wout, woutb, w_out, 8, D)
                sbo4 = sbt(sd4, "sbo4", [128, 4, NQ], BF16); sbo4b = B.buf("sbo4")
                B.dma("sp", sbo4[:], sbo_s, sbo4b, reads=[sbosb], writes=[sbo4b])
                cl = [sbt(sd4, "cl%d" % i, [128, 2, TO], F32) for i in range(2)]
                clb = [B.buf("cl%d" % i) for i in range(2)]
                mg1 = sbt(sd4, "mg", [128, 8, TO], BF16)
                mg = [mg1, mg1]
                mgb1 = B.buf("mg")
                mgb = [mgb1, mgb1]
                mo = [sbt(sd4, "mo%d" % i, [128, 8, TO], F32) for i in range(2)]
                mob = [B.buf("mo%d" % i) for i in range(2)]
                sqm = [sbt(sd4, "sqm%d" % i, [128, TO], BF16) for i in range(2)]
                sqmb = [B.buf("sqm%d" % i) for i in range(2)]
                lnm = sbt(sd4, "lnm", [128, TO], F32); lnmb = B.buf("lnm")
                rsm1 = sbt(sd4, "rsm1", [128, TO], F32); rsm1b = B.buf("rsm1")
                rsm2 = sbt(sd4, "rsm2", [128, TO], F32); rsm2b = B.buf("rsm2")
                xr = [sbt(sd4, "xr%d" % i, [128, BPT, 128], F32) for i in range(2)]
                xrb = [B.buf("xr%d" % i) for i in range(2)]
                n2t = sbt(sd4, "n2t", [128, 8, TO], BF16); n2tb = B.buf("n2t")
                xoh4 = xoh.rearrange("d (b t) -> d b t", t=144)

                def bd_S1(t):
                    for dc in range(8):
                        ci = dc % 2
                        B.dma("sp", cl[ci][:, 0, :], c0_s[dc * 128:(dc + 1) * 128, t * TO:(t + 1) * TO], clb[ci],
                              reads=[c0sb], writes=[clb[ci]])
                        B.dma("sp", cl[ci][:, 1, :], c2_s[dc * 128:(dc + 1) * 128, t * TO:(t + 1) * TO], clb[ci],
                              reads=[c2sb], acc=[clb[ci]])
                        ypb = 4 + (dc % 2)
                        gpb = (dc % 2)
                        for c4 in range(4):
                            mm(PS[ypb][:, 0:TO], wsbo[:, c4, dc * 128:(dc + 1) * 128], sbo4[:, c4, t * TO:(t + 1) * TO],
                               start=(c4 == 0), stop=(c4 == 3), R=[wsbob, sbo4b],
                               **({"W": [PB[ypb]]} if c4 == 0 else {"A": [PB[ypb]]}))
                        gate_mm(gpb, wg1, wg1b, dc, t)
                        ot, otb = gated(ypb, gpb)
                        tt("pool", cl[ci][:, 0, :], cl[ci][:, 0, :], cl[ci][:, 1, :], ALU.add, R=[clb[ci]], A=[clb[ci]])
                        tt("dve", mg[t % 2][:, dc, :], ot[:], cl[ci][:, 0, :], ALU.add, R=[otb, clb[ci]],
                           **({"W": [mgb[t % 2]]} if dc == 0 else {"A": [mgb[t % 2]]}))
                        yield

                def bd_S2(t):
                    m_, mb_ = mo[t % 2], mob[t % 2]
                    for dc in range(8):
                        pb = 2 + (dc % 2)
                        for k in range(8):
                            mm(PS[pb][:, 0:TO], wout[:, k, dc * 128:(dc + 1) * 128], mg[t % 2][:, k, :],
                               start=(k == 0), stop=(k == 7), R=[woutb, mgb[t % 2]],
                               **({"W": [PB[pb]]} if k == 0 else {"A": [PB[pb]]}))
                        cp("dve", m_[:, dc, :], PS[pb][:, 0:TO], R=[PB[pb]], **({"W": [mb_]} if dc == 0 else {"A": [mb_]}))
                        act(sqm[dc % 2][:], m_[:, dc, :], AF.Square, R=[mb_], W=[sqmb[dc % 2]])
                        mm(PS[6][:, 0:TO], ones[:], sqm[dc % 2][:], start=(dc == 0), stop=(dc == 7),
                           R=[onesb, sqmb[dc % 2]], **({"W": [PB[6]]} if dc == 0 else {"A": [PB[6]]}))

                def bd_S3(t):
                    m_, mb_ = mo[t % 2], mob[t % 2]
                    rstd_from(PS[6][:, 0:TO], PB[6], TO, lnm[:], lnmb, rsm1[:], rsm1b)
                    for dc in range(8):
                        xi = dc % 2
                        B.dma("sp", xr[xi][:], xoh4[dc * 128:(dc + 1) * 128, t * BPT:(t + 1) * BPT, 16:144], xrb[xi],
                              writes=[xrb[xi]])
                        stt(m_[:, dc, :], m_[:, dc, :], vec[:, V_POST + dc:V_POST + dc + 1], rsm1[:],
                            R=[mb_, vecb, rsm1b], A=[mb_])
                        tt("pool", m_[:, dc, :], m_[:, dc, :], xr[xi][:].rearrange("p b t -> p (b t)"), ALU.add,
                           R=[mb_, xrb[xi]], A=[mb_])
                        act(sqm[dc % 2][:], m_[:, dc, :], AF.Square, R=[mb_], W=[sqmb[dc % 2]])
                        mm(PS[7][:, 0:TO], ones[:], sqm[dc % 2][:], start=(dc == 0), stop=(dc == 7),
                           R=[onesb, sqmb[dc % 2]], **({"W": [PB[7]]} if dc == 0 else {"A": [PB[7]]}))
                        yield
                    B.dma("sp", h1_s[:, t * TO:(t + 1) * TO].rearrange("(c p) n -> p c n", p=128), m_[:], mb_, store=True,
                          reads=[mb_], acc=[h1sb])
                    rstd_from(PS[7][:, 0:TO], PB[7], TO, lnm[:], lnmb, rsm2[:], rsm2b)
                    for dc in range(8):
                        stt(n2t[:, dc, :], m_[:, dc, :], vec[:, V_FPRE + dc:V_FPRE + dc + 1], rsm2[:],
                            R=[mb_, rsm2b, vecb], **({"W": [n2tb]} if dc == 0 else {"A": [n2tb]}))
                    B.dma("sp", n2_s[:, t * TO:(t + 1) * TO].rearrange("(c p) n -> p c n", p=128), n2t[:], n2tb, store=True,
                          reads=[n2tb], acc=[n2sb])

                run(bd_S1(0))
                for t in range(NT):
                    bd_S2(t)
                    if t + 1 < NT:
                        interleave(bd_S3(t), bd_S1(t + 1))
                    else:
                        run(bd_S3(t))
                B.barrier()
        if debug:
            with contextlib.ExitStack() as sd:
                dd = sbt(sd, "dd2", [128, 8, NQ], F32); ddb = B.buf("dd2")
                B.dma("sp", dd[:], h1_s.rearrange("(c p) n -> p c n", p=128), ddb, reads=[h1sb], writes=[ddb])
                B.dma("sp", dbg["h1"].rearrange("(c p) n -> p c n", p=128), dd[:], ddb, store=True, reads=[ddb])
                B.barrier()

        if stop_after == "P4":
            B.barrier()
            B.emit()
            return nc
        with contextlib.ExitStack() as s56:
            actT = sbt(s56, "actT", [128, NFC, NQ], BF16); actTb = B.buf("actT")
            wfo = sbt(s56, "wfo", [128, NFC, D], BF16); wfob = B.buf("wfo")
            with contextlib.ExitStack() as s5:
                n2 = sbt(s5, "n2", [128, 8, NQ], BF16); n2b = B.buf("n2")
                new_stg(s5)
                B.dma("sp", n2[:], n2_s.rearrange("(c p) n -> p c n", p=128), n2b, reads=[n2sb], writes=[n2b])
                wfi = [sbt(s5, "wfi%d" % i, [128, 8, 256], BF16) for i in range(2)]
                wfib = [B.buf("wfi%d" % i) for i in range(2)]
                sil = [sbt(s5, "sil%d" % i, [128, TO], F32) for i in range(2)]
                silb = [B.buf("sil%d" % i) for i in range(2)]

                def p5_loadw(f):
                    sl = f % 2
                    load_w(wfi[sl][:, :, 0:128], wfib[sl], w_fi[:, f * 128:(f + 1) * 128], 8, 128, gcol=V_FPRE, first=True)
                    load_w(wfi[sl][:, :, 128:256], wfib[sl], w_fi[:, DFF + f * 128:DFF + (f + 1) * 128], 8, 128,
                           gcol=V_FPRE, first=False)

                p5_loadw(0)
                cnt5 = 0
                for f in range(NFC):
                    if f + 1 < NFC:
                        p5_loadw(f + 1)
                    load_w(wfo[:, f:f + 1, :], wfob, w_fo[f * 128:(f + 1) * 128, :], 1, D, first=(f == 0))
                    sl = f % 2
                    for t in range(NT):
                        gp = (cnt5 % 2) * 2
                        up = gp + 1
                        si = cnt5 % 2
                        cnt5 += 1
                        for k in range(8):
                            mm(PS[gp][:, 0:TO], wfi[sl][:, k, 0:128], n2[:, k, t * TO:(t + 1) * TO],
                               start=(k == 0), stop=(k == 7), R=[wfib[sl], n2b],
                               **({"W": [PB[gp]]} if k == 0 else {"A": [PB[gp]]}))
                        for k in range(8):
                            mm(PS[up][:, 0:TO], wfi[sl][:, k, 128:256], n2[:, k, t * TO:(t + 1) * TO],
                               start=(k == 0), stop=(k == 7), R=[wfib[sl], n2b],
                               **({"W": [PB[up]]} if k == 0 else {"A": [PB[up]]}))
                        act(sil[si][:], PS[gp][:, 0:TO], AF.Silu, R=[PB[gp]], W=[silb[si]])
                        tt("dve", actT[:, f, t * TO:(t + 1) * TO], sil[si][:], PS[up][:, 0:TO], ALU.mult,
                           R=[silb[si], PB[up]], A=[actTb])
                B.barrier()
            with contextlib.ExitStack() as s6:
                ff = [sbt(s6, "ff%d" % i, [128, 8, TO], F32) for i in range(2)]
                ffb = [B.buf("ff%d" % i) for i in range(2)]
                sq6 = [sbt(s6, "sq6_%d" % i, [128, TO], BF16) for i in range(2)]
                sq6b = [B.buf("sq6_%d" % i) for i in range(2)]
                ln6 = sbt(s6, "ln6", [128, TO], F32); ln6b = B.buf("ln6")
                rs6 = sbt(s6, "rs6", [128, TO], F32); rs6b = B.buf("rs6")
                h1t = [sbt(s6, "h1t%d" % i, [128, TO], F32) for i in range(3)]
                h1tb = [B.buf("h1t%d" % i) for i in range(3)]
                outsb = B.buf("outT")
                cnt6 = {"h": 0}

                def p6_M(t):
                    f_, fb_ = ff[t % 2], ffb[t % 2]
                    ssb = 4 + (t % 2)
                    for dc in range(8):
                        pb = dc % 4
                        for f in range(NFC):
                            mm(PS[pb][:, 0:TO], wfo[:, f, dc * 128:(dc + 1) * 128], actT[:, f, t * TO:(t + 1) * TO],
                               start=(f == 0), stop=(f == NFC - 1), R=[wfob, actTb],
                               **({"W": [PB[pb]]} if f == 0 else {"A": [PB[pb]]}))
                        cp("dve", f_[:, dc, :], PS[pb][:, 0:TO], R=[PB[pb]], **({"W": [fb_]} if dc == 0 else {"A": [fb_]}))
                        act(sq6[dc % 2][:], f_[:, dc, :], AF.Square, R=[fb_], W=[sq6b[dc % 2]])
                        mm(PS[ssb][:, 0:TO], ones[:], sq6[dc % 2][:], start=(dc == 0), stop=(dc == 7),
                           R=[onesb, sq6b[dc % 2]], **({"W": [PB[ssb]]} if dc == 0 else {"A": [PB[ssb]]}))
                        yield

                def p6_E(t):
                    f_, fb_ = ff[t % 2], ffb[t % 2]
                    ssb = 4 + (t % 2)
                    rstd_from(PS[ssb][:, 0:TO], PB[ssb], TO, ln6[:], ln6b, rs6[:], rs6b)
                    for dc in range(8):
                        hi = cnt6["h"] % 3
                        cnt6["h"] += 1
                        B.dma("sp", h1t[hi][:], h1_s[dc * 128:(dc + 1) * 128, t * TO:(t + 1) * TO], h1tb[hi],
                              reads=[h1sb], writes=[h1tb[hi]])
                        stt(f_[:, dc, :], f_[:, dc, :], vec[:, V_FPOST + dc:V_FPOST + dc + 1], rs6[:],
                            R=[fb_, vecb, rs6b], A=[fb_])
                        tt("pool", f_[:, dc, :], f_[:, dc, :], h1t[hi][:], ALU.add, R=[fb_, h1tb[hi]], A=[fb_])
                        yield
                    B.dma("sp", outT[:, t * TO:(t + 1) * TO].rearrange("(c p) n -> p c n", p=128), f_[:], fb_, store=True,
                          reads=[fb_], acc=[outsb])

                run(p6_M(0))
                for t in range(NT):
                    if t + 1 < NT:
                        interleave(p6_M(t + 1), p6_E(t))
                    else:
                        run(p6_E(t))
                B.barrier()
        B.barrier()
        B.emit()
    return nc


def _own_blocks(r, G):
    blks = []
    for g in range(G):
        blks.append(8 * g + r)
        blks.append(8 * g + 7 - r)
    return blks


def _pack_vecs(norm_mix_pre, norm_mem, norm_mix_post, norm_ffn_pre, norm_ffn_post, pool_scale):
    cols = []
    for v in (norm_mix_pre, norm_mem, norm_mix_post, norm_ffn_pre, norm_ffn_post):
        cols.append(np.asarray(v, np.float32).reshape(8, 128).T)
    cols.append(np.asarray(pool_scale, np.float32).reshape(4, 128).T)
    return np.ascontiguousarray(np.concatenate(cols, axis=1))


_PROG_CACHE = {}


def run_layer(x, mem, norm_mix_pre, w_in, w_pool_mix, pool_scale, w_pool_o, w_sb_o, norm_mem, w_mem_kv, w_x_o,
              w_out, norm_mix_post, norm_ffn_pre, w_ffn_in, w_ffn_out, norm_ffn_post, debug=False, stop_after=None):
    x = np.asarray(x, np.float32)
    mem = np.asarray(mem, np.float32)
    Bn, S, _ = x.shape
    G = S // 1024
    assert Bn == 2 and S == 1024 * G
    key = (G, debug, stop_after)
    if key not in _PROG_CACHE:
        _PROG_CACHE[key] = build_program(G, debug, stop_after)
    nc = _PROG_CACHE[key]
    f32 = lambda a: np.ascontiguousarray(np.asarray(a, np.float32))
    shared = {
        "w_in": f32(w_in[0]), "w_pmix": f32(np.asarray(w_pool_mix[0]).reshape(512, 128)), "w_po": f32(w_pool_o[0]),
        "w_sbo": f32(w_sb_o[0]), "w_kv": f32(w_mem_kv[0]), "w_xo": f32(w_x_o[0]), "w_out": f32(w_out[0]),
        "w_fi": f32(w_ffn_in[0]), "w_fo": f32(w_ffn_out[0]),
        "vecs": _pack_vecs(norm_mix_pre[0], norm_mem[0], norm_mix_post[0], norm_ffn_pre[0], norm_ffn_post[0],
                           pool_scale[0]),
    }
    xTb = [np.ascontiguousarray(x[b].T) for b in range(2)]
    memTb = [np.ascontiguousarray(mem[b].T) for b in range(2)]
    in_maps = []
    for core in range(8):
        b, r = core // 4, core % 4
        blks = _own_blocks(r, G)
        xoh = np.zeros((D, len(blks) * 144), np.float32)
        posv = np.zeros((len(blks) * 128,), np.float32)
        for n, blk in enumerate(blks):
            s0 = blk * 128
            lo = max(0, s0 - 16)
            xoh[:, n * 144 + 16 - (s0 - lo):(n + 1) * 144] = xTb[b][:, lo:s0 + 128]
            posv[n * 128:(n + 1) * 128] = np.arange(s0, s0 + 128, dtype=np.float32)
        m = dict(shared)
        m.update({"xT": xTb[b], "xoh": xoh, "pos": posv, "memT": memTb[b]})
        in_maps.append(m)
    res = run_bass_kernel_spmd(nc, in_maps, core_ids=list(range(8)))
    out = np.empty((2, S, D), np.float32)
    for core in range(8):
        b, r = core // 4, core % 4
        oT = np.asarray(res.results[core]["outT"])
        for n, blk in enumerate(_own_blocks(r, G)):
            out[b, blk * 128:(blk + 1) * 128, :] = oT[:, n * 128:(n + 1) * 128].T
    if debug:
        return out, res.results
    return out


def kernel(**inputs):
    return run_layer(**inputs)
```

```python
import contextlib
import numpy as np
import concourse.bass as bass
import concourse.mybir as mybir
from concourse.bass_utils import run_bass_kernel_spmd

F32 = mybir.dt.float32
BF16 = mybir.dt.bfloat16
I32 = mybir.dt.int32
AF = mybir.ActivationFunctionType
ALU = mybir.AluOpType
AX = mybir.AxisListType

ENGS = ("pe", "act", "dve", "pool", "sp")
SAME_ENG_RAW = True

D = 1024
DFF = 2816
NFC = DFF // 128
INW = 5376
O_Q, O_K, O_V, O_XQ, O_G = 512, 1024, 1536, 2048, 2304
EPS = 1e-6
NEG_BIG = -30000.0
V_PRE, V_MEM, V_POST, V_FPRE, V_FPOST, V_PSC, NVEC = 0, 8, 16, 24, 32, 40, 44


class Buf:
    __slots__ = ("name", "w", "r", "excl")

    def __init__(self, name):
        self.name = name
        self.w = {}
        self.r = {}
        self.excl = False


class Builder:
    def __init__(self, nc, es):
        self.nc = nc
        self.es = es
        self.q = {e: [] for e in ENGS}
        self.cnt = {}
        self.seen = {e: {} for e in ENGS}
        self.semh = {}
        self.nbuf = 0
        for e in ENGS:
            if e != "sp":
                self._newsem(e)

    def _newsem(self, key):
        self.semh[key] = self.es.enter_context(self.nc.semaphore("s%d" % len(self.semh)))
        self.cnt[key] = 0

    def buf(self, name=None):
        self.nbuf += 1
        return Buf("%s#%d" % (name or "b", self.nbuf))

    def _wait(self, eng, key, val):
        if self.seen[eng].get(key, 0) >= val:
            return
        self.seen[eng][key] = val
        sem = self.semh[key]
        self.q[eng].append(lambda e, sem=sem, val=val: e.wait_ge(sem, val))

    def _deps(self, eng, reads, writes, acc):
        raw = {}
        oth = {}

        def add(dst, d):
            for k, v in d.items():
                if dst.get(k, 0) < v:
                    dst[k] = v
        for b in reads:
            add(raw, b.w)
            if b.excl:
                add(oth, b.r)
        for b in writes:
            add(oth, b.w)
            add(oth, b.r)
        for b in acc:
            add(oth, b.r)
        for k, v in raw.items():
            if k == eng and (eng == "pe" or not SAME_ENG_RAW):
                continue
            self._wait(eng, k, v)
        for k, v in oth.items():
            if k == eng:
                continue
            self._wait(eng, k, v)

    def _record(self, key, val, reads, writes, acc):
        for b in reads:
            if b.r.get(key, 0) < val:
                b.r[key] = val
        for b in writes:
            b.w = {key: val}
            b.r = {}
        for b in acc:
            b.w[key] = val
            b.r = {}

    def op(self, eng, fn, reads=(), writes=(), acc=()):
        self._deps(eng, reads, writes, acc)
        self.cnt[eng] += 1
        c = self.cnt[eng]
        sem = self.semh[eng]
        self.q[eng].append(lambda e, fn=fn, sem=sem: fn(e).then_inc(sem, 1))
        self._record(eng, c, reads, writes, acc)

    def dma(self, qeng, out_ap, in_ap, sb, store=False, reads=(), writes=(), acc=(), **kw):
        self._deps(qeng, reads, writes, acc)
        key = ("st" if store else "ld", sb.name)
        if key not in self.semh:
            self._newsem(key)
        self.cnt[key] += 16
        c = self.cnt[key]
        sem = self.semh[key]
        self.q[qeng].append(
            lambda e, o=out_ap, i=in_ap, sem=sem, kw=kw: e.dma_start(out=o, in_=i, **kw).then_inc(sem, 16))
        self._record(key, c, reads, writes, acc)

    def barrier(self):
        for e in ENGS:
            for k, c in self.cnt.items():
                if k == e or c == 0:
                    continue
                self._wait(e, k, c)

    def emit(self):
        q = self.q
        with self.nc.Block() as block:
            @block.tensor
            def _(e):
                for t in q["pe"]:
                    t(e)

            @block.scalar
            def _(e):
                for t in q["act"]:
                    t(e)

            @block.vector
            def _(e):
                for t in q["dve"]:
                    t(e)

            @block.gpsimd
            def _(e):
                for t in q["pool"]:
                    t(e)

            @block.sync
            def _(e):
                for t in q["sp"]:
                    t(e)


def build_program(G, debug=False, stop_after=None):
    S = 1024 * G
    NBLK = S // 128
    NB = 2 * G
    NQ = 128 * NB
    BPT = 4 if G >= 2 else 2
    TO = 128 * BPT
    TOH = 144 * BPT
    NT = NB // BPT
    ST = S // 512

    nc = bass.Bass("TRN2", target_bir_lowering=False)

    def din(name, shape, dt=F32):
        return nc.dram_tensor(name, list(shape), dt, kind="ExternalInput").ap()

    xT = din("xT", [D, S])
    xoh = din("xoh", [D, NB * 144])
    pos = din("pos", [NQ])
    memT = din("memT", [D, 256])
    w_in = din("w_in", [D, INW])
    w_pmix = din("w_pmix", [512, 128])
    w_po = din("w_po", [512, D])
    w_sbo = din("w_sbo", [512, D])
    w_kv = din("w_kv", [D, 512])
    w_xo = din("w_xo", [256, D])
    w_out = din("w_out", [D, D])
    w_fi = din("w_fi", [D, 2 * DFF])
    w_fo = din("w_fo", [DFF, D])
    vecs = din("vecs", [128, NVEC])
    outT = nc.dram_tensor("outT", [D, NQ], F32, kind="ExternalOutput").ap()
    kT_s = nc.dram_tensor("kT_s", [4, 128, S], BF16).ap()
    v_s = nc.dram_tensor("v_s", [4, 128, NBLK, 128], BF16).ap()
    c0_s = nc.dram_tensor("c0_s", [D, NQ], F32).ap()
    c2_s = nc.dram_tensor("c2_s", [D, NQ], F32).ap()
    h1_s = nc.dram_tensor("h1_s", [D, NQ], F32).ap()
    n2_s = nc.dram_tensor("n2_s", [D, NQ], BF16).ap()
    dbg = {}
    if debug:
        dbg["qT"] = nc.dram_tensor("dbg_qT", [128, 4, 2 * NQ], BF16, kind="ExternalOutput").ap()
        dbg["sbo"] = nc.dram_tensor("dbg_sbo", [128, 4, NQ], BF16, kind="ExternalOutput").ap()
        dbg["kT"] = nc.dram_tensor("dbg_kT", [4, 128, S], BF16, kind="ExternalOutput").ap()
        dbg["v"] = nc.dram_tensor("dbg_v", [4, 128, NBLK, 128], BF16, kind="ExternalOutput").ap()
        dbg["c0"] = nc.dram_tensor("dbg_c0", [D, NQ], F32, kind="ExternalOutput").ap()
        dbg["c2"] = nc.dram_tensor("dbg_c2", [D, NQ], F32, kind="ExternalOutput").ap()
        dbg["h1"] = nc.dram_tensor("dbg_h1", [D, NQ], F32, kind="ExternalOutput").ap()

    with contextlib.ExitStack() as es:
        B = Builder(nc, es)

        def sbt(scope, name, shape, dt):
            return scope.enter_context(nc.sbuf_tensor(name, list(shape), dt))

        def mm(out, lhsT, rhs, start, stop, R, W=(), A=()):
            B.op("pe", lambda e: e.matmul(out, lhsT, rhs, start=start, stop=stop), reads=R, writes=W, acc=A)

        def act(out, in_, func, R, W=(), A=(), **kw):
            B.op("act", lambda e: e.activation(out=out, in_=in_, func=func, **kw), reads=R, writes=W, acc=A)

        def tt(eng, out, in0, in1, op, R, W=(), A=()):
            B.op(eng, lambda e: e.tensor_tensor(out=out, in0=in0, in1=in1, op=op), reads=R, writes=W, acc=A)

        def ts(eng, out, in0, s1, s2, op0, op1, R, W=(), A=()):
            if op1 is None:
                B.op(eng, lambda e: e.tensor_scalar(out=out, in0=in0, scalar1=s1, scalar2=s2, op0=op0),
                     reads=R, writes=W, acc=A)
            else:
                B.op(eng, lambda e: e.tensor_scalar(out=out, in0=in0, scalar1=s1, scalar2=s2, op0=op0, op1=op1),
                     reads=R, writes=W, acc=A)

        def cp(eng, out, in_, R, W=(), A=()):
            if eng == "act":
                act(out, in_, AF.Copy, R, W, A)
            else:
                B.op(eng, lambda e: e.tensor_copy(out=out, in_=in_), reads=R, writes=W, acc=A)

        PSALL = es.enter_context(nc.psum_tensor("psall", [128, 4096], F32))
        PS = [PSALL[:, i * 512:(i + 1) * 512] for i in range(8)]
        PB = [B.buf("ps%d" % i) for i in range(8)]
        for b_ in PB:
            b_.excl = True
        PST = PS[7].bitcast(BF16)
        PSTB = PB[7]

        gs = es
        vec = sbt(gs, "vec", [128, NVEC], F32); vecb = B.buf("vec")
        ones = sbt(gs, "ones", [128, 128], BF16); onesb = B.buf("ones")
        nTin = sbt(gs, "nTin", [128, 128], BF16); nTinb = B.buf("nTin")
        nOnes = sbt(gs, "nOnes", [128, 128], BF16); nOnesb = B.buf("nOnes")
        nBigI = sbt(gs, "nBigI", [128, 128], BF16); nBigIb = B.buf("nBigI")
        ident = sbt(gs, "ident", [128, 128], BF16); identb = B.buf("ident")
        dif_i = sbt(gs, "dif_i", [128, 128], I32); difib = B.buf("dif_i")
        dif_f = sbt(gs, "dif_f", [128, 128], F32); diffb = B.buf("dif_f")
        kp8_i = sbt(gs, "kp8_i", [128, 8], I32); kp8ib = B.buf("kp8i")
        kp8 = sbt(gs, "kp8", [128, 8], F32); kp8b = B.buf("kp8")
        qpos = sbt(gs, "qpos", [128, NQ], F32); qposb = B.buf("qpos")
        notM = sbt(gs, "notM", [128, 8, 2, 256], BF16); notMb = B.buf("notM")
        NSTG = 2
        st_state = {"i": 0, "e": 0, "n": 0}

        def new_stg(scope):
            st_state["n"] += 1
            st_state["stg"] = [sbt(scope, "stg%d_%d" % (st_state["n"], i), [128, 1536], F32) for i in range(NSTG)]
            st_state["stgb"] = [B.buf("stg%d" % i) for i in range(NSTG)]

        B.dma("sp", vec[:], vecs, vecb, writes=[vecb])
        B.dma("sp", qpos[:], pos.partition_broadcast(128), qposb, writes=[qposb])
        B.op("dve", lambda e: e.memset(ones[:], 1.0), writes=[onesb])
        B.op("dve", lambda e: e.memset(nOnes[:], -1.0), writes=[nOnesb])
        B.op("pool", lambda e: e.iota(dif_i[:], pattern=[[-1, 128]], base=0, channel_multiplier=1), writes=[difib])
        cp("dve", dif_f[:], dif_i[:], R=[difib], W=[diffb])
        ts("dve", nTin[:], dif_f[:], 0.0, -1.0, ALU.is_ge, ALU.mult, R=[diffb], W=[nTinb])
        ts("dve", nBigI[:], dif_f[:], 0.0, NEG_BIG, ALU.is_equal, ALU.mult, R=[diffb], W=[nBigIb])
        ts("dve", ident[:], dif_f[:], 0.0, None, ALU.is_equal, None, R=[diffb], W=[identb])
        B.op("pool", lambda e: e.iota(kp8_i[:], pattern=[[128, 8]], base=0, channel_multiplier=1), writes=[kp8ib])
        cp("dve", kp8[:], kp8_i[:], R=[kp8ib], W=[kp8b])
        for jj in range(8):
            for h in range(2):
                ts("dve", notM[:, jj, h, :], qpos[:, 0:256], kp8[:, jj:jj + 1], None, ALU.is_le, None,
                   R=[qposb, kp8b], **({"W": [notMb]} if (jj == 0 and h == 0) else {"A": [notMb]}))

        def load_w(dst, dstb, src, kc, ncols, gcol=None, first=True):
            kg = max(1, min(kc, 1536 // ncols))
            k0 = 0
            while k0 < kc:
                kn = min(kg, kc - k0)
                i = st_state["i"] % NSTG
                st_state["i"] += 1
                stg, stgb = st_state["stg"], st_state["stgb"]
                sv = stg[i][:, 0:kn * ncols].rearrange("p (k n) -> p k n", k=kn)
                B.dma("sp", sv, src[k0 * 128:(k0 + kn) * 128, :].rearrange("(k p) n -> p k n", p=128), stgb[i],
                      writes=[stgb[i]])
                eng = "dve" if st_state["e"] % 2 == 0 else "act"
                st_state["e"] += 1
                kw = {"W": [dstb]} if (first and k0 == 0) else {"A": [dstb]}
                cp(eng, dst[:, k0:k0 + kn, :], sv, R=[stgb[i]], **kw)
                k0 += kn

        def run(gen):
            for _ in gen:
                pass

        def interleave(*gens):
            gens = list(gens)
            while gens:
                for g_ in list(gens):
                    try:
                        next(g_)
                    except StopIteration:
                        gens.remove(g_)

        def stt(out, in0, scal, in1, R, W=(), A=()):
            B.op("dve", lambda e: e.scalar_tensor_tensor(out=out, in0=in0, scalar=scal, in1=in1,
                                                         op0=ALU.mult, op1=ALU.mult), reads=R, writes=W, acc=A)

        def rstd_from(ss_ps, ss_b, ncols, ln_t, ln_b, out_t, out_b, out_is_write=True):
            act(ln_t, ss_ps, AF.Ln, R=[ss_b], W=[ln_b], scale=1.0 / D, bias=EPS)
            act(out_t, ln_t, AF.Exp, R=[ln_b], W=[out_b], scale=-0.5)

        s13 = contextlib.ExitStack()
        es.enter_context(s13)
        NGT = BPT // 2
        Qz = sbt(s13, "Qz", [128, 4, G, 2, 256], BF16); Qzb = B.buf("Qz")
        B.op("pool", lambda e: e.memset(Qz[:].rearrange("p a g h q -> p (a g h q)"), 0.0), writes=[Qzb])
        wqkv = sbt(s13, "wqkv", [128, 8, 1536], BF16); wqkvb = B.buf("wqkv")

        xn_s = nc.dram_tensor("xn_s", [128, 8, NB * 144], BF16).ap()
        xnsb = B.buf("xn_s")
        with contextlib.ExitStack() as s1:
            new_stg(s1)
            load_w(wqkv, wqkvb, w_in[:, O_Q:O_Q + 1536], 8, 1536)
            xo_t = [sbt(s1, "xo_t%d" % i, [128, 8, TOH], F32) for i in range(2)]
            xo_b = [B.buf("xo_t%d" % i) for i in range(2)]
            sq1 = sbt(s1, "sq1", [128, 8, TOH], BF16); sq1b = B.buf("sq1")
            ln1 = sbt(s1, "ln1", [128, TOH], F32); ln1b = B.buf("ln1")
            rs1 = sbt(s1, "rs1", [128, TOH], F32); rs1b = B.buf("rs1")
            xn1 = [sbt(s1, "xn1_%d" % i, [128, 8, BPT, 144], BF16) for i in range(2)]
            xn1b = [B.buf("xn1_%d" % i) for i in range(2)]
            HB = TOH // 2
            for t in range(NT):
                sl = t % 2
                B.dma("sp", xo_t[sl][:], xoh[:, t * TOH:(t + 1) * TOH].rearrange("(c p) n -> p c n", p=128),
                      xo_b[sl], writes=[xo_b[sl]])
                act(sq1[:], xo_t[sl][:], AF.Square, R=[xo_b[sl]], W=[sq1b])
                for hf in range(2):
                    for k in range(8):
                        mm(PS[hf][:, 0:HB], ones[:], sq1[:, k, hf * HB:(hf + 1) * HB], start=(k == 0), stop=(k == 7),
                           R=[onesb, sq1b], **({"W": [PB[hf]]} if k == 0 else {"A": [PB[hf]]}))
                for hf in range(2):
                    act(ln1[:, hf * HB:(hf + 1) * HB], PS[hf][:, 0:HB], AF.Ln, R=[PB[hf]],
                        **({"W": [ln1b]} if hf == 0 else {"A": [ln1b]}), scale=1.0 / D, bias=EPS)
                act(rs1[:], ln1[:], AF.Exp, R=[ln1b], W=[rs1b], scale=-0.5)
                xnv = xn1[sl][:].rearrange("p c b t -> p c (b t)")
                for k in range(8):
                    stt(xnv[:, k, :], xo_t[sl][:, k, :], vec[:, V_PRE + k:V_PRE + k + 1], rs1[:],
                        R=[xo_b[sl], rs1b, vecb], **({"W": [xn1b[sl]]} if k == 0 else {"A": [xn1b[sl]]}))
                B.dma("sp", xn_s[:, :, t * TOH:(t + 1) * TOH], xnv, xn1b[sl], store=True,
                      reads=[xn1b[sl]], acc=[xnsb])
                for c4 in range(4):
                    pb = 2 + (c4 % 2)
                    for k in range(8):
                        mm(PS[pb][:, 0:TO].rearrange("p (b t) -> p b t", b=BPT),
                           wqkv[:, k, c4 * 128:(c4 + 1) * 128], xn1[sl][:, k, :, 16:144],
                           start=(k == 0), stop=(k == 7), R=[wqkvb, xn1b[sl]],
                           **({"W": [PB[pb]]} if k == 0 else {"A": [PB[pb]]}))
                    for h in range(2):
                        act(Qz[h * 64:(h + 1) * 64, c4, t * NGT:(t + 1) * NGT, h, :],
                            PS[pb][h * 64:(h + 1) * 64, 0:TO].rearrange("p (g q) -> p g q", g=NGT), AF.Copy,
                            R=[PB[pb]], A=[Qzb], scale=0.125)
        B.barrier()
        if debug:
            B.dma("sp", dbg["qT"], Qz[:].rearrange("p a g h q -> p a (g h q)"), Qzb, store=True, reads=[Qzb])
        if stop_after == "P1":
            B.barrier()
            B.emit()
            return nc

        kTsb = B.buf("kT_s"); vsb = B.buf("v_s")
        with contextlib.ExitStack() as s2:
            x_t = [sbt(s2, "x_t%d" % i, [128, 8, 512], F32) for i in range(3)]
            x_b = [B.buf("x_t%d" % i) for i in range(3)]
            sq2 = [sbt(s2, "sq2_%d" % i, [128, 8, 512], BF16) for i in range(2)]
            sq2b = [B.buf("sq2_%d" % i) for i in range(2)]
            ln2 = sbt(s2, "ln2", [128, 512], F32); ln2b = B.buf("ln2")
            rs2 = sbt(s2, "rs2", [128, 512], F32); rs2b = B.buf("rs2")
            xn2 = [sbt(s2, "xn2_%d" % i, [128, 8, 512], BF16) for i in range(2)]
            xn2b = [B.buf("xn2_%d" % i) for i in range(2)]
            kst = [sbt(s2, "kst%d" % i, [128, 4, 512], BF16) for i in range(2)]
            kstb = [B.buf("kst%d" % i) for i in range(2)]
            vst = [sbt(s2, "vst%d" % i, [128, 4, 512], BF16) for i in range(2)]
            vstb = [B.buf("vst%d" % i) for i in range(2)]

            def p2_load(t):
                B.dma("sp", x_t[t % 3][:], xT[:, t * 512:(t + 1) * 512].rearrange("(c p) n -> p c n", p=128),
                      x_b[t % 3], writes=[x_b[t % 3]])

            def p2_sq(t):
                sl = t % 2
                act(sq2[sl][:], x_t[t % 3][:], AF.Square, R=[x_b[t % 3]], W=[sq2b[sl]])

            def p2_norm(t):
                sl = t % 2
                for k in range(8):
                    mm(PS[0][:], ones[:], sq2[sl][:, k, :], start=(k == 0), stop=(k == 7), R=[onesb, sq2b[sl]],
                       **({"W": [PB[0]]} if k == 0 else {"A": [PB[0]]}))
                rstd_from(PS[0][:], PB[0], 512, ln2[:], ln2b, rs2[:], rs2b)
                for k in range(8):
                    stt(xn2[sl][:, k, :], x_t[t % 3][:, k, :], vec[:, V_PRE + k:V_PRE + k + 1], rs2[:],
                        R=[x_b[t % 3], rs2b, vecb], **({"W": [xn2b[sl]]} if k == 0 else {"A": [xn2b[sl]]}))

            def p2_kv(t):
                sl = t % 2
                for c4 in range(4):
                    pb = 1 + (c4 % 3)
                    for k in range(8):
                        mm(PS[pb][:], wqkv[:, k, 512 + c4 * 128:512 + (c4 + 1) * 128], xn2[sl][:, k, :],
                           start=(k == 0), stop=(k == 7), R=[wqkvb, xn2b[sl]],
                           **({"W": [PB[pb]]} if k == 0 else {"A": [PB[pb]]}))
                    cp("dve", kst[sl][:, c4, :], PS[pb][:], R=[PB[pb]],
                       **({"W": [kstb[sl]]} if c4 == 0 else {"A": [kstb[sl]]}))
                B.dma("pool", kT_s[:, :, t * 512:(t + 1) * 512].rearrange("c p s -> p c s"), kst[sl][:], kstb[sl],
                      store=True, reads=[kstb[sl]], acc=[kTsb])
                for bk in range(4):
                    pb = 4 + (bk % 3)
                    for k in range(8):
                        mm(PS[pb][:], xn2[sl][:, k, bk * 128:(bk + 1) * 128], wqkv[:, k, 1024:1536],
                           start=(k == 0), stop=(k == 7), R=[wqkvb, xn2b[sl]],
                           **({"W": [PB[pb]]} if k == 0 else {"A": [PB[pb]]}))
                    cp("dve", vst[sl][:, bk, :], PS[pb][:], R=[PB[pb]],
                       **({"W": [vstb[sl]]} if bk == 0 else {"A": [vstb[sl]]}))
                for c4 in range(4):
                    B.dma("pool", v_s[c4, :, 4 * t:4 * t + 4, :], vst[sl][:, :, c4 * 128:(c4 + 1) * 128], vstb[sl],
                          store=True, reads=[vstb[sl]], acc=[vsb])

            p2_load(0)
            p2_sq(0)
            p2_norm(0)
            if ST > 1:
                p2_load(1)
                p2_sq(1)
            for t in range(ST):
                if t + 2 < ST:
                    p2_load(t + 2)
                    p2_sq(t + 2)
                if t + 1 < ST:
                    p2_norm(t + 1)
                p2_kv(t)
        B.barrier()
        if debug:
            with contextlib.ExitStack() as sd:
                dk = sbt(sd, "dk", [128, S], BF16); dkb = B.buf("dk")
                dv = sbt(sd, "dv", [128, NBLK, 128], BF16); dvb = B.buf("dv")
                for c4 in range(4):
                    B.dma("sp", dk[:], kT_s[c4], dkb, reads=[kTsb], writes=[dkb])
                    B.dma("sp", dbg["kT"][c4], dk[:], dkb, store=True, reads=[dkb])
                    B.dma("sp", dv[:], v_s[c4], dvb, reads=[vsb], writes=[dvb])
                    B.dma("sp", dbg["v"][c4], dv[:], dvb, store=True, reads=[dvb])
                B.barrier()

        if stop_after == "P2":
            B.barrier()
            B.emit()
            return nc
        s34 = contextlib.ExitStack()
        with contextlib.ExitStack() as s3:
            sbo = sbt(s3, "sbo", [128, 4, NQ], BF16); sbob = B.buf("sbo")
            kt = [sbt(s3, "kt%d" % i, [128, S], BF16) for i in range(2)]
            ktb = [B.buf("kt%d" % i) for i in range(2)]
            vt = [sbt(s3, "vt%d" % i, [128, NBLK, 128], BF16) for i in range(2)]
            vtb = [B.buf("vt%d" % i) for i in range(2)]
            NE = 2
            e_t = [sbt(s3, "e_t%d" % i, [128, 1024], F32) for i in range(NE)]
            e_b = [B.buf("e_t%d" % i) for i in range(NE)]
            NSP = 3
            sp_t = [sbt(s3, "sp_t%d" % i, [128, 1024], BF16) for i in range(NSP)]
            sp_b = [B.buf("sp_t%d" % i) for i in range(NSP)]
            a_t = [sbt(s3, "a_t%d" % i, [128, 1024], BF16) for i in range(NSP)]
            a_b = [B.buf("a_t%d" % i) for i in range(NSP)]
            R_t = [sbt(s3, "R_t%d" % i, [128, 512], F32) for i in range(2)]
            R_b = [B.buf("R_t%d" % i) for i in range(2)]
            Rb_t = [sbt(s3, "Rb_t%d" % i, [128, 512], BF16) for i in range(4)]
            Rb_b = [B.buf("Rb_t%d" % i) for i in range(4)]
            NZP = 3

            def p3_load(c):
                B.dma("sp", kt[c % 2][:], kT_s[c], ktb[c % 2], reads=[kTsb], writes=[ktb[c % 2]])
                B.dma("sp", vt[c % 2][:], v_s[c], vtb[c % 2], reads=[vsb], writes=[vtb[c % 2]])

            pairs = []
            chain_id = 0
            for c in range(4):
                for g in range(G):
                    nj = 8 * g + 8
                    for pi in range(nj // 2):
                        jh = nj - 1 - 2 * pi
                        pairs.append(dict(c=c, g=g, jh=jh, jl=jh - 1, pi=pi, last=(jh == 1), chain=chain_id))
                    chain_id += 1
            for s_i, T in enumerate(pairs):
                T["zs"] = s_i % NZP
                T["sl3"] = s_i % NSP
                T["esl"] = s_i % NE
                T["ob"] = 6 + (T["chain"] % 2)
                T["rs"] = T["chain"] % 2

            def st_P1(T):
                c, g = T["c"], T["g"]
                qv = Qz[:, c, g, :, :].rearrange("p h q -> p (h q)")
                for i, j in enumerate((T["jh"], T["jl"])):
                    bk_ = 2 * T["zs"] + i
                    mm(PS[bk_][:], kt[c % 2][:, j * 128:(j + 1) * 128], qv, start=True, stop=False,
                       R=[ktb[c % 2], Qzb], W=[PB[bk_]])
                    jj = j - 8 * g
                    if jj >= 0:
                        mm(PS[bk_][:], nBigI[:], notM[:, jj, :, :].rearrange("p h q -> p (h q)"),
                           start=False, stop=False, R=[nBigIb, notMb], A=[PB[bk_]])

            def zpair(T):
                zs = T["zs"]
                return PSALL[:, zs * 1024:(zs + 1) * 1024], [PB[2 * zs], PB[2 * zs + 1]]

            def st_A1a(T):
                zap, zbufs = zpair(T)
                e_ = e_t[T["esl"]]; eb = e_b[T["esl"]]
                act(e_[:], zap, AF.Exp, R=zbufs, W=[eb])

            def st_A1b(T):
                e_ = e_t[T["esl"]]; eb = e_b[T["esl"]]
                sp_ = sp_t[T["sl3"]]; spb = sp_b[T["sl3"]]
                act(sp_[:], e_[:], AF.Ln, R=[eb], W=[spb], bias=1.0)

            def st_P2(T):
                b0 = 2 * T["zs"]; b1 = b0 + 1
                sp_ = sp_t[T["sl3"]]; spb = sp_b[T["sl3"]]
                first = (T["pi"] == 0)
                rbi = (T["chain"] % 2) * 2 + (T["pi"] % 2)
                mm(PS[b0][:], nTin[:], sp_[:, 0:512], start=False, stop=first, R=[nTinb, spb], A=[PB[b0]])
                if not first:
                    mm(PS[b0][:], nOnes[:], Rb_t[rbi][:], start=False, stop=True, R=[nOnesb, Rb_b[rbi]], A=[PB[b0]])
                mm(PS[b1][:], nTin[:], sp_[:, 512:1024], start=False, stop=False, R=[nTinb, spb], A=[PB[b1]])
                mm(PS[b1][:], nOnes[:], sp_[:, 0:512], start=False, stop=first, R=[nOnesb, spb], A=[PB[b1]])
                if not first:
                    mm(PS[b1][:], nOnes[:], Rb_t[rbi][:], start=False, stop=True, R=[nOnesb, Rb_b[rbi]], A=[PB[b1]])

            def st_G(T):
                if T["last"]:
                    return
                sp_ = sp_t[T["sl3"]]; spb = sp_b[T["sl3"]]
                R_ = R_t[T["rs"]]; Rbuf = R_b[T["rs"]]
                rbn = (T["chain"] % 2) * 2 + ((T["pi"] + 1) % 2)
                if T["pi"] == 0:
                    tt("dve", R_[:], sp_[:, 0:512], sp_[:, 512:1024], ALU.add, R=[spb], W=[Rbuf])
                else:
                    tt("dve", R_[:], R_[:], sp_[:, 0:512], ALU.add, R=[spb, Rbuf], A=[Rbuf])
                    tt("dve", R_[:], R_[:], sp_[:, 512:1024], ALU.add, R=[spb, Rbuf], A=[Rbuf])
                cp("dve", Rb_t[rbn][:], R_[:], R=[Rbuf], W=[Rb_b[rbn]])

            def st_A2(T):
                zap, zbufs = zpair(T)
                a_ = a_t[T["sl3"]]; ab = a_b[T["sl3"]]
                act(a_[:], zap, AF.Exp, R=zbufs, W=[ab])

            def st_P3(T):
                c, g = T["c"], T["g"]
                a_ = a_t[T["sl3"]]; ab = a_b[T["sl3"]]
                ob = T["ob"]
                first = (T["pi"] == 0)
                mm(PS[ob][:], vt[c % 2][:, T["jh"], :], a_[:, 0:512], start=first, stop=False, R=[vtb[c % 2], ab],
                   **({"W": [PB[ob]]} if first else {"A": [PB[ob]]}))
                mm(PS[ob][:], vt[c % 2][:, T["jl"], :], a_[:, 512:1024], start=False, stop=T["last"],
                   R=[vtb[c % 2], ab], A=[PB[ob]])
                if T["last"]:
                    q0 = g * 256
                    for h in range(2):
                        cp("dve", sbo[h * 64:(h + 1) * 64, c, q0:q0 + 256],
                           PS[ob][h * 64:(h + 1) * 64, h * 256:(h + 1) * 256], R=[PB[ob]], A=[sbob])

            p3_load(0)
            n_t = len(pairs)
            for s_i in range(n_t + 2):
                if 0 <= s_i - 2 < n_t:
                    T2 = pairs[s_i - 2]
                    if T2["g"] == 0 and T2["pi"] == 0 and T2["c"] + 1 < 4:
                        p3_load(T2["c"] + 1)
                if s_i < n_t:
                    T = pairs[s_i]
                    st_P1(T)
                    st_A1a(T)
                    st_A1b(T)
                if 0 <= s_i - 1 < n_t:
                    T1 = pairs[s_i - 1]
                    st_P2(T1)
                    st_G(T1)
                    st_A2(T1)
                if 0 <= s_i - 2 < n_t:
                    st_P3(pairs[s_i - 2])
            B.barrier()
            if debug:
                B.dma("sp", dbg["sbo"], sbo[:], sbob, store=True, reads=[sbob])
                B.barrier()
            if stop_after == "P3":
                B.barrier()
                B.emit()
                return nc
            sbo_s = nc.dram_tensor("sbo_s", [128, 4, NQ], BF16).ap()
            sbosb = B.buf("sbo_s")
            B.dma("sp", sbo_s, sbo[:], sbob, store=True, reads=[sbob], writes=[sbosb])
            B.barrier()
        s13.close()

        with contextlib.ExitStack() as s4:
            xn = sbt(s4, "xn", [128, 8, NB, 144], BF16); xnb = B.buf("xn")
            B.dma("sp", xn[:].rearrange("p c b t -> p c (b t)"), xn_s, xnb, reads=[xnsb], writes=[xnb])
            NOUT = 3
            o_t = [sbt(s4, "o_t%d" % i, [128, TO], F32) for i in range(NOUT)]
            o_b = [B.buf("o_t%d" % i) for i in range(NOUT)]
            sg_t = [sbt(s4, "sg_t%d" % i, [128, TO], F32) for i in range(2)]
            sg_b = [B.buf("sg_t%d" % i) for i in range(2)]
            cnt4 = {"o": 0, "sg": 0}

            def gate_mm(pb, wg, wgb, dc, t):
                for k in range(8):
                    mm(PS[pb][:, 0:TO].rearrange("p (b t) -> p b t", b=BPT),
                       wg[:, k, dc * 128:(dc + 1) * 128], xn[:, k, t * BPT:(t + 1) * BPT, 16:144],
                       start=(k == 0), stop=(k == 7), R=[wgb, xnb],
                       **({"W": [PB[pb]]} if k == 0 else {"A": [PB[pb]]}))

            def gated(ypb, gpb):
                si = cnt4["sg"] % 2; cnt4["sg"] += 1
                oi = cnt4["o"] % NOUT; cnt4["o"] += 1
                act(sg_t[si][:], PS[gpb][:, 0:TO], AF.Sigmoid, R=[PB[gpb]], W=[sg_b[si]])
                tt("dve", o_t[oi][:], sg_t[si][:], PS[ypb][:, 0:TO], ALU.mult, R=[sg_b[si], PB[ypb]], W=[o_b[oi]])
                return o_t[oi], o_b[oi]

            c0sb = B.buf("c0_s")
            with contextlib.ExitStack() as sa:
                wpi = sbt(sa, "wpi", [128, 8, 512], BF16); wpib = B.buf("wpi")
                wmix = sbt(sa, "wmix", [128, 4, 128], BF16); wmixb = B.buf("wmix")
                wpo = sbt(sa, "wpo", [128, 4, D], BF16); wpob = B.buf("wpo")
                wg0 = sbt(sa, "wg0", [128, 8, D], BF16); wg0b = B.buf("wg0")
                new_stg(sa)
                load_w(wpi, wpib, w_in[:, 0:512], 8, 512, gcol=V_PRE)
                load_w(wmix, wmixb, w_pmix, 4, 128)
                load_w(wpo, wpob, w_po, 4, D)
                load_w(wg0, wg0b, w_in[:, O_G:O_G + D], 8, D, gcol=V_PRE)
                pa = [sbt(sa, "pa%d" % i, [128, BPT, 144], F32) for i in range(3)]
                pab = [B.buf("pa%d" % i) for i in range(3)]
                icn = sbt(sa, "icn", [128, 4, TO], F32); icnb = B.buf("icn")
                mixd = sbt(sa, "mixd", [128, 4, TO], BF16); mixdb = B.buf("mixd")
                pm = [sbt(sa, "pm%d" % i, [128, 4, TO], BF16) for i in range(2)]
                pmb = [B.buf("pm%d" % i) for i in range(2)]
                tmpa = sbt(sa, "tmpa", [128, BPT, 128], F32); tmpab = B.buf("tmpa")
                HB = TOH // 2
                HBK = BPT // 2

                def pa_pool(t):
                    for gi in range(4):
                        ts("dve", icn[:, gi, :], qpos[:, t * TO:(t + 1) * TO], 1.0, float(2 << gi), ALU.add, ALU.min,
                           R=[qposb], **({"W": [icnb]} if gi == 0 else {"A": [icnb]}))
                    B.op("dve", lambda e: e.reciprocal(out=icn[:], in_=icn[:]), reads=[icnb], writes=[icnb])
                    for gi in range(4):
                        p0, p0b = pa[0], pab[0]
                        for hf in range(2):
                            pb = hf
                            for k in range(8):
                                mm(PS[pb][:, 0:HB].rearrange("p (b t) -> p b t", b=HBK),
                                   wpi[:, k, gi * 128:(gi + 1) * 128],
                                   xn[:, k, t * BPT + hf * HBK:t * BPT + (hf + 1) * HBK, :],
                                   start=(k == 0), stop=(k == 7), R=[wpib, xnb],
                                   **({"W": [PB[pb]]} if k == 0 else {"A": [PB[pb]]}))
                            cp("act", p0[:, hf * HBK:(hf + 1) * HBK, :],
                               PS[pb][:, 0:HB].rearrange("p (b t) -> p b t", b=HBK), R=[PB[pb]],
                               **({"W": [p0b]} if hf == 0 else {"A": [p0b]}))
                        cur, curb = p0, p0b
                        pp = 1
                        dsh = 1
                        lo = 0
                        for step in range(gi + 1):
                            nxt, nxtb = pa[pp], pab[pp]
                            lo = lo + dsh
                            tt("dve" if step % 2 == 0 else "pool", nxt[:, :, lo:144], cur[:, :, lo:144],
                               cur[:, :, lo - dsh:144 - dsh], ALU.add, R=[curb], W=[nxtb])
                            cur, curb = nxt, nxtb
                            pp = 3 - pp
                            dsh *= 2
                        tt("dve", tmpa[:], cur[:, :, 16:144], icn[:, gi, :].rearrange("p (b t) -> p b t", b=BPT),
                           ALU.mult, R=[curb, icnb], W=[tmpab])
                        tt("dve", mixd[:, gi, :].rearrange("p (b t) -> p b t", b=BPT), tmpa[:], p0[:, :, 16:144],
                           ALU.subtract, R=[tmpab, p0b], **({"W": [mixdb]} if gi == 0 else {"A": [mixdb]}))
                        yield
                    for gi in range(4):
                        pb = 2 + (gi % 2)
                        mm(PS[pb][:, 0:TO], wmix[:, gi, :], mixd[:, gi, :], start=True, stop=True,
                           R=[wmixb, mixdb], W=[PB[pb]])
                        ts("dve", pm[t % 2][:, gi, :], PS[pb][:, 0:TO], vec[:, V_PSC + gi:V_PSC + gi + 1], None,
                           ALU.mult, None, R=[PB[pb], vecb], **({"W": [pmb[t % 2]]} if gi == 0 else {"A": [pmb[t % 2]]}))
                    yield

                def pa_proj(t):
                    for dc in range(8):
                        ypb = 4 + (dc % 2)
                        gpb = 6 + (dc % 2)
                        for gi in range(4):
                            mm(PS[ypb][:, 0:TO], wpo[:, gi, dc * 128:(dc + 1) * 128], pm[t % 2][:, gi, :],
                               start=(gi == 0), stop=(gi == 3), R=[wpob, pmb[t % 2]],
                               **({"W": [PB[ypb]]} if gi == 0 else {"A": [PB[ypb]]}))
                        gate_mm(gpb, wg0, wg0b, dc, t)
                        ot, otb = gated(ypb, gpb)
                        B.dma("sp", c0_s[dc * 128:(dc + 1) * 128, t * TO:(t + 1) * TO], ot[:], otb, store=True,
                              reads=[otb], acc=[c0sb])
                        yield

                run(pa_pool(0))
                for t in range(NT):
                    if t + 1 < NT:
                        interleave(pa_proj(t), pa_pool(t + 1))
                    else:
                        run(pa_proj(t))
                B.barrier()

            c2sb = B.buf("c2_s")
            with contextlib.ExitStack() as sc:
                wxq = sbt(sc, "wxq", [128, 8, 256], BF16); wxqb = B.buf("wxq")
                wkv = sbt(sc, "wkv", [128, 8, 512], BF16); wkvb = B.buf("wkv")
                wxo = sbt(sc, "wxo", [128, 2, D], BF16); wxob = B.buf("wxo")
                wg2 = sbt(sc, "wg2", [128, 8, D], BF16); wg2b = B.buf("wg2")
                new_stg(sc)
                load_w(wkv, wkvb, w_kv, 8, 512, gcol=V_MEM)
                load_w(wxq, wxqb, w_in[:, O_XQ:O_XQ + 256], 8, 256, gcol=V_PRE)
                load_w(wxo, wxob, w_xo, 2, D)
                load_w(wg2, wg2b, w_in[:, O_G + 2 * D:O_G + 3 * D], 8, D, gcol=V_PRE)
                m_t = sbt(sc, "m_t", [128, 8, 256], F32); m_b = B.buf("m_t")
                msq = sbt(sc, "msq", [128, 8, 256], BF16); msqb = B.buf("msq")
                mln = sbt(sc, "mln", [128, 256], F32); mlnb = B.buf("mln")
                mrs = sbt(sc, "mrs", [128, 256], F32); mrsb = B.buf("mrs")
                mn = sbt(sc, "mn", [128, 8, 256], BF16); mnb = B.buf("mn")
                mkT = sbt(sc, "mkT", [128, 2, 256], BF16); mkTb = B.buf("mkT")
                mvz = sbt(sc, "mvz", [128, 4, 2, 128], BF16); mvb = B.buf("mvz")
                xqz = sbt(sc, "xqz", [128, 4, TO], BF16); xqTb = B.buf("xqz")
                B.op("pool", lambda e: e.memset(mvz[:].rearrange("p a b c -> p (a b c)"), 0.0), writes=[mvb])
                B.op("pool", lambda e: e.memset(xqz[:].rearrange("p a b -> p (a b)"), 0.0), writes=[xqTb])
                nmx = sbt(sc, "nmx", [128, 4], F32); nmxb = B.buf("nmx")
                ssum = sbt(sc, "ssum", [128, 4], F32); ssumb = B.buf("ssum")
                rsm = sbt(sc, "rsm", [128, 4], F32); rsmb = B.buf("rsm")
                P_t = sbt(sc, "P_t", [128, 4, 256], F32); P_b = B.buf("P_t")
                Pn = sbt(sc, "Pn", [128, 4, 256], BF16); Pnb = B.buf("Pn")
                PT = sbt(sc, "PT", [128, 8, 128], BF16); PTb = B.buf("PT")
                xoT = sbt(sc, "xoT", [128, 2, TO], BF16); xoTb = B.buf("xoT")

                B.dma("sp", m_t[:], memT.rearrange("(c p) n -> p c n", p=128), m_b, writes=[m_b])
                act(msq[:], m_t[:], AF.Square, R=[m_b], W=[msqb])
                for k in range(8):
                    mm(PS[0][:, 0:256], ones[:], msq[:, k, :], start=(k == 0), stop=(k == 7), R=[onesb, msqb],
                       **({"W": [PB[0]]} if k == 0 else {"A": [PB[0]]}))
                rstd_from(PS[0][:, 0:256], PB[0], 256, mln[:], mlnb, mrs[:], mrsb)
                for k in range(8):
                    stt(mn[:, k, :], m_t[:, k, :], vec[:, V_MEM + k:V_MEM + k + 1], mrs[:], R=[m_b, mrsb, vecb],
                        **({"W": [mnb]} if k == 0 else {"A": [mnb]}))
                for ch in range(2):
                    for k in range(8):
                        mm(PS[1][:, 0:256], wkv[:, k, ch * 128:(ch + 1) * 128], mn[:, k, :], start=(k == 0), stop=(k == 7),
                           R=[wkvb, mnb], **({"W": [PB[1]]} if k == 0 else {"A": [PB[1]]}))
                    cp("dve", mkT[:, ch, :], PS[1][:, 0:256], R=[PB[1]], **({"W": [mkTb]} if ch == 0 else {"A": [mkTb]}))
                for mc in range(2):
                    for k in range(8):
                        mm(PS[2][:, 0:256], mn[:, k, mc * 128:(mc + 1) * 128], wkv[:, k, 256:512], start=(k == 0), stop=(k == 7),
                           R=[wkvb, mnb], **({"W": [PB[2]]} if k == 0 else {"A": [PB[2]]}))
                    for h in range(4):
                        cp("dve", mvz[:, h, mc, (h % 2) * 64:(h % 2 + 1) * 64], PS[2][:, h * 64:(h + 1) * 64],
                           R=[PB[2]], A=[mvb])

                for t in range(NT):
                    for ch in range(2):
                        for k in range(8):
                            mm(PS[3][:, 0:TO].rearrange("p (b t) -> p b t", b=BPT),
                               wxq[:, k, ch * 128:(ch + 1) * 128], xn[:, k, t * BPT:(t + 1) * BPT, 16:144],
                               start=(k == 0), stop=(k == 7), R=[wxqb, xnb],
                               **({"W": [PB[3]]} if k == 0 else {"A": [PB[3]]}))
                        for hp in range(2):
                            act(xqz[hp * 64:(hp + 1) * 64, ch * 2 + hp, :], PS[3][hp * 64:(hp + 1) * 64, 0:TO], AF.Copy,
                                R=[PB[3]], A=[xqTb], scale=0.125)
                    for bk in range(BPT):
                        for h in range(4):
                            pb = h // 2
                            hp = h % 2
                            mm(PS[pb][:, hp * 256:(hp + 1) * 256],
                               xqz[:, h, bk * 128:(bk + 1) * 128], mkT[:, h // 2, :],
                               start=True, stop=True, R=[xqTb, mkTb],
                               **({"W": [PB[pb]]} if hp == 0 else {"A": [PB[pb]]}))
                        for pb in range(2):
                            B.op("dve", lambda e, pb=pb: e.reduce_max(
                                out=nmx[:, 2 * pb:2 * pb + 2], in_=PS[pb][:].rearrange("p (h m) -> p h m", h=2),
                                axis=AX.X, negate=True), reads=[PB[pb]],
                                **({"writes": [nmxb]} if pb == 0 else {"acc": [nmxb]}))
                        for h in range(4):
                            pb = h // 2
                            hp = h % 2
                            act(P_t[:, h, :], PS[pb][:, hp * 256:(hp + 1) * 256], AF.Exp, R=[PB[pb], nmxb],
                                **({"W": [P_b, ssumb]} if h == 0 else {"A": [P_b, ssumb]}),
                                bias=nmx[:, h:h + 1], accum_out=ssum[:, h:h + 1])
                        B.op("dve", lambda e: e.reciprocal(out=rsm[:], in_=ssum[:]), reads=[ssumb], writes=[rsmb])
                        for h in range(4):
                            ts("dve", Pn[:, h, :], P_t[:, h, :], rsm[:, h:h + 1], None, ALU.mult, None,
                               R=[P_b, rsmb], **({"W": [Pnb]} if h == 0 else {"A": [Pnb]}))
                        for h in range(4):
                            for mc in range(2):
                                i8 = h * 2 + mc
                                B.op("pe", lambda e, h=h, mc=mc, i8=i8: e.transpose(
                                    PST[:, i8 * 128:(i8 + 1) * 128], Pn[:, h, mc * 128:(mc + 1) * 128], ident[:]),
                                    reads=[Pnb, identb], **({"writes": [PSTB]} if i8 == 0 else {"acc": [PSTB]}))
                        cp("act", PT[:].rearrange("p a b -> p (a b)"), PST[:], R=[PSTB], W=[PTb])
                        for ch in range(2):
                            pb = 2 + ch
                            n4 = 0
                            for h in (2 * ch, 2 * ch + 1):
                                for mc in range(2):
                                    mm(PS[pb][:, bk * 128:(bk + 1) * 128], mvz[:, h, mc, :], PT[:, h * 2 + mc, :],
                                       start=(n4 == 0), stop=(n4 == 3), R=[mvb, PTb],
                                       **({"W": [PB[pb]]} if (bk == 0 and n4 == 0) else {"A": [PB[pb]]}))
                                    n4 += 1
                    for ch in range(2):
                        cp("dve", xoT[:, ch, :], PS[2 + ch][:, 0:TO], R=[PB[2 + ch]],
                           **({"W": [xoTb]} if ch == 0 else {"A": [xoTb]}))
                    for dc in range(8):
                        ypb = 4 + (dc % 2)
                        gpb = 6
                        for ch in range(2):
                            mm(PS[ypb][:, 0:TO], wxo[:, ch, dc * 128:(dc + 1) * 128], xoT[:, ch, :],
                               start=(ch == 0), stop=(ch == 1), R=[wxob, xoTb],
                               **({"W": [PB[ypb]]} if ch == 0 else {"A": [PB[ypb]]}))
                        gate_mm(gpb, wg2, wg2b, dc, t)
                        ot, otb = gated(ypb, gpb)
                        B.dma("pool", c2_s[dc * 128:(dc + 1) * 128, t * TO:(t + 1) * TO], ot[:], otb, store=True,
                              reads=[otb], acc=[c2sb])
                B.barrier()
            if debug:
                with contextlib.ExitStack() as sd:
                    dd = sbt(sd, "dd", [128, 8, NQ], F32); ddb = B.buf("dd")
                    for nm, src, srcb in (("c0", c0_s, c0sb), ("c2", c2_s, c2sb)):
                        B.dma("sp", dd[:], src.rearrange("(c p) n -> p c n", p=128), ddb, reads=[srcb], writes=[ddb])
                        B.dma("sp", dbg[nm].rearrange("(c p) n -> p c n", p=128), dd[:], ddb, store=True, reads=[ddb])
                    B.barrier()

            if stop_after == "P4c":
                B.barrier()
                B.emit()
                return nc
            h1sb = B.buf("h1_s"); n2sb = B.buf("n2_s")
            with contextlib.ExitStack() as sd4:
                wsbo = sbt(sd4, "wsbo", [128, 4, D], BF16); wsbob = B.buf("wsbo")
                wg1 = sbt(sd4, "wg1", [128, 8, D], BF16); wg1b = B.buf("wg1")
                wout = sbt(sd4, "wout", [128, 8, D], BF16); woutb = B.buf("wout")
                with contextlib.ExitStack() as sw:
                    new_stg(sw)
                    load_w(wsbo, wsbob, w_sbo, 4, D)
                    load_w(wg1, wg1b, w_in[:, O_G + D:O_G + 2 * D], 8, D, gcol=V_PRE)
                    load_w(wout, woutb, w_out, 8, D)
                    B.barrier()
                sbo4 = sbt(sd4, "sbo4", [128, 4, NQ], BF16); sbo4b = B.buf("sbo4")
                B.dma("sp", sbo4[:], sbo_s, sbo4b, reads=[sbosb], writes=[sbo4b])
                NCL = 3
                cl = [sbt(sd4, "cl%d" % i, [128, 2, TO], F32) for i in range(NCL)]
                clb = [B.buf("cl%d" % i) for i in range(NCL)]
                clc = {"i": 0}
                mg1 = sbt(sd4, "mg", [128, 8, TO], BF16)
                mg = [mg1, mg1]
                mgb1 = B.buf("mg")
                mgb = [mgb1, mgb1]
                mo = [sbt(sd4, "mo%d" % i, [128, 8, TO], F32) for i in range(2)]
                mob = [B.buf("mo%d" % i) for i in range(2)]
                sqm = [sbt(sd4, "sqm%d" % i, [128, TO], BF16) for i in range(2)]
                sqmb = [B.buf("sqm%d" % i) for i in range(2)]
                lnm = sbt(sd4, "lnm", [128, TO], F32); lnmb = B.buf("lnm")
                rsm1 = sbt(sd4, "rsm1", [128, TO], F32); rsm1b = B.buf("rsm1")
                rsm2 = sbt(sd4, "rsm2", [128, TO], F32); rsm2b = B.buf("rsm2")
                xr = sbt(sd4, "xr", [128, 8, BPT, 128], F32); xrb = B.buf("xr")
                n2t = sbt(sd4, "n2t", [128, 8, TO], BF16); n2tb = B.buf("n2t")
                xoh4 = xoh.rearrange("d (b t) -> d b t", t=144)

                def bd_S1(t):
                    for dc in range(8):
                        ci = clc["i"] % NCL
                        clc["i"] += 1
                        B.dma("sp", cl[ci][:, 0, :], c0_s[dc * 128:(dc + 1) * 128, t * TO:(t + 1) * TO], clb[ci],
                              reads=[c0sb], writes=[clb[ci]])
                        B.dma("sp", cl[ci][:, 1, :], c2_s[dc * 128:(dc + 1) * 128, t * TO:(t + 1) * TO], clb[ci],
                              reads=[c2sb], acc=[clb[ci]])
                        ypb = 4 + (dc % 2)
                        gpb = (dc % 2)
                        for c4 in range(4):
                            mm(PS[ypb][:, 0:TO], wsbo[:, c4, dc * 128:(dc + 1) * 128], sbo4[:, c4, t * TO:(t + 1) * TO],
                               start=(c4 == 0), stop=(c4 == 3), R=[wsbob, sbo4b],
                               **({"W": [PB[ypb]]} if c4 == 0 else {"A": [PB[ypb]]}))
                        gate_mm(gpb, wg1, wg1b, dc, t)
                        ot, otb = gated(ypb, gpb)
                        tt("pool", cl[ci][:, 0, :], cl[ci][:, 0, :], cl[ci][:, 1, :], ALU.add, R=[clb[ci]], A=[clb[ci]])
                        tt("dve", mg[t % 2][:, dc, :], ot[:], cl[ci][:, 0, :], ALU.add, R=[otb, clb[ci]],
                           **({"W": [mgb[t % 2]]} if dc == 0 else {"A": [mgb[t % 2]]}))
                        yield

                def bd_S2(t):
                    for dc in range(8):
                        B.dma("sp", xr[:, dc, :, :], xoh4[dc * 128:(dc + 1) * 128, t * BPT:(t + 1) * BPT, 16:144], xrb,
                              **({"writes": [xrb]} if dc == 0 else {"acc": [xrb]}))
                    m_, mb_ = mo[t % 2], mob[t % 2]
                    for dc in range(8):
                        pb = 2 + (dc % 2)
                        for k in range(8):
                            mm(PS[pb][:, 0:TO], wout[:, k, dc * 128:(dc + 1) * 128], mg[t % 2][:, k, :],
                               start=(k == 0), stop=(k == 7), R=[woutb, mgb[t % 2]],
                               **({"W": [PB[pb]]} if k == 0 else {"A": [PB[pb]]}))
                        cp("dve", m_[:, dc, :], PS[pb][:, 0:TO], R=[PB[pb]], **({"W": [mb_]} if dc == 0 else {"A": [mb_]}))
                        act(sqm[dc % 2][:], m_[:, dc, :], AF.Square, R=[mb_], W=[sqmb[dc % 2]])
                        mm(PS[6][:, 0:TO], ones[:], sqm[dc % 2][:], start=(dc == 0), stop=(dc == 7),
                           R=[onesb, sqmb[dc % 2]], **({"W": [PB[6]]} if dc == 0 else {"A": [PB[6]]}))

                def bd_S3(t):
                    m_, mb_ = mo[t % 2], mob[t % 2]
                    rstd_from(PS[6][:, 0:TO], PB[6], TO, lnm[:], lnmb, rsm1[:], rsm1b)
                    for dc in range(8):
                        stt(m_[:, dc, :], m_[:, dc, :], vec[:, V_POST + dc:V_POST + dc + 1], rsm1[:],
                            R=[mb_, vecb, rsm1b], A=[mb_])
                        tt("pool", m_[:, dc, :], m_[:, dc, :], xr[:, dc, :, :].rearrange("p b t -> p (b t)"), ALU.add,
                           R=[mb_, xrb], A=[mb_])
                        act(sqm[dc % 2][:], m_[:, dc, :], AF.Square, R=[mb_], W=[sqmb[dc % 2]])
                        mm(PS[7][:, 0:TO], ones[:], sqm[dc % 2][:], start=(dc == 0), stop=(dc == 7),
                           R=[onesb, sqmb[dc % 2]], **({"W": [PB[7]]} if dc == 0 else {"A": [PB[7]]}))
                        yield
                    B.dma("sp", h1_s[:, t * TO:(t + 1) * TO].rearrange("(c p) n -> p c n", p=128), m_[:], mb_, store=True,
                          reads=[mb_], acc=[h1sb])
                    rstd_from(PS[7][:, 0:TO], PB[7], TO, lnm[:], lnmb, rsm2[:], rsm2b)
                    for dc in range(8):
                        stt(n2t[:, dc, :], m_[:, dc, :], vec[:, V_FPRE + dc:V_FPRE + dc + 1], rsm2[:],
                            R=[mb_, rsm2b, vecb], **({"W": [n2tb]} if dc == 0 else {"A": [n2tb]}))
                    B.dma("sp", n2_s[:, t * TO:(t + 1) * TO].rearrange("(c p) n -> p c n", p=128), n2t[:], n2tb, store=True,
                          reads=[n2tb], acc=[n2sb])

                run(bd_S1(0))
                for t in range(NT):
                    bd_S2(t)
                    if t + 1 < NT:
                        interleave(bd_S3(t), bd_S1(t + 1))
                    else:
                        run(bd_S3(t))
                B.barrier()
        if debug:
            with contextlib.ExitStack() as sd:
                dd = sbt(sd, "dd2", [128, 8, NQ], F32); ddb = B.buf("dd2")
                B.dma("sp", dd[:], h1_s.rearrange("(c p) n -> p c n", p=128), ddb, reads=[h1sb], writes=[ddb])
                B.dma("sp", dbg["h1"].rearrange("(c p) n -> p c n", p=128), dd[:], ddb, store=True, reads=[ddb])
                B.barrier()

        if stop_after == "P4":
            B.barrier()
            B.emit()
            return nc
        with contextlib.ExitStack() as s56:
            actT = sbt(s56, "actT", [128, NFC, NQ], BF16); actTb = B.buf("actT")
            wfo = sbt(s56, "wfo", [128, NFC, D], BF16); wfob = B.buf("wfo")
            with contextlib.ExitStack() as s5:
                n2 = sbt(s5, "n2", [128, 8, NQ], BF16); n2b = B.buf("n2")
                new_stg(s5)
                B.dma("sp", n2[:], n2_s.rearrange("(c p) n -> p c n", p=128), n2b, reads=[n2sb], writes=[n2b])
                wfi = [sbt(s5, "wfi%d" % i, [128, 8, 256], BF16) for i in range(2)]
                wfib = [B.buf("wfi%d" % i) for i in range(2)]
                sil = [sbt(s5, "sil%d" % i, [128, TO], F32) for i in range(2)]
                silb = [B.buf("sil%d" % i) for i in range(2)]

                def p5_loadw(f):
                    sl = f % 2
                    load_w(wfi[sl][:, :, 0:128], wfib[sl], w_fi[:, f * 128:(f + 1) * 128], 8, 128, gcol=V_FPRE, first=True)
                    load_w(wfi[sl][:, :, 128:256], wfib[sl], w_fi[:, DFF + f * 128:DFF + (f + 1) * 128], 8, 128,
                           gcol=V_FPRE, first=False)

                p5_loadw(0)
                cnt5 = 0
                for f in range(NFC):
                    if f + 1 < NFC:
                        p5_loadw(f + 1)
                    load_w(wfo[:, f:f + 1, :], wfob, w_fo[f * 128:(f + 1) * 128, :], 1, D, first=(f == 0))
                    sl = f % 2
                    for t in range(NT):
                        gp = (cnt5 % 2) * 2
                        up = gp + 1
                        si = cnt5 % 2
                        cnt5 += 1
                        for k in range(8):
                            mm(PS[gp][:, 0:TO], wfi[sl][:, k, 0:128], n2[:, k, t * TO:(t + 1) * TO],
                               start=(k == 0), stop=(k == 7), R=[wfib[sl], n2b],
                               **({"W": [PB[gp]]} if k == 0 else {"A": [PB[gp]]}))
                        for k in range(8):
                            mm(PS[up][:, 0:TO], wfi[sl][:, k, 128:256], n2[:, k, t * TO:(t + 1) * TO],
                               start=(k == 0), stop=(k == 7), R=[wfib[sl], n2b],
                               **({"W": [PB[up]]} if k == 0 else {"A": [PB[up]]}))
                        act(sil[si][:], PS[gp][:, 0:TO], AF.Silu, R=[PB[gp]], W=[silb[si]])
                        tt("dve", actT[:, f, t * TO:(t + 1) * TO], sil[si][:], PS[up][:, 0:TO], ALU.mult,
                           R=[silb[si], PB[up]], A=[actTb])
                B.barrier()
            with contextlib.ExitStack() as s6:
                ff = [sbt(s6, "ff%d" % i, [128, 8, TO], F32) for i in range(2)]
                ffb = [B.buf("ff%d" % i) for i in range(2)]
                sq6 = [sbt(s6, "sq6_%d" % i, [128, TO], BF16) for i in range(2)]
                sq6b = [B.buf("sq6_%d" % i) for i in range(2)]
                ln6 = sbt(s6, "ln6", [128, TO], F32); ln6b = B.buf("ln6")
                rs6 = sbt(s6, "rs6", [128, TO], F32); rs6b = B.buf("rs6")
                h1t = [sbt(s6, "h1t%d" % i, [128, TO], F32) for i in range(3)]
                h1tb = [B.buf("h1t%d" % i) for i in range(3)]
                outsb = B.buf("outT")
                cnt6 = {"h": 0}

                def p6_M(t):
                    f_, fb_ = ff[t % 2], ffb[t % 2]
                    ssb = 4 + (t % 2)
                    for dc in range(8):
                        pb = dc % 4
                        for f in range(NFC):
                            mm(PS[pb][:, 0:TO], wfo[:, f, dc * 128:(dc + 1) * 128], actT[:, f, t * TO:(t + 1) * TO],
                               start=(f == 0), stop=(f == NFC - 1), R=[wfob, actTb],
                               **({"W": [PB[pb]]} if f == 0 else {"A": [PB[pb]]}))
                        cp("dve", f_[:, dc, :], PS[pb][:, 0:TO], R=[PB[pb]], **({"W": [fb_]} if dc == 0 else {"A": [fb_]}))
                        act(sq6[dc % 2][:], f_[:, dc, :], AF.Square, R=[fb_], W=[sq6b[dc % 2]])
                        mm(PS[ssb][:, 0:TO], ones[:], sq6[dc % 2][:], start=(dc == 0), stop=(dc == 7),
                           R=[onesb, sq6b[dc % 2]], **({"W": [PB[ssb]]} if dc == 0 else {"A": [PB[ssb]]}))
                        yield

                def p6_E(t):
                    f_, fb_ = ff[t % 2], ffb[t % 2]
                    ssb = 4 + (t % 2)
                    rstd_from(PS[ssb][:, 0:TO], PB[ssb], TO, ln6[:], ln6b, rs6[:], rs6b)
                    for dc in range(8):
                        hi = cnt6["h"] % 3
                        cnt6["h"] += 1
                        B.dma("sp", h1t[hi][:], h1_s[dc * 128:(dc + 1) * 128, t * TO:(t + 1) * TO], h1tb[hi],
                              reads=[h1sb], writes=[h1tb[hi]])
                        stt(f_[:, dc, :], f_[:, dc, :], vec[:, V_FPOST + dc:V_FPOST + dc + 1], rs6[:],
                            R=[fb_, vecb, rs6b], A=[fb_])
                        tt("pool", f_[:, dc, :], f_[:, dc, :], h1t[hi][:], ALU.add, R=[fb_, h1tb[hi]], A=[fb_])
                        yield
                    B.dma("sp", outT[:, t * TO:(t + 1) * TO].rearrange("(c p) n -> p c n", p=128), f_[:], fb_, store=True,
                          reads=[fb_], acc=[outsb])

                run(p6_M(0))
                for t in range(NT):
                    if t + 1 < NT:
                        interleave(p6_M(t + 1), p6_E(t))
                    else:
                        run(p6_E(t))
                B.barrier()
        B.barrier()
        B.emit()
    return nc


def _own_blocks(r, G):
    blks = []
    for g in range(G):
        blks.append(8 * g + r)
        blks.append(8 * g + 7 - r)
    return blks


def _pack_vecs(norm_mix_pre, norm_mem, norm_mix_post, norm_ffn_pre, norm_ffn_post, pool_scale):
    cols = []
    for v in (norm_mix_pre, norm_mem, norm_mix_post, norm_ffn_pre, norm_ffn_post):
        cols.append(np.asarray(v, np.float32).reshape(8, 128).T)
    cols.append(np.asarray(pool_scale, np.float32).reshape(4, 128).T)
    return np.ascontiguousarray(np.concatenate(cols, axis=1))


_PROG_CACHE = {}


def run_layer(x, mem, norm_mix_pre, w_in, w_pool_mix, pool_scale, w_pool_o, w_sb_o, norm_mem, w_mem_kv, w_x_o,
              w_out, norm_mix_post, norm_ffn_pre, w_ffn_in, w_ffn_out, norm_ffn_post, debug=False, stop_after=None):
    x = np.asarray(x, np.float32)
    mem = np.asarray(mem, np.float32)
    Bn, S, _ = x.shape
    G = S // 1024
    assert Bn == 2 and S == 1024 * G
    key = (G, debug, stop_after)
    if key not in _PROG_CACHE:
        _PROG_CACHE[key] = build_program(G, debug, stop_after)
    nc = _PROG_CACHE[key]
    f32 = lambda a: np.ascontiguousarray(np.asarray(a, np.float32))
    shared = {
        "w_in": f32(w_in[0]), "w_pmix": f32(np.asarray(w_pool_mix[0]).reshape(512, 128)), "w_po": f32(w_pool_o[0]),
        "w_sbo": f32(w_sb_o[0]), "w_kv": f32(w_mem_kv[0]), "w_xo": f32(w_x_o[0]), "w_out": f32(w_out[0]),
        "w_fi": f32(w_ffn_in[0]), "w_fo": f32(w_ffn_out[0]),
        "vecs": _pack_vecs(norm_mix_pre[0], norm_mem[0], norm_mix_post[0], norm_ffn_pre[0], norm_ffn_post[0],
                           pool_scale[0]),
    }
    xTb = [np.ascontiguousarray(x[b].T) for b in range(2)]
    memTb = [np.ascontiguousarray(mem[b].T) for b in range(2)]
    in_maps = []
    for core in range(8):
        b, r = core // 4, core % 4
        blks = _own_blocks(r, G)
        xoh = np.zeros((D, len(blks) * 144), np.float32)
        posv = np.zeros((len(blks) * 128,), np.float32)
        for n, blk in enumerate(blks):
            s0 = blk * 128
            lo = max(0, s0 - 16)
            xoh[:, n * 144 + 16 - (s0 - lo):(n + 1) * 144] = xTb[b][:, lo:s0 + 128]
            posv[n * 128:(n + 1) * 128] = np.arange(s0, s0 + 128, dtype=np.float32)
        m = dict(shared)
        m.update({"xT": xTb[b], "xoh": xoh, "pos": posv, "memT": memTb[b]})
        in_maps.append(m)
    res = run_bass_kernel_spmd(nc, in_maps, core_ids=list(range(8)))
    out = np.empty((2, S, D), np.float32)
    for core in range(8):
        b, r = core // 4, core % 4
        oT = np.asarray(res.results[core]["outT"])
        for n, blk in enumerate(_own_blocks(r, G)):
            out[b, blk * 128:(blk + 1) * 128, :] = oT[:, n * 128:(n + 1) * 128].T
    if debug:
        return out, res.results
    return out


def kernel(**inputs):
    return run_layer(**inputs)
```

```python
import contextlib
import numpy as np
import concourse.bass as bass
import concourse.mybir as mybir
from concourse.bass_utils import run_bass_kernel_spmd

F32 = mybir.dt.float32
BF16 = mybir.dt.bfloat16
I32 = mybir.dt.int32
AF = mybir.ActivationFunctionType
ALU = mybir.AluOpType
AX = mybir.AxisListType

ENGS = ("pe", "act", "dve", "pool", "sp")
SAME_ENG_RAW = True

D = 1024
DFF = 2816
NFC = DFF // 128
INW = 5376
O_Q, O_K, O_V, O_XQ, O_G = 512, 1024, 1536, 2048, 2304
EPS = 1e-6
NEG_BIG = -30000.0
V_PRE, V_MEM, V_POST, V_FPRE, V_FPOST, V_PSC, NVEC = 0, 8, 16, 24, 32, 40, 44


class Buf:
    __slots__ = ("name", "w", "r", "excl")

    def __init__(self, name):
        self.name = name
        self.w = {}
        self.r = {}
        self.excl = False


class Builder:
    def __init__(self, nc, es):
        self.nc = nc
        self.es = es
        self.q = {e: [] for e in ENGS}
        self.cnt = {}
        self.seen = {e: {} for e in ENGS}
        self.semh = {}
        self.nbuf = 0
        for e in ENGS:
            if e != "sp":
                self._newsem(e)

    def _newsem(self, key):
        self.semh[key] = self.es.enter_context(self.nc.semaphore("s%d" % len(self.semh)))
        self.cnt[key] = 0

    def buf(self, name=None):
        self.nbuf += 1
        return Buf("%s#%d" % (name or "b", self.nbuf))

    def _wait(self, eng, key, val):
        if self.seen[eng].get(key, 0) >= val:
            return
        self.seen[eng][key] = val
        sem = self.semh[key]
        self.q[eng].append(lambda e, sem=sem, val=val: e.wait_ge(sem, val))

    def _deps(self, eng, reads, writes, acc):
        raw = {}
        oth = {}

        def add(dst, d):
            for k, v in d.items():
                if dst.get(k, 0) < v:
                    dst[k] = v
        for b in reads:
            add(raw, b.w)
            if b.excl:
                add(oth, b.r)
        for b in writes:
            add(oth, b.w)
            add(oth, b.r)
        for b in acc:
            add(oth, b.r)
        for k, v in raw.items():
            if k == eng and (eng == "pe" or not SAME_ENG_RAW):
                continue
            self._wait(eng, k, v)
        for k, v in oth.items():
            if k == eng:
                continue
            self._wait(eng, k, v)

    def _record(self, key, val, reads, writes, acc):
        for b in reads:
            if b.r.get(key, 0) < val:
                b.r[key] = val
        for b in writes:
            b.w = {key: val}
            b.r = {}
        for b in acc:
            b.w[key] = val
            b.r = {}

    def op(self, eng, fn, reads=(), writes=(), acc=()):
        self._deps(eng, reads, writes, acc)
        self.cnt[eng] += 1
        c = self.cnt[eng]
        sem = self.semh[eng]
        self.q[eng].append(lambda e, fn=fn, sem=sem: fn(e).then_inc(sem, 1))
        self._record(eng, c, reads, writes, acc)

    def dma(self, qeng, out_ap, in_ap, sb, store=False, reads=(), writes=(), acc=(), **kw):
        self._deps(qeng, reads, writes, acc)
        key = ("st" if store else "ld", sb.name)
        if key not in self.semh:
            self._newsem(key)
        self.cnt[key] += 16
        c = self.cnt[key]
        sem = self.semh[key]
        self.q[qeng].append(
            lambda e, o=out_ap, i=in_ap, sem=sem, kw=kw: e.dma_start(out=o, in_=i, **kw).then_inc(sem, 16))
        self._record(key, c, reads, writes, acc)

    def barrier(self):
        for e in ENGS:
            for k, c in self.cnt.items():
                if k == e or c == 0:
                    continue
                self._wait(e, k, c)

    def emit(self):
        q = self.q
        with self.nc.Block() as block:
            @block.tensor
            def _(e):
                for t in q["pe"]:
                    t(e)

            @block.scalar
            def _(e):
                for t in q["act"]:
                    t(e)

            @block.vector
            def _(e):
                for t in q["dve"]:
                    t(e)

            @block.gpsimd
            def _(e):
                for t in q["pool"]:
                    t(e)

            @block.sync
            def _(e):
                for t in q["sp"]:
                    t(e)


def build_program(G, debug=False, stop_after=None):
    S = 1024 * G
    NBLK = S // 128
    NB = 2 * G
    NQ = 128 * NB
    BPT = 4 if G >= 2 else 2
    TO = 128 * BPT
    TOH = 144 * BPT
    NT = NB // BPT
    ST = S // 512

    nc = bass.Bass("TRN2", target_bir_lowering=False)

    def din(name, shape, dt=F32):
        return nc.dram_tensor(name, list(shape), dt, kind="ExternalInput").ap()

    xT = din("xT", [D, S])
    xoh = din("xoh", [D, NB * 144])
    pos = din("pos", [NQ])
    memT = din("memT", [D, 256])
    w_in = din("w_in", [D, INW])
    w_pmix = din("w_pmix", [512, 128])
    w_po = din("w_po", [512, D])
    w_sbo = din("w_sbo", [512, D])
    w_kv = din("w_kv", [D, 512])
    w_xo = din("w_xo", [256, D])
    w_out = din("w_out", [D, D])
    w_fi = din("w_fi", [D, 2 * DFF])
    w_fo = din("w_fo", [DFF, D])
    vecs = din("vecs", [128, NVEC])
    outT = nc.dram_tensor("outT", [D, NQ], F32, kind="ExternalOutput").ap()
    kT_s = nc.dram_tensor("kT_s", [4, 128, S], BF16).ap()
    v_s = nc.dram_tensor("v_s", [4, 128, NBLK, 128], BF16).ap()
    c0_s = nc.dram_tensor("c0_s", [D, NQ], F32).ap()
    c2_s = nc.dram_tensor("c2_s", [D, NQ], F32).ap()
    h1_s = nc.dram_tensor("h1_s", [D, NQ], F32).ap()
    n2_s = nc.dram_tensor("n2_s", [D, NQ], BF16).ap()
    dbg = {}
    if debug:
        dbg["qT"] = nc.dram_tensor("dbg_qT", [128, 4, 2 * NQ], BF16, kind="ExternalOutput").ap()
        dbg["sbo"] = nc.dram_tensor("dbg_sbo", [128, 4, NQ], BF16, kind="ExternalOutput").ap()
        dbg["kT"] = nc.dram_tensor("dbg_kT", [4, 128, S], BF16, kind="ExternalOutput").ap()
        dbg["v"] = nc.dram_tensor("dbg_v", [4, 128, NBLK, 128], BF16, kind="ExternalOutput").ap()
        dbg["c0"] = nc.dram_tensor("dbg_c0", [D, NQ], F32, kind="ExternalOutput").ap()
        dbg["c2"] = nc.dram_tensor("dbg_c2", [D, NQ], F32, kind="ExternalOutput").ap()
        dbg["h1"] = nc.dram_tensor("dbg_h1", [D, NQ], F32, kind="ExternalOutput").ap()

    with contextlib.ExitStack() as es:
        B = Builder(nc, es)

        def sbt(scope, name, shape, dt):
            return scope.enter_context(nc.sbuf_tensor(name, list(shape), dt))

        def mm(out, lhsT, rhs, start, stop, R, W=(), A=()):
            B.op("pe", lambda e: e.matmul(out, lhsT, rhs, start=start, stop=stop), reads=R, writes=W, acc=A)

        def act(out, in_, func, R, W=(), A=(), **kw):
            B.op("act", lambda e: e.activation(out=out, in_=in_, func=func, **kw), reads=R, writes=W, acc=A)

        def tt(eng, out, in0, in1, op, R, W=(), A=()):
            B.op(eng, lambda e: e.tensor_tensor(out=out, in0=in0, in1=in1, op=op), reads=R, writes=W, acc=A)

        def ts(eng, out, in0, s1, s2, op0, op1, R, W=(), A=()):
            if op1 is None:
                B.op(eng, lambda e: e.tensor_scalar(out=out, in0=in0, scalar1=s1, scalar2=s2, op0=op0),
                     reads=R, writes=W, acc=A)
            else:
                B.op(eng, lambda e: e.tensor_scalar(out=out, in0=in0, scalar1=s1, scalar2=s2, op0=op0, op1=op1),
                     reads=R, writes=W, acc=A)

        def cp(eng, out, in_, R, W=(), A=()):
            if eng == "act":
                act(out, in_, AF.Copy, R, W, A)
            else:
                B.op(eng, lambda e: e.tensor_copy(out=out, in_=in_), reads=R, writes=W, acc=A)

        PSALL = es.enter_context(nc.psum_tensor("psall", [128, 4096], F32))
        PS = [PSALL[:, i * 512:(i + 1) * 512] for i in range(8)]
        PB = [B.buf("ps%d" % i) for i in range(8)]
        for b_ in PB:
            b_.excl = True
        PST = PS[7].bitcast(BF16)
        PSTB = PB[7]

        gs = es
        vec = sbt(gs, "vec", [128, NVEC], F32); vecb = B.buf("vec")
        ones = sbt(gs, "ones", [128, 128], BF16); onesb = B.buf("ones")
        nTin = sbt(gs, "nTin", [128, 128], BF16); nTinb = B.buf("nTin")
        nOnes = sbt(gs, "nOnes", [128, 128], BF16); nOnesb = B.buf("nOnes")
        nBigI = sbt(gs, "nBigI", [128, 128], BF16); nBigIb = B.buf("nBigI")
        ident = sbt(gs, "ident", [128, 128], BF16); identb = B.buf("ident")
        dif_i = sbt(gs, "dif_i", [128, 128], I32); difib = B.buf("dif_i")
        dif_f = sbt(gs, "dif_f", [128, 128], F32); diffb = B.buf("dif_f")
        kp8_i = sbt(gs, "kp8_i", [128, 8], I32); kp8ib = B.buf("kp8i")
        kp8 = sbt(gs, "kp8", [128, 8], F32); kp8b = B.buf("kp8")
        qpos = sbt(gs, "qpos", [128, NQ], F32); qposb = B.buf("qpos")
        notM = sbt(gs, "notM", [128, 8, 2, 256], BF16); notMb = B.buf("notM")
        NSTG = 2
        st_state = {"i": 0, "e": 0, "n": 0}

        def new_stg(scope):
            st_state["n"] += 1
            st_state["stg"] = [sbt(scope, "stg%d_%d" % (st_state["n"], i), [128, 1536], F32) for i in range(NSTG)]
            st_state["stgb"] = [B.buf("stg%d" % i) for i in range(NSTG)]

        B.dma("sp", vec[:], vecs, vecb, writes=[vecb])
        B.dma("sp", qpos[:], pos.partition_broadcast(128), qposb, writes=[qposb])
        B.op("dve", lambda e: e.memset(ones[:], 1.0), writes=[onesb])
        B.op("dve", lambda e: e.memset(nOnes[:], -1.0), writes=[nOnesb])
        B.op("pool", lambda e: e.iota(dif_i[:], pattern=[[-1, 128]], base=0, channel_multiplier=1), writes=[difib])
        cp("dve", dif_f[:], dif_i[:], R=[difib], W=[diffb])
        ts("dve", nTin[:], dif_f[:], 0.0, -1.0, ALU.is_ge, ALU.mult, R=[diffb], W=[nTinb])
        ts("dve", nBigI[:], dif_f[:], 0.0, NEG_BIG, ALU.is_equal, ALU.mult, R=[diffb], W=[nBigIb])
        ts("dve", ident[:], dif_f[:], 0.0, None, ALU.is_equal, None, R=[diffb], W=[identb])
        B.op("pool", lambda e: e.iota(kp8_i[:], pattern=[[128, 8]], base=0, channel_multiplier=1), writes=[kp8ib])
        cp("dve", kp8[:], kp8_i[:], R=[kp8ib], W=[kp8b])
        for jj in range(8):
            for h in range(2):
                ts("dve", notM[:, jj, h, :], qpos[:, 0:256], kp8[:, jj:jj + 1], None, ALU.is_le, None,
                   R=[qposb, kp8b], **({"W": [notMb]} if (jj == 0 and h == 0) else {"A": [notMb]}))

        def load_w(dst, dstb, src, kc, ncols, gcol=None, first=True):
            kg = max(1, min(kc, 1536 // ncols))
            k0 = 0
            while k0 < kc:
                kn = min(kg, kc - k0)
                i = st_state["i"] % NSTG
                st_state["i"] += 1
                stg, stgb = st_state["stg"], st_state["stgb"]
                sv = stg[i][:, 0:kn * ncols].rearrange("p (k n) -> p k n", k=kn)
                B.dma("sp", sv, src[k0 * 128:(k0 + kn) * 128, :].rearrange("(k p) n -> p k n", p=128), stgb[i],
                      writes=[stgb[i]])
                eng = "dve" if st_state["e"] % 2 == 0 else "act"
                st_state["e"] += 1
                kw = {"W": [dstb]} if (first and k0 == 0) else {"A": [dstb]}
                cp(eng, dst[:, k0:k0 + kn, :], sv, R=[stgb[i]], **kw)
                k0 += kn

        def run(gen):
            for _ in gen:
                pass

        def interleave(*gens):
            gens = list(gens)
            while gens:
                for g_ in list(gens):
                    try:
                        next(g_)
                    except StopIteration:
                        gens.remove(g_)

        def stt(out, in0, scal, in1, R, W=(), A=()):
            B.op("dve", lambda e: e.scalar_tensor_tensor(out=out, in0=in0, scalar=scal, in1=in1,
                                                         op0=ALU.mult, op1=ALU.mult), reads=R, writes=W, acc=A)

        def rstd_from(ss_ps, ss_b, ncols, ln_t, ln_b, out_t, out_b, out_is_write=True):
            act(ln_t, ss_ps, AF.Ln, R=[ss_b], W=[ln_b], scale=1.0 / D, bias=EPS)
            act(out_t, ln_t, AF.Exp, R=[ln_b], W=[out_b], scale=-0.5)

        s13 = contextlib.ExitStack()
        es.enter_context(s13)
        NGT = BPT // 2
        Qz = sbt(s13, "Qz", [128, 4, G, 2, 256], BF16); Qzb = B.buf("Qz")
        B.op("pool", lambda e: e.memset(Qz[:].rearrange("p a g h q -> p (a g h q)"), 0.0), writes=[Qzb])
        wqkv = sbt(s13, "wqkv", [128, 8, 1536], BF16); wqkvb = B.buf("wqkv")

        xn_s = nc.dram_tensor("xn_s", [128, 8, NB * 144], BF16).ap()
        xnsb = B.buf("xn_s")
        with contextlib.ExitStack() as s1:
            new_stg(s1)
            load_w(wqkv, wqkvb, w_in[:, O_Q:O_Q + 1536], 8, 1536)
            xo_t = [sbt(s1, "xo_t%d" % i, [128, 8, TOH], F32) for i in range(2)]
            xo_b = [B.buf("xo_t%d" % i) for i in range(2)]
            sq1 = sbt(s1, "sq1", [128, 8, TOH], BF16); sq1b = B.buf("sq1")
            ln1 = sbt(s1, "ln1", [128, TOH], F32); ln1b = B.buf("ln1")
            rs1 = sbt(s1, "rs1", [128, TOH], F32); rs1b = B.buf("rs1")
            xn1 = [sbt(s1, "xn1_%d" % i, [128, 8, BPT, 144], BF16) for i in range(2)]
            xn1b = [B.buf("xn1_%d" % i) for i in range(2)]
            HB = TOH // 2
            for t in range(NT):
                sl = t % 2
                B.dma("sp", xo_t[sl][:], xoh[:, t * TOH:(t + 1) * TOH].rearrange("(c p) n -> p c n", p=128),
                      xo_b[sl], writes=[xo_b[sl]])
                act(sq1[:], xo_t[sl][:], AF.Square, R=[xo_b[sl]], W=[sq1b])
                for hf in range(2):
                    for k in range(8):
                        mm(PS[hf][:, 0:HB], ones[:], sq1[:, k, hf * HB:(hf + 1) * HB], start=(k == 0), stop=(k == 7),
                           R=[onesb, sq1b], **({"W": [PB[hf]]} if k == 0 else {"A": [PB[hf]]}))
                for hf in range(2):
                    act(ln1[:, hf * HB:(hf + 1) * HB], PS[hf][:, 0:HB], AF.Ln, R=[PB[hf]],
                        **({"W": [ln1b]} if hf == 0 else {"A": [ln1b]}), scale=1.0 / D, bias=EPS)
                act(rs1[:], ln1[:], AF.Exp, R=[ln1b], W=[rs1b], scale=-0.5)
                xnv = xn1[sl][:].rearrange("p c b t -> p c (b t)")
                for k in range(8):
                    stt(xnv[:, k, :], xo_t[sl][:, k, :], vec[:, V_PRE + k:V_PRE + k + 1], rs1[:],
                        R=[xo_b[sl], rs1b, vecb], **({"W": [xn1b[sl]]} if k == 0 else {"A": [xn1b[sl]]}))
                B.dma("sp", xn_s[:, :, t * TOH:(t + 1) * TOH], xnv, xn1b[sl], store=True,
                      reads=[xn1b[sl]], acc=[xnsb])
                for c4 in range(4):
                    pb = 2 + (c4 % 2)
                    for k in range(8):
                        mm(PS[pb][:, 0:TO].rearrange("p (b t) -> p b t", b=BPT),
                           wqkv[:, k, c4 * 128:(c4 + 1) * 128], xn1[sl][:, k, :, 16:144],
                           start=(k == 0), stop=(k == 7), R=[wqkvb, xn1b[sl]],
                           **({"W": [PB[pb]]} if k == 0 else {"A": [PB[pb]]}))
                    for h in range(2):
                        act(Qz[h * 64:(h + 1) * 64, c4, t * NGT:(t + 1) * NGT, h, :],
                            PS[pb][h * 64:(h + 1) * 64, 0:TO].rearrange("p (g q) -> p g q", g=NGT), AF.Copy,
                            R=[PB[pb]], A=[Qzb], scale=0.125)
        B.barrier()
        if debug:
            B.dma("sp", dbg["qT"], Qz[:].rearrange("p a g h q -> p a (g h q)"), Qzb, store=True, reads=[Qzb])
        if stop_after == "P1":
            B.barrier()
            B.emit()
            return nc

        kTsb = B.buf("kT_s"); vsb = B.buf("v_s")
        with contextlib.ExitStack() as s2:
            x_t = [sbt(s2, "x_t%d" % i, [128, 8, 512], F32) for i in range(3)]
            x_b = [B.buf("x_t%d" % i) for i in range(3)]
            sq2 = [sbt(s2, "sq2_%d" % i, [128, 8, 512], BF16) for i in range(2)]
            sq2b = [B.buf("sq2_%d" % i) for i in range(2)]
            ln2 = sbt(s2, "ln2", [128, 512], F32); ln2b = B.buf("ln2")
            rs2 = sbt(s2, "rs2", [128, 512], F32); rs2b = B.buf("rs2")
            xn2 = [sbt(s2, "xn2_%d" % i, [128, 8, 512], BF16) for i in range(2)]
            xn2b = [B.buf("xn2_%d" % i) for i in range(2)]
            kst = [sbt(s2, "kst%d" % i, [128, 4, 512], BF16) for i in range(2)]
            kstb = [B.buf("kst%d" % i) for i in range(2)]
            vst = [sbt(s2, "vst%d" % i, [128, 4, 512], BF16) for i in range(2)]
            vstb = [B.buf("vst%d" % i) for i in range(2)]

            def p2_load(t):
                B.dma("sp", x_t[t % 3][:], xT[:, t * 512:(t + 1) * 512].rearrange("(c p) n -> p c n", p=128),
                      x_b[t % 3], writes=[x_b[t % 3]])

            def p2_sq(t):
                sl = t % 2
                act(sq2[sl][:], x_t[t % 3][:], AF.Square, R=[x_b[t % 3]], W=[sq2b[sl]])

            def p2_norm(t):
                sl = t % 2
                for k in range(8):
                    mm(PS[0][:], ones[:], sq2[sl][:, k, :], start=(k == 0), stop=(k == 7), R=[onesb, sq2b[sl]],
                       **({"W": [PB[0]]} if k == 0 else {"A": [PB[0]]}))
                rstd_from(PS[0][:], PB[0], 512, ln2[:], ln2b, rs2[:], rs2b)
                for k in range(8):
                    stt(xn2[sl][:, k, :], x_t[t % 3][:, k, :], vec[:, V_PRE + k:V_PRE + k + 1], rs2[:],
                        R=[x_b[t % 3], rs2b, vecb], **({"W": [xn2b[sl]]} if k == 0 else {"A": [xn2b[sl]]}))

            def p2_kv(t):
                sl = t % 2
                for c4 in range(4):
                    pb = 1 + (c4 % 3)
                    for k in range(8):
                        mm(PS[pb][:], wqkv[:, k, 512 + c4 * 128:512 + (c4 + 1) * 128], xn2[sl][:, k, :],
                           start=(k == 0), stop=(k == 7), R=[wqkvb, xn2b[sl]],
                           **({"W": [PB[pb]]} if k == 0 else {"A": [PB[pb]]}))
                    cp("dve", kst[sl][:, c4, :], PS[pb][:], R=[PB[pb]],
                       **({"W": [kstb[sl]]} if c4 == 0 else {"A": [kstb[sl]]}))
                B.dma("pool", kT_s[:, :, t * 512:(t + 1) * 512].rearrange("c p s -> p c s"), kst[sl][:], kstb[sl],
                      store=True, reads=[kstb[sl]], acc=[kTsb])
                for bk in range(4):
                    pb = 4 + (bk % 3)
                    for k in range(8):
                        mm(PS[pb][:], xn2[sl][:, k, bk * 128:(bk + 1) * 128], wqkv[:, k, 1024:1536],
                           start=(k == 0), stop=(k == 7), R=[wqkvb, xn2b[sl]],
                           **({"W": [PB[pb]]} if k == 0 else {"A": [PB[pb]]}))
                    cp("dve", vst[sl][:, bk, :], PS[pb][:], R=[PB[pb]],
                       **({"W": [vstb[sl]]} if bk == 0 else {"A": [vstb[sl]]}))
                for c4 in range(4):
                    B.dma("pool", v_s[c4, :, 4 * t:4 * t + 4, :], vst[sl][:, :, c4 * 128:(c4 + 1) * 128], vstb[sl],
                          store=True, reads=[vstb[sl]], acc=[vsb])

            p2_load(0)
            p2_sq(0)
            p2_norm(0)
            if ST > 1:
                p2_load(1)
                p2_sq(1)
            for t in range(ST):
                if t + 2 < ST:
                    p2_load(t + 2)
                    p2_sq(t + 2)
                if t + 1 < ST:
                    p2_norm(t + 1)
                p2_kv(t)
        B.barrier()
        if debug:
            with contextlib.ExitStack() as sd:
                dk = sbt(sd, "dk", [128, S], BF16); dkb = B.buf("dk")
                dv = sbt(sd, "dv", [128, NBLK, 128], BF16); dvb = B.buf("dv")
                for c4 in range(4):
                    B.dma("sp", dk[:], kT_s[c4], dkb, reads=[kTsb], writes=[dkb])
                    B.dma("sp", dbg["kT"][c4], dk[:], dkb, store=True, reads=[dkb])
                    B.dma("sp", dv[:], v_s[c4], dvb, reads=[vsb], writes=[dvb])
                    B.dma("sp", dbg["v"][c4], dv[:], dvb, store=True, reads=[dvb])
                B.barrier()

        if stop_after == "P2":
            B.barrier()
            B.emit()
            return nc
        s34 = contextlib.ExitStack()
        with contextlib.ExitStack() as s3:
            sbo = sbt(s3, "sbo", [128, 4, NQ], BF16); sbob = B.buf("sbo")
            kt = [sbt(s3, "kt%d" % i, [128, S], BF16) for i in range(2)]
            ktb = [B.buf("kt%d" % i) for i in range(2)]
            vt = [sbt(s3, "vt%d" % i, [128, NBLK, 128], BF16) for i in range(2)]
            vtb = [B.buf("vt%d" % i) for i in range(2)]
            NE = 2
            e_t = [sbt(s3, "e_t%d" % i, [128, 1024], F32) for i in range(NE)]
            e_b = [B.buf("e_t%d" % i) for i in range(NE)]
            NSP = 3
            sp_t = [sbt(s3, "sp_t%d" % i, [128, 1024], BF16) for i in range(NSP)]
            sp_b = [B.buf("sp_t%d" % i) for i in range(NSP)]
            a_t = [sbt(s3, "a_t%d" % i, [128, 1024], BF16) for i in range(NSP)]
            a_b = [B.buf("a_t%d" % i) for i in range(NSP)]
            R_t = [sbt(s3, "R_t%d" % i, [128, 512], F32) for i in range(2)]
            R_b = [B.buf("R_t%d" % i) for i in range(2)]
            Rb_t = [sbt(s3, "Rb_t%d" % i, [128, 512], BF16) for i in range(4)]
            Rb_b = [B.buf("Rb_t%d" % i) for i in range(4)]
            NZP = 3

            def p3_load(c):
                B.dma("sp", kt[c % 2][:], kT_s[c], ktb[c % 2], reads=[kTsb], writes=[ktb[c % 2]])
                B.dma("sp", vt[c % 2][:], v_s[c], vtb[c % 2], reads=[vsb], writes=[vtb[c % 2]])

            pairs = []
            chain_id = 0
            for c in range(4):
                for g in range(G):
                    nj = 8 * g + 8
                    for pi in range(nj // 2):
                        jh = nj - 1 - 2 * pi
                        pairs.append(dict(c=c, g=g, jh=jh, jl=jh - 1, pi=pi, last=(jh == 1), chain=chain_id))
                    chain_id += 1
            for s_i, T in enumerate(pairs):
                T["zs"] = s_i % NZP
                T["sl3"] = s_i % NSP
                T["esl"] = s_i % NE
                T["ob"] = 6 + (T["chain"] % 2)
                T["rs"] = T["chain"] % 2

            def st_P1(T):
                c, g = T["c"], T["g"]
                qv = Qz[:, c, g, :, :].rearrange("p h q -> p (h q)")
                for i, j in enumerate((T["jh"], T["jl"])):
                    bk_ = 2 * T["zs"] + i
                    mm(PS[bk_][:], kt[c % 2][:, j * 128:(j + 1) * 128], qv, start=True, stop=False,
                       R=[ktb[c % 2], Qzb], W=[PB[bk_]])
                    jj = j - 8 * g
                    if jj >= 0:
                        mm(PS[bk_][:], nBigI[:], notM[:, jj, :, :].rearrange("p h q -> p (h q)"),
                           start=False, stop=False, R=[nBigIb, notMb], A=[PB[bk_]])

            def zpair(T):
                zs = T["zs"]
                return PSALL[:, zs * 1024:(zs + 1) * 1024], [PB[2 * zs], PB[2 * zs + 1]]

            def st_A1a(T):
                zap, zbufs = zpair(T)
                e_ = e_t[T["esl"]]; eb = e_b[T["esl"]]
                act(e_[:], zap, AF.Exp, R=zbufs, W=[eb])

            def st_A1b(T):
                e_ = e_t[T["esl"]]; eb = e_b[T["esl"]]
                sp_ = sp_t[T["sl3"]]; spb = sp_b[T["sl3"]]
                act(sp_[:], e_[:], AF.Ln, R=[eb], W=[spb], bias=1.0)

            def st_P2(T):
                b0 = 2 * T["zs"]; b1 = b0 + 1
                sp_ = sp_t[T["sl3"]]; spb = sp_b[T["sl3"]]
                first = (T["pi"] == 0)
                rbi = (T["chain"] % 2) * 2 + (T["pi"] % 2)
                mm(PS[b0][:], nTin[:], sp_[:, 0:512], start=False, stop=first, R=[nTinb, spb], A=[PB[b0]])
                if not first:
                    mm(PS[b0][:], nOnes[:], Rb_t[rbi][:], start=False, stop=True, R=[nOnesb, Rb_b[rbi]], A=[PB[b0]])
                mm(PS[b1][:], nTin[:], sp_[:, 512:1024], start=False, stop=False, R=[nTinb, spb], A=[PB[b1]])
                mm(PS[b1][:], nOnes[:], sp_[:, 0:512], start=False, stop=first, R=[nOnesb, spb], A=[PB[b1]])
                if not first:
                    mm(PS[b1][:], nOnes[:], Rb_t[rbi][:], start=False, stop=True, R=[nOnesb, Rb_b[rbi]], A=[PB[b1]])

            def st_G(T):
                if T["last"]:
                    return
                sp_ = sp_t[T["sl3"]]; spb = sp_b[T["sl3"]]
                R_ = R_t[T["rs"]]; Rbuf = R_b[T["rs"]]
                rbn = (T["chain"] % 2) * 2 + ((T["pi"] + 1) % 2)
                if T["pi"] == 0:
                    tt("dve", R_[:], sp_[:, 0:512], sp_[:, 512:1024], ALU.add, R=[spb], W=[Rbuf])
                else:
                    tt("dve", R_[:], R_[:], sp_[:, 0:512], ALU.add, R=[spb, Rbuf], A=[Rbuf])
                    tt("dve", R_[:], R_[:], sp_[:, 512:1024], ALU.add, R=[spb, Rbuf], A=[Rbuf])
                cp("dve", Rb_t[rbn][:], R_[:], R=[Rbuf], W=[Rb_b[rbn]])

            def st_A2(T):
                zap, zbufs = zpair(T)
                a_ = a_t[T["sl3"]]; ab = a_b[T["sl3"]]
                act(a_[:], zap, AF.Exp, R=zbufs, W=[ab])

            def st_P3(T):
                c, g = T["c"], T["g"]
                a_ = a_t[T["sl3"]]; ab = a_b[T["sl3"]]
                ob = T["ob"]
                first = (T["pi"] == 0)
                mm(PS[ob][:], vt[c % 2][:, T["jh"], :], a_[:, 0:512], start=first, stop=False, R=[vtb[c % 2], ab],
                   **({"W": [PB[ob]]} if first else {"A": [PB[ob]]}))
                mm(PS[ob][:], vt[c % 2][:, T["jl"], :], a_[:, 512:1024], start=False, stop=T["last"],
                   R=[vtb[c % 2], ab], A=[PB[ob]])
                if T["last"]:
                    q0 = g * 256
                    for h in range(2):
                        cp("dve", sbo[h * 64:(h + 1) * 64, c, q0:q0 + 256],
                           PS[ob][h * 64:(h + 1) * 64, h * 256:(h + 1) * 256], R=[PB[ob]], A=[sbob])

            p3_load(0)
            n_t = len(pairs)
            for s_i in range(n_t + 2):
                if 0 <= s_i - 2 < n_t:
                    T2 = pairs[s_i - 2]
                    if T2["g"] == 0 and T2["pi"] == 0 and T2["c"] + 1 < 4:
                        p3_load(T2["c"] + 1)
                if s_i < n_t:
                    T = pairs[s_i]
                    st_P1(T)
                    st_A1a(T)
                    st_A1b(T)
                if 0 <= s_i - 1 < n_t:
                    T1 = pairs[s_i - 1]
                    st_P2(T1)
                    st_G(T1)
                    st_A2(T1)
                if 0 <= s_i - 2 < n_t:
                    st_P3(pairs[s_i - 2])
            B.barrier()
            if debug:
                B.dma("sp", dbg["sbo"], sbo[:], sbob, store=True, reads=[sbob])
                B.barrier()
            if stop_after == "P3":
                B.barrier()
                B.emit()
                return nc
            sbo_s = nc.dram_tensor("sbo_s", [128, 4, NQ], BF16).ap()
            sbosb = B.buf("sbo_s")
            B.dma("sp", sbo_s, sbo[:], sbob, store=True, reads=[sbob], writes=[sbosb])
            B.barrier()
        s13.close()

        with contextlib.ExitStack() as s4:
            xn = sbt(s4, "xn", [128, 8, NB, 144], BF16); xnb = B.buf("xn")
            B.dma("sp", xn[:].rearrange("p c b t -> p c (b t)"), xn_s, xnb, reads=[xnsb], writes=[xnb])
            NOUT = 3
            o_t = [sbt(s4, "o_t%d" % i, [128, TO], F32) for i in range(NOUT)]
            o_b = [B.buf("o_t%d" % i) for i in range(NOUT)]
            sg_t = [sbt(s4, "sg_t%d" % i, [128, TO], F32) for i in range(2)]
            sg_b = [B.buf("sg_t%d" % i) for i in range(2)]
            cnt4 = {"o": 0, "sg": 0}

            def gate_mm(pb, wg, wgb, dc, t):
                for k in range(8):
                    mm(PS[pb][:, 0:TO].rearrange("p (b t) -> p b t", b=BPT),
                       wg[:, k, dc * 128:(dc + 1) * 128], xn[:, k, t * BPT:(t + 1) * BPT, 16:144],
                       start=(k == 0), stop=(k == 7), R=[wgb, xnb],
                       **({"W": [PB[pb]]} if k == 0 else {"A": [PB[pb]]}))

            def gated(ypb, gpb):
                si = cnt4["sg"] % 2; cnt4["sg"] += 1
                oi = cnt4["o"] % NOUT; cnt4["o"] += 1
                act(sg_t[si][:], PS[gpb][:, 0:TO], AF.Sigmoid, R=[PB[gpb]], W=[sg_b[si]])
                tt("dve", o_t[oi][:], sg_t[si][:], PS[ypb][:, 0:TO], ALU.mult, R=[sg_b[si], PB[ypb]], W=[o_b[oi]])
                return o_t[oi], o_b[oi]

            c0sb = B.buf("c0_s")
            with contextlib.ExitStack() as sa:
                wpi = sbt(sa, "wpi", [128, 8, 512], BF16); wpib = B.buf("wpi")
                wmix = sbt(sa, "wmix", [128, 4, 128], BF16); wmixb = B.buf("wmix")
                wpo = sbt(sa, "wpo", [128, 4, D], BF16); wpob = B.buf("wpo")
                wg0 = sbt(sa, "wg0", [128, 8, D], BF16); wg0b = B.buf("wg0")
                new_stg(sa)
                load_w(wpi, wpib, w_in[:, 0:512], 8, 512, gcol=V_PRE)
                load_w(wmix, wmixb, w_pmix, 4, 128)
                load_w(wpo, wpob, w_po, 4, D)
                load_w(wg0, wg0b, w_in[:, O_G:O_G + D], 8, D, gcol=V_PRE)
                pa = [sbt(sa, "pa%d" % i, [128, BPT, 144], F32) for i in range(3)]
                pab = [B.buf("pa%d" % i) for i in range(3)]
                icn = sbt(sa, "icn", [128, 4, TO], F32); icnb = B.buf("icn")
                mixd = sbt(sa, "mixd", [128, 4, TO], BF16); mixdb = B.buf("mixd")
                pm = [sbt(sa, "pm%d" % i, [128, 4, TO], BF16) for i in range(2)]
                pmb = [B.buf("pm%d" % i) for i in range(2)]
                tmpa = sbt(sa, "tmpa", [128, BPT, 128], F32); tmpab = B.buf("tmpa")
                HB = TOH // 2
                HBK = BPT // 2

                def pa_pool(t):
                    for gi in range(4):
                        ts("dve", icn[:, gi, :], qpos[:, t * TO:(t + 1) * TO], 1.0, float(2 << gi), ALU.add, ALU.min,
                           R=[qposb], **({"W": [icnb]} if gi == 0 else {"A": [icnb]}))
                    B.op("dve", lambda e: e.reciprocal(out=icn[:], in_=icn[:]), reads=[icnb], writes=[icnb])
                    for gi in range(4):
                        p0, p0b = pa[0], pab[0]
                        for hf in range(2):
                            pb = hf
                            for k in range(8):
                                mm(PS[pb][:, 0:HB].rearrange("p (b t) -> p b t", b=HBK),
                                   wpi[:, k, gi * 128:(gi + 1) * 128],
                                   xn[:, k, t * BPT + hf * HBK:t * BPT + (hf + 1) * HBK, :],
                                   start=(k == 0), stop=(k == 7), R=[wpib, xnb],
                                   **({"W": [PB[pb]]} if k == 0 else {"A": [PB[pb]]}))
                            cp("act", p0[:, hf * HBK:(hf + 1) * HBK, :],
                               PS[pb][:, 0:HB].rearrange("p (b t) -> p b t", b=HBK), R=[PB[pb]],
                               **({"W": [p0b]} if hf == 0 else {"A": [p0b]}))
                        cur, curb = p0, p0b
                        pp = 1
                        dsh = 1
                        lo = 0
                        for step in range(gi + 1):
                            nxt, nxtb = pa[pp], pab[pp]
                            lo = lo + dsh
                            tt("dve" if step % 2 == 0 else "pool", nxt[:, :, lo:144], cur[:, :, lo:144],
                               cur[:, :, lo - dsh:144 - dsh], ALU.add, R=[curb], W=[nxtb])
                            cur, curb = nxt, nxtb
                            pp = 3 - pp
                            dsh *= 2
                        tt("dve", tmpa[:], cur[:, :, 16:144], icn[:, gi, :].rearrange("p (b t) -> p b t", b=BPT),
                           ALU.mult, R=[curb, icnb], W=[tmpab])
                        tt("dve", mixd[:, gi, :].rearrange("p (b t) -> p b t", b=BPT), tmpa[:], p0[:, :, 16:144],
                           ALU.subtract, R=[tmpab, p0b], **({"W": [mixdb]} if gi == 0 else {"A": [mixdb]}))
                        yield
                    for gi in range(4):
                        pb = 2 + (gi % 2)
                        mm(PS[pb][:, 0:TO], wmix[:, gi, :], mixd[:, gi, :], start=True, stop=True,
                           R=[wmixb, mixdb], W=[PB[pb]])
                        ts("dve", pm[t % 2][:, gi, :], PS[pb][:, 0:TO], vec[:, V_PSC + gi:V_PSC + gi + 1], None,
                           ALU.mult, None, R=[PB[pb], vecb], **({"W": [pmb[t % 2]]} if gi == 0 else {"A": [pmb[t % 2]]}))
                    yield

                def pa_proj(t):
                    for dc in range(8):
                        ypb = 4 + (dc % 2)
                        gpb = 6 + (dc % 2)
                        for gi in range(4):
                            mm(PS[ypb][:, 0:TO], wpo[:, gi, dc * 128:(dc + 1) * 128], pm[t % 2][:, gi, :],
                               start=(gi == 0), stop=(gi == 3), R=[wpob, pmb[t % 2]],
                               **({"W": [PB[ypb]]} if gi == 0 else {"A": [PB[ypb]]}))
                        gate_mm(gpb, wg0, wg0b, dc, t)
                        ot, otb = gated(ypb, gpb)
                        B.dma("sp", c0_s[dc * 128:(dc + 1) * 128, t * TO:(t + 1) * TO], ot[:], otb, store=True,
                              reads=[otb], acc=[c0sb])
                        yield

                run(pa_pool(0))
                for t in range(NT):
                    if t + 1 < NT:
                        interleave(pa_proj(t), pa_pool(t + 1))
                    else:
                        run(pa_proj(t))
                B.barrier()

            c2sb = B.buf("c2_s")
            with contextlib.ExitStack() as sc:
                wxq = sbt(sc, "wxq", [128, 8, 256], BF16); wxqb = B.buf("wxq")
                wkv = sbt(sc, "wkv", [128, 8, 512], BF16); wkvb = B.buf("wkv")
                wxo = sbt(sc, "wxo", [128, 2, D], BF16); wxob = B.buf("wxo")
                wg2 = sbt(sc, "wg2", [128, 8, D], BF16); wg2b = B.buf("wg2")
                new_stg(sc)
                load_w(wkv, wkvb, w_kv, 8, 512, gcol=V_MEM)
                load_w(wxq, wxqb, w_in[:, O_XQ:O_XQ + 256], 8, 256, gcol=V_PRE)
                load_w(wxo, wxob, w_xo, 2, D)
                load_w(wg2, wg2b, w_in[:, O_G + 2 * D:O_G + 3 * D], 8, D, gcol=V_PRE)
                m_t = sbt(sc, "m_t", [128, 8, 256], F32); m_b = B.buf("m_t")
                msq = sbt(sc, "msq", [128, 8, 256], BF16); msqb = B.buf("msq")
                mln = sbt(sc, "mln", [128, 256], F32); mlnb = B.buf("mln")
                mrs = sbt(sc, "mrs", [128, 256], F32); mrsb = B.buf("mrs")
                mn = sbt(sc, "mn", [128, 8, 256], BF16); mnb = B.buf("mn")
                mkT = sbt(sc, "mkT", [128, 2, 256], BF16); mkTb = B.buf("mkT")
                mvz = sbt(sc, "mvz", [128, 4, 2, 128], BF16); mvb = B.buf("mvz")
                xqz = sbt(sc, "xqz", [128, 4, TO], BF16); xqTb = B.buf("xqz")
                B.op("pool", lambda e: e.memset(mvz[:].rearrange("p a b c -> p (a b c)"), 0.0), writes=[mvb])
                B.op("pool", lambda e: e.memset(xqz[:].rearrange("p a b -> p (a b)"), 0.0), writes=[xqTb])
                nmx = sbt(sc, "nmx", [128, 4], F32); nmxb = B.buf("nmx")
                ssum = sbt(sc, "ssum", [128, 4], F32); ssumb = B.buf("ssum")
                rsm = sbt(sc, "rsm", [128, 4], F32); rsmb = B.buf("rsm")
                P_t = sbt(sc, "P_t", [128, 4, 256], F32); P_b = B.buf("P_t")
                Pn = sbt(sc, "Pn", [128, 4, 256], BF16); Pnb = B.buf("Pn")
                PT = sbt(sc, "PT", [128, 8, 128], BF16); PTb = B.buf("PT")
                xoT = sbt(sc, "xoT", [128, 2, TO], BF16); xoTb = B.buf("xoT")

                B.dma("sp", m_t[:], memT.rearrange("(c p) n -> p c n", p=128), m_b, writes=[m_b])
                act(msq[:], m_t[:], AF.Square, R=[m_b], W=[msqb])
                for k in range(8):
                    mm(PS[0][:, 0:256], ones[:], msq[:, k, :], start=(k == 0), stop=(k == 7), R=[onesb, msqb],
                       **({"W": [PB[0]]} if k == 0 else {"A": [PB[0]]}))
                rstd_from(PS[0][:, 0:256], PB[0], 256, mln[:], mlnb, mrs[:], mrsb)
                for k in range(8):
                    stt(mn[:, k, :], m_t[:, k, :], vec[:, V_MEM + k:V_MEM + k + 1], mrs[:], R=[m_b, mrsb, vecb],
                        **({"W": [mnb]} if k == 0 else {"A": [mnb]}))
                for ch in range(2):
                    for k in range(8):
                        mm(PS[1][:, 0:256], wkv[:, k, ch * 128:(ch + 1) * 128], mn[:, k, :], start=(k == 0), stop=(k == 7),
                           R=[wkvb, mnb], **({"W": [PB[1]]} if k == 0 else {"A": [PB[1]]}))
                    cp("dve", mkT[:, ch, :], PS[1][:, 0:256], R=[PB[1]], **({"W": [mkTb]} if ch == 0 else {"A": [mkTb]}))
                for mc in range(2):
                    for k in range(8):
                        mm(PS[2][:, 0:256], mn[:, k, mc * 128:(mc + 1) * 128], wkv[:, k, 256:512], start=(k == 0), stop=(k == 7),
                           R=[wkvb, mnb], **({"W": [PB[2]]} if k == 0 else {"A": [PB[2]]}))
                    for h in range(4):
                        cp("dve", mvz[:, h, mc, (h % 2) * 64:(h % 2 + 1) * 64], PS[2][:, h * 64:(h + 1) * 64],
                           R=[PB[2]], A=[mvb])

                for t in range(NT):
                    for ch in range(2):
                        for k in range(8):
                            mm(PS[3][:, 0:TO].rearrange("p (b t) -> p b t", b=BPT),
                               wxq[:, k, ch * 128:(ch + 1) * 128], xn[:, k, t * BPT:(t + 1) * BPT, 16:144],
                               start=(k == 0), stop=(k == 7), R=[wxqb, xnb],
                               **({"W": [PB[3]]} if k == 0 else {"A": [PB[3]]}))
                        for hp in range(2):
                            act(xqz[hp * 64:(hp + 1) * 64, ch * 2 + hp, :], PS[3][hp * 64:(hp + 1) * 64, 0:TO], AF.Copy,
                                R=[PB[3]], A=[xqTb], scale=0.125)
                    for bk in range(BPT):
                        for h in range(4):
                            pb = h // 2
                            hp = h % 2
                            mm(PS[pb][:, hp * 256:(hp + 1) * 256],
                               xqz[:, h, bk * 128:(bk + 1) * 128], mkT[:, h // 2, :],
                               start=True, stop=True, R=[xqTb, mkTb],
                               **({"W": [PB[pb]]} if hp == 0 else {"A": [PB[pb]]}))
                        for pb in range(2):
                            B.op("dve", lambda e, pb=pb: e.reduce_max(
                                out=nmx[:, 2 * pb:2 * pb + 2], in_=PS[pb][:].rearrange("p (h m) -> p h m", h=2),
                                axis=AX.X, negate=True), reads=[PB[pb]],
                                **({"writes": [nmxb]} if pb == 0 else {"acc": [nmxb]}))
                        for h in range(4):
                            pb = h // 2
                            hp = h % 2
                            act(P_t[:, h, :], PS[pb][:, hp * 256:(hp + 1) * 256], AF.Exp, R=[PB[pb], nmxb],
                                **({"W": [P_b, ssumb]} if h == 0 else {"A": [P_b, ssumb]}),
                                bias=nmx[:, h:h + 1], accum_out=ssum[:, h:h + 1])
                        B.op("dve", lambda e: e.reciprocal(out=rsm[:], in_=ssum[:]), reads=[ssumb], writes=[rsmb])
                        for h in range(4):
                            ts("dve", Pn[:, h, :], P_t[:, h, :], rsm[:, h:h + 1], None, ALU.mult, None,
                               R=[P_b, rsmb], **({"W": [Pnb]} if h == 0 else {"A": [Pnb]}))
                        for h in range(4):
                            for mc in range(2):
                                i8 = h * 2 + mc
                                B.op("pe", lambda e, h=h, mc=mc, i8=i8: e.transpose(
                                    PST[:, i8 * 128:(i8 + 1) * 128], Pn[:, h, mc * 128:(mc + 1) * 128], ident[:]),
                                    reads=[Pnb, identb], **({"writes": [PSTB]} if i8 == 0 else {"acc": [PSTB]}))
                        cp("act", PT[:].rearrange("p a b -> p (a b)"), PST[:], R=[PSTB], W=[PTb])
                        for ch in range(2):
                            pb = 2 + ch
                            n4 = 0
                            for h in (2 * ch, 2 * ch + 1):
                                for mc in range(2):
                                    mm(PS[pb][:, bk * 128:(bk + 1) * 128], mvz[:, h, mc, :], PT[:, h * 2 + mc, :],
                                       start=(n4 == 0), stop=(n4 == 3), R=[mvb, PTb],
                                       **({"W": [PB[pb]]} if (bk == 0 and n4 == 0) else {"A": [PB[pb]]}))
                                    n4 += 1
                    for ch in range(2):
                        cp("dve", xoT[:, ch, :], PS[2 + ch][:, 0:TO], R=[PB[2 + ch]],
                           **({"W": [xoTb]} if ch == 0 else {"A": [xoTb]}))
                    for dc in range(8):
                        ypb = 4 + (dc % 2)
                        gpb = 6
                        for ch in range(2):
                            mm(PS[ypb][:, 0:TO], wxo[:, ch, dc * 128:(dc + 1) * 128], xoT[:, ch, :],
                               start=(ch == 0), stop=(ch == 1), R=[wxob, xoTb],
                               **({"W": [PB[ypb]]} if ch == 0 else {"A": [PB[ypb]]}))
                        gate_mm(gpb, wg2, wg2b, dc, t)
                        ot, otb = gated(ypb, gpb)
                        B.dma("pool", c2_s[dc * 128:(dc + 1) * 128, t * TO:(t + 1) * TO], ot[:], otb, store=True,
                              reads=[otb], acc=[c2sb])
                B.barrier()
            if debug:
                with contextlib.ExitStack() as sd:
                    dd = sbt(sd, "dd", [128, 8, NQ], F32); ddb = B.buf("dd")
                    for nm, src, srcb in (("c0", c0_s, c0sb), ("c2", c2_s, c2sb)):
                        B.dma("sp", dd[:], src.rearrange("(c p) n -> p c n", p=128), ddb, reads=[srcb], writes=[ddb])
                        B.dma("sp", dbg[nm].rearrange("(c p) n -> p c n", p=128), dd[:], ddb, store=True, reads=[ddb])
                    B.barrier()

            if stop_after == "P4c":
                B.barrier()
                B.emit()
                return nc
            h1sb = B.buf("h1_s"); n2sb = B.buf("n2_s")
            with contextlib.ExitStack() as sd4:
                wsbo = sbt(sd4, "wsbo", [128, 4, D], BF16); wsbob = B.buf("wsbo")
                wg1 = sbt(sd4, "wg1", [128, 8, D], BF16); wg1b = B.buf("wg1")
                wout = sbt(sd4, "wout", [128, 8, D], BF16); woutb = B.buf("wout")
                with contextlib.ExitStack() as sw:
                    new_stg(sw)
                    load_w(wsbo, wsbob, w_sbo, 4, D)
                    load_w(wg1, wg1b, w_in[:, O_G + D:O_G + 2 * D], 8, D, gcol=V_PRE)
                    load_w(wout, woutb, w_out, 8, D)
                    B.barrier()
                sbo4 = sbt(sd4, "sbo4", [128, 4, NQ], BF16); sbo4b = B.buf("sbo4")
                B.dma("sp", sbo4[:], sbo_s, sbo4b, reads=[sbosb], writes=[sbo4b])
                NCL = 3
                cl = [sbt(sd4, "cl%d" % i, [128, 2, TO], F32) for i in range(NCL)]
                clb = [B.buf("cl%d" % i) for i in range(NCL)]
                clc = {"i": 0}
                mg1 = sbt(sd4, "mg", [128, 8, TO], BF16)
                mg = [mg1, mg1]
                mgb1 = B.buf("mg")
                mgb = [mgb1, mgb1]
                mo = [sbt(sd4, "mo%d" % i, [128, 8, TO], F32) for i in range(2)]
                mob = [B.buf("mo%d" % i) for i in range(2)]
                sqm = [sbt(sd4, "sqm%d" % i, [128, TO], BF16) for i in range(2)]
                sqmb = [B.buf("sqm%d" % i) for i in range(2)]
                lnm = sbt(sd4, "lnm", [128, TO], F32); lnmb = B.buf("lnm")
                rsm1 = sbt(sd4, "rsm1", [128, TO], F32); rsm1b = B.buf("rsm1")
                rsm2 = sbt(sd4, "rsm2", [128, TO], F32); rsm2b = B.buf("rsm2")
                xr = sbt(sd4, "xr", [128, 8, BPT, 128], F32); xrb = B.buf("xr")
                n2t = sbt(sd4, "n2t", [128, 8, TO], BF16); n2tb = B.buf("n2t")
                xoh4 = xoh.rearrange("d (b t) -> d b t", t=144)

                def bd_S1(t):
                    for dc in range(8):
                        ci = clc["i"] % NCL
                        clc["i"] += 1
                        B.dma("sp", cl[ci][:, 0, :], c0_s[dc * 128:(dc + 1) * 128, t * TO:(t + 1) * TO], clb[ci],
                              reads=[c0sb], writes=[clb[ci]])
                        B.dma("sp", cl[ci][:, 1, :], c2_s[dc * 128:(dc + 1) * 128, t * TO:(t + 1) * TO], clb[ci],
                              reads=[c2sb], acc=[clb[ci]])
                        ypb = 4 + (dc % 2)
                        gpb = (dc % 2)
                        for c4 in range(4):
                            mm(PS[ypb][:, 0:TO], wsbo[:, c4, dc * 128:(dc + 1) * 128], sbo4[:, c4, t * TO:(t + 1) * TO],
                               start=(c4 == 0), stop=(c4 == 3), R=[wsbob, sbo4b],
                               **({"W": [PB[ypb]]} if c4 == 0 else {"A": [PB[ypb]]}))
                        gate_mm(gpb, wg1, wg1b, dc, t)
                        ot, otb = gated(ypb, gpb)
                        tt("pool", cl[ci][:, 0, :], cl[ci][:, 0, :], cl[ci][:, 1, :], ALU.add, R=[clb[ci]], A=[clb[ci]])
                        tt("dve", mg[t % 2][:, dc, :], ot[:], cl[ci][:, 0, :], ALU.add, R=[otb, clb[ci]],
                           **({"W": [mgb[t % 2]]} if dc == 0 else {"A": [mgb[t % 2]]}))
                        yield

                def bd_S2(t):
                    for dc in range(8):
                        B.dma("sp", xr[:, dc, :, :], xoh4[dc * 128:(dc + 1) * 128, t * BPT:(t + 1) * BPT, 16:144], xrb,
                              **({"writes": [xrb]} if dc == 0 else {"acc": [xrb]}))
                    m_, mb_ = mo[t % 2], mob[t % 2]
                    for dc in range(8):
                        pb = 2 + (dc % 2)
                        for k in range(8):
                            mm(PS[pb][:, 0:TO], wout[:, k, dc * 128:(dc + 1) * 128], mg[t % 2][:, k, :],
                               start=(k == 0), stop=(k == 7), R=[woutb, mgb[t % 2]],
                               **({"W": [PB[pb]]} if k == 0 else {"A": [PB[pb]]}))
                        if dc > 0:
                            d1 = dc - 1
                            mm(PS[6][:, 0:TO], ones[:], sqm[d1 % 2][:], start=(d1 == 0), stop=False,
                               R=[onesb, sqmb[d1 % 2]], **({"W": [PB[6]]} if d1 == 0 else {"A": [PB[6]]}))
                        cp("dve", m_[:, dc, :], PS[pb][:, 0:TO], R=[PB[pb]], **({"W": [mb_]} if dc == 0 else {"A": [mb_]}))
                        act(sqm[dc % 2][:], m_[:, dc, :], AF.Square, R=[mb_], W=[sqmb[dc % 2]])
                    mm(PS[6][:, 0:TO], ones[:], sqm[7 % 2][:], start=False, stop=True,
                       R=[onesb, sqmb[7 % 2]], A=[PB[6]])

                def bd_S3(t):
                    m_, mb_ = mo[t % 2], mob[t % 2]
                    rstd_from(PS[6][:, 0:TO], PB[6], TO, lnm[:], lnmb, rsm1[:], rsm1b)
                    for dc in range(8):
                        stt(m_[:, dc, :], m_[:, dc, :], vec[:, V_POST + dc:V_POST + dc + 1], rsm1[:],
                            R=[mb_, vecb, rsm1b], A=[mb_])
                        tt("pool", m_[:, dc, :], m_[:, dc, :], xr[:, dc, :, :].rearrange("p b t -> p (b t)"), ALU.add,
                           R=[mb_, xrb], A=[mb_])
                        act(sqm[dc % 2][:], m_[:, dc, :], AF.Square, R=[mb_], W=[sqmb[dc % 2]])
                        mm(PS[7][:, 0:TO], ones[:], sqm[dc % 2][:], start=(dc == 0), stop=(dc == 7),
                           R=[onesb, sqmb[dc % 2]], **({"W": [PB[7]]} if dc == 0 else {"A": [PB[7]]}))
                        yield
                    B.dma("sp", h1_s[:, t * TO:(t + 1) * TO].rearrange("(c p) n -> p c n", p=128), m_[:], mb_, store=True,
                          reads=[mb_], acc=[h1sb])
                    rstd_from(PS[7][:, 0:TO], PB[7], TO, lnm[:], lnmb, rsm2[:], rsm2b)
                    for dc in range(8):
                        stt(n2t[:, dc, :], m_[:, dc, :], vec[:, V_FPRE + dc:V_FPRE + dc + 1], rsm2[:],
                            R=[mb_, rsm2b, vecb], **({"W": [n2tb]} if dc == 0 else {"A": [n2tb]}))
                    B.dma("sp", n2_s[:, t * TO:(t + 1) * TO].rearrange("(c p) n -> p c n", p=128), n2t[:], n2tb, store=True,
                          reads=[n2tb], acc=[n2sb])

                run(bd_S1(0))
                for t in range(NT):
                    bd_S2(t)
                    if t + 1 < NT:
                        interleave(bd_S3(t), bd_S1(t + 1))
                    else:
                        run(bd_S3(t))
                B.barrier()
        if debug:
            with contextlib.ExitStack() as sd:
                dd = sbt(sd, "dd2", [128, 8, NQ], F32); ddb = B.buf("dd2")
                B.dma("sp", dd[:], h1_s.rearrange("(c p) n -> p c n", p=128), ddb, reads=[h1sb], writes=[ddb])
                B.dma("sp", dbg["h1"].rearrange("(c p) n -> p c n", p=128), dd[:], ddb, store=True, reads=[ddb])
                B.barrier()

        if stop_after == "P4":
            B.barrier()
            B.emit()
            return nc
        with contextlib.ExitStack() as s56:
            actT = sbt(s56, "actT", [128, NFC, NQ], BF16); actTb = B.buf("actT")
            wfo = sbt(s56, "wfo", [128, NFC, D], BF16); wfob = B.buf("wfo")
            with contextlib.ExitStack() as s5:
                n2 = sbt(s5, "n2", [128, 8, NQ], BF16); n2b = B.buf("n2")
                new_stg(s5)
                B.dma("sp", n2[:], n2_s.rearrange("(c p) n -> p c n", p=128), n2b, reads=[n2sb], writes=[n2b])
                wfi = [sbt(s5, "wfi%d" % i, [128, 8, 256], BF16) for i in range(2)]
                wfib = [B.buf("wfi%d" % i) for i in range(2)]
                sil = [sbt(s5, "sil%d" % i, [128, TO], F32) for i in range(2)]
                silb = [B.buf("sil%d" % i) for i in range(2)]

                def p5_loadw(f):
                    sl = f % 2
                    load_w(wfi[sl][:, :, 0:128], wfib[sl], w_fi[:, f * 128:(f + 1) * 128], 8, 128, gcol=V_FPRE, first=True)
                    load_w(wfi[sl][:, :, 128:256], wfib[sl], w_fi[:, DFF + f * 128:DFF + (f + 1) * 128], 8, 128,
                           gcol=V_FPRE, first=False)

                p5_loadw(0)
                cnt5 = 0
                for f in range(NFC):
                    if f + 1 < NFC:
                        p5_loadw(f + 1)
                    load_w(wfo[:, f:f + 1, :], wfob, w_fo[f * 128:(f + 1) * 128, :], 1, D, first=(f == 0))
                    sl = f % 2
                    for t in range(NT):
                        gp = (cnt5 % 2) * 2
                        up = gp + 1
                        si = cnt5 % 2
                        cnt5 += 1
                        for k in range(8):
                            mm(PS[gp][:, 0:TO], wfi[sl][:, k, 0:128], n2[:, k, t * TO:(t + 1) * TO],
                               start=(k == 0), stop=(k == 7), R=[wfib[sl], n2b],
                               **({"W": [PB[gp]]} if k == 0 else {"A": [PB[gp]]}))
                        for k in range(8):
                            mm(PS[up][:, 0:TO], wfi[sl][:, k, 128:256], n2[:, k, t * TO:(t + 1) * TO],
                               start=(k == 0), stop=(k == 7), R=[wfib[sl], n2b],
                               **({"W": [PB[up]]} if k == 0 else {"A": [PB[up]]}))
                        act(sil[si][:], PS[gp][:, 0:TO], AF.Silu, R=[PB[gp]], W=[silb[si]])
                        tt("dve", actT[:, f, t * TO:(t + 1) * TO], sil[si][:], PS[up][:, 0:TO], ALU.mult,
                           R=[silb[si], PB[up]], A=[actTb])
                B.barrier()
            with contextlib.ExitStack() as s6:
                ff = [sbt(s6, "ff%d" % i, [128, 8, TO], F32) for i in range(2)]
                ffb = [B.buf("ff%d" % i) for i in range(2)]
                sq6 = [sbt(s6, "sq6_%d" % i, [128, TO], BF16) for i in range(2)]
                sq6b = [B.buf("sq6_%d" % i) for i in range(2)]
                ln6 = sbt(s6, "ln6", [128, TO], F32); ln6b = B.buf("ln6")
                rs6 = sbt(s6, "rs6", [128, TO], F32); rs6b = B.buf("rs6")
                h1t = [sbt(s6, "h1t%d" % i, [128, TO], F32) for i in range(3)]
                h1tb = [B.buf("h1t%d" % i) for i in range(3)]
                outsb = B.buf("outT")
                cnt6 = {"h": 0}

                def p6_M(t):
                    f_, fb_ = ff[t % 2], ffb[t % 2]
                    ssb = 4 + (t % 2)
                    for dc in range(8):
                        pb = dc % 4
                        for f in range(NFC):
                            mm(PS[pb][:, 0:TO], wfo[:, f, dc * 128:(dc + 1) * 128], actT[:, f, t * TO:(t + 1) * TO],
                               start=(f == 0), stop=(f == NFC - 1), R=[wfob, actTb],
                               **({"W": [PB[pb]]} if f == 0 else {"A": [PB[pb]]}))
                        if dc > 0:
                            d1 = dc - 1
                            mm(PS[ssb][:, 0:TO], ones[:], sq6[d1 % 2][:], start=(d1 == 0), stop=False,
                               R=[onesb, sq6b[d1 % 2]], **({"W": [PB[ssb]]} if d1 == 0 else {"A": [PB[ssb]]}))
                        cp("dve", f_[:, dc, :], PS[pb][:, 0:TO], R=[PB[pb]], **({"W": [fb_]} if dc == 0 else {"A": [fb_]}))
                        act(sq6[dc % 2][:], f_[:, dc, :], AF.Square, R=[fb_], W=[sq6b[dc % 2]])
                        yield
                    mm(PS[ssb][:, 0:TO], ones[:], sq6[7 % 2][:], start=False, stop=True,
                       R=[onesb, sq6b[7 % 2]], A=[PB[ssb]])

                def p6_E(t):
                    f_, fb_ = ff[t % 2], ffb[t % 2]
                    ssb = 4 + (t % 2)
                    rstd_from(PS[ssb][:, 0:TO], PB[ssb], TO, ln6[:], ln6b, rs6[:], rs6b)
                    for dc in range(8):
                        hi = cnt6["h"] % 3
                        cnt6["h"] += 1
                        B.dma("sp", h1t[hi][:], h1_s[dc * 128:(dc + 1) * 128, t * TO:(t + 1) * TO], h1tb[hi],
                              reads=[h1sb], writes=[h1tb[hi]])
                        stt(f_[:, dc, :], f_[:, dc, :], vec[:, V_FPOST + dc:V_FPOST + dc + 1], rs6[:],
                            R=[fb_, vecb, rs6b], A=[fb_])
                        tt("pool", f_[:, dc, :], f_[:, dc, :], h1t[hi][:], ALU.add, R=[fb_, h1tb[hi]], A=[fb_])
                        yield
                    B.dma("sp", outT[:, t * TO:(t + 1) * TO].rearrange("(c p) n -> p c n", p=128), f_[:], fb_, store=True,
                          reads=[fb_], acc=[outsb])

                run(p6_M(0))
                for t in range(NT):
                    if t + 1 < NT:
                        interleave(p6_M(t + 1), p6_E(t))
                    else:
                        run(p6_E(t))
                B.barrier()
        B.barrier()
        B.emit()
    return nc


def _own_blocks(r, G):
    blks = []
    for g in range(G):
        blks.append(8 * g + r)
        blks.append(8 * g + 7 - r)
    return blks


def _pack_vecs(norm_mix_pre, norm_mem, norm_mix_post, norm_ffn_pre, norm_ffn_post, pool_scale):
    cols = []
    for v in (norm_mix_pre, norm_mem, norm_mix_post, norm_ffn_pre, norm_ffn_post):
        cols.append(np.asarray(v, np.float32).reshape(8, 128).T)
    cols.append(np.asarray(pool_scale, np.float32).reshape(4, 128).T)
    return np.ascontiguousarray(np.concatenate(cols, axis=1))


_PROG_CACHE = {}


def run_layer(x, mem, norm_mix_pre, w_in, w_pool_mix, pool_scale, w_pool_o, w_sb_o, norm_mem, w_mem_kv, w_x_o,
              w_out, norm_mix_post, norm_ffn_pre, w_ffn_in, w_ffn_out, norm_ffn_post, debug=False, stop_after=None):
    x = np.asarray(x, np.float32)
    mem = np.asarray(mem, np.float32)
    Bn, S, _ = x.shape
    G = S // 1024
    assert Bn == 2 and S == 1024 * G
    key = (G, debug, stop_after)
    if key not in _PROG_CACHE:
        _PROG_CACHE[key] = build_program(G, debug, stop_after)
    nc = _PROG_CACHE[key]
    f32 = lambda a: np.ascontiguousarray(np.asarray(a, np.float32))
    shared = {
        "w_in": f32(w_in[0]), "w_pmix": f32(np.asarray(w_pool_mix[0]).reshape(512, 128)), "w_po": f32(w_pool_o[0]),
        "w_sbo": f32(w_sb_o[0]), "w_kv": f32(w_mem_kv[0]), "w_xo": f32(w_x_o[0]), "w_out": f32(w_out[0]),
        "w_fi": f32(w_ffn_in[0]), "w_fo": f32(w_ffn_out[0]),
        "vecs": _pack_vecs(norm_mix_pre[0], norm_mem[0], norm_mix_post[0], norm_ffn_pre[0], norm_ffn_post[0],
                           pool_scale[0]),
    }
    xTb = [np.ascontiguousarray(x[b].T) for b in range(2)]
    memTb = [np.ascontiguousarray(mem[b].T) for b in range(2)]
    in_maps = []
    for core in range(8):
        b, r = core // 4, core % 4
        blks = _own_blocks(r, G)
        xoh = np.zeros((D, len(blks) * 144), np.float32)
        posv = np.zeros((len(blks) * 128,), np.float32)
        for n, blk in enumerate(blks):
            s0 = blk * 128
            lo = max(0, s0 - 16)
            xoh[:, n * 144 + 16 - (s0 - lo):(n + 1) * 144] = xTb[b][:, lo:s0 + 128]
            posv[n * 128:(n + 1) * 128] = np.arange(s0, s0 + 128, dtype=np.float32)
        m = dict(shared)
        m.update({"xT": xTb[b], "xoh": xoh, "pos": posv, "memT": memTb[b]})
        in_maps.append(m)
    res = run_bass_kernel_spmd(nc, in_maps, core_ids=list(range(8)))
    out = np.empty((2, S, D), np.float32)
    for core in range(8):
        b, r = core // 4, core % 4
        oT = np.asarray(res.results[core]["outT"])
        for n, blk in enumerate(_own_blocks(r, G)):
            out[b, blk * 128:(blk + 1) * 128, :] = oT[:, n * 128:(n + 1) * 128].T
    if debug:
        return out, res.results
    return out


def kernel(**inputs):
    return run_layer(**inputs)
```

```python
import contextlib
import numpy as np
import concourse.bass as bass
import concourse.mybir as mybir
from concourse.bass_utils import run_bass_kernel_spmd

F32 = mybir.dt.float32
BF16 = mybir.dt.bfloat16
I32 = mybir.dt.int32
AF = mybir.ActivationFunctionType
ALU = mybir.AluOpType
AX = mybir.AxisListType

ENGS = ("pe", "act", "dve", "pool", "sp")
SAME_ENG_RAW = True

D = 1024
DFF = 2816
NFC = DFF // 128
INW = 5376
O_Q, O_K, O_V, O_XQ, O_G = 512, 1024, 1536, 2048, 2304
EPS = 1e-6
NEG_BIG = -30000.0
V_PRE, V_MEM, V_POST, V_FPRE, V_FPOST, V_PSC, NVEC = 0, 8, 16, 24, 32, 40, 44


class Buf:
    __slots__ = ("name", "w", "r", "excl", "base")

    def __init__(self, name):
        self.name = name
        self.w = {}
        self.r = {}
        self.base = {}
        self.excl = False


class Builder:
    def __init__(self, nc, es):
        self.nc = nc
        self.es = es
        self.q = {e: [] for e in ENGS}
        self.cnt = {}
        self.seen = {e: {} for e in ENGS}
        self.semh = {}
        self.nbuf = 0
        for e in ENGS:
            if e != "sp":
                self._newsem(e)

    def _newsem(self, key):
        self.semh[key] = self.es.enter_context(self.nc.semaphore("s%d" % len(self.semh)))
        self.cnt[key] = 0

    def buf(self, name=None):
        self.nbuf += 1
        return Buf("%s#%d" % (name or "b", self.nbuf))

    def _wait(self, eng, key, val):
        if self.seen[eng].get(key, 0) >= val:
            return
        self.seen[eng][key] = val
        sem = self.semh[key]
        self.q[eng].append(lambda e, sem=sem, val=val: e.wait_ge(sem, val))

    def _deps(self, eng, reads, writes, acc):
        raw = {}
        oth = {}

        def add(dst, d):
            for k, v in d.items():
                if dst.get(k, 0) < v:
                    dst[k] = v
        for b in reads:
            add(raw, b.w)
            if b.excl:
                add(oth, b.r)
        for b in writes:
            add(oth, b.w)
            add(oth, b.r)
        for b in acc:
            add(oth, b.r)
            add(oth, b.base)
        for k, v in raw.items():
            if k == eng and (eng == "pe" or not SAME_ENG_RAW):
                continue
            self._wait(eng, k, v)
        for k, v in oth.items():
            if k == eng:
                continue
            self._wait(eng, k, v)

    def _record(self, key, val, reads, writes, acc):
        for b in reads:
            if b.r.get(key, 0) < val:
                b.r[key] = val
        for b in writes:
            b.w = {key: val}
            b.base = {key: val}
            b.r = {}
        for b in acc:
            b.w[key] = val
            b.r = {}

    def op(self, eng, fn, reads=(), writes=(), acc=()):
        self._deps(eng, reads, writes, acc)
        self.cnt[eng] += 1
        c = self.cnt[eng]
        sem = self.semh[eng]
        self.q[eng].append(lambda e, fn=fn, sem=sem: fn(e).then_inc(sem, 1))
        self._record(eng, c, reads, writes, acc)

    def dma(self, qeng, out_ap, in_ap, sb, store=False, reads=(), writes=(), acc=(), **kw):
        self._deps(qeng, reads, writes, acc)
        key = ("st" if store else "ld", sb.name, "sw" if qeng == "pool" else "hw")
        if key not in self.semh:
            self._newsem(key)
        self.cnt[key] += 16
        c = self.cnt[key]
        sem = self.semh[key]
        self.q[qeng].append(
            lambda e, o=out_ap, i=in_ap, sem=sem, kw=kw: e.dma_start(out=o, in_=i, **kw).then_inc(sem, 16))
        self._record(key, c, reads, writes, acc)

    def barrier(self):
        for e in ENGS:
            for k, c in self.cnt.items():
                if k == e or c == 0:
                    continue
                self._wait(e, k, c)

    def emit(self):
        q = self.q
        with self.nc.Block() as block:
            @block.tensor
            def _(e):
                for t in q["pe"]:
                    t(e)

            @block.scalar
            def _(e):
                for t in q["act"]:
                    t(e)

            @block.vector
            def _(e):
                for t in q["dve"]:
                    t(e)

            @block.gpsimd
            def _(e):
                for t in q["pool"]:
                    t(e)

            @block.sync
            def _(e):
                for t in q["sp"]:
                    t(e)


def build_program(G, debug=False, stop_after=None):
    S = 1024 * G
    NBLK = S // 128
    NB = 2 * G
    NQ = 128 * NB
    BPT = 4 if G >= 2 else 2
    TO = 128 * BPT
    TOH = 144 * BPT
    NT = NB // BPT
    ST = S // 512

    nc = bass.Bass("TRN2", target_bir_lowering=False)

    def din(name, shape, dt=F32):
        return nc.dram_tensor(name, list(shape), dt, kind="ExternalInput").ap()

    xT = din("xT", [D, S])
    xoh = din("xoh", [D, NB * 144])
    pos = din("pos", [NQ])
    memT = din("memT", [D, 256])
    w_in = din("w_in", [D, INW])
    w_pmix = din("w_pmix", [512, 128])
    w_po = din("w_po", [512, D])
    w_sbo = din("w_sbo", [512, D])
    w_kv = din("w_kv", [D, 512])
    w_xo = din("w_xo", [256, D])
    w_out = din("w_out", [D, D])
    w_fi = din("w_fi", [D, 2 * DFF])
    w_fo = din("w_fo", [DFF, D])
    vecs = din("vecs", [128, NVEC])
    outT = nc.dram_tensor("outT", [D, NQ], F32, kind="ExternalOutput").ap()
    kT_s = nc.dram_tensor("kT_s", [4, 128, S], BF16).ap()
    v_s = nc.dram_tensor("v_s", [4, 128, NBLK, 128], BF16).ap()
    c0_s = nc.dram_tensor("c0_s", [D, NQ], F32).ap()
    c2_s = nc.dram_tensor("c2_s", [D, NQ], F32).ap()
    h1_s = nc.dram_tensor("h1_s", [D, NQ], F32).ap()
    n2_s = nc.dram_tensor("n2_s", [D, NQ], BF16).ap()
    dbg = {}
    if debug:
        dbg["qT"] = nc.dram_tensor("dbg_qT", [128, 4, 2 * NQ], BF16, kind="ExternalOutput").ap()
        dbg["sbo"] = nc.dram_tensor("dbg_sbo", [128, 4, NQ], BF16, kind="ExternalOutput").ap()
        dbg["kT"] = nc.dram_tensor("dbg_kT", [4, 128, S], BF16, kind="ExternalOutput").ap()
        dbg["v"] = nc.dram_tensor("dbg_v", [4, 128, NBLK, 128], BF16, kind="ExternalOutput").ap()
        dbg["c0"] = nc.dram_tensor("dbg_c0", [D, NQ], F32, kind="ExternalOutput").ap()
        dbg["c2"] = nc.dram_tensor("dbg_c2", [D, NQ], F32, kind="ExternalOutput").ap()
        dbg["h1"] = nc.dram_tensor("dbg_h1", [D, NQ], F32, kind="ExternalOutput").ap()

    with contextlib.ExitStack() as es:
        B = Builder(nc, es)

        def sbt(scope, name, shape, dt):
            return scope.enter_context(nc.sbuf_tensor(name, list(shape), dt))

        def mm(out, lhsT, rhs, start, stop, R, W=(), A=(), sgc=False):
            B.op("pe", lambda e: e.matmul(out, lhsT, rhs, start=start, stop=stop, skip_group_check=sgc),
                 reads=R, writes=W, acc=A)

        def act(out, in_, func, R, W=(), A=(), **kw):
            B.op("act", lambda e: e.activation(out=out, in_=in_, func=func, **kw), reads=R, writes=W, acc=A)

        def tt(eng, out, in0, in1, op, R, W=(), A=()):
            B.op(eng, lambda e: e.tensor_tensor(out=out, in0=in0, in1=in1, op=op), reads=R, writes=W, acc=A)

        def ts(eng, out, in0, s1, s2, op0, op1, R, W=(), A=()):
            if op1 is None:
                B.op(eng, lambda e: e.tensor_scalar(out=out, in0=in0, scalar1=s1, scalar2=s2, op0=op0),
                     reads=R, writes=W, acc=A)
            else:
                B.op(eng, lambda e: e.tensor_scalar(out=out, in0=in0, scalar1=s1, scalar2=s2, op0=op0, op1=op1),
                     reads=R, writes=W, acc=A)

        def cp(eng, out, in_, R, W=(), A=()):
            if eng == "act":
                act(out, in_, AF.Copy, R, W, A)
            else:
                B.op(eng, lambda e: e.tensor_copy(out=out, in_=in_), reads=R, writes=W, acc=A)

        PSALL = es.enter_context(nc.psum_tensor("psall", [128, 4096], F32))
        PS = [PSALL[:, i * 512:(i + 1) * 512] for i in range(8)]
        PB = [B.buf("ps%d" % i) for i in range(8)]
        for b_ in PB:
            b_.excl = True
        PST = PS[7].bitcast(BF16)
        PSTB = PB[7]

        gs = es
        vec = sbt(gs, "vec", [128, NVEC], F32); vecb = B.buf("vec")
        ones = sbt(gs, "ones", [128, 128], BF16); onesb = B.buf("ones")
        nTin = sbt(gs, "nTin", [128, 128], BF16); nTinb = B.buf("nTin")
        nOnes = sbt(gs, "nOnes", [128, 128], BF16); nOnesb = B.buf("nOnes")
        nBigI = sbt(gs, "nBigI", [128, 128], BF16); nBigIb = B.buf("nBigI")
        ident = sbt(gs, "ident", [128, 128], BF16); identb = B.buf("ident")
        dif_i = sbt(gs, "dif_i", [128, 128], I32); difib = B.buf("dif_i")
        dif_f = sbt(gs, "dif_f", [128, 128], F32); diffb = B.buf("dif_f")
        kp8_i = sbt(gs, "kp8_i", [128, 8], I32); kp8ib = B.buf("kp8i")
        kp8 = sbt(gs, "kp8", [128, 8], F32); kp8b = B.buf("kp8")
        qpos = sbt(gs, "qpos", [128, NQ], F32); qposb = B.buf("qpos")
        notM = sbt(gs, "notM", [128, 8, 2, 256], BF16); notMb = B.buf("notM")
        NSTG = 2
        st_state = {"i": 0, "e": 0, "n": 0}

        def new_stg(scope):
            st_state["n"] += 1
            st_state["stg"] = [sbt(scope, "stg%d_%d" % (st_state["n"], i), [128, 1536], F32) for i in range(NSTG)]
            st_state["stgb"] = [B.buf("stg%d" % i) for i in range(NSTG)]

        B.dma("sp", vec[:], vecs, vecb, writes=[vecb])
        B.dma("sp", qpos[:], pos.partition_broadcast(128), qposb, writes=[qposb])
        B.op("dve", lambda e: e.memset(ones[:], 1.0), writes=[onesb])
        B.op("dve", lambda e: e.memset(nOnes[:], -1.0), writes=[nOnesb])
        B.op("pool", lambda e: e.iota(dif_i[:], pattern=[[-1, 128]], base=0, channel_multiplier=1), writes=[difib])
        cp("dve", dif_f[:], dif_i[:], R=[difib], W=[diffb])
        ts("dve", nTin[:], dif_f[:], 0.0, -1.0, ALU.is_ge, ALU.mult, R=[diffb], W=[nTinb])
        ts("dve", nBigI[:], dif_f[:], 0.0, NEG_BIG, ALU.is_equal, ALU.mult, R=[diffb], W=[nBigIb])
        ts("dve", ident[:], dif_f[:], 0.0, None, ALU.is_equal, None, R=[diffb], W=[identb])
        B.op("pool", lambda e: e.iota(kp8_i[:], pattern=[[128, 8]], base=0, channel_multiplier=1), writes=[kp8ib])
        cp("dve", kp8[:], kp8_i[:], R=[kp8ib], W=[kp8b])
        for jj in range(8):
            for h in range(2):
                ts("dve", notM[:, jj, h, :], qpos[:, 0:256], kp8[:, jj:jj + 1], None, ALU.is_le, None,
                   R=[qposb, kp8b], **({"W": [notMb]} if (jj == 0 and h == 0) else {"A": [notMb]}))

        def load_w(dst, dstb, src, kc, ncols, gcol=None, first=True):
            kg = max(1, min(kc, 1536 // ncols))
            k0 = 0
            while k0 < kc:
                kn = min(kg, kc - k0)
                i = st_state["i"] % NSTG
                st_state["i"] += 1
                stg, stgb = st_state["stg"], st_state["stgb"]
                sv = stg[i][:, 0:kn * ncols].rearrange("p (k n) -> p k n", k=kn)
                B.dma("sp", sv, src[k0 * 128:(k0 + kn) * 128, :].rearrange("(k p) n -> p k n", p=128), stgb[i],
                      writes=[stgb[i]])
                eng = "dve" if st_state["e"] % 2 == 0 else "act"
                st_state["e"] += 1
                kw = {"W": [dstb]} if (first and k0 == 0) else {"A": [dstb]}
                cp(eng, dst[:, k0:k0 + kn, :], sv, R=[stgb[i]], **kw)
                k0 += kn

        def run(gen):
            for _ in gen:
                pass

        def interleave(*gens):
            gens = list(gens)
            while gens:
                for g_ in list(gens):
                    try:
                        next(g_)
                    except StopIteration:
                        gens.remove(g_)

        def stt(out, in0, scal, in1, R, W=(), A=()):
            B.op("dve", lambda e: e.scalar_tensor_tensor(out=out, in0=in0, scalar=scal, in1=in1,
                                                         op0=ALU.mult, op1=ALU.mult), reads=R, writes=W, acc=A)

        def rstd_from(ss_ps, ss_b, ncols, ln_t, ln_b, out_t, out_b, out_is_write=True):
            act(ln_t, ss_ps, AF.Ln, R=[ss_b], W=[ln_b], scale=1.0 / D, bias=EPS)
            act(out_t, ln_t, AF.Exp, R=[ln_b], W=[out_b], scale=-0.5)

        s13 = contextlib.ExitStack()
        es.enter_context(s13)
        NGT = BPT // 2
        Qz = sbt(s13, "Qz", [128, 4, G, 2, 256], BF16); Qzb = B.buf("Qz")
        B.op("pool", lambda e: e.memset(Qz[:].rearrange("p a g h q -> p (a g h q)"), 0.0), writes=[Qzb])
        wqkv = sbt(s13, "wqkv", [128, 8, 1536], BF16); wqkvb = B.buf("wqkv")

        xn_s = nc.dram_tensor("xn_s", [128, 8, NB * 144], BF16).ap()
        xnsb = B.buf("xn_s")
        with contextlib.ExitStack() as s1:
            new_stg(s1)
            load_w(wqkv, wqkvb, w_in[:, O_Q:O_Q + 1536], 8, 1536)
            xo_t = [sbt(s1, "xo_t%d" % i, [128, 8, TOH], F32) for i in range(2)]
            xo_b = [B.buf("xo_t%d" % i) for i in range(2)]
            sq1 = sbt(s1, "sq1", [128, 8, TOH], BF16); sq1b = B.buf("sq1")
            ln1 = sbt(s1, "ln1", [128, TOH], F32); ln1b = B.buf("ln1")
            rs1 = sbt(s1, "rs1", [128, TOH], F32); rs1b = B.buf("rs1")
            xn1 = [sbt(s1, "xn1_%d" % i, [128, 8, BPT, 144], BF16) for i in range(2)]
            xn1b = [B.buf("xn1_%d" % i) for i in range(2)]
            HB = TOH // 2
            for t in range(NT):
                sl = t % 2
                B.dma("sp", xo_t[sl][:], xoh[:, t * TOH:(t + 1) * TOH].rearrange("(c p) n -> p c n", p=128),
                      xo_b[sl], writes=[xo_b[sl]])
                act(sq1[:], xo_t[sl][:], AF.Square, R=[xo_b[sl]], W=[sq1b])
                for hf in range(2):
                    for k in range(8):
                        mm(PS[hf][:, 0:HB], ones[:], sq1[:, k, hf * HB:(hf + 1) * HB], start=(k == 0), stop=(k == 7),
                           R=[onesb, sq1b], **({"W": [PB[hf]]} if k == 0 else {"A": [PB[hf]]}))
                for hf in range(2):
                    act(ln1[:, hf * HB:(hf + 1) * HB], PS[hf][:, 0:HB], AF.Ln, R=[PB[hf]],
                        **({"W": [ln1b]} if hf == 0 else {"A": [ln1b]}), scale=1.0 / D, bias=EPS)
                act(rs1[:], ln1[:], AF.Exp, R=[ln1b], W=[rs1b], scale=-0.5)
                xnv = xn1[sl][:].rearrange("p c b t -> p c (b t)")
                for k in range(8):
                    stt(xnv[:, k, :], xo_t[sl][:, k, :], vec[:, V_PRE + k:V_PRE + k + 1], rs1[:],
                        R=[xo_b[sl], rs1b, vecb], **({"W": [xn1b[sl]]} if k == 0 else {"A": [xn1b[sl]]}))
                B.dma("sp", xn_s[:, :, t * TOH:(t + 1) * TOH], xnv, xn1b[sl], store=True,
                      reads=[xn1b[sl]], acc=[xnsb])
                for c4 in range(4):
                    pb = 2 + (c4 % 2)
                    for k in range(8):
                        mm(PS[pb][:, 0:TO].rearrange("p (b t) -> p b t", b=BPT),
                           wqkv[:, k, c4 * 128:(c4 + 1) * 128], xn1[sl][:, k, :, 16:144],
                           start=(k == 0), stop=(k == 7), R=[wqkvb, xn1b[sl]],
                           **({"W": [PB[pb]]} if k == 0 else {"A": [PB[pb]]}))
                    for h in range(2):
                        act(Qz[h * 64:(h + 1) * 64, c4, t * NGT:(t + 1) * NGT, h, :],
                            PS[pb][h * 64:(h + 1) * 64, 0:TO].rearrange("p (g q) -> p g q", g=NGT), AF.Copy,
                            R=[PB[pb]], A=[Qzb], scale=0.125)
        B.barrier()
        if debug:
            B.dma("sp", dbg["qT"], Qz[:].rearrange("p a g h q -> p a (g h q)"), Qzb, store=True, reads=[Qzb])
        if stop_after == "P1":
            B.barrier()
            B.emit()
            return nc

        kTsb = B.buf("kT_s"); vsb = B.buf("v_s")
        with contextlib.ExitStack() as s2:
            x_t = [sbt(s2, "x_t%d" % i, [128, 8, 512], F32) for i in range(3)]
            x_b = [B.buf("x_t%d" % i) for i in range(3)]
            sq2 = [sbt(s2, "sq2_%d" % i, [128, 8, 512], BF16) for i in range(2)]
            sq2b = [B.buf("sq2_%d" % i) for i in range(2)]
            ln2 = sbt(s2, "ln2", [128, 512], F32); ln2b = B.buf("ln2")
            rs2 = sbt(s2, "rs2", [128, 512], F32); rs2b = B.buf("rs2")
            xn2 = [sbt(s2, "xn2_%d" % i, [128, 8, 512], BF16) for i in range(2)]
            xn2b = [B.buf("xn2_%d" % i) for i in range(2)]
            kst = [sbt(s2, "kst%d" % i, [128, 4, 512], BF16) for i in range(2)]
            kstb = [B.buf("kst%d" % i) for i in range(2)]
            vst = [sbt(s2, "vst%d" % i, [128, 4, 512], BF16) for i in range(2)]
            vstb = [B.buf("vst%d" % i) for i in range(2)]

            def p2_load(t):
                B.dma("sp", x_t[t % 3][:], xT[:, t * 512:(t + 1) * 512].rearrange("(c p) n -> p c n", p=128),
                      x_b[t % 3], writes=[x_b[t % 3]])

            def p2_sq(t):
                sl = t % 2
                act(sq2[sl][:], x_t[t % 3][:], AF.Square, R=[x_b[t % 3]], W=[sq2b[sl]])

            def p2_norm(t):
                sl = t % 2
                for k in range(8):
                    mm(PS[0][:], ones[:], sq2[sl][:, k, :], start=(k == 0), stop=(k == 7), R=[onesb, sq2b[sl]],
                       **({"W": [PB[0]]} if k == 0 else {"A": [PB[0]]}))
                rstd_from(PS[0][:], PB[0], 512, ln2[:], ln2b, rs2[:], rs2b)
                for k in range(8):
                    stt(xn2[sl][:, k, :], x_t[t % 3][:, k, :], vec[:, V_PRE + k:V_PRE + k + 1], rs2[:],
                        R=[x_b[t % 3], rs2b, vecb], **({"W": [xn2b[sl]]} if k == 0 else {"A": [xn2b[sl]]}))

            def p2_kv(t):
                sl = t % 2
                for c4 in range(4):
                    pb = 1 + (c4 % 3)
                    for k in range(8):
                        mm(PS[pb][:], wqkv[:, k, 512 + c4 * 128:512 + (c4 + 1) * 128], xn2[sl][:, k, :],
                           start=(k == 0), stop=(k == 7), R=[wqkvb, xn2b[sl]],
                           **({"W": [PB[pb]]} if k == 0 else {"A": [PB[pb]]}))
                    cp("dve", kst[sl][:, c4, :], PS[pb][:], R=[PB[pb]],
                       **({"W": [kstb[sl]]} if c4 == 0 else {"A": [kstb[sl]]}))
                B.dma("pool", kT_s[:, :, t * 512:(t + 1) * 512].rearrange("c p s -> p c s"), kst[sl][:], kstb[sl],
                      store=True, reads=[kstb[sl]], acc=[kTsb])
                for bk in range(4):
                    pb = 4 + (bk % 3)
                    for k in range(8):
                        mm(PS[pb][:], xn2[sl][:, k, bk * 128:(bk + 1) * 128], wqkv[:, k, 1024:1536],
                           start=(k == 0), stop=(k == 7), R=[wqkvb, xn2b[sl]],
                           **({"W": [PB[pb]]} if k == 0 else {"A": [PB[pb]]}))
                    cp("dve", vst[sl][:, bk, :], PS[pb][:], R=[PB[pb]],
                       **({"W": [vstb[sl]]} if bk == 0 else {"A": [vstb[sl]]}))
                for c4 in range(4):
                    B.dma("pool", v_s[c4, :, 4 * t:4 * t + 4, :], vst[sl][:, :, c4 * 128:(c4 + 1) * 128], vstb[sl],
                          store=True, reads=[vstb[sl]], acc=[vsb])

            p2_load(0)
            p2_sq(0)
            p2_norm(0)
            if ST > 1:
                p2_load(1)
                p2_sq(1)
            for t in range(ST):
                if t + 2 < ST:
                    p2_load(t + 2)
                    p2_sq(t + 2)
                if t + 1 < ST:
                    p2_norm(t + 1)
                p2_kv(t)
        B.barrier()
        if debug:
            with contextlib.ExitStack() as sd:
                dk = sbt(sd, "dk", [128, S], BF16); dkb = B.buf("dk")
                dv = sbt(sd, "dv", [128, NBLK, 128], BF16); dvb = B.buf("dv")
                for c4 in range(4):
                    B.dma("sp", dk[:], kT_s[c4], dkb, reads=[kTsb], writes=[dkb])
                    B.dma("sp", dbg["kT"][c4], dk[:], dkb, store=True, reads=[dkb])
                    B.dma("sp", dv[:], v_s[c4], dvb, reads=[vsb], writes=[dvb])
                    B.dma("sp", dbg["v"][c4], dv[:], dvb, store=True, reads=[dvb])
                B.barrier()

        if stop_after == "P2":
            B.barrier()
            B.emit()
            return nc
        s34 = contextlib.ExitStack()
        with contextlib.ExitStack() as s3:
            sbo = sbt(s3, "sbo", [128, 4, NQ], BF16); sbob = B.buf("sbo")
            kt = [sbt(s3, "kt%d" % i, [128, S], BF16) for i in range(2)]
            ktb = [B.buf("kt%d" % i) for i in range(2)]
            vt = [sbt(s3, "vt%d" % i, [128, NBLK, 128], BF16) for i in range(2)]
            vtb = [B.buf("vt%d" % i) for i in range(2)]
            NE = 2
            e_t = [sbt(s3, "e_t%d" % i, [128, 1024], F32) for i in range(NE)]
            e_b = [B.buf("e_t%d" % i) for i in range(NE)]
            NSP = 3
            sp_t = [sbt(s3, "sp_t%d" % i, [128, 1024], BF16) for i in range(NSP)]
            sp_b = [B.buf("sp_t%d" % i) for i in range(NSP)]
            a_t = [sbt(s3, "a_t%d" % i, [128, 1024], BF16) for i in range(NSP)]
            a_b = [B.buf("a_t%d" % i) for i in range(NSP)]
            R_t = [sbt(s3, "R_t%d" % i, [128, 512], F32) for i in range(2)]
            R_b = [B.buf("R_t%d" % i) for i in range(2)]
            Rb_t = [sbt(s3, "Rb_t%d" % i, [128, 512], BF16) for i in range(4)]
            Rb_b = [B.buf("Rb_t%d" % i) for i in range(4)]
            NZP = 3

            def p3_load(c):
                B.dma("sp", kt[c % 2][:], kT_s[c], ktb[c % 2], reads=[kTsb], writes=[ktb[c % 2]])
                B.dma("sp", vt[c % 2][:], v_s[c], vtb[c % 2], reads=[vsb], writes=[vtb[c % 2]])

            pairs = []
            chain_id = 0
            for c in range(4):
                for g in range(G):
                    nj = 8 * g + 8
                    for pi in range(nj // 2):
                        jh = nj - 1 - 2 * pi
                        pairs.append(dict(c=c, g=g, jh=jh, jl=jh - 1, pi=pi, last=(jh == 1), chain=chain_id))
                    chain_id += 1
            for s_i, T in enumerate(pairs):
                T["zs"] = s_i % NZP
                T["sl3"] = s_i % NSP
                T["esl"] = s_i % NE
                T["ob"] = 6 + (T["chain"] % 2)
                T["rs"] = T["chain"] % 2

            def st_P1(T):
                c, g = T["c"], T["g"]
                qv = Qz[:, c, g, :, :].rearrange("p h q -> p (h q)")
                for i, j in enumerate((T["jh"], T["jl"])):
                    bk_ = 2 * T["zs"] + i
                    jj = j - 8 * g
                    mm(PS[bk_][:], kt[c % 2][:, j * 128:(j + 1) * 128], qv, start=True, stop=(jj < 0),
                       R=[ktb[c % 2], Qzb], W=[PB[bk_]])
                    if jj >= 0:
                        mm(PS[bk_][:], nBigI[:], notM[:, jj, :, :].rearrange("p h q -> p (h q)"),
                           start=False, stop=True, R=[nBigIb, notMb], A=[PB[bk_]])

            def zpair(T):
                zs = T["zs"]
                return PSALL[:, zs * 1024:(zs + 1) * 1024], [PB[2 * zs], PB[2 * zs + 1]]

            def st_A1a(T):
                zap, zbufs = zpair(T)
                e_ = e_t[T["esl"]]; eb = e_b[T["esl"]]
                act(e_[:], zap, AF.Exp, R=zbufs, W=[eb])

            def st_A1b(T):
                e_ = e_t[T["esl"]]; eb = e_b[T["esl"]]
                sp_ = sp_t[T["sl3"]]; spb = sp_b[T["sl3"]]
                act(sp_[:], e_[:], AF.Ln, R=[eb], W=[spb], bias=1.0)

            def st_P2(T):
                b0 = 2 * T["zs"]; b1 = b0 + 1
                sp_ = sp_t[T["sl3"]]; spb = sp_b[T["sl3"]]
                first = (T["pi"] == 0)
                rbi = (T["chain"] % 2) * 2 + (T["pi"] % 2)
                mm(PS[b0][:], nTin[:], sp_[:, 0:512], start=False, stop=first, R=[nTinb, spb], A=[PB[b0]], sgc=True)
                if not first:
                    mm(PS[b0][:], nOnes[:], Rb_t[rbi][:], start=False, stop=True, R=[nOnesb, Rb_b[rbi]], A=[PB[b0]],
                       sgc=True)
                mm(PS[b1][:], nTin[:], sp_[:, 512:1024], start=False, stop=False, R=[nTinb, spb], A=[PB[b1]], sgc=True)
                mm(PS[b1][:], nOnes[:], sp_[:, 0:512], start=False, stop=first, R=[nOnesb, spb], A=[PB[b1]], sgc=True)
                if not first:
                    mm(PS[b1][:], nOnes[:], Rb_t[rbi][:], start=False, stop=True, R=[nOnesb, Rb_b[rbi]], A=[PB[b1]],
                       sgc=True)

            def st_G(T):
                if T["last"]:
                    return
                sp_ = sp_t[T["sl3"]]; spb = sp_b[T["sl3"]]
                R_ = R_t[T["rs"]]; Rbuf = R_b[T["rs"]]
                rbn = (T["chain"] % 2) * 2 + ((T["pi"] + 1) % 2)
                if T["pi"] == 0:
                    tt("dve", R_[:], sp_[:, 0:512], sp_[:, 512:1024], ALU.add, R=[spb], W=[Rbuf])
                else:
                    tt("dve", R_[:], R_[:], sp_[:, 0:512], ALU.add, R=[spb, Rbuf], A=[Rbuf])
                    tt("dve", R_[:], R_[:], sp_[:, 512:1024], ALU.add, R=[spb, Rbuf], A=[Rbuf])
                cp("dve", Rb_t[rbn][:], R_[:], R=[Rbuf], W=[Rb_b[rbn]])

            def st_A2(T):
                zap, zbufs = zpair(T)
                a_ = a_t[T["sl3"]]; ab = a_b[T["sl3"]]
                act(a_[:], zap, AF.Exp, R=zbufs, W=[ab])

            def st_P3(T):
                c, g = T["c"], T["g"]
                a_ = a_t[T["sl3"]]; ab = a_b[T["sl3"]]
                ob = T["ob"]
                first = (T["pi"] == 0)
                mm(PS[ob][:], vt[c % 2][:, T["jh"], :], a_[:, 0:512], start=first, stop=False, R=[vtb[c % 2], ab],
                   **({"W": [PB[ob]]} if first else {"A": [PB[ob]]}))
                mm(PS[ob][:], vt[c % 2][:, T["jl"], :], a_[:, 512:1024], start=False, stop=T["last"],
                   R=[vtb[c % 2], ab], A=[PB[ob]])
                if T["last"]:
                    q0 = g * 256
                    for h in range(2):
                        cp("dve", sbo[h * 64:(h + 1) * 64, c, q0:q0 + 256],
                           PS[ob][h * 64:(h + 1) * 64, h * 256:(h + 1) * 256], R=[PB[ob]], A=[sbob])

            p3_load(0)
            n_t = len(pairs)
            for s_i in range(n_t + 2):
                if 0 <= s_i - 2 < n_t:
                    T2 = pairs[s_i - 2]
                    if T2["g"] == 0 and T2["pi"] == 0 and T2["c"] + 1 < 4:
                        p3_load(T2["c"] + 1)
                if s_i < n_t:
                    T = pairs[s_i]
                    st_P1(T)
                    st_A1a(T)
                    st_A1b(T)
                if 0 <= s_i - 1 < n_t:
                    T1 = pairs[s_i - 1]
                    st_P2(T1)
                    st_G(T1)
                    st_A2(T1)
                if 0 <= s_i - 2 < n_t:
                    st_P3(pairs[s_i - 2])
            B.barrier()
            if debug:
                B.dma("sp", dbg["sbo"], sbo[:], sbob, store=True, reads=[sbob])
                B.barrier()
            if stop_after == "P3":
                B.barrier()
                B.emit()
                return nc
            sbo_s = nc.dram_tensor("sbo_s", [128, 4, NQ], BF16).ap()
            sbosb = B.buf("sbo_s")
            B.dma("sp", sbo_s, sbo[:], sbob, store=True, reads=[sbob], writes=[sbosb])
            B.barrier()
        s13.close()

        with contextlib.ExitStack() as s4:
            xn = sbt(s4, "xn", [128, 8, NB, 144], BF16); xnb = B.buf("xn")
            B.dma("sp", xn[:].rearrange("p c b t -> p c (b t)"), xn_s, xnb, reads=[xnsb], writes=[xnb])
            NOUT = 3
            o_t = [sbt(s4, "o_t%d" % i, [128, TO], F32) for i in range(NOUT)]
            o_b = [B.buf("o_t%d" % i) for i in range(NOUT)]
            sg_t = [sbt(s4, "sg_t%d" % i, [128, TO], F32) for i in range(2)]
            sg_b = [B.buf("sg_t%d" % i) for i in range(2)]
            cnt4 = {"o": 0, "sg": 0}

            def gate_mm(pb, wg, wgb, dc, t):
                for k in range(8):
                    mm(PS[pb][:, 0:TO].rearrange("p (b t) -> p b t", b=BPT),
                       wg[:, k, dc * 128:(dc + 1) * 128], xn[:, k, t * BPT:(t + 1) * BPT, 16:144],
                       start=(k == 0), stop=(k == 7), R=[wgb, xnb],
                       **({"W": [PB[pb]]} if k == 0 else {"A": [PB[pb]]}))

            def gated(ypb, gpb):
                si = cnt4["sg"] % 2; cnt4["sg"] += 1
                oi = cnt4["o"] % NOUT; cnt4["o"] += 1
                act(sg_t[si][:], PS[gpb][:, 0:TO], AF.Sigmoid, R=[PB[gpb]], W=[sg_b[si]])
                tt("dve", o_t[oi][:], sg_t[si][:], PS[ypb][:, 0:TO], ALU.mult, R=[sg_b[si], PB[ypb]], W=[o_b[oi]])
                return o_t[oi], o_b[oi]

            c0sb = B.buf("c0_s")
            with contextlib.ExitStack() as sa:
                wpi = sbt(sa, "wpi", [128, 8, 512], BF16); wpib = B.buf("wpi")
                wmix = sbt(sa, "wmix", [128, 4, 128], BF16); wmixb = B.buf("wmix")
                wpo = sbt(sa, "wpo", [128, 4, D], BF16); wpob = B.buf("wpo")
                wg0 = sbt(sa, "wg0", [128, 8, D], BF16); wg0b = B.buf("wg0")
                new_stg(sa)
                load_w(wpi, wpib, w_in[:, 0:512], 8, 512, gcol=V_PRE)
                load_w(wmix, wmixb, w_pmix, 4, 128)
                load_w(wpo, wpob, w_po, 4, D)
                load_w(wg0, wg0b, w_in[:, O_G:O_G + D], 8, D, gcol=V_PRE)
                pa = [sbt(sa, "pa%d" % i, [128, BPT, 144], F32) for i in range(3)]
                pab = [B.buf("pa%d" % i) for i in range(3)]
                icn = sbt(sa, "icn", [128, 4, TO], F32); icnb = B.buf("icn")
                mixd = sbt(sa, "mixd", [128, 4, TO], BF16); mixdb = B.buf("mixd")
                pm = [sbt(sa, "pm%d" % i, [128, 4, TO], BF16) for i in range(2)]
                pmb = [B.buf("pm%d" % i) for i in range(2)]
                tmpa = sbt(sa, "tmpa", [128, BPT, 128], F32); tmpab = B.buf("tmpa")
                HB = TOH // 2
                HBK = BPT // 2

                def pa_pool(t):
                    for gi in range(4):
                        ts("dve", icn[:, gi, :], qpos[:, t * TO:(t + 1) * TO], 1.0, float(2 << gi), ALU.add, ALU.min,
                           R=[qposb], **({"W": [icnb]} if gi == 0 else {"A": [icnb]}))
                    B.op("dve", lambda e: e.reciprocal(out=icn[:], in_=icn[:]), reads=[icnb], writes=[icnb])
                    for gi in range(4):
                        p0, p0b = pa[0], pab[0]
                        for hf in range(2):
                            pb = hf
                            for k in range(8):
                                mm(PS[pb][:, 0:HB].rearrange("p (b t) -> p b t", b=HBK),
                                   wpi[:, k, gi * 128:(gi + 1) * 128],
                                   xn[:, k, t * BPT + hf * HBK:t * BPT + (hf + 1) * HBK, :],
                                   start=(k == 0), stop=(k == 7), R=[wpib, xnb],
                                   **({"W": [PB[pb]]} if k == 0 else {"A": [PB[pb]]}))
                            cp("act", p0[:, hf * HBK:(hf + 1) * HBK, :],
                               PS[pb][:, 0:HB].rearrange("p (b t) -> p b t", b=HBK), R=[PB[pb]],
                               **({"W": [p0b]} if hf == 0 else {"A": [p0b]}))
                        cur, curb = p0, p0b
                        pp = 1
                        dsh = 1
                        lo = 0
                        for step in range(gi + 1):
                            nxt, nxtb = pa[pp], pab[pp]
                            lo = lo + dsh
                            tt("dve" if step % 2 == 0 else "pool", nxt[:, :, lo:144], cur[:, :, lo:144],
                               cur[:, :, lo - dsh:144 - dsh], ALU.add, R=[curb], W=[nxtb])
                            cur, curb = nxt, nxtb
                            pp = 3 - pp
                            dsh *= 2
                        tt("dve", tmpa[:], cur[:, :, 16:144], icn[:, gi, :].rearrange("p (b t) -> p b t", b=BPT),
                           ALU.mult, R=[curb, icnb], W=[tmpab])
                        tt("dve", mixd[:, gi, :].rearrange("p (b t) -> p b t", b=BPT), tmpa[:], p0[:, :, 16:144],
                           ALU.subtract, R=[tmpab, p0b], **({"W": [mixdb]} if gi == 0 else {"A": [mixdb]}))
                        yield
                    for gi in range(4):
                        pb = 2 + (gi % 2)
                        mm(PS[pb][:, 0:TO], wmix[:, gi, :], mixd[:, gi, :], start=True, stop=True,
                           R=[wmixb, mixdb], W=[PB[pb]])
                        ts("dve", pm[t % 2][:, gi, :], PS[pb][:, 0:TO], vec[:, V_PSC + gi:V_PSC + gi + 1], None,
                           ALU.mult, None, R=[PB[pb], vecb], **({"W": [pmb[t % 2]]} if gi == 0 else {"A": [pmb[t % 2]]}))
                    yield

                def pa_proj(t):
                    for dc in range(8):
                        ypb = 4 + (dc % 2)
                        gpb = 6 + (dc % 2)
                        for gi in range(4):
                            mm(PS[ypb][:, 0:TO], wpo[:, gi, dc * 128:(dc + 1) * 128], pm[t % 2][:, gi, :],
                               start=(gi == 0), stop=(gi == 3), R=[wpob, pmb[t % 2]],
                               **({"W": [PB[ypb]]} if gi == 0 else {"A": [PB[ypb]]}))
                        gate_mm(gpb, wg0, wg0b, dc, t)
                        ot, otb = gated(ypb, gpb)
                        B.dma("sp", c0_s[dc * 128:(dc + 1) * 128, t * TO:(t + 1) * TO], ot[:], otb, store=True,
                              reads=[otb], acc=[c0sb])
                        yield

                run(pa_pool(0))
                for t in range(NT):
                    if t + 1 < NT:
                        interleave(pa_proj(t), pa_pool(t + 1))
                    else:
                        run(pa_proj(t))
                B.barrier()

            c2sb = B.buf("c2_s")
            with contextlib.ExitStack() as sc:
                wxq = sbt(sc, "wxq", [128, 8, 256], BF16); wxqb = B.buf("wxq")
                wkv = sbt(sc, "wkv", [128, 8, 512], BF16); wkvb = B.buf("wkv")
                wxo = sbt(sc, "wxo", [128, 2, D], BF16); wxob = B.buf("wxo")
                wg2 = sbt(sc, "wg2", [128, 8, D], BF16); wg2b = B.buf("wg2")
                new_stg(sc)
                load_w(wkv, wkvb, w_kv, 8, 512, gcol=V_MEM)
                load_w(wxq, wxqb, w_in[:, O_XQ:O_XQ + 256], 8, 256, gcol=V_PRE)
                load_w(wxo, wxob, w_xo, 2, D)
                load_w(wg2, wg2b, w_in[:, O_G + 2 * D:O_G + 3 * D], 8, D, gcol=V_PRE)
                m_t = sbt(sc, "m_t", [128, 8, 256], F32); m_b = B.buf("m_t")
                msq = sbt(sc, "msq", [128, 8, 256], BF16); msqb = B.buf("msq")
                mln = sbt(sc, "mln", [128, 256], F32); mlnb = B.buf("mln")
                mrs = sbt(sc, "mrs", [128, 256], F32); mrsb = B.buf("mrs")
                mn = sbt(sc, "mn", [128, 8, 256], BF16); mnb = B.buf("mn")
                mkT = sbt(sc, "mkT", [128, 2, 256], BF16); mkTb = B.buf("mkT")
                mvz = sbt(sc, "mvz", [128, 4, 2, 128], BF16); mvb = B.buf("mvz")
                xqz = sbt(sc, "xqz", [128, 4, TO], BF16); xqTb = B.buf("xqz")
                B.op("pool", lambda e: e.memset(mvz[:].rearrange("p a b c -> p (a b c)"), 0.0), writes=[mvb])
                B.op("pool", lambda e: e.memset(xqz[:].rearrange("p a b -> p (a b)"), 0.0), writes=[xqTb])
                nmx = sbt(sc, "nmx", [128, 4], F32); nmxb = B.buf("nmx")
                ssum = sbt(sc, "ssum", [128, 4], F32); ssumb = B.buf("ssum")
                rsm = sbt(sc, "rsm", [128, 4], F32); rsmb = B.buf("rsm")
                P_t = sbt(sc, "P_t", [128, 4, 256], F32); P_b = B.buf("P_t")
                Pn = sbt(sc, "Pn", [128, 4, 256], BF16); Pnb = B.buf("Pn")
                PT = sbt(sc, "PT", [128, 8, 128], BF16); PTb = B.buf("PT")
                xoT = sbt(sc, "xoT", [128, 2, TO], BF16); xoTb = B.buf("xoT")

                B.dma("sp", m_t[:], memT.rearrange("(c p) n -> p c n", p=128), m_b, writes=[m_b])
                act(msq[:], m_t[:], AF.Square, R=[m_b], W=[msqb])
                for k in range(8):
                    mm(PS[0][:, 0:256], ones[:], msq[:, k, :], start=(k == 0), stop=(k == 7), R=[onesb, msqb],
                       **({"W": [PB[0]]} if k == 0 else {"A": [PB[0]]}))
                rstd_from(PS[0][:, 0:256], PB[0], 256, mln[:], mlnb, mrs[:], mrsb)
                for k in range(8):
                    stt(mn[:, k, :], m_t[:, k, :], vec[:, V_MEM + k:V_MEM + k + 1], mrs[:], R=[m_b, mrsb, vecb],
                        **({"W": [mnb]} if k == 0 else {"A": [mnb]}))
                for ch in range(2):
                    for k in range(8):
                        mm(PS[1][:, 0:256], wkv[:, k, ch * 128:(ch + 1) * 128], mn[:, k, :], start=(k == 0), stop=(k == 7),
                           R=[wkvb, mnb], **({"W": [PB[1]]} if k == 0 else {"A": [PB[1]]}))
                    cp("dve", mkT[:, ch, :], PS[1][:, 0:256], R=[PB[1]], **({"W": [mkTb]} if ch == 0 else {"A": [mkTb]}))
                for mc in range(2):
                    for k in range(8):
                        mm(PS[2][:, 0:256], mn[:, k, mc * 128:(mc + 1) * 128], wkv[:, k, 256:512], start=(k == 0), stop=(k == 7),
                           R=[wkvb, mnb], **({"W": [PB[2]]} if k == 0 else {"A": [PB[2]]}))
                    for h in range(4):
                        cp("dve", mvz[:, h, mc, (h % 2) * 64:(h % 2 + 1) * 64], PS[2][:, h * 64:(h + 1) * 64],
                           R=[PB[2]], A=[mvb])

                for t in range(NT):
                    for ch in range(2):
                        for k in range(8):
                            mm(PS[3][:, 0:TO].rearrange("p (b t) -> p b t", b=BPT),
                               wxq[:, k, ch * 128:(ch + 1) * 128], xn[:, k, t * BPT:(t + 1) * BPT, 16:144],
                               start=(k == 0), stop=(k == 7), R=[wxqb, xnb],
                               **({"W": [PB[3]]} if k == 0 else {"A": [PB[3]]}))
                        for hp in range(2):
                            act(xqz[hp * 64:(hp + 1) * 64, ch * 2 + hp, :], PS[3][hp * 64:(hp + 1) * 64, 0:TO], AF.Copy,
                                R=[PB[3]], A=[xqTb], scale=0.125)
                    for bk in range(BPT):
                        for h in range(4):
                            pb = h // 2
                            hp = h % 2
                            mm(PS[pb][:, hp * 256:(hp + 1) * 256],
                               xqz[:, h, bk * 128:(bk + 1) * 128], mkT[:, h // 2, :],
                               start=True, stop=True, R=[xqTb, mkTb],
                               **({"W": [PB[pb]]} if hp == 0 else {"A": [PB[pb]]}))
                        for pb in range(2):
                            B.op("dve", lambda e, pb=pb: e.reduce_max(
                                out=nmx[:, 2 * pb:2 * pb + 2], in_=PS[pb][:].rearrange("p (h m) -> p h m", h=2),
                                axis=AX.X, negate=True), reads=[PB[pb]],
                                **({"writes": [nmxb]} if pb == 0 else {"acc": [nmxb]}))
                        for h in range(4):
                            pb = h // 2
                            hp = h % 2
                            act(P_t[:, h, :], PS[pb][:, hp * 256:(hp + 1) * 256], AF.Exp, R=[PB[pb], nmxb],
                                **({"W": [P_b, ssumb]} if h == 0 else {"A": [P_b, ssumb]}),
                                bias=nmx[:, h:h + 1], accum_out=ssum[:, h:h + 1])
                        B.op("dve", lambda e: e.reciprocal(out=rsm[:], in_=ssum[:]), reads=[ssumb], writes=[rsmb])
                        for h in range(4):
                            ts("dve", Pn[:, h, :], P_t[:, h, :], rsm[:, h:h + 1], None, ALU.mult, None,
                               R=[P_b, rsmb], **({"W": [Pnb]} if h == 0 else {"A": [Pnb]}))
                        for h in range(4):
                            for mc in range(2):
                                i8 = h * 2 + mc
                                B.op("pe", lambda e, h=h, mc=mc, i8=i8: e.transpose(
                                    PST[:, i8 * 128:(i8 + 1) * 128], Pn[:, h, mc * 128:(mc + 1) * 128], ident[:]),
                                    reads=[Pnb, identb], **({"writes": [PSTB]} if i8 == 0 else {"acc": [PSTB]}))
                        cp("act", PT[:].rearrange("p a b -> p (a b)"), PST[:], R=[PSTB], W=[PTb])
                        for ch in range(2):
                            pb = 2 + ch
                            n4 = 0
                            for h in (2 * ch, 2 * ch + 1):
                                for mc in range(2):
                                    mm(PS[pb][:, bk * 128:(bk + 1) * 128], mvz[:, h, mc, :], PT[:, h * 2 + mc, :],
                                       start=(n4 == 0), stop=(n4 == 3), R=[mvb, PTb],
                                       **({"W": [PB[pb]]} if (bk == 0 and n4 == 0) else {"A": [PB[pb]]}))
                                    n4 += 1
                    for ch in range(2):
                        cp("dve", xoT[:, ch, :], PS[2 + ch][:, 0:TO], R=[PB[2 + ch]],
                           **({"W": [xoTb]} if ch == 0 else {"A": [xoTb]}))
                    for dc in range(8):
                        ypb = 4 + (dc % 2)
                        gpb = 6
                        for ch in range(2):
                            mm(PS[ypb][:, 0:TO], wxo[:, ch, dc * 128:(dc + 1) * 128], xoT[:, ch, :],
                               start=(ch == 0), stop=(ch == 1), R=[wxob, xoTb],
                               **({"W": [PB[ypb]]} if ch == 0 else {"A": [PB[ypb]]}))
                        gate_mm(gpb, wg2, wg2b, dc, t)
                        ot, otb = gated(ypb, gpb)
                        B.dma("pool", c2_s[dc * 128:(dc + 1) * 128, t * TO:(t + 1) * TO], ot[:], otb, store=True,
                              reads=[otb], acc=[c2sb])
                B.barrier()
            if debug:
                with contextlib.ExitStack() as sd:
                    dd = sbt(sd, "dd", [128, 8, NQ], F32); ddb = B.buf("dd")
                    for nm, src, srcb in (("c0", c0_s, c0sb), ("c2", c2_s, c2sb)):
                        B.dma("sp", dd[:], src.rearrange("(c p) n -> p c n", p=128), ddb, reads=[srcb], writes=[ddb])
                        B.dma("sp", dbg[nm].rearrange("(c p) n -> p c n", p=128), dd[:], ddb, store=True, reads=[ddb])
                    B.barrier()

            if stop_after == "P4c":
                B.barrier()
                B.emit()
                return nc
            h1sb = B.buf("h1_s"); n2sb = B.buf("n2_s")
            with contextlib.ExitStack() as sd4:
                wsbo = sbt(sd4, "wsbo", [128, 4, D], BF16); wsbob = B.buf("wsbo")
                wg1 = sbt(sd4, "wg1", [128, 8, D], BF16); wg1b = B.buf("wg1")
                wout = sbt(sd4, "wout", [128, 8, D], BF16); woutb = B.buf("wout")
                with contextlib.ExitStack() as sw:
                    new_stg(sw)
                    load_w(wsbo, wsbob, w_sbo, 4, D)
                    load_w(wg1, wg1b, w_in[:, O_G + D:O_G + 2 * D], 8, D, gcol=V_PRE)
                    load_w(wout, woutb, w_out, 8, D)
                    B.barrier()
                sbo4 = sbt(sd4, "sbo4", [128, 4, NQ], BF16); sbo4b = B.buf("sbo4")
                B.dma("sp", sbo4[:], sbo_s, sbo4b, reads=[sbosb], writes=[sbo4b])
                NCL = 3
                cl = [sbt(sd4, "cl%d" % i, [128, 2, TO], F32) for i in range(NCL)]
                clb = [B.buf("cl%d" % i) for i in range(NCL)]
                clc = {"i": 0}
                mg1 = sbt(sd4, "mg", [128, 8, TO], BF16)
                mg = [mg1, mg1]
                mgb1 = B.buf("mg")
                mgb = [mgb1, mgb1]
                mo = [sbt(sd4, "mo%d" % i, [128, 8, TO], F32) for i in range(2)]
                mob = [B.buf("mo%d" % i) for i in range(2)]
                sqm = [sbt(sd4, "sqm%d" % i, [128, TO], BF16) for i in range(2)]
                sqmb = [B.buf("sqm%d" % i) for i in range(2)]
                lnm = sbt(sd4, "lnm", [128, TO], F32); lnmb = B.buf("lnm")
                rsm1 = sbt(sd4, "rsm1", [128, TO], F32); rsm1b = B.buf("rsm1")
                rsm2 = sbt(sd4, "rsm2", [128, TO], F32); rsm2b = B.buf("rsm2")
                xr = sbt(sd4, "xr", [128, 8, BPT, 128], F32); xrb = B.buf("xr")
                n2t = sbt(sd4, "n2t", [128, 8, TO], BF16); n2tb = B.buf("n2t")
                xoh4 = xoh.rearrange("d (b t) -> d b t", t=144)

                def bd_S1(t):
                    for dc in range(8):
                        ci = clc["i"] % NCL
                        clc["i"] += 1
                        B.dma("sp", cl[ci][:, 0, :], c0_s[dc * 128:(dc + 1) * 128, t * TO:(t + 1) * TO], clb[ci],
                              reads=[c0sb], writes=[clb[ci]])
                        B.dma("sp", cl[ci][:, 1, :], c2_s[dc * 128:(dc + 1) * 128, t * TO:(t + 1) * TO], clb[ci],
                              reads=[c2sb], acc=[clb[ci]])
                        ypb = 4 + (dc % 2)
                        gpb = (dc % 2)
                        for c4 in range(4):
                            mm(PS[ypb][:, 0:TO], wsbo[:, c4, dc * 128:(dc + 1) * 128], sbo4[:, c4, t * TO:(t + 1) * TO],
                               start=(c4 == 0), stop=(c4 == 3), R=[wsbob, sbo4b],
                               **({"W": [PB[ypb]]} if c4 == 0 else {"A": [PB[ypb]]}))
                        gate_mm(gpb, wg1, wg1b, dc, t)
                        ot, otb = gated(ypb, gpb)
                        tt("pool", cl[ci][:, 0, :], cl[ci][:, 0, :], cl[ci][:, 1, :], ALU.add, R=[clb[ci]], A=[clb[ci]])
                        tt("dve", mg[t % 2][:, dc, :], ot[:], cl[ci][:, 0, :], ALU.add, R=[otb, clb[ci]],
                           **({"W": [mgb[t % 2]]} if dc == 0 else {"A": [mgb[t % 2]]}))
                        yield

                def bd_S2(t):
                    for dc in range(8):
                        B.dma("sp", xr[:, dc, :, :], xoh4[dc * 128:(dc + 1) * 128, t * BPT:(t + 1) * BPT, 16:144], xrb,
                              **({"writes": [xrb]} if dc == 0 else {"acc": [xrb]}))
                    m_, mb_ = mo[t % 2], mob[t % 2]
                    for dc in range(8):
                        pb = 2 + (dc % 2)
                        for k in range(8):
                            mm(PS[pb][:, 0:TO], wout[:, k, dc * 128:(dc + 1) * 128], mg[t % 2][:, k, :],
                               start=(k == 0), stop=(k == 7), R=[woutb, mgb[t % 2]],
                               **({"W": [PB[pb]]} if k == 0 else {"A": [PB[pb]]}))
                        if dc > 0:
                            d1 = dc - 1
                            mm(PS[6][:, 0:TO], ones[:], sqm[d1 % 2][:], start=(d1 == 0), stop=False,
                               R=[onesb, sqmb[d1 % 2]], **({"W": [PB[6]]} if d1 == 0 else {"A": [PB[6]]}))
                        cp("dve", m_[:, dc, :], PS[pb][:, 0:TO], R=[PB[pb]], **({"W": [mb_]} if dc == 0 else {"A": [mb_]}))
                        act(sqm[dc % 2][:], m_[:, dc, :], AF.Square, R=[mb_], W=[sqmb[dc % 2]])
                    mm(PS[6][:, 0:TO], ones[:], sqm[7 % 2][:], start=False, stop=True,
                       R=[onesb, sqmb[7 % 2]], A=[PB[6]])

                def bd_S3(t):
                    m_, mb_ = mo[t % 2], mob[t % 2]
                    rstd_from(PS[6][:, 0:TO], PB[6], TO, lnm[:], lnmb, rsm1[:], rsm1b)
                    for dc in range(8):
                        stt(m_[:, dc, :], m_[:, dc, :], vec[:, V_POST + dc:V_POST + dc + 1], rsm1[:],
                            R=[mb_, vecb, rsm1b], A=[mb_])
                        tt("pool", m_[:, dc, :], m_[:, dc, :], xr[:, dc, :, :].rearrange("p b t -> p (b t)"), ALU.add,
                           R=[mb_, xrb], A=[mb_])
                        act(sqm[dc % 2][:], m_[:, dc, :], AF.Square, R=[mb_], W=[sqmb[dc % 2]])
                        mm(PS[7][:, 0:TO], ones[:], sqm[dc % 2][:], start=(dc == 0), stop=(dc == 7),
                           R=[onesb, sqmb[dc % 2]], **({"W": [PB[7]]} if dc == 0 else {"A": [PB[7]]}))
                        yield
                    B.dma("sp", h1_s[:, t * TO:(t + 1) * TO].rearrange("(c p) n -> p c n", p=128), m_[:], mb_, store=True,
                          reads=[mb_], acc=[h1sb])
                    rstd_from(PS[7][:, 0:TO], PB[7], TO, lnm[:], lnmb, rsm2[:], rsm2b)
                    for dc in range(8):
                        stt(n2t[:, dc, :], m_[:, dc, :], vec[:, V_FPRE + dc:V_FPRE + dc + 1], rsm2[:],
                            R=[mb_, rsm2b, vecb], **({"W": [n2tb]} if dc == 0 else {"A": [n2tb]}))
                    B.dma("sp", n2_s[:, t * TO:(t + 1) * TO].rearrange("(c p) n -> p c n", p=128), n2t[:], n2tb, store=True,
                          reads=[n2tb], acc=[n2sb])

                run(bd_S1(0))
                for t in range(NT):
                    bd_S2(t)
                    if t + 1 < NT:
                        interleave(bd_S3(t), bd_S1(t + 1))
                    else:
                        run(bd_S3(t))
                B.barrier()
        if debug:
            with contextlib.ExitStack() as sd:
                dd = sbt(sd, "dd2", [128, 8, NQ], F32); ddb = B.buf("dd2")
                B.dma("sp", dd[:], h1_s.rearrange("(c p) n -> p c n", p=128), ddb, reads=[h1sb], writes=[ddb])
                B.dma("sp", dbg["h1"].rearrange("(c p) n -> p c n", p=128), dd[:], ddb, store=True, reads=[ddb])
                B.barrier()

        if stop_after == "P4":
            B.barrier()
            B.emit()
            return nc
        with contextlib.ExitStack() as s56:
            actT = sbt(s56, "actT", [128, NFC, NQ], BF16); actTb = B.buf("actT")
            wfo = sbt(s56, "wfo", [128, NFC, D], BF16); wfob = B.buf("wfo")
            with contextlib.ExitStack() as s5:
                n2 = sbt(s5, "n2", [128, 8, NQ], BF16); n2b = B.buf("n2")
                new_stg(s5)
                B.dma("sp", n2[:], n2_s.rearrange("(c p) n -> p c n", p=128), n2b, reads=[n2sb], writes=[n2b])
                wfi = [sbt(s5, "wfi%d" % i, [128, 8, 256], BF16) for i in range(2)]
                wfib = [B.buf("wfi%d" % i) for i in range(2)]
                sil = [sbt(s5, "sil%d" % i, [128, TO], F32) for i in range(2)]
                silb = [B.buf("sil%d" % i) for i in range(2)]

                def p5_loadw(f):
                    sl = f % 2
                    load_w(wfi[sl][:, :, 0:128], wfib[sl], w_fi[:, f * 128:(f + 1) * 128], 8, 128, gcol=V_FPRE, first=True)
                    load_w(wfi[sl][:, :, 128:256], wfib[sl], w_fi[:, DFF + f * 128:DFF + (f + 1) * 128], 8, 128,
                           gcol=V_FPRE, first=False)

                p5_loadw(0)
                cnt5 = 0
                for f in range(NFC):
                    if f + 1 < NFC:
                        p5_loadw(f + 1)
                    load_w(wfo[:, f:f + 1, :], wfob, w_fo[f * 128:(f + 1) * 128, :], 1, D, first=(f == 0))
                    sl = f % 2
                    for t in range(NT):
                        gp = (cnt5 % 2) * 2
                        up = gp + 1
                        si = cnt5 % 2
                        cnt5 += 1
                        for k in range(8):
                            mm(PS[gp][:, 0:TO], wfi[sl][:, k, 0:128], n2[:, k, t * TO:(t + 1) * TO],
                               start=(k == 0), stop=(k == 7), R=[wfib[sl], n2b],
                               **({"W": [PB[gp]]} if k == 0 else {"A": [PB[gp]]}))
                        for k in range(8):
                            mm(PS[up][:, 0:TO], wfi[sl][:, k, 128:256], n2[:, k, t * TO:(t + 1) * TO],
                               start=(k == 0), stop=(k == 7), R=[wfib[sl], n2b],
                               **({"W": [PB[up]]} if k == 0 else {"A": [PB[up]]}))
                        act(sil[si][:], PS[gp][:, 0:TO], AF.Silu, R=[PB[gp]], W=[silb[si]])
                        tt("dve", actT[:, f, t * TO:(t + 1) * TO], sil[si][:], PS[up][:, 0:TO], ALU.mult,
                           R=[silb[si], PB[up]], A=[actTb])
                B.barrier()
            with contextlib.ExitStack() as s6:
                ff = [sbt(s6, "ff%d" % i, [128, 8, TO], F32) for i in range(2)]
                ffb = [B.buf("ff%d" % i) for i in range(2)]
                sq6 = [sbt(s6, "sq6_%d" % i, [128, TO], BF16) for i in range(2)]
                sq6b = [B.buf("sq6_%d" % i) for i in range(2)]
                ln6 = sbt(s6, "ln6", [128, TO], F32); ln6b = B.buf("ln6")
                rs6 = sbt(s6, "rs6", [128, TO], F32); rs6b = B.buf("rs6")
                h1t = [sbt(s6, "h1t%d" % i, [128, TO], F32) for i in range(3)]
                h1tb = [B.buf("h1t%d" % i) for i in range(3)]
                outsb = B.buf("outT")
                cnt6 = {"h": 0}

                def p6_M(t):
                    f_, fb_ = ff[t % 2], ffb[t % 2]
                    ssb = 4 + (t % 2)
                    for dc in range(8):
                        pb = dc % 4
                        for f in range(NFC):
                            mm(PS[pb][:, 0:TO], wfo[:, f, dc * 128:(dc + 1) * 128], actT[:, f, t * TO:(t + 1) * TO],
                               start=(f == 0), stop=(f == NFC - 1), R=[wfob, actTb],
                               **({"W": [PB[pb]]} if f == 0 else {"A": [PB[pb]]}))
                        if dc > 0:
                            d1 = dc - 1
                            mm(PS[ssb][:, 0:TO], ones[:], sq6[d1 % 2][:], start=(d1 == 0), stop=False,
                               R=[onesb, sq6b[d1 % 2]], **({"W": [PB[ssb]]} if d1 == 0 else {"A": [PB[ssb]]}))
                        cp("dve", f_[:, dc, :], PS[pb][:, 0:TO], R=[PB[pb]], **({"W": [fb_]} if dc == 0 else {"A": [fb_]}))
                        act(sq6[dc % 2][:], f_[:, dc, :], AF.Square, R=[fb_], W=[sq6b[dc % 2]])
                        yield
                    mm(PS[ssb][:, 0:TO], ones[:], sq6[7 % 2][:], start=False, stop=True,
                       R=[onesb, sq6b[7 % 2]], A=[PB[ssb]])

                def p6_E(t):
                    f_, fb_ = ff[t % 2], ffb[t % 2]
                    ssb = 4 + (t % 2)
                    rstd_from(PS[ssb][:, 0:TO], PB[ssb], TO, ln6[:], ln6b, rs6[:], rs6b)
                    for dc in range(8):
                        hi = cnt6["h"] % 3
                        cnt6["h"] += 1
                        B.dma("sp", h1t[hi][:], h1_s[dc * 128:(dc + 1) * 128, t * TO:(t + 1) * TO], h1tb[hi],
                              reads=[h1sb], writes=[h1tb[hi]])
                        stt(f_[:, dc, :], f_[:, dc, :], vec[:, V_FPOST + dc:V_FPOST + dc + 1], rs6[:],
                            R=[fb_, vecb, rs6b], A=[fb_])
                        tt("pool", f_[:, dc, :], f_[:, dc, :], h1t[hi][:], ALU.add, R=[fb_, h1tb[hi]], A=[fb_])
                        yield
                    B.dma("sp", outT[:, t * TO:(t + 1) * TO].rearrange("(c p) n -> p c n", p=128), f_[:], fb_, store=True,
                          reads=[fb_], acc=[outsb])

                run(p6_M(0))
                for t in range(NT):
                    if t + 1 < NT:
                        interleave(p6_M(t + 1), p6_E(t))
                    else:
                        run(p6_E(t))
                B.barrier()
        B.barrier()
        B.emit()
    return nc


def _own_blocks(r, G):
    blks = []
    for g in range(G):
        blks.append(8 * g + r)
        blks.append(8 * g + 7 - r)
    return blks


def _pack_vecs(norm_mix_pre, norm_mem, norm_mix_post, norm_ffn_pre, norm_ffn_post, pool_scale):
    cols = []
    for v in (norm_mix_pre, norm_mem, norm_mix_post, norm_ffn_pre, norm_ffn_post):
        cols.append(np.asarray(v, np.float32).reshape(8, 128).T)
    cols.append(np.asarray(pool_scale, np.float32).reshape(4, 128).T)
    return np.ascontiguousarray(np.concatenate(cols, axis=1))


_PROG_CACHE = {}


def run_layer(x, mem, norm_mix_pre, w_in, w_pool_mix, pool_scale, w_pool_o, w_sb_o, norm_mem, w_mem_kv, w_x_o,
              w_out, norm_mix_post, norm_ffn_pre, w_ffn_in, w_ffn_out, norm_ffn_post, debug=False, stop_after=None):
    x = np.asarray(x, np.float32)
    mem = np.asarray(mem, np.float32)
    Bn, S, _ = x.shape
    G = S // 1024
    assert Bn == 2 and S == 1024 * G
    key = (G, debug, stop_after)
    if key not in _PROG_CACHE:
        _PROG_CACHE[key] = build_program(G, debug, stop_after)
    nc = _PROG_CACHE[key]
    f32 = lambda a: np.ascontiguousarray(np.asarray(a, np.float32))
    shared = {
        "w_in": f32(w_in[0]), "w_pmix": f32(np.asarray(w_pool_mix[0]).reshape(512, 128)), "w_po": f32(w_pool_o[0]),
        "w_sbo": f32(w_sb_o[0]), "w_kv": f32(w_mem_kv[0]), "w_xo": f32(w_x_o[0]), "w_out": f32(w_out[0]),
        "w_fi": f32(w_ffn_in[0]), "w_fo": f32(w_ffn_out[0]),
        "vecs": _pack_vecs(norm_mix_pre[0], norm_mem[0], norm_mix_post[0], norm_ffn_pre[0], norm_ffn_post[0],
                           pool_scale[0]),
    }
    xTb = [np.ascontiguousarray(x[b].T) for b in range(2)]
    memTb = [np.ascontiguousarray(mem[b].T) for b in range(2)]
    in_maps = []
    for core in range(8):
        b, r = core // 4, core % 4
        blks = _own_blocks(r, G)
        xoh = np.zeros((D, len(blks) * 144), np.float32)
        posv = np.zeros((len(blks) * 128,), np.float32)
        for n, blk in enumerate(blks):
            s0 = blk * 128
            lo = max(0, s0 - 16)
            xoh[:, n * 144 + 16 - (s0 - lo):(n + 1) * 144] = xTb[b][:, lo:s0 + 128]
            posv[n * 128:(n + 1) * 128] = np.arange(s0, s0 + 128, dtype=np.float32)
        m = dict(shared)
        m.update({"xT": xTb[b], "xoh": xoh, "pos": posv, "memT": memTb[b]})
        in_maps.append(m)
    res = run_bass_kernel_spmd(nc, in_maps, core_ids=list(range(8)))
    out = np.empty((2, S, D), np.float32)
    for core in range(8):
        b, r = core // 4, core % 4
        oT = np.asarray(res.results[core]["outT"])
        for n, blk in enumerate(_own_blocks(r, G)):
            out[b, blk * 128:(blk + 1) * 128, :] = oT[:, n * 128:(n + 1) * 128].T
    if debug:
        return out, res.results
    return out


def kernel(**inputs):
    return run_layer(**inputs)
```

```python
import contextlib
import numpy as np
import concourse.bass as bass
import concourse.mybir as mybir
from concourse.bass_utils import run_bass_kernel_spmd

F32 = mybir.dt.float32
BF16 = mybir.dt.bfloat16
I32 = mybir.dt.int32
AF = mybir.ActivationFunctionType
ALU = mybir.AluOpType
AX = mybir.AxisListType

ENGS = ("pe", "act", "dve", "pool", "sp")
SAME_ENG_RAW = True

D = 1024
DFF = 2816
NFC = DFF // 128
INW = 5376
O_Q, O_K, O_V, O_XQ, O_G = 512, 1024, 1536, 2048, 2304
EPS = 1e-6
NEG_BIG = -30000.0
V_PRE, V_MEM, V_POST, V_FPRE, V_FPOST, V_PSC, NVEC = 0, 8, 16, 24, 32, 40, 44


class Buf:
    __slots__ = ("name", "w", "r", "excl", "base")

    def __init__(self, name):
        self.name = name
        self.w = {}
        self.r = {}
        self.base = {}
        self.excl = False


class Builder:
    def __init__(self, nc, es):
        self.nc = nc
        self.es = es
        self.q = {e: [] for e in ENGS}
        self.cnt = {}
        self.seen = {e: {} for e in ENGS}
        self.semh = {}
        self.nbuf = 0
        for e in ENGS:
            if e != "sp":
                self._newsem(e)

    def _newsem(self, key):
        self.semh[key] = self.es.enter_context(self.nc.semaphore("s%d" % len(self.semh)))
        self.cnt[key] = 0

    def buf(self, name=None):
        self.nbuf += 1
        return Buf("%s#%d" % (name or "b", self.nbuf))

    def _wait(self, eng, key, val):
        if self.seen[eng].get(key, 0) >= val:
            return
        self.seen[eng][key] = val
        sem = self.semh[key]
        self.q[eng].append(lambda e, sem=sem, val=val: e.wait_ge(sem, val))

    def _deps(self, eng, reads, writes, acc):
        raw = {}
        oth = {}

        def add(dst, d):
            for k, v in d.items():
                if dst.get(k, 0) < v:
                    dst[k] = v
        for b in reads:
            add(raw, b.w)
            if b.excl:
                add(oth, b.r)
        for b in writes:
            add(oth, b.w)
            add(oth, b.r)
        for b in acc:
            add(oth, b.r)
            add(oth, b.base)
        for k, v in raw.items():
            if k == eng and (eng == "pe" or not SAME_ENG_RAW):
                continue
            self._wait(eng, k, v)
        for k, v in oth.items():
            if k == eng:
                continue
            self._wait(eng, k, v)

    def _record(self, key, val, reads, writes, acc):
        for b in reads:
            if b.r.get(key, 0) < val:
                b.r[key] = val
        for b in writes:
            b.w = {key: val}
            b.base = {key: val}
            b.r = {}
        for b in acc:
            b.w[key] = val
            b.r = {}

    def op(self, eng, fn, reads=(), writes=(), acc=()):
        self._deps(eng, reads, writes, acc)
        self.cnt[eng] += 1
        c = self.cnt[eng]
        sem = self.semh[eng]
        self.q[eng].append(lambda e, fn=fn, sem=sem: fn(e).then_inc(sem, 1))
        self._record(eng, c, reads, writes, acc)

    def dma(self, qeng, out_ap, in_ap, sb, store=False, reads=(), writes=(), acc=(), **kw):
        self._deps(qeng, reads, writes, acc)
        key = ("st" if store else "ld", sb.name, "sw" if qeng == "pool" else "hw")
        if key not in self.semh:
            self._newsem(key)
        self.cnt[key] += 16
        c = self.cnt[key]
        sem = self.semh[key]
        self.q[qeng].append(
            lambda e, o=out_ap, i=in_ap, sem=sem, kw=kw: e.dma_start(out=o, in_=i, **kw).then_inc(sem, 16))
        self._record(key, c, reads, writes, acc)

    def barrier(self):
        for e in ENGS:
            for k, c in self.cnt.items():
                if k == e or c == 0:
                    continue
                self._wait(e, k, c)

    def emit(self):
        q = self.q
        with self.nc.Block() as block:
            @block.tensor
            def _(e):
                for t in q["pe"]:
                    t(e)

            @block.scalar
            def _(e):
                for t in q["act"]:
                    t(e)

            @block.vector
            def _(e):
                for t in q["dve"]:
                    t(e)

            @block.gpsimd
            def _(e):
                for t in q["pool"]:
                    t(e)

            @block.sync
            def _(e):
                for t in q["sp"]:
                    t(e)


def build_program(G, debug=False, stop_after=None):
    S = 1024 * G
    NBLK = S // 128
    NB = 2 * G
    NQ = 128 * NB
    BPT = 4 if G >= 2 else 2
    TO = 128 * BPT
    TOH = 144 * BPT
    NT = NB // BPT
    ST = S // 512

    nc = bass.Bass("TRN2", target_bir_lowering=False)

    def din(name, shape, dt=F32):
        return nc.dram_tensor(name, list(shape), dt, kind="ExternalInput").ap()

    xT = din("xT", [D, S])
    xoh = din("xoh", [D, NB * 144])
    pos = din("pos", [NQ])
    memT = din("memT", [D, 256])
    w_in = din("w_in", [D, INW])
    w_pmix = din("w_pmix", [512, 128])
    w_po = din("w_po", [512, D])
    w_sbo = din("w_sbo", [512, D])
    w_kv = din("w_kv", [D, 512])
    w_xo = din("w_xo", [256, D])
    w_out = din("w_out", [D, D])
    w_fi = din("w_fi", [D, 2 * DFF])
    w_fo = din("w_fo", [DFF, D])
    vecs = din("vecs", [128, NVEC])
    outT = nc.dram_tensor("outT", [D, NQ], F32, kind="ExternalOutput").ap()
    kT_s = nc.dram_tensor("kT_s", [4, 128, S], BF16).ap()
    v_s = nc.dram_tensor("v_s", [4, 128, NBLK, 128], BF16).ap()
    c0_s = nc.dram_tensor("c0_s", [D, NQ], F32).ap()
    c2_s = nc.dram_tensor("c2_s", [D, NQ], F32).ap()
    h1_s = nc.dram_tensor("h1_s", [D, NQ], F32).ap()
    n2_s = nc.dram_tensor("n2_s", [D, NQ], BF16).ap()
    dbg = {}
    if debug:
        dbg["qT"] = nc.dram_tensor("dbg_qT", [128, 4, 2 * NQ], BF16, kind="ExternalOutput").ap()
        dbg["sbo"] = nc.dram_tensor("dbg_sbo", [128, 4, NQ], BF16, kind="ExternalOutput").ap()
        dbg["kT"] = nc.dram_tensor("dbg_kT", [4, 128, S], BF16, kind="ExternalOutput").ap()
        dbg["v"] = nc.dram_tensor("dbg_v", [4, 128, NBLK, 128], BF16, kind="ExternalOutput").ap()
        dbg["c0"] = nc.dram_tensor("dbg_c0", [D, NQ], F32, kind="ExternalOutput").ap()
        dbg["c2"] = nc.dram_tensor("dbg_c2", [D, NQ], F32, kind="ExternalOutput").ap()
        dbg["h1"] = nc.dram_tensor("dbg_h1", [D, NQ], F32, kind="ExternalOutput").ap()

    with contextlib.ExitStack() as es:
        B = Builder(nc, es)

        def sbt(scope, name, shape, dt):
            return scope.enter_context(nc.sbuf_tensor(name, list(shape), dt))

        def mm(out, lhsT, rhs, start, stop, R, W=(), A=(), sgc=False):
            B.op("pe", lambda e: e.matmul(out, lhsT, rhs, start=start, stop=stop, skip_group_check=sgc),
                 reads=R, writes=W, acc=A)

        def act(out, in_, func, R, W=(), A=(), **kw):
            B.op("act", lambda e: e.activation(out=out, in_=in_, func=func, **kw), reads=R, writes=W, acc=A)

        def tt(eng, out, in0, in1, op, R, W=(), A=()):
            B.op(eng, lambda e: e.tensor_tensor(out=out, in0=in0, in1=in1, op=op), reads=R, writes=W, acc=A)

        def ts(eng, out, in0, s1, s2, op0, op1, R, W=(), A=()):
            if op1 is None:
                B.op(eng, lambda e: e.tensor_scalar(out=out, in0=in0, scalar1=s1, scalar2=s2, op0=op0),
                     reads=R, writes=W, acc=A)
            else:
                B.op(eng, lambda e: e.tensor_scalar(out=out, in0=in0, scalar1=s1, scalar2=s2, op0=op0, op1=op1),
                     reads=R, writes=W, acc=A)

        def cp(eng, out, in_, R, W=(), A=()):
            if eng == "act":
                act(out, in_, AF.Copy, R, W, A)
            else:
                B.op(eng, lambda e: e.tensor_copy(out=out, in_=in_), reads=R, writes=W, acc=A)

        PSALL = es.enter_context(nc.psum_tensor("psall", [128, 4096], F32))
        PS = [PSALL[:, i * 512:(i + 1) * 512] for i in range(8)]
        PB = [B.buf("ps%d" % i) for i in range(8)]
        for b_ in PB:
            b_.excl = True
        PST = PS[7].bitcast(BF16)
        PSTB = PB[7]

        gs = es
        vec = sbt(gs, "vec", [128, NVEC], F32); vecb = B.buf("vec")
        ones = sbt(gs, "ones", [128, 128], BF16); onesb = B.buf("ones")
        nTin = sbt(gs, "nTin", [128, 128], BF16); nTinb = B.buf("nTin")
        nOnes = sbt(gs, "nOnes", [128, 128], BF16); nOnesb = B.buf("nOnes")
        nBigI = sbt(gs, "nBigI", [128, 128], BF16); nBigIb = B.buf("nBigI")
        ident = sbt(gs, "ident", [128, 128], BF16); identb = B.buf("ident")
        dif_i = sbt(gs, "dif_i", [128, 128], I32); difib = B.buf("dif_i")
        dif_f = sbt(gs, "dif_f", [128, 128], F32); diffb = B.buf("dif_f")
        kp8_i = sbt(gs, "kp8_i", [128, 8], I32); kp8ib = B.buf("kp8i")
        kp8 = sbt(gs, "kp8", [128, 8], F32); kp8b = B.buf("kp8")
        qpos = sbt(gs, "qpos", [128, NQ], F32); qposb = B.buf("qpos")
        notM = sbt(gs, "notM", [128, 8, 2, 256], BF16); notMb = B.buf("notM")
        NSTG = 2
        st_state = {"i": 0, "e": 0, "n": 0}

        def new_stg(scope):
            st_state["n"] += 1
            st_state["stg"] = [sbt(scope, "stg%d_%d" % (st_state["n"], i), [128, 1536], F32) for i in range(NSTG)]
            st_state["stgb"] = [B.buf("stg%d" % i) for i in range(NSTG)]

        B.dma("sp", vec[:], vecs, vecb, writes=[vecb])
        B.dma("sp", qpos[:], pos.partition_broadcast(128), qposb, writes=[qposb])
        B.op("dve", lambda e: e.memset(ones[:], 1.0), writes=[onesb])
        B.op("dve", lambda e: e.memset(nOnes[:], -1.0), writes=[nOnesb])
        B.op("pool", lambda e: e.iota(dif_i[:], pattern=[[-1, 128]], base=0, channel_multiplier=1), writes=[difib])
        cp("dve", dif_f[:], dif_i[:], R=[difib], W=[diffb])
        ts("dve", nTin[:], dif_f[:], 0.0, -1.0, ALU.is_ge, ALU.mult, R=[diffb], W=[nTinb])
        ts("dve", nBigI[:], dif_f[:], 0.0, NEG_BIG, ALU.is_equal, ALU.mult, R=[diffb], W=[nBigIb])
        ts("dve", ident[:], dif_f[:], 0.0, None, ALU.is_equal, None, R=[diffb], W=[identb])
        B.op("pool", lambda e: e.iota(kp8_i[:], pattern=[[128, 8]], base=0, channel_multiplier=1), writes=[kp8ib])
        cp("dve", kp8[:], kp8_i[:], R=[kp8ib], W=[kp8b])
        for jj in range(8):
            for h in range(2):
                ts("dve", notM[:, jj, h, :], qpos[:, 0:256], kp8[:, jj:jj + 1], None, ALU.is_le, None,
                   R=[qposb, kp8b], **({"W": [notMb]} if (jj == 0 and h == 0) else {"A": [notMb]}))

        def load_w(dst, dstb, src, kc, ncols, gcol=None, first=True):
            kg = max(1, min(kc, 1536 // ncols))
            k0 = 0
            while k0 < kc:
                kn = min(kg, kc - k0)
                i = st_state["i"] % NSTG
                st_state["i"] += 1
                stg, stgb = st_state["stg"], st_state["stgb"]
                sv = stg[i][:, 0:kn * ncols].rearrange("p (k n) -> p k n", k=kn)
                B.dma("sp", sv, src[k0 * 128:(k0 + kn) * 128, :].rearrange("(k p) n -> p k n", p=128), stgb[i],
                      writes=[stgb[i]])
                eng = "dve" if st_state["e"] % 2 == 0 else "act"
                st_state["e"] += 1
                kw = {"W": [dstb]} if (first and k0 == 0) else {"A": [dstb]}
                cp(eng, dst[:, k0:k0 + kn, :], sv, R=[stgb[i]], **kw)
                k0 += kn

        def run(gen):
            for _ in gen:
                pass

        def interleave(*gens):
            gens = list(gens)
            while gens:
                for g_ in list(gens):
                    try:
                        next(g_)
                    except StopIteration:
                        gens.remove(g_)

        def stt(out, in0, scal, in1, R, W=(), A=()):
            B.op("dve", lambda e: e.scalar_tensor_tensor(out=out, in0=in0, scalar=scal, in1=in1,
                                                         op0=ALU.mult, op1=ALU.mult), reads=R, writes=W, acc=A)

        def rstd_from(ss_ps, ss_b, ncols, ln_t, ln_b, out_t, out_b, out_is_write=True):
            act(ln_t, ss_ps, AF.Ln, R=[ss_b], W=[ln_b], scale=1.0 / D, bias=EPS)
            act(out_t, ln_t, AF.Exp, R=[ln_b], W=[out_b], scale=-0.5)

        s13 = contextlib.ExitStack()
        es.enter_context(s13)
        NGT = BPT // 2
        Qz = sbt(s13, "Qz", [128, 4, G, 2, 256], BF16); Qzb = B.buf("Qz")
        B.op("pool", lambda e: e.memset(Qz[:].rearrange("p a g h q -> p (a g h q)"), 0.0), writes=[Qzb])
        wqkv = sbt(s13, "wqkv", [128, 8, 1536], BF16); wqkvb = B.buf("wqkv")

        xn_s = nc.dram_tensor("xn_s", [128, 8, NB * 144], BF16).ap()
        xnsb = B.buf("xn_s")
        with contextlib.ExitStack() as s1:
            new_stg(s1)
            load_w(wqkv, wqkvb, w_in[:, O_Q:O_Q + 1536], 8, 1536)
            xo_t = [sbt(s1, "xo_t%d" % i, [128, 8, TOH], F32) for i in range(2)]
            xo_b = [B.buf("xo_t%d" % i) for i in range(2)]
            sq1 = sbt(s1, "sq1", [128, 8, TOH], BF16); sq1b = B.buf("sq1")
            ln1 = sbt(s1, "ln1", [128, TOH], F32); ln1b = B.buf("ln1")
            rs1 = sbt(s1, "rs1", [128, TOH], F32); rs1b = B.buf("rs1")
            xn1 = [sbt(s1, "xn1_%d" % i, [128, 8, BPT, 144], BF16) for i in range(2)]
            xn1b = [B.buf("xn1_%d" % i) for i in range(2)]
            HB = TOH // 2
            for t in range(NT):
                sl = t % 2
                B.dma("sp", xo_t[sl][:], xoh[:, t * TOH:(t + 1) * TOH].rearrange("(c p) n -> p c n", p=128),
                      xo_b[sl], writes=[xo_b[sl]])
                act(sq1[:], xo_t[sl][:], AF.Square, R=[xo_b[sl]], W=[sq1b])
                for hf in range(2):
                    for k in range(8):
                        mm(PS[hf][:, 0:HB], ones[:], sq1[:, k, hf * HB:(hf + 1) * HB], start=(k == 0), stop=(k == 7),
                           R=[onesb, sq1b], **({"W": [PB[hf]]} if k == 0 else {"A": [PB[hf]]}))
                for hf in range(2):
                    act(ln1[:, hf * HB:(hf + 1) * HB], PS[hf][:, 0:HB], AF.Ln, R=[PB[hf]],
                        **({"W": [ln1b]} if hf == 0 else {"A": [ln1b]}), scale=1.0 / D, bias=EPS)
                act(rs1[:], ln1[:], AF.Exp, R=[ln1b], W=[rs1b], scale=-0.5)
                xnv = xn1[sl][:].rearrange("p c b t -> p c (b t)")
                for k in range(8):
                    stt(xnv[:, k, :], xo_t[sl][:, k, :], vec[:, V_PRE + k:V_PRE + k + 1], rs1[:],
                        R=[xo_b[sl], rs1b, vecb], **({"W": [xn1b[sl]]} if k == 0 else {"A": [xn1b[sl]]}))
                B.dma("sp", xn_s[:, :, t * TOH:(t + 1) * TOH], xnv, xn1b[sl], store=True,
                      reads=[xn1b[sl]], acc=[xnsb])
                for c4 in range(4):
                    pb = 2 + (c4 % 2)
                    for k in range(8):
                        mm(PS[pb][:, 0:TO].rearrange("p (b t) -> p b t", b=BPT),
                           wqkv[:, k, c4 * 128:(c4 + 1) * 128], xn1[sl][:, k, :, 16:144],
                           start=(k == 0), stop=(k == 7), R=[wqkvb, xn1b[sl]],
                           **({"W": [PB[pb]]} if k == 0 else {"A": [PB[pb]]}))
                    for h in range(2):
                        act(Qz[h * 64:(h + 1) * 64, c4, t * NGT:(t + 1) * NGT, h, :],
                            PS[pb][h * 64:(h + 1) * 64, 0:TO].rearrange("p (g q) -> p g q", g=NGT), AF.Copy,
                            R=[PB[pb]], A=[Qzb], scale=0.125)
        B.barrier()
        if debug:
            B.dma("sp", dbg["qT"], Qz[:].rearrange("p a g h q -> p a (g h q)"), Qzb, store=True, reads=[Qzb])
        if stop_after == "P1":
            B.barrier()
            B.emit()
            return nc

        kTsb = B.buf("kT_s"); vsb = B.buf("v_s")
        with contextlib.ExitStack() as s2:
            x_t = [sbt(s2, "x_t%d" % i, [128, 8, 512], F32) for i in range(3)]
            x_b = [B.buf("x_t%d" % i) for i in range(3)]
            sq2 = [sbt(s2, "sq2_%d" % i, [128, 8, 512], BF16) for i in range(2)]
            sq2b = [B.buf("sq2_%d" % i) for i in range(2)]
            ln2 = sbt(s2, "ln2", [128, 512], F32); ln2b = B.buf("ln2")
            rs2 = sbt(s2, "rs2", [128, 512], F32); rs2b = B.buf("rs2")
            xn2 = [sbt(s2, "xn2_%d" % i, [128, 8, 512], BF16) for i in range(2)]
            xn2b = [B.buf("xn2_%d" % i) for i in range(2)]
            kst = [sbt(s2, "kst%d" % i, [128, 4, 512], BF16) for i in range(2)]
            kstb = [B.buf("kst%d" % i) for i in range(2)]
            vst = [sbt(s2, "vst%d" % i, [128, 4, 512], BF16) for i in range(2)]
            vstb = [B.buf("vst%d" % i) for i in range(2)]

            def p2_load(t):
                B.dma("sp", x_t[t % 3][:], xT[:, t * 512:(t + 1) * 512].rearrange("(c p) n -> p c n", p=128),
                      x_b[t % 3], writes=[x_b[t % 3]])

            def p2_sq(t):
                sl = t % 2
                act(sq2[sl][:], x_t[t % 3][:], AF.Square, R=[x_b[t % 3]], W=[sq2b[sl]])

            def p2_norm(t):
                sl = t % 2
                for k in range(8):
                    mm(PS[0][:], ones[:], sq2[sl][:, k, :], start=(k == 0), stop=(k == 7), R=[onesb, sq2b[sl]],
                       **({"W": [PB[0]]} if k == 0 else {"A": [PB[0]]}))
                rstd_from(PS[0][:], PB[0], 512, ln2[:], ln2b, rs2[:], rs2b)
                for k in range(8):
                    stt(xn2[sl][:, k, :], x_t[t % 3][:, k, :], vec[:, V_PRE + k:V_PRE + k + 1], rs2[:],
                        R=[x_b[t % 3], rs2b, vecb], **({"W": [xn2b[sl]]} if k == 0 else {"A": [xn2b[sl]]}))

            def p2_kv(t):
                sl = t % 2
                for c4 in range(4):
                    pb = 1 + (c4 % 3)
                    for k in range(8):
                        mm(PS[pb][:], wqkv[:, k, 512 + c4 * 128:512 + (c4 + 1) * 128], xn2[sl][:, k, :],
                           start=(k == 0), stop=(k == 7), R=[wqkvb, xn2b[sl]],
                           **({"W": [PB[pb]]} if k == 0 else {"A": [PB[pb]]}))
                    cp("dve", kst[sl][:, c4, :], PS[pb][:], R=[PB[pb]],
                       **({"W": [kstb[sl]]} if c4 == 0 else {"A": [kstb[sl]]}))
                B.dma("pool", kT_s[:, :, t * 512:(t + 1) * 512].rearrange("c p s -> p c s"), kst[sl][:], kstb[sl],
                      store=True, reads=[kstb[sl]], acc=[kTsb])
                for bk in range(4):
                    pb = 4 + (bk % 3)
                    for k in range(8):
                        mm(PS[pb][:], xn2[sl][:, k, bk * 128:(bk + 1) * 128], wqkv[:, k, 1024:1536],
                           start=(k == 0), stop=(k == 7), R=[wqkvb, xn2b[sl]],
                           **({"W": [PB[pb]]} if k == 0 else {"A": [PB[pb]]}))
                    cp("dve", vst[sl][:, bk, :], PS[pb][:], R=[PB[pb]],
                       **({"W": [vstb[sl]]} if bk == 0 else {"A": [vstb[sl]]}))
                for c4 in range(4):
                    B.dma("pool", v_s[c4, :, 4 * t:4 * t + 4, :], vst[sl][:, :, c4 * 128:(c4 + 1) * 128], vstb[sl],
                          store=True, reads=[vstb[sl]], acc=[vsb])

            p2_load(0)
            p2_sq(0)
            p2_norm(0)
            if ST > 1:
                p2_load(1)
                p2_sq(1)
            for t in range(ST):
                if t + 2 < ST:
                    p2_load(t + 2)
                    p2_sq(t + 2)
                if t + 1 < ST:
                    p2_norm(t + 1)
                p2_kv(t)
        B.barrier()
        if debug:
            with contextlib.ExitStack() as sd:
                dk = sbt(sd, "dk", [128, S], BF16); dkb = B.buf("dk")
                dv = sbt(sd, "dv", [128, NBLK, 128], BF16); dvb = B.buf("dv")
                for c4 in range(4):
                    B.dma("sp", dk[:], kT_s[c4], dkb, reads=[kTsb], writes=[dkb])
                    B.dma("sp", dbg["kT"][c4], dk[:], dkb, store=True, reads=[dkb])
                    B.dma("sp", dv[:], v_s[c4], dvb, reads=[vsb], writes=[dvb])
                    B.dma("sp", dbg["v"][c4], dv[:], dvb, store=True, reads=[dvb])
                B.barrier()

        if stop_after == "P2":
            B.barrier()
            B.emit()
            return nc
        s34 = contextlib.ExitStack()
        with contextlib.ExitStack() as s3:
            sbo = sbt(s3, "sbo", [128, 4, NQ], BF16); sbob = B.buf("sbo")
            kt = [sbt(s3, "kt%d" % i, [128, S], BF16) for i in range(2)]
            ktb = [B.buf("kt%d" % i) for i in range(2)]
            vt = [sbt(s3, "vt%d" % i, [128, NBLK, 128], BF16) for i in range(2)]
            vtb = [B.buf("vt%d" % i) for i in range(2)]
            NE = 2
            e_t = [sbt(s3, "e_t%d" % i, [128, 1024], F32) for i in range(NE)]
            e_b = [B.buf("e_t%d" % i) for i in range(NE)]
            NSP = 3
            sp_t = [sbt(s3, "sp_t%d" % i, [128, 1024], BF16) for i in range(NSP)]
            sp_b = [B.buf("sp_t%d" % i) for i in range(NSP)]
            a_t = [sbt(s3, "a_t%d" % i, [128, 1024], BF16) for i in range(NSP)]
            a_b = [B.buf("a_t%d" % i) for i in range(NSP)]
            R_t = [sbt(s3, "R_t%d" % i, [128, 512], F32) for i in range(2)]
            R_b = [B.buf("R_t%d" % i) for i in range(2)]
            Rb_t = [sbt(s3, "Rb_t%d" % i, [128, 512], BF16) for i in range(4)]
            Rb_b = [B.buf("Rb_t%d" % i) for i in range(4)]
            NZP = 3

            def p3_load(c):
                B.dma("sp", kt[c % 2][:], kT_s[c], ktb[c % 2], reads=[kTsb], writes=[ktb[c % 2]])
                B.dma("sp", vt[c % 2][:], v_s[c], vtb[c % 2], reads=[vsb], writes=[vtb[c % 2]])

            pairs = []
            chain_id = 0
            for c in range(4):
                for g in range(G):
                    nj = 8 * g + 8
                    for pi in range(nj // 2):
                        jh = nj - 1 - 2 * pi
                        pairs.append(dict(c=c, g=g, jh=jh, jl=jh - 1, pi=pi, last=(jh == 1), chain=chain_id))
                    chain_id += 1
            for s_i, T in enumerate(pairs):
                T["zs"] = s_i % NZP
                T["sl3"] = s_i % NSP
                T["esl"] = s_i % NE
                T["ob"] = 6 + (T["chain"] % 2)
                T["rs"] = T["chain"] % 2

            def st_P1(T):
                c, g = T["c"], T["g"]
                qv = Qz[:, c, g, :, :].rearrange("p h q -> p (h q)")
                for i, j in enumerate((T["jh"], T["jl"])):
                    bk_ = 2 * T["zs"] + i
                    jj = j - 8 * g
                    mm(PS[bk_][:], kt[c % 2][:, j * 128:(j + 1) * 128], qv, start=True, stop=(jj < 0),
                       R=[ktb[c % 2], Qzb], W=[PB[bk_]])
                    if jj >= 0:
                        mm(PS[bk_][:], nBigI[:], notM[:, jj, :, :].rearrange("p h q -> p (h q)"),
                           start=False, stop=True, R=[nBigIb, notMb], A=[PB[bk_]])

            def zpair(T):
                zs = T["zs"]
                return PSALL[:, zs * 1024:(zs + 1) * 1024], [PB[2 * zs], PB[2 * zs + 1]]

            def bcols(ap):
                return ap.rearrange("p (j h q) -> p j h q", j=2, h=2)[:, :, :, 128:256]

            def bonly(T):
                return T["jl"] - 8 * T["g"] >= 4

            def st_A1a(T):
                zap, zbufs = zpair(T)
                e_ = e_t[T["esl"]]; eb = e_b[T["esl"]]
                if bonly(T):
                    act(bcols(e_[:]), bcols(zap), AF.Exp, R=zbufs, W=[eb])
                else:
                    act(e_[:], zap, AF.Exp, R=zbufs, W=[eb])

            def st_A1b(T):
                e_ = e_t[T["esl"]]; eb = e_b[T["esl"]]
                sp_ = sp_t[T["sl3"]]; spb = sp_b[T["sl3"]]
                if bonly(T):
                    B.op("pool", lambda e: e.memset(sp_[:], 0.0), writes=[spb])
                    act(bcols(sp_[:]), bcols(e_[:]), AF.Ln, R=[eb], A=[spb], bias=1.0)
                else:
                    act(sp_[:], e_[:], AF.Ln, R=[eb], W=[spb], bias=1.0)

            def st_P2(T):
                b0 = 2 * T["zs"]; b1 = b0 + 1
                sp_ = sp_t[T["sl3"]]; spb = sp_b[T["sl3"]]
                first = (T["pi"] == 0)
                rbi = (T["chain"] % 2) * 2 + (T["pi"] % 2)
                mm(PS[b0][:], nTin[:], sp_[:, 0:512], start=False, stop=first, R=[nTinb, spb], A=[PB[b0]], sgc=True)
                if not first:
                    mm(PS[b0][:], nOnes[:], Rb_t[rbi][:], start=False, stop=True, R=[nOnesb, Rb_b[rbi]], A=[PB[b0]],
                       sgc=True)
                mm(PS[b1][:], nTin[:], sp_[:, 512:1024], start=False, stop=False, R=[nTinb, spb], A=[PB[b1]], sgc=True)
                mm(PS[b1][:], nOnes[:], sp_[:, 0:512], start=False, stop=first, R=[nOnesb, spb], A=[PB[b1]], sgc=True)
                if not first:
                    mm(PS[b1][:], nOnes[:], Rb_t[rbi][:], start=False, stop=True, R=[nOnesb, Rb_b[rbi]], A=[PB[b1]],
                       sgc=True)

            def st_G(T):
                if T["last"]:
                    return
                sp_ = sp_t[T["sl3"]]; spb = sp_b[T["sl3"]]
                R_ = R_t[T["rs"]]; Rbuf = R_b[T["rs"]]
                rbn = (T["chain"] % 2) * 2 + ((T["pi"] + 1) % 2)
                if T["pi"] == 0:
                    tt("dve", R_[:], sp_[:, 0:512], sp_[:, 512:1024], ALU.add, R=[spb], W=[Rbuf])
                else:
                    tt("dve", R_[:], R_[:], sp_[:, 0:512], ALU.add, R=[spb, Rbuf], A=[Rbuf])
                    tt("dve", R_[:], R_[:], sp_[:, 512:1024], ALU.add, R=[spb, Rbuf], A=[Rbuf])
                cp("dve", Rb_t[rbn][:], R_[:], R=[Rbuf], W=[Rb_b[rbn]])

            def st_A2(T):
                zap, zbufs = zpair(T)
                a_ = a_t[T["sl3"]]; ab = a_b[T["sl3"]]
                if bonly(T):
                    B.op("pool", lambda e: e.memset(a_[:], 0.0), writes=[ab])
                    act(bcols(a_[:]), bcols(zap), AF.Exp, R=zbufs, A=[ab])
                else:
                    act(a_[:], zap, AF.Exp, R=zbufs, W=[ab])

            def st_P3(T):
                c, g = T["c"], T["g"]
                a_ = a_t[T["sl3"]]; ab = a_b[T["sl3"]]
                ob = T["ob"]
                first = (T["pi"] == 0)
                mm(PS[ob][:], vt[c % 2][:, T["jh"], :], a_[:, 0:512], start=first, stop=False, R=[vtb[c % 2], ab],
                   **({"W": [PB[ob]]} if first else {"A": [PB[ob]]}))
                mm(PS[ob][:], vt[c % 2][:, T["jl"], :], a_[:, 512:1024], start=False, stop=T["last"],
                   R=[vtb[c % 2], ab], A=[PB[ob]])
                if T["last"]:
                    q0 = g * 256
                    for h in range(2):
                        cp("dve", sbo[h * 64:(h + 1) * 64, c, q0:q0 + 256],
                           PS[ob][h * 64:(h + 1) * 64, h * 256:(h + 1) * 256], R=[PB[ob]], A=[sbob])

            p3_load(0)
            n_t = len(pairs)
            for s_i in range(n_t + 2):
                if 0 <= s_i - 2 < n_t:
                    T2 = pairs[s_i - 2]
                    if T2["g"] == 0 and T2["pi"] == 0 and T2["c"] + 1 < 4:
                        p3_load(T2["c"] + 1)
                if s_i < n_t:
                    T = pairs[s_i]
                    st_P1(T)
                    st_A1a(T)
                    st_A1b(T)
                if 0 <= s_i - 1 < n_t:
                    T1 = pairs[s_i - 1]
                    st_P2(T1)
                    st_G(T1)
                    st_A2(T1)
                if 0 <= s_i - 2 < n_t:
                    st_P3(pairs[s_i - 2])
            B.barrier()
            if debug:
                B.dma("sp", dbg["sbo"], sbo[:], sbob, store=True, reads=[sbob])
                B.barrier()
            if stop_after == "P3":
                B.barrier()
                B.emit()
                return nc
            sbo_s = nc.dram_tensor("sbo_s", [128, 4, NQ], BF16).ap()
            sbosb = B.buf("sbo_s")
            B.dma("sp", sbo_s, sbo[:], sbob, store=True, reads=[sbob], writes=[sbosb])
            B.barrier()
        s13.close()

        with contextlib.ExitStack() as s4:
            xn = sbt(s4, "xn", [128, 8, NB, 144], BF16); xnb = B.buf("xn")
            B.dma("sp", xn[:].rearrange("p c b t -> p c (b t)"), xn_s, xnb, reads=[xnsb], writes=[xnb])
            NOUT = 3
            o_t = [sbt(s4, "o_t%d" % i, [128, TO], F32) for i in range(NOUT)]
            o_b = [B.buf("o_t%d" % i) for i in range(NOUT)]
            sg_t = [sbt(s4, "sg_t%d" % i, [128, TO], F32) for i in range(2)]
            sg_b = [B.buf("sg_t%d" % i) for i in range(2)]
            cnt4 = {"o": 0, "sg": 0}

            def gate_mm(pb, wg, wgb, dc, t):
                for k in range(8):
                    mm(PS[pb][:, 0:TO].rearrange("p (b t) -> p b t", b=BPT),
                       wg[:, k, dc * 128:(dc + 1) * 128], xn[:, k, t * BPT:(t + 1) * BPT, 16:144],
                       start=(k == 0), stop=(k == 7), R=[wgb, xnb],
                       **({"W": [PB[pb]]} if k == 0 else {"A": [PB[pb]]}))

            def gated(ypb, gpb):
                si = cnt4["sg"] % 2; cnt4["sg"] += 1
                oi = cnt4["o"] % NOUT; cnt4["o"] += 1
                act(sg_t[si][:], PS[gpb][:, 0:TO], AF.Sigmoid, R=[PB[gpb]], W=[sg_b[si]])
                tt("dve", o_t[oi][:], sg_t[si][:], PS[ypb][:, 0:TO], ALU.mult, R=[sg_b[si], PB[ypb]], W=[o_b[oi]])
                return o_t[oi], o_b[oi]

            c0sb = B.buf("c0_s")
            with contextlib.ExitStack() as sa:
                wpi = sbt(sa, "wpi", [128, 8, 512], BF16); wpib = B.buf("wpi")
                wmix = sbt(sa, "wmix", [128, 4, 128], BF16); wmixb = B.buf("wmix")
                wpo = sbt(sa, "wpo", [128, 4, D], BF16); wpob = B.buf("wpo")
                wg0 = sbt(sa, "wg0", [128, 8, D], BF16); wg0b = B.buf("wg0")
                new_stg(sa)
                load_w(wpi, wpib, w_in[:, 0:512], 8, 512, gcol=V_PRE)
                load_w(wmix, wmixb, w_pmix, 4, 128)
                load_w(wpo, wpob, w_po, 4, D)
                load_w(wg0, wg0b, w_in[:, O_G:O_G + D], 8, D, gcol=V_PRE)
                pa = [sbt(sa, "pa%d" % i, [128, BPT, 144], F32) for i in range(3)]
                pab = [B.buf("pa%d" % i) for i in range(3)]
                icn = sbt(sa, "icn", [128, 4, TO], F32); icnb = B.buf("icn")
                mixd = sbt(sa, "mixd", [128, 4, TO], BF16); mixdb = B.buf("mixd")
                pm = [sbt(sa, "pm%d" % i, [128, 4, TO], BF16) for i in range(2)]
                pmb = [B.buf("pm%d" % i) for i in range(2)]
                tmpa = sbt(sa, "tmpa", [128, BPT, 128], F32); tmpab = B.buf("tmpa")
                HB = TOH // 2
                HBK = BPT // 2

                def pa_pool(t):
                    for gi in range(4):
                        ts("dve", icn[:, gi, :], qpos[:, t * TO:(t + 1) * TO], 1.0, float(2 << gi), ALU.add, ALU.min,
                           R=[qposb], **({"W": [icnb]} if gi == 0 else {"A": [icnb]}))
                    B.op("dve", lambda e: e.reciprocal(out=icn[:], in_=icn[:]), reads=[icnb], writes=[icnb])
                    for gi in range(4):
                        p0, p0b = pa[0], pab[0]
                        for hf in range(2):
                            pb = hf
                            for k in range(8):
                                mm(PS[pb][:, 0:HB].rearrange("p (b t) -> p b t", b=HBK),
                                   wpi[:, k, gi * 128:(gi + 1) * 128],
                                   xn[:, k, t * BPT + hf * HBK:t * BPT + (hf + 1) * HBK, :],
                                   start=(k == 0), stop=(k == 7), R=[wpib, xnb],
                                   **({"W": [PB[pb]]} if k == 0 else {"A": [PB[pb]]}))
                            cp("act", p0[:, hf * HBK:(hf + 1) * HBK, :],
                               PS[pb][:, 0:HB].rearrange("p (b t) -> p b t", b=HBK), R=[PB[pb]],
                               **({"W": [p0b]} if hf == 0 else {"A": [p0b]}))
                        cur, curb = p0, p0b
                        pp = 1
                        dsh = 1
                        lo = 0
                        for step in range(gi + 1):
                            nxt, nxtb = pa[pp], pab[pp]
                            lo = lo + dsh
                            tt("dve" if step % 2 == 0 else "pool", nxt[:, :, lo:144], cur[:, :, lo:144],
                               cur[:, :, lo - dsh:144 - dsh], ALU.add, R=[curb], W=[nxtb])
                            cur, curb = nxt, nxtb
                            pp = 3 - pp
                            dsh *= 2
                        tt("dve", tmpa[:], cur[:, :, 16:144], icn[:, gi, :].rearrange("p (b t) -> p b t", b=BPT),
                           ALU.mult, R=[curb, icnb], W=[tmpab])
                        tt("dve", mixd[:, gi, :].rearrange("p (b t) -> p b t", b=BPT), tmpa[:], p0[:, :, 16:144],
                           ALU.subtract, R=[tmpab, p0b], **({"W": [mixdb]} if gi == 0 else {"A": [mixdb]}))
                        yield
                    for gi in range(4):
                        pb = 2 + (gi % 2)
                        mm(PS[pb][:, 0:TO], wmix[:, gi, :], mixd[:, gi, :], start=True, stop=True,
                           R=[wmixb, mixdb], W=[PB[pb]])
                        ts("dve", pm[t % 2][:, gi, :], PS[pb][:, 0:TO], vec[:, V_PSC + gi:V_PSC + gi + 1], None,
                           ALU.mult, None, R=[PB[pb], vecb], **({"W": [pmb[t % 2]]} if gi == 0 else {"A": [pmb[t % 2]]}))
                    yield

                def pa_proj(t):
                    for dc in range(8):
                        ypb = 4 + (dc % 2)
                        gpb = 6 + (dc % 2)
                        for gi in range(4):
                            mm(PS[ypb][:, 0:TO], wpo[:, gi, dc * 128:(dc + 1) * 128], pm[t % 2][:, gi, :],
                               start=(gi == 0), stop=(gi == 3), R=[wpob, pmb[t % 2]],
                               **({"W": [PB[ypb]]} if gi == 0 else {"A": [PB[ypb]]}))
                        gate_mm(gpb, wg0, wg0b, dc, t)
                        ot, otb = gated(ypb, gpb)
                        B.dma("sp", c0_s[dc * 128:(dc + 1) * 128, t * TO:(t + 1) * TO], ot[:], otb, store=True,
                              reads=[otb], acc=[c0sb])
                        yield

                run(pa_pool(0))
                for t in range(NT):
                    if t + 1 < NT:
                        interleave(pa_proj(t), pa_pool(t + 1))
                    else:
                        run(pa_proj(t))
                B.barrier()

            c2sb = B.buf("c2_s")
            with contextlib.ExitStack() as sc:
                wxq = sbt(sc, "wxq", [128, 8, 256], BF16); wxqb = B.buf("wxq")
                wkv = sbt(sc, "wkv", [128, 8, 512], BF16); wkvb = B.buf("wkv")
                wxo = sbt(sc, "wxo", [128, 2, D], BF16); wxob = B.buf("wxo")
                wg2 = sbt(sc, "wg2", [128, 8, D], BF16); wg2b = B.buf("wg2")
                new_stg(sc)
                load_w(wkv, wkvb, w_kv, 8, 512, gcol=V_MEM)
                load_w(wxq, wxqb, w_in[:, O_XQ:O_XQ + 256], 8, 256, gcol=V_PRE)
                load_w(wxo, wxob, w_xo, 2, D)
                load_w(wg2, wg2b, w_in[:, O_G + 2 * D:O_G + 3 * D], 8, D, gcol=V_PRE)
                m_t = sbt(sc, "m_t", [128, 8, 256], F32); m_b = B.buf("m_t")
                msq = sbt(sc, "msq", [128, 8, 256], BF16); msqb = B.buf("msq")
                mln = sbt(sc, "mln", [128, 256], F32); mlnb = B.buf("mln")
                mrs = sbt(sc, "mrs", [128, 256], F32); mrsb = B.buf("mrs")
                mn = sbt(sc, "mn", [128, 8, 256], BF16); mnb = B.buf("mn")
                mkT = sbt(sc, "mkT", [128, 2, 256], BF16); mkTb = B.buf("mkT")
                mvz = sbt(sc, "mvz", [128, 4, 2, 128], BF16); mvb = B.buf("mvz")
                xqz = sbt(sc, "xqz", [128, 4, TO], BF16); xqTb = B.buf("xqz")
                B.op("pool", lambda e: e.memset(mvz[:].rearrange("p a b c -> p (a b c)"), 0.0), writes=[mvb])
                B.op("pool", lambda e: e.memset(xqz[:].rearrange("p a b -> p (a b)"), 0.0), writes=[xqTb])
                nmx = sbt(sc, "nmx", [128, 4], F32); nmxb = B.buf("nmx")
                ssum = sbt(sc, "ssum", [128, 4], F32); ssumb = B.buf("ssum")
                rsm = sbt(sc, "rsm", [128, 4], F32); rsmb = B.buf("rsm")
                P_t = sbt(sc, "P_t", [128, 4, 256], F32); P_b = B.buf("P_t")
                Pn = sbt(sc, "Pn", [128, 4, 256], BF16); Pnb = B.buf("Pn")
                PT = sbt(sc, "PT", [128, 8, 128], BF16); PTb = B.buf("PT")
                xoT = sbt(sc, "xoT", [128, 2, TO], BF16); xoTb = B.buf("xoT")

                B.dma("sp", m_t[:], memT.rearrange("(c p) n -> p c n", p=128), m_b, writes=[m_b])
                act(msq[:], m_t[:], AF.Square, R=[m_b], W=[msqb])
                for k in range(8):
                    mm(PS[0][:, 0:256], ones[:], msq[:, k, :], start=(k == 0), stop=(k == 7), R=[onesb, msqb],
                       **({"W": [PB[0]]} if k == 0 else {"A": [PB[0]]}))
                rstd_from(PS[0][:, 0:256], PB[0], 256, mln[:], mlnb, mrs[:], mrsb)
                for k in range(8):
                    stt(mn[:, k, :], m_t[:, k, :], vec[:, V_MEM + k:V_MEM + k + 1], mrs[:], R=[m_b, mrsb, vecb],
                        **({"W": [mnb]} if k == 0 else {"A": [mnb]}))
                for ch in range(2):
                    for k in range(8):
                        mm(PS[1][:, 0:256], wkv[:, k, ch * 128:(ch + 1) * 128], mn[:, k, :], start=(k == 0), stop=(k == 7),
                           R=[wkvb, mnb], **({"W": [PB[1]]} if k == 0 else {"A": [PB[1]]}))
                    cp("dve", mkT[:, ch, :], PS[1][:, 0:256], R=[PB[1]], **({"W": [mkTb]} if ch == 0 else {"A": [mkTb]}))
                for mc in range(2):
                    for k in range(8):
                        mm(PS[2][:, 0:256], mn[:, k, mc * 128:(mc + 1) * 128], wkv[:, k, 256:512], start=(k == 0), stop=(k == 7),
                           R=[wkvb, mnb], **({"W": [PB[2]]} if k == 0 else {"A": [PB[2]]}))
                    for h in range(4):
                        cp("dve", mvz[:, h, mc, (h % 2) * 64:(h % 2 + 1) * 64], PS[2][:, h * 64:(h + 1) * 64],
                           R=[PB[2]], A=[mvb])

                for t in range(NT):
                    for ch in range(2):
                        for k in range(8):
                            mm(PS[3][:, 0:TO].rearrange("p (b t) -> p b t", b=BPT),
                               wxq[:, k, ch * 128:(ch + 1) * 128], xn[:, k, t * BPT:(t + 1) * BPT, 16:144],
                               start=(k == 0), stop=(k == 7), R=[wxqb, xnb],
                               **({"W": [PB[3]]} if k == 0 else {"A": [PB[3]]}))
                        for hp in range(2):
                            act(xqz[hp * 64:(hp + 1) * 64, ch * 2 + hp, :], PS[3][hp * 64:(hp + 1) * 64, 0:TO], AF.Copy,
                                R=[PB[3]], A=[xqTb], scale=0.125)
                    for bk in range(BPT):
                        for h in range(4):
                            pb = h // 2
                            hp = h % 2
                            mm(PS[pb][:, hp * 256:(hp + 1) * 256],
                               xqz[:, h, bk * 128:(bk + 1) * 128], mkT[:, h // 2, :],
                               start=True, stop=True, R=[xqTb, mkTb],
                               **({"W": [PB[pb]]} if hp == 0 else {"A": [PB[pb]]}))
                        for pb in range(2):
                            B.op("dve", lambda e, pb=pb: e.reduce_max(
                                out=nmx[:, 2 * pb:2 * pb + 2], in_=PS[pb][:].rearrange("p (h m) -> p h m", h=2),
                                axis=AX.X, negate=True), reads=[PB[pb]],
                                **({"writes": [nmxb]} if pb == 0 else {"acc": [nmxb]}))
                        for h in range(4):
                            pb = h // 2
                            hp = h % 2
                            act(P_t[:, h, :], PS[pb][:, hp * 256:(hp + 1) * 256], AF.Exp, R=[PB[pb], nmxb],
                                **({"W": [P_b, ssumb]} if h == 0 else {"A": [P_b, ssumb]}),
                                bias=nmx[:, h:h + 1], accum_out=ssum[:, h:h + 1])
                        B.op("dve", lambda e: e.reciprocal(out=rsm[:], in_=ssum[:]), reads=[ssumb], writes=[rsmb])
                        for h in range(4):
                            ts("dve", Pn[:, h, :], P_t[:, h, :], rsm[:, h:h + 1], None, ALU.mult, None,
                               R=[P_b, rsmb], **({"W": [Pnb]} if h == 0 else {"A": [Pnb]}))
                        for h in range(4):
                            for mc in range(2):
                                i8 = h * 2 + mc
                                B.op("pe", lambda e, h=h, mc=mc, i8=i8: e.transpose(
                                    PST[:, i8 * 128:(i8 + 1) * 128], Pn[:, h, mc * 128:(mc + 1) * 128], ident[:]),
                                    reads=[Pnb, identb], **({"writes": [PSTB]} if i8 == 0 else {"acc": [PSTB]}))
                        cp("act", PT[:].rearrange("p a b -> p (a b)"), PST[:], R=[PSTB], W=[PTb])
                        for ch in range(2):
                            pb = 2 + ch
                            n4 = 0
                            for h in (2 * ch, 2 * ch + 1):
                                for mc in range(2):
                                    mm(PS[pb][:, bk * 128:(bk + 1) * 128], mvz[:, h, mc, :], PT[:, h * 2 + mc, :],
                                       start=(n4 == 0), stop=(n4 == 3), R=[mvb, PTb],
                                       **({"W": [PB[pb]]} if (bk == 0 and n4 == 0) else {"A": [PB[pb]]}))
                                    n4 += 1
                    for ch in range(2):
                        cp("dve", xoT[:, ch, :], PS[2 + ch][:, 0:TO], R=[PB[2 + ch]],
                           **({"W": [xoTb]} if ch == 0 else {"A": [xoTb]}))
                    for dc in range(8):
                        ypb = 4 + (dc % 2)
                        gpb = 6
                        for ch in range(2):
                            mm(PS[ypb][:, 0:TO], wxo[:, ch, dc * 128:(dc + 1) * 128], xoT[:, ch, :],
                               start=(ch == 0), stop=(ch == 1), R=[wxob, xoTb],
                               **({"W": [PB[ypb]]} if ch == 0 else {"A": [PB[ypb]]}))
                        gate_mm(gpb, wg2, wg2b, dc, t)
                        ot, otb = gated(ypb, gpb)
                        B.dma("pool", c2_s[dc * 128:(dc + 1) * 128, t * TO:(t + 1) * TO], ot[:], otb, store=True,
                              reads=[otb], acc=[c2sb])
                B.barrier()
            if debug:
                with contextlib.ExitStack() as sd:
                    dd = sbt(sd, "dd", [128, 8, NQ], F32); ddb = B.buf("dd")
                    for nm, src, srcb in (("c0", c0_s, c0sb), ("c2", c2_s, c2sb)):
                        B.dma("sp", dd[:], src.rearrange("(c p) n -> p c n", p=128), ddb, reads=[srcb], writes=[ddb])
                        B.dma("sp", dbg[nm].rearrange("(c p) n -> p c n", p=128), dd[:], ddb, store=True, reads=[ddb])
                    B.barrier()

            if stop_after == "P4c":
                B.barrier()
                B.emit()
                return nc
            h1sb = B.buf("h1_s"); n2sb = B.buf("n2_s")
            with contextlib.ExitStack() as sd4:
                wsbo = sbt(sd4, "wsbo", [128, 4, D], BF16); wsbob = B.buf("wsbo")
                wg1 = sbt(sd4, "wg1", [128, 8, D], BF16); wg1b = B.buf("wg1")
                wout = sbt(sd4, "wout", [128, 8, D], BF16); woutb = B.buf("wout")
                with contextlib.ExitStack() as sw:
                    new_stg(sw)
                    load_w(wsbo, wsbob, w_sbo, 4, D)
                    load_w(wg1, wg1b, w_in[:, O_G + D:O_G + 2 * D], 8, D, gcol=V_PRE)
                    load_w(wout, woutb, w_out, 8, D)
                    B.barrier()
                sbo4 = sbt(sd4, "sbo4", [128, 4, NQ], BF16); sbo4b = B.buf("sbo4")
                B.dma("sp", sbo4[:], sbo_s, sbo4b, reads=[sbosb], writes=[sbo4b])
                NCL = 3
                cl = [sbt(sd4, "cl%d" % i, [128, 2, TO], F32) for i in range(NCL)]
                clb = [B.buf("cl%d" % i) for i in range(NCL)]
                clc = {"i": 0}
                mg1 = sbt(sd4, "mg", [128, 8, TO], BF16)
                mg = [mg1, mg1]
                mgb1 = B.buf("mg")
                mgb = [mgb1, mgb1]
                mo = [sbt(sd4, "mo%d" % i, [128, 8, TO], F32) for i in range(2)]
                mob = [B.buf("mo%d" % i) for i in range(2)]
                sqm = [sbt(sd4, "sqm%d" % i, [128, TO], BF16) for i in range(2)]
                sqmb = [B.buf("sqm%d" % i) for i in range(2)]
                lnm = sbt(sd4, "lnm", [128, TO], F32); lnmb = B.buf("lnm")
                rsm1 = sbt(sd4, "rsm1", [128, TO], F32); rsm1b = B.buf("rsm1")
                rsm2 = sbt(sd4, "rsm2", [128, TO], F32); rsm2b = B.buf("rsm2")
                xr = sbt(sd4, "xr", [128, 8, BPT, 128], F32); xrb = B.buf("xr")
                n2t = sbt(sd4, "n2t", [128, 8, TO], BF16); n2tb = B.buf("n2t")
                xoh4 = xoh.rearrange("d (b t) -> d b t", t=144)

                def bd_S1(t):
                    for dc in range(8):
                        ci = clc["i"] % NCL
                        clc["i"] += 1
                        B.dma("sp", cl[ci][:, 0, :], c0_s[dc * 128:(dc + 1) * 128, t * TO:(t + 1) * TO], clb[ci],
                              reads=[c0sb], writes=[clb[ci]])
                        B.dma("sp", cl[ci][:, 1, :], c2_s[dc * 128:(dc + 1) * 128, t * TO:(t + 1) * TO], clb[ci],
                              reads=[c2sb], acc=[clb[ci]])
                        ypb = 4 + (dc % 2)
                        gpb = (dc % 2)
                        for c4 in range(4):
                            mm(PS[ypb][:, 0:TO], wsbo[:, c4, dc * 128:(dc + 1) * 128], sbo4[:, c4, t * TO:(t + 1) * TO],
                               start=(c4 == 0), stop=(c4 == 3), R=[wsbob, sbo4b],
                               **({"W": [PB[ypb]]} if c4 == 0 else {"A": [PB[ypb]]}))
                        gate_mm(gpb, wg1, wg1b, dc, t)
                        ot, otb = gated(ypb, gpb)
                        tt("pool", cl[ci][:, 0, :], cl[ci][:, 0, :], cl[ci][:, 1, :], ALU.add, R=[clb[ci]], A=[clb[ci]])
                        tt("dve", mg[t % 2][:, dc, :], ot[:], cl[ci][:, 0, :], ALU.add, R=[otb, clb[ci]],
                           **({"W": [mgb[t % 2]]} if dc == 0 else {"A": [mgb[t % 2]]}))
                        yield

                def bd_S2(t):
                    for dc in range(8):
                        B.dma("sp", xr[:, dc, :, :], xoh4[dc * 128:(dc + 1) * 128, t * BPT:(t + 1) * BPT, 16:144], xrb,
                              **({"writes": [xrb]} if dc == 0 else {"acc": [xrb]}))
                    m_, mb_ = mo[t % 2], mob[t % 2]
                    for dc in range(8):
                        pb = 2 + (dc % 2)
                        for k in range(8):
                            mm(PS[pb][:, 0:TO], wout[:, k, dc * 128:(dc + 1) * 128], mg[t % 2][:, k, :],
                               start=(k == 0), stop=(k == 7), R=[woutb, mgb[t % 2]],
                               **({"W": [PB[pb]]} if k == 0 else {"A": [PB[pb]]}))
                        if dc > 0:
                            d1 = dc - 1
                            mm(PS[6][:, 0:TO], ones[:], sqm[d1 % 2][:], start=(d1 == 0), stop=False,
                               R=[onesb, sqmb[d1 % 2]], **({"W": [PB[6]]} if d1 == 0 else {"A": [PB[6]]}))
                        cp("dve", m_[:, dc, :], PS[pb][:, 0:TO], R=[PB[pb]], **({"W": [mb_]} if dc == 0 else {"A": [mb_]}))
                        act(sqm[dc % 2][:], m_[:, dc, :], AF.Square, R=[mb_], W=[sqmb[dc % 2]])
                    mm(PS[6][:, 0:TO], ones[:], sqm[7 % 2][:], start=False, stop=True,
                       R=[onesb, sqmb[7 % 2]], A=[PB[6]])

                def bd_S3(t):
                    m_, mb_ = mo[t % 2], mob[t % 2]
                    rstd_from(PS[6][:, 0:TO], PB[6], TO, lnm[:], lnmb, rsm1[:], rsm1b)
                    for dc in range(8):
                        stt(m_[:, dc, :], m_[:, dc, :], vec[:, V_POST + dc:V_POST + dc + 1], rsm1[:],
                            R=[mb_, vecb, rsm1b], A=[mb_])
                        tt("pool", m_[:, dc, :], m_[:, dc, :], xr[:, dc, :, :].rearrange("p b t -> p (b t)"), ALU.add,
                           R=[mb_, xrb], A=[mb_])
                        act(sqm[dc % 2][:], m_[:, dc, :], AF.Square, R=[mb_], W=[sqmb[dc % 2]])
                        mm(PS[7][:, 0:TO], ones[:], sqm[dc % 2][:], start=(dc == 0), stop=(dc == 7),
                           R=[onesb, sqmb[dc % 2]], **({"W": [PB[7]]} if dc == 0 else {"A": [PB[7]]}))
                        yield
                    B.dma("sp", h1_s[:, t * TO:(t + 1) * TO].rearrange("(c p) n -> p c n", p=128), m_[:], mb_, store=True,
                          reads=[mb_], acc=[h1sb])
                    rstd_from(PS[7][:, 0:TO], PB[7], TO, lnm[:], lnmb, rsm2[:], rsm2b)
                    for dc in range(8):
                        stt(n2t[:, dc, :], m_[:, dc, :], vec[:, V_FPRE + dc:V_FPRE + dc + 1], rsm2[:],
                            R=[mb_, rsm2b, vecb], **({"W": [n2tb]} if dc == 0 else {"A": [n2tb]}))
                    B.dma("sp", n2_s[:, t * TO:(t + 1) * TO].rearrange("(c p) n -> p c n", p=128), n2t[:], n2tb, store=True,
                          reads=[n2tb], acc=[n2sb])

                run(bd_S1(0))
                for t in range(NT):
                    bd_S2(t)
                    if t + 1 < NT:
                        interleave(bd_S3(t), bd_S1(t + 1))
                    else:
                        run(bd_S3(t))
                B.barrier()
        if debug:
            with contextlib.ExitStack() as sd:
                dd = sbt(sd, "dd2", [128, 8, NQ], F32); ddb = B.buf("dd2")
                B.dma("sp", dd[:], h1_s.rearrange("(c p) n -> p c n", p=128), ddb, reads=[h1sb], writes=[ddb])
                B.dma("sp", dbg["h1"].rearrange("(c p) n -> p c n", p=128), dd[:], ddb, store=True, reads=[ddb])
                B.barrier()

        if stop_after == "P4":
            B.barrier()
            B.emit()
            return nc
        with contextlib.ExitStack() as s56:
            actT = sbt(s56, "actT", [128, NFC, NQ], BF16); actTb = B.buf("actT")
            wfo = sbt(s56, "wfo", [128, NFC, D], BF16); wfob = B.buf("wfo")
            with contextlib.ExitStack() as s5:
                n2 = sbt(s5, "n2", [128, 8, NQ], BF16); n2b = B.buf("n2")
                new_stg(s5)
                B.dma("sp", n2[:], n2_s.rearrange("(c p) n -> p c n", p=128), n2b, reads=[n2sb], writes=[n2b])
                wfi = [sbt(s5, "wfi%d" % i, [128, 8, 256], BF16) for i in range(2)]
                wfib = [B.buf("wfi%d" % i) for i in range(2)]
                sil = [sbt(s5, "sil%d" % i, [128, TO], F32) for i in range(2)]
                silb = [B.buf("sil%d" % i) for i in range(2)]

                def p5_loadw(f):
                    sl = f % 2
                    load_w(wfi[sl][:, :, 0:128], wfib[sl], w_fi[:, f * 128:(f + 1) * 128], 8, 128, gcol=V_FPRE, first=True)
                    load_w(wfi[sl][:, :, 128:256], wfib[sl], w_fi[:, DFF + f * 128:DFF + (f + 1) * 128], 8, 128,
                           gcol=V_FPRE, first=False)

                p5_loadw(0)
                cnt5 = 0
                for f in range(NFC):
                    if f + 1 < NFC:
                        p5_loadw(f + 1)
                    load_w(wfo[:, f:f + 1, :], wfob, w_fo[f * 128:(f + 1) * 128, :], 1, D, first=(f == 0))
                    sl = f % 2
                    for t in range(NT):
                        gp = (cnt5 % 2) * 2
                        up = gp + 1
                        si = cnt5 % 2
                        cnt5 += 1
                        for k in range(8):
                            mm(PS[gp][:, 0:TO], wfi[sl][:, k, 0:128], n2[:, k, t * TO:(t + 1) * TO],
                               start=(k == 0), stop=(k == 7), R=[wfib[sl], n2b],
                               **({"W": [PB[gp]]} if k == 0 else {"A": [PB[gp]]}))
                        for k in range(8):
                            mm(PS[up][:, 0:TO], wfi[sl][:, k, 128:256], n2[:, k, t * TO:(t + 1) * TO],
                               start=(k == 0), stop=(k == 7), R=[wfib[sl], n2b],
                               **({"W": [PB[up]]} if k == 0 else {"A": [PB[up]]}))
                        act(sil[si][:], PS[gp][:, 0:TO], AF.Silu, R=[PB[gp]], W=[silb[si]])
                        tt("dve", actT[:, f, t * TO:(t + 1) * TO], sil[si][:], PS[up][:, 0:TO], ALU.mult,
                           R=[silb[si], PB[up]], A=[actTb])
                B.barrier()
            with contextlib.ExitStack() as s6:
                ff = [sbt(s6, "ff%d" % i, [128, 8, TO], F32) for i in range(2)]
                ffb = [B.buf("ff%d" % i) for i in range(2)]
                sq6 = [sbt(s6, "sq6_%d" % i, [128, TO], BF16) for i in range(2)]
                sq6b = [B.buf("sq6_%d" % i) for i in range(2)]
                ln6 = sbt(s6, "ln6", [128, TO], F32); ln6b = B.buf("ln6")
                rs6 = sbt(s6, "rs6", [128, TO], F32); rs6b = B.buf("rs6")
                h1t = [sbt(s6, "h1t%d" % i, [128, TO], F32) for i in range(3)]
                h1tb = [B.buf("h1t%d" % i) for i in range(3)]
                outsb = B.buf("outT")
                cnt6 = {"h": 0}

                def p6_M(t):
                    f_, fb_ = ff[t % 2], ffb[t % 2]
                    ssb = 4 + (t % 2)
                    for dc in range(8):
                        pb = dc % 4
                        for f in range(NFC):
                            mm(PS[pb][:, 0:TO], wfo[:, f, dc * 128:(dc + 1) * 128], actT[:, f, t * TO:(t + 1) * TO],
                               start=(f == 0), stop=(f == NFC - 1), R=[wfob, actTb],
                               **({"W": [PB[pb]]} if f == 0 else {"A": [PB[pb]]}))
                        if dc > 0:
                            d1 = dc - 1
                            mm(PS[ssb][:, 0:TO], ones[:], sq6[d1 % 2][:], start=(d1 == 0), stop=False,
                               R=[onesb, sq6b[d1 % 2]], **({"W": [PB[ssb]]} if d1 == 0 else {"A": [PB[ssb]]}))
                        cp("dve", f_[:, dc, :], PS[pb][:, 0:TO], R=[PB[pb]], **({"W": [fb_]} if dc == 0 else {"A": [fb_]}))
                        act(sq6[dc % 2][:], f_[:, dc, :], AF.Square, R=[fb_], W=[sq6b[dc % 2]])
                        yield
                    mm(PS[ssb][:, 0:TO], ones[:], sq6[7 % 2][:], start=False, stop=True,
                       R=[onesb, sq6b[7 % 2]], A=[PB[ssb]])

                def p6_E(t):
                    f_, fb_ = ff[t % 2], ffb[t % 2]
                    ssb = 4 + (t % 2)
                    rstd_from(PS[ssb][:, 0:TO], PB[ssb], TO, ln6[:], ln6b, rs6[:], rs6b)
                    for dc in range(8):
                        hi = cnt6["h"] % 3
                        cnt6["h"] += 1
                        B.dma("sp", h1t[hi][:], h1_s[dc * 128:(dc + 1) * 128, t * TO:(t + 1) * TO], h1tb[hi],
                              reads=[h1sb], writes=[h1tb[hi]])
                        stt(f_[:, dc, :], f_[:, dc, :], vec[:, V_FPOST + dc:V_FPOST + dc + 1], rs6[:],
                            R=[fb_, vecb, rs6b], A=[fb_])
                        tt("pool", f_[:, dc, :], f_[:, dc, :], h1t[hi][:], ALU.add, R=[fb_, h1tb[hi]], A=[fb_])
                        yield
                    B.dma("sp", outT[:, t * TO:(t + 1) * TO].rearrange("(c p) n -> p c n", p=128), f_[:], fb_, store=True,
                          reads=[fb_], acc=[outsb])

                run(p6_M(0))
                for t in range(NT):
                    if t + 1 < NT:
                        interleave(p6_M(t + 1), p6_E(t))
                    else:
                        run(p6_E(t))
                B.barrier()
        B.barrier()
        B.emit()
    return nc


def _own_blocks(r, G):
    blks = []
    for g in range(G):
        blks.append(8 * g + r)
        blks.append(8 * g + 7 - r)
    return blks


def _pack_vecs(norm_mix_pre, norm_mem, norm_mix_post, norm_ffn_pre, norm_ffn_post, pool_scale):
    cols = []
    for v in (norm_mix_pre, norm_mem, norm_mix_post, norm_ffn_pre, norm_ffn_post):
        cols.append(np.asarray(v, np.float32).reshape(8, 128).T)
    cols.append(np.asarray(pool_scale, np.float32).reshape(4, 128).T)
    return np.ascontiguousarray(np.concatenate(cols, axis=1))


_PROG_CACHE = {}


def run_layer(x, mem, norm_mix_pre, w_in, w_pool_mix, pool_scale, w_pool_o, w_sb_o, norm_mem, w_mem_kv, w_x_o,
              w_out, norm_mix_post, norm_ffn_pre, w_ffn_in, w_ffn_out, norm_ffn_post, debug=False, stop_after=None):
    x = np.asarray(x, np.float32)
    mem = np.asarray(mem, np.float32)
    Bn, S, _ = x.shape
    G = S // 1024
    assert Bn == 2 and S == 1024 * G
    key = (G, debug, stop_after)
    if key not in _PROG_CACHE:
        _PROG_CACHE[key] = build_program(G, debug, stop_after)
    nc = _PROG_CACHE[key]
    f32 = lambda a: np.ascontiguousarray(np.asarray(a, np.float32))
    shared = {
        "w_in": f32(w_in[0]), "w_pmix": f32(np.asarray(w_pool_mix[0]).reshape(512, 128)), "w_po": f32(w_pool_o[0]),
        "w_sbo": f32(w_sb_o[0]), "w_kv": f32(w_mem_kv[0]), "w_xo": f32(w_x_o[0]), "w_out": f32(w_out[0]),
        "w_fi": f32(w_ffn_in[0]), "w_fo": f32(w_ffn_out[0]),
        "vecs": _pack_vecs(norm_mix_pre[0], norm_mem[0], norm_mix_post[0], norm_ffn_pre[0], norm_ffn_post[0],
                           pool_scale[0]),
    }
    xTb = [np.ascontiguousarray(x[b].T) for b in range(2)]
    memTb = [np.ascontiguousarray(mem[b].T) for b in range(2)]
    in_maps = []
    for core in range(8):
        b, r = core // 4, core % 4
        blks = _own_blocks(r, G)
        xoh = np.zeros((D, len(blks) * 144), np.float32)
        posv = np.zeros((len(blks) * 128,), np.float32)
        for n, blk in enumerate(blks):
            s0 = blk * 128
            lo = max(0, s0 - 16)
            xoh[:, n * 144 + 16 - (s0 - lo):(n + 1) * 144] = xTb[b][:, lo:s0 + 128]
            posv[n * 128:(n + 1) * 128] = np.arange(s0, s0 + 128, dtype=np.float32)
        m = dict(shared)
        m.update({"xT": xTb[b], "xoh": xoh, "pos": posv, "memT": memTb[b]})
        in_maps.append(m)
    res = run_bass_kernel_spmd(nc, in_maps, core_ids=list(range(8)))
    out = np.empty((2, S, D), np.float32)
    for core in range(8):
        b, r = core // 4, core % 4
        oT = np.asarray(res.results[core]["outT"])
        for n, blk in enumerate(_own_blocks(r, G)):
            out[b, blk * 128:(blk + 1) * 128, :] = oT[:, n * 128:(n + 1) * 128].T
    if debug:
        return out, res.results
    return out


def kernel(**inputs):
    return run_layer(**inputs)
```

```python
import contextlib
import numpy as np
import concourse.bass as bass
import concourse.mybir as mybir
from concourse.bass_utils import run_bass_kernel_spmd

F32 = mybir.dt.float32
BF16 = mybir.dt.bfloat16
I32 = mybir.dt.int32
AF = mybir.ActivationFunctionType
ALU = mybir.AluOpType
AX = mybir.AxisListType

ENGS = ("pe", "act", "dve", "pool", "sp")
SAME_ENG_RAW = True

D = 1024
DFF = 2816
NFC = DFF // 128
INW = 5376
O_Q, O_K, O_V, O_XQ, O_G = 512, 1024, 1536, 2048, 2304
EPS = 1e-6
NEG_BIG = -30000.0
V_PRE, V_MEM, V_POST, V_FPRE, V_FPOST, V_PSC, NVEC = 0, 8, 16, 24, 32, 40, 44


class Buf:
    __slots__ = ("name", "w", "r", "excl", "base")

    def __init__(self, name):
        self.name = name
        self.w = {}
        self.r = {}
        self.base = {}
        self.excl = False


class Builder:
    def __init__(self, nc, es):
        self.nc = nc
        self.es = es
        self.q = {e: [] for e in ENGS}
        self.cnt = {}
        self.seen = {e: {} for e in ENGS}
        self.semh = {}
        self.nbuf = 0
        for e in ENGS:
            if e != "sp":
                self._newsem(e)

    def _newsem(self, key):
        self.semh[key] = self.es.enter_context(self.nc.semaphore("s%d" % len(self.semh)))
        self.cnt[key] = 0

    def buf(self, name=None):
        self.nbuf += 1
        return Buf("%s#%d" % (name or "b", self.nbuf))

    def _wait(self, eng, key, val):
        if self.seen[eng].get(key, 0) >= val:
            return
        self.seen[eng][key] = val
        sem = self.semh[key]
        self.q[eng].append(lambda e, sem=sem, val=val: e.wait_ge(sem, val))

    def _deps(self, eng, reads, writes, acc):
        raw = {}
        oth = {}

        def add(dst, d):
            for k, v in d.items():
                if dst.get(k, 0) < v:
                    dst[k] = v
        for b in reads:
            add(raw, b.w)
            if b.excl:
                add(oth, b.r)
        for b in writes:
            add(oth, b.w)
            add(oth, b.r)
        for b in acc:
            add(oth, b.r)
            add(oth, b.base)
        for k, v in raw.items():
            if k == eng and (eng == "pe" or not SAME_ENG_RAW):
                continue
            self._wait(eng, k, v)
        for k, v in oth.items():
            if k == eng:
                continue
            self._wait(eng, k, v)

    def _record(self, key, val, reads, writes, acc):
        for b in reads:
            if b.r.get(key, 0) < val:
                b.r[key] = val
        for b in writes:
            b.w = {key: val}
            b.base = {key: val}
            b.r = {}
        for b in acc:
            b.w[key] = val
            b.r = {}

    def op(self, eng, fn, reads=(), writes=(), acc=()):
        self._deps(eng, reads, writes, acc)
        self.cnt[eng] += 1
        c = self.cnt[eng]
        sem = self.semh[eng]
        self.q[eng].append(lambda e, fn=fn, sem=sem: fn(e).then_inc(sem, 1))
        self._record(eng, c, reads, writes, acc)

    def dma(self, qeng, out_ap, in_ap, sb, store=False, reads=(), writes=(), acc=(), **kw):
        self._deps(qeng, reads, writes, acc)
        key = ("st" if store else "ld", sb.name, "sw" if qeng == "pool" else "hw")
        if key not in self.semh:
            self._newsem(key)
        self.cnt[key] += 16
        c = self.cnt[key]
        sem = self.semh[key]
        self.q[qeng].append(
            lambda e, o=out_ap, i=in_ap, sem=sem, kw=kw: e.dma_start(out=o, in_=i, **kw).then_inc(sem, 16))
        self._record(key, c, reads, writes, acc)

    def barrier(self):
        for e in ENGS:
            for k, c in self.cnt.items():
                if k == e or c == 0:
                    continue
                self._wait(e, k, c)

    def emit(self):
        q = self.q
        with self.nc.Block() as block:
            @block.tensor
            def _(e):
                for t in q["pe"]:
                    t(e)

            @block.scalar
            def _(e):
                for t in q["act"]:
                    t(e)

            @block.vector
            def _(e):
                for t in q["dve"]:
                    t(e)

            @block.gpsimd
            def _(e):
                for t in q["pool"]:
                    t(e)

            @block.sync
            def _(e):
                for t in q["sp"]:
                    t(e)


def build_program(G, debug=False, stop_after=None):
    S = 1024 * G
    NBLK = S // 128
    NB = 2 * G
    NQ = 128 * NB
    BPT = 4 if G >= 2 else 2
    TO = 128 * BPT
    TOH = 144 * BPT
    NT = NB // BPT
    ST = S // 512

    nc = bass.Bass("TRN2", target_bir_lowering=False)

    def din(name, shape, dt=F32):
        return nc.dram_tensor(name, list(shape), dt, kind="ExternalInput").ap()

    xT = din("xT", [D, S])
    xoh = din("xoh", [D, NB * 144])
    pos = din("pos", [NQ])
    memT = din("memT", [D, 256])
    w_in = din("w_in", [D, INW])
    w_pmix = din("w_pmix", [512, 128])
    w_po = din("w_po", [512, D])
    w_sbo = din("w_sbo", [512, D])
    w_kv = din("w_kv", [D, 512])
    w_xo = din("w_xo", [256, D])
    w_out = din("w_out", [D, D])
    w_fi = din("w_fi", [D, 2 * DFF])
    w_fo = din("w_fo", [DFF, D])
    vecs = din("vecs", [128, NVEC])
    outT = nc.dram_tensor("outT", [D, NQ], F32, kind="ExternalOutput").ap()
    kT_s = nc.dram_tensor("kT_s", [4, 128, S], BF16).ap()
    v_s = nc.dram_tensor("v_s", [4, 128, NBLK, 128], BF16).ap()
    c0_s = nc.dram_tensor("c0_s", [D, NQ], F32).ap()
    c2_s = nc.dram_tensor("c2_s", [D, NQ], F32).ap()
    h1_s = nc.dram_tensor("h1_s", [D, NQ], F32).ap()
    n2_s = nc.dram_tensor("n2_s", [D, NQ], BF16).ap()
    dbg = {}
    if debug:
        dbg["qT"] = nc.dram_tensor("dbg_qT", [128, 4, 2 * NQ], BF16, kind="ExternalOutput").ap()
        dbg["sbo"] = nc.dram_tensor("dbg_sbo", [128, 4, NQ], BF16, kind="ExternalOutput").ap()
        dbg["kT"] = nc.dram_tensor("dbg_kT", [4, 128, S], BF16, kind="ExternalOutput").ap()
        dbg["v"] = nc.dram_tensor("dbg_v", [4, 128, NBLK, 128], BF16, kind="ExternalOutput").ap()
        dbg["c0"] = nc.dram_tensor("dbg_c0", [D, NQ], F32, kind="ExternalOutput").ap()
        dbg["c2"] = nc.dram_tensor("dbg_c2", [D, NQ], F32, kind="ExternalOutput").ap()
        dbg["h1"] = nc.dram_tensor("dbg_h1", [D, NQ], F32, kind="ExternalOutput").ap()

    with contextlib.ExitStack() as es:
        B = Builder(nc, es)

        def sbt(scope, name, shape, dt):
            return scope.enter_context(nc.sbuf_tensor(name, list(shape), dt))

        def mm(out, lhsT, rhs, start, stop, R, W=(), A=(), sgc=False):
            B.op("pe", lambda e: e.matmul(out, lhsT, rhs, start=start, stop=stop, skip_group_check=sgc),
                 reads=R, writes=W, acc=A)

        def act(out, in_, func, R, W=(), A=(), **kw):
            B.op("act", lambda e: e.activation(out=out, in_=in_, func=func, **kw), reads=R, writes=W, acc=A)

        def tt(eng, out, in0, in1, op, R, W=(), A=()):
            B.op(eng, lambda e: e.tensor_tensor(out=out, in0=in0, in1=in1, op=op), reads=R, writes=W, acc=A)

        def ts(eng, out, in0, s1, s2, op0, op1, R, W=(), A=()):
            if op1 is None:
                B.op(eng, lambda e: e.tensor_scalar(out=out, in0=in0, scalar1=s1, scalar2=s2, op0=op0),
                     reads=R, writes=W, acc=A)
            else:
                B.op(eng, lambda e: e.tensor_scalar(out=out, in0=in0, scalar1=s1, scalar2=s2, op0=op0, op1=op1),
                     reads=R, writes=W, acc=A)

        def cp(eng, out, in_, R, W=(), A=()):
            if eng == "act":
                act(out, in_, AF.Copy, R, W, A)
            else:
                B.op(eng, lambda e: e.tensor_copy(out=out, in_=in_), reads=R, writes=W, acc=A)

        PSALL = es.enter_context(nc.psum_tensor("psall", [128, 4096], F32))
        PS = [PSALL[:, i * 512:(i + 1) * 512] for i in range(8)]
        PB = [B.buf("ps%d" % i) for i in range(8)]
        for b_ in PB:
            b_.excl = True
        PST = PS[7].bitcast(BF16)
        PSTB = PB[7]

        gs = es
        vec = sbt(gs, "vec", [128, NVEC], F32); vecb = B.buf("vec")
        ones = sbt(gs, "ones", [128, 128], BF16); onesb = B.buf("ones")
        nTin = sbt(gs, "nTin", [128, 128], BF16); nTinb = B.buf("nTin")
        nOnes = sbt(gs, "nOnes", [128, 128], BF16); nOnesb = B.buf("nOnes")
        nBigI = sbt(gs, "nBigI", [128, 128], BF16); nBigIb = B.buf("nBigI")
        ident = sbt(gs, "ident", [128, 128], BF16); identb = B.buf("ident")
        dif_i = sbt(gs, "dif_i", [128, 128], I32); difib = B.buf("dif_i")
        dif_f = sbt(gs, "dif_f", [128, 128], F32); diffb = B.buf("dif_f")
        kp8_i = sbt(gs, "kp8_i", [128, 8], I32); kp8ib = B.buf("kp8i")
        kp8 = sbt(gs, "kp8", [128, 8], F32); kp8b = B.buf("kp8")
        qpos = sbt(gs, "qpos", [128, NQ], F32); qposb = B.buf("qpos")
        notM = sbt(gs, "notM", [128, 8, 2, 256], BF16); notMb = B.buf("notM")
        NSTG = 2
        st_state = {"i": 0, "e": 0, "n": 0}

        def new_stg(scope):
            st_state["n"] += 1
            st_state["stg"] = [sbt(scope, "stg%d_%d" % (st_state["n"], i), [128, 1536], F32) for i in range(NSTG)]
            st_state["stgb"] = [B.buf("stg%d" % i) for i in range(NSTG)]

        B.dma("sp", vec[:], vecs, vecb, writes=[vecb])
        B.dma("sp", qpos[:], pos.partition_broadcast(128), qposb, writes=[qposb])
        B.op("dve", lambda e: e.memset(ones[:], 1.0), writes=[onesb])
        B.op("dve", lambda e: e.memset(nOnes[:], -1.0), writes=[nOnesb])
        B.op("pool", lambda e: e.iota(dif_i[:], pattern=[[-1, 128]], base=0, channel_multiplier=1), writes=[difib])
        cp("dve", dif_f[:], dif_i[:], R=[difib], W=[diffb])
        ts("dve", nTin[:], dif_f[:], 0.0, -1.0, ALU.is_ge, ALU.mult, R=[diffb], W=[nTinb])
        ts("dve", nBigI[:], dif_f[:], 0.0, NEG_BIG, ALU.is_equal, ALU.mult, R=[diffb], W=[nBigIb])
        ts("dve", ident[:], dif_f[:], 0.0, None, ALU.is_equal, None, R=[diffb], W=[identb])
        B.op("pool", lambda e: e.iota(kp8_i[:], pattern=[[128, 8]], base=0, channel_multiplier=1), writes=[kp8ib])
        cp("dve", kp8[:], kp8_i[:], R=[kp8ib], W=[kp8b])
        for jj in range(8):
            for h in range(2):
                ts("dve", notM[:, jj, h, :], qpos[:, 0:256], kp8[:, jj:jj + 1], None, ALU.is_le, None,
                   R=[qposb, kp8b], **({"W": [notMb]} if (jj == 0 and h == 0) else {"A": [notMb]}))

        def load_w(dst, dstb, src, kc, ncols, gcol=None, first=True):
            kg = max(1, min(kc, 1536 // ncols))
            k0 = 0
            while k0 < kc:
                kn = min(kg, kc - k0)
                i = st_state["i"] % NSTG
                st_state["i"] += 1
                stg, stgb = st_state["stg"], st_state["stgb"]
                sv = stg[i][:, 0:kn * ncols].rearrange("p (k n) -> p k n", k=kn)
                B.dma("sp", sv, src[k0 * 128:(k0 + kn) * 128, :].rearrange("(k p) n -> p k n", p=128), stgb[i],
                      writes=[stgb[i]])
                eng = "dve" if st_state["e"] % 2 == 0 else "act"
                st_state["e"] += 1
                kw = {"W": [dstb]} if (first and k0 == 0) else {"A": [dstb]}
                cp(eng, dst[:, k0:k0 + kn, :], sv, R=[stgb[i]], **kw)
                k0 += kn

        def run(gen):
            for _ in gen:
                pass

        def interleave(*gens):
            gens = list(gens)
            while gens:
                for g_ in list(gens):
                    try:
                        next(g_)
                    except StopIteration:
                        gens.remove(g_)

        def stt(out, in0, scal, in1, R, W=(), A=()):
            B.op("dve", lambda e: e.scalar_tensor_tensor(out=out, in0=in0, scalar=scal, in1=in1,
                                                         op0=ALU.mult, op1=ALU.mult), reads=R, writes=W, acc=A)

        def rstd_from(ss_ps, ss_b, ncols, ln_t, ln_b, out_t, out_b, out_is_write=True):
            act(ln_t, ss_ps, AF.Ln, R=[ss_b], W=[ln_b], scale=1.0 / D, bias=EPS)
            act(out_t, ln_t, AF.Exp, R=[ln_b], W=[out_b], scale=-0.5)

        s13 = contextlib.ExitStack()
        es.enter_context(s13)
        NGT = BPT // 2
        Qz = sbt(s13, "Qz", [128, 4, G, 2, 256], BF16); Qzb = B.buf("Qz")
        B.op("pool", lambda e: e.memset(Qz[:].rearrange("p a g h q -> p (a g h q)"), 0.0), writes=[Qzb])
        wqkv = sbt(s13, "wqkv", [128, 8, 1536], BF16); wqkvb = B.buf("wqkv")

        xn_s = nc.dram_tensor("xn_s", [128, 8, NB * 144], BF16).ap()
        xnsb = B.buf("xn_s")
        with contextlib.ExitStack() as s1:
            new_stg(s1)
            load_w(wqkv, wqkvb, w_in[:, O_Q:O_Q + 1536], 8, 1536)
            xo_t = [sbt(s1, "xo_t%d" % i, [128, 8, TOH], F32) for i in range(2)]
            xo_b = [B.buf("xo_t%d" % i) for i in range(2)]
            sq1 = sbt(s1, "sq1", [128, 8, TOH], BF16); sq1b = B.buf("sq1")
            ln1 = sbt(s1, "ln1", [128, TOH], F32); ln1b = B.buf("ln1")
            rs1 = sbt(s1, "rs1", [128, TOH], F32); rs1b = B.buf("rs1")
            xn1 = [sbt(s1, "xn1_%d" % i, [128, 8, BPT, 144], BF16) for i in range(2)]
            xn1b = [B.buf("xn1_%d" % i) for i in range(2)]
            HB = TOH // 2
            def p1_norm(t):
                sl = t % 2
                B.dma("sp", xo_t[sl][:], xoh[:, t * TOH:(t + 1) * TOH].rearrange("(c p) n -> p c n", p=128),
                      xo_b[sl], writes=[xo_b[sl]])
                act(sq1[:], xo_t[sl][:], AF.Square, R=[xo_b[sl]], W=[sq1b])
                for hf in range(2):
                    for k in range(8):
                        mm(PS[hf][:, 0:HB], ones[:], sq1[:, k, hf * HB:(hf + 1) * HB], start=(k == 0), stop=(k == 7),
                           R=[onesb, sq1b], **({"W": [PB[hf]]} if k == 0 else {"A": [PB[hf]]}))
                for hf in range(2):
                    act(ln1[:, hf * HB:(hf + 1) * HB], PS[hf][:, 0:HB], AF.Ln, R=[PB[hf]],
                        **({"W": [ln1b]} if hf == 0 else {"A": [ln1b]}), scale=1.0 / D, bias=EPS)
                act(rs1[:], ln1[:], AF.Exp, R=[ln1b], W=[rs1b], scale=-0.5)
                xnv = xn1[sl][:].rearrange("p c b t -> p c (b t)")
                for k in range(8):
                    stt(xnv[:, k, :], xo_t[sl][:, k, :], vec[:, V_PRE + k:V_PRE + k + 1], rs1[:],
                        R=[xo_b[sl], rs1b, vecb], **({"W": [xn1b[sl]]} if k == 0 else {"A": [xn1b[sl]]}))
                B.dma("pool", xn_s[:, :, t * TOH:(t + 1) * TOH], xnv, xn1b[sl], store=True,
                      reads=[xn1b[sl]], acc=[xnsb])

            def p1_q(t):
                sl = t % 2
                for c4 in range(4):
                    pb = 2 + (c4 % 2)
                    for k in range(8):
                        mm(PS[pb][:, 0:TO].rearrange("p (b t) -> p b t", b=BPT),
                           wqkv[:, k, c4 * 128:(c4 + 1) * 128], xn1[sl][:, k, :, 16:144],
                           start=(k == 0), stop=(k == 7), R=[wqkvb, xn1b[sl]],
                           **({"W": [PB[pb]]} if k == 0 else {"A": [PB[pb]]}))
                    for h in range(2):
                        act(Qz[h * 64:(h + 1) * 64, c4, t * NGT:(t + 1) * NGT, h, :],
                            PS[pb][h * 64:(h + 1) * 64, 0:TO].rearrange("p (g q) -> p g q", g=NGT), AF.Copy,
                            R=[PB[pb]], A=[Qzb], scale=0.125)

            p1_norm(0)
            for t in range(NT):
                if t + 1 < NT:
                    p1_norm(t + 1)
                p1_q(t)
        B.barrier()
        if debug:
            B.dma("sp", dbg["qT"], Qz[:].rearrange("p a g h q -> p a (g h q)"), Qzb, store=True, reads=[Qzb])
        if stop_after == "P1":
            B.barrier()
            B.emit()
            return nc

        kTsb = B.buf("kT_s"); vsb = B.buf("v_s")
        with contextlib.ExitStack() as s2:
            x_t = [sbt(s2, "x_t%d" % i, [128, 8, 512], F32) for i in range(3)]
            x_b = [B.buf("x_t%d" % i) for i in range(3)]
            sq2 = [sbt(s2, "sq2_%d" % i, [128, 8, 512], BF16) for i in range(2)]
            sq2b = [B.buf("sq2_%d" % i) for i in range(2)]
            ln2 = sbt(s2, "ln2", [128, 512], F32); ln2b = B.buf("ln2")
            rs2 = sbt(s2, "rs2", [128, 512], F32); rs2b = B.buf("rs2")
            xn2 = [sbt(s2, "xn2_%d" % i, [128, 8, 512], BF16) for i in range(2)]
            xn2b = [B.buf("xn2_%d" % i) for i in range(2)]
            kst = [sbt(s2, "kst%d" % i, [128, 4, 512], BF16) for i in range(2)]
            kstb = [B.buf("kst%d" % i) for i in range(2)]
            vst = [sbt(s2, "vst%d" % i, [128, 4, 512], BF16) for i in range(2)]
            vstb = [B.buf("vst%d" % i) for i in range(2)]

            def p2_load(t):
                B.dma("sp", x_t[t % 3][:], xT[:, t * 512:(t + 1) * 512].rearrange("(c p) n -> p c n", p=128),
                      x_b[t % 3], writes=[x_b[t % 3]])

            def p2_sq(t):
                sl = t % 2
                act(sq2[sl][:], x_t[t % 3][:], AF.Square, R=[x_b[t % 3]], W=[sq2b[sl]])

            def p2_norm(t):
                sl = t % 2
                for k in range(8):
                    mm(PS[0][:], ones[:], sq2[sl][:, k, :], start=(k == 0), stop=(k == 7), R=[onesb, sq2b[sl]],
                       **({"W": [PB[0]]} if k == 0 else {"A": [PB[0]]}))
                rstd_from(PS[0][:], PB[0], 512, ln2[:], ln2b, rs2[:], rs2b)
                for k in range(8):
                    stt(xn2[sl][:, k, :], x_t[t % 3][:, k, :], vec[:, V_PRE + k:V_PRE + k + 1], rs2[:],
                        R=[x_b[t % 3], rs2b, vecb], **({"W": [xn2b[sl]]} if k == 0 else {"A": [xn2b[sl]]}))

            def p2_kv(t):
                sl = t % 2
                for c4 in range(4):
                    pb = 1 + (c4 % 3)
                    for k in range(8):
                        mm(PS[pb][:], wqkv[:, k, 512 + c4 * 128:512 + (c4 + 1) * 128], xn2[sl][:, k, :],
                           start=(k == 0), stop=(k == 7), R=[wqkvb, xn2b[sl]],
                           **({"W": [PB[pb]]} if k == 0 else {"A": [PB[pb]]}))
                    cp("dve", kst[sl][:, c4, :], PS[pb][:], R=[PB[pb]],
                       **({"W": [kstb[sl]]} if c4 == 0 else {"A": [kstb[sl]]}))
                B.dma("pool", kT_s[:, :, t * 512:(t + 1) * 512].rearrange("c p s -> p c s"), kst[sl][:], kstb[sl],
                      store=True, reads=[kstb[sl]], acc=[kTsb])
                for bk in range(4):
                    pb = 4 + (bk % 3)
                    for k in range(8):
                        mm(PS[pb][:], xn2[sl][:, k, bk * 128:(bk + 1) * 128], wqkv[:, k, 1024:1536],
                           start=(k == 0), stop=(k == 7), R=[wqkvb, xn2b[sl]],
                           **({"W": [PB[pb]]} if k == 0 else {"A": [PB[pb]]}))
                    cp("dve", vst[sl][:, bk, :], PS[pb][:], R=[PB[pb]],
                       **({"W": [vstb[sl]]} if bk == 0 else {"A": [vstb[sl]]}))
                for c4 in range(4):
                    B.dma("pool", v_s[c4, :, 4 * t:4 * t + 4, :], vst[sl][:, :, c4 * 128:(c4 + 1) * 128], vstb[sl],
                          store=True, reads=[vstb[sl]], acc=[vsb])

            p2_load(0)
            p2_sq(0)
            p2_norm(0)
            if ST > 1:
                p2_load(1)
                p2_sq(1)
            for t in range(ST):
                if t + 2 < ST:
                    p2_load(t + 2)
                    p2_sq(t + 2)
                if t + 1 < ST:
                    p2_norm(t + 1)
                p2_kv(t)
        B.barrier()
        if debug:
            with contextlib.ExitStack() as sd:
                dk = sbt(sd, "dk", [128, S], BF16); dkb = B.buf("dk")
                dv = sbt(sd, "dv", [128, NBLK, 128], BF16); dvb = B.buf("dv")
                for c4 in range(4):
                    B.dma("sp", dk[:], kT_s[c4], dkb, reads=[kTsb], writes=[dkb])
                    B.dma("sp", dbg["kT"][c4], dk[:], dkb, store=True, reads=[dkb])
                    B.dma("sp", dv[:], v_s[c4], dvb, reads=[vsb], writes=[dvb])
                    B.dma("sp", dbg["v"][c4], dv[:], dvb, store=True, reads=[dvb])
                B.barrier()

        if stop_after == "P2":
            B.barrier()
            B.emit()
            return nc
        s34 = contextlib.ExitStack()
        with contextlib.ExitStack() as s3:
            sbo = sbt(s3, "sbo", [128, 4, NQ], BF16); sbob = B.buf("sbo")
            kt = [sbt(s3, "kt%d" % i, [128, S], BF16) for i in range(2)]
            ktb = [B.buf("kt%d" % i) for i in range(2)]
            vt = [sbt(s3, "vt%d" % i, [128, NBLK, 128], BF16) for i in range(2)]
            vtb = [B.buf("vt%d" % i) for i in range(2)]
            NE = 2
            e_t = [sbt(s3, "e_t%d" % i, [128, 1024], F32) for i in range(NE)]
            e_b = [B.buf("e_t%d" % i) for i in range(NE)]
            NSP = 3
            sp_t = [sbt(s3, "sp_t%d" % i, [128, 1024], BF16) for i in range(NSP)]
            sp_b = [B.buf("sp_t%d" % i) for i in range(NSP)]
            a_t = [sbt(s3, "a_t%d" % i, [128, 1024], BF16) for i in range(NSP)]
            a_b = [B.buf("a_t%d" % i) for i in range(NSP)]
            R_t = [sbt(s3, "R_t%d" % i, [128, 512], F32) for i in range(2)]
            R_b = [B.buf("R_t%d" % i) for i in range(2)]
            Rb_t = [sbt(s3, "Rb_t%d" % i, [128, 512], BF16) for i in range(4)]
            Rb_b = [B.buf("Rb_t%d" % i) for i in range(4)]
            NZP = 3

            def p3_load(c):
                B.dma("sp", kt[c % 2][:], kT_s[c], ktb[c % 2], reads=[kTsb], writes=[ktb[c % 2]])
                B.dma("sp", vt[c % 2][:], v_s[c], vtb[c % 2], reads=[vsb], writes=[vtb[c % 2]])

            pairs = []
            chain_id = 0
            for c in range(4):
                for g in range(G):
                    nj = 8 * g + 8
                    for pi in range(nj // 2):
                        jh = nj - 1 - 2 * pi
                        pairs.append(dict(c=c, g=g, jh=jh, jl=jh - 1, pi=pi, last=(jh == 1), chain=chain_id))
                    chain_id += 1
            for s_i, T in enumerate(pairs):
                T["zs"] = s_i % NZP
                T["sl3"] = s_i % NSP
                T["esl"] = s_i % NE
                T["ob"] = 6 + (T["chain"] % 2)
                T["rs"] = T["chain"] % 2

            def st_P1(T):
                c, g = T["c"], T["g"]
                qv = Qz[:, c, g, :, :].rearrange("p h q -> p (h q)")
                for i, j in enumerate((T["jh"], T["jl"])):
                    bk_ = 2 * T["zs"] + i
                    jj = j - 8 * g
                    mm(PS[bk_][:], kt[c % 2][:, j * 128:(j + 1) * 128], qv, start=True, stop=(jj < 0),
                       R=[ktb[c % 2], Qzb], W=[PB[bk_]])
                    if jj >= 0:
                        mm(PS[bk_][:], nBigI[:], notM[:, jj, :, :].rearrange("p h q -> p (h q)"),
                           start=False, stop=True, R=[nBigIb, notMb], A=[PB[bk_]])

            def zpair(T):
                zs = T["zs"]
                return PSALL[:, zs * 1024:(zs + 1) * 1024], [PB[2 * zs], PB[2 * zs + 1]]

            def bcols(ap):
                return ap.rearrange("p (j h q) -> p j h q", j=2, h=2)[:, :, :, 128:256]

            def bonly(T):
                return T["jl"] - 8 * T["g"] >= 4

            def st_A1a(T):
                zap, zbufs = zpair(T)
                e_ = e_t[T["esl"]]; eb = e_b[T["esl"]]
                if bonly(T):
                    act(bcols(e_[:]), bcols(zap), AF.Exp, R=zbufs, W=[eb])
                else:
                    act(e_[:], zap, AF.Exp, R=zbufs, W=[eb])

            def st_A1b(T):
                e_ = e_t[T["esl"]]; eb = e_b[T["esl"]]
                sp_ = sp_t[T["sl3"]]; spb = sp_b[T["sl3"]]
                if bonly(T):
                    B.op("pool", lambda e: e.memset(sp_[:], 0.0), writes=[spb])
                    act(bcols(sp_[:]), bcols(e_[:]), AF.Ln, R=[eb], A=[spb], bias=1.0)
                else:
                    act(sp_[:], e_[:], AF.Ln, R=[eb], W=[spb], bias=1.0)

            def st_P2(T):
                b0 = 2 * T["zs"]; b1 = b0 + 1
                sp_ = sp_t[T["sl3"]]; spb = sp_b[T["sl3"]]
                first = (T["pi"] == 0)
                rbi = (T["chain"] % 2) * 2 + (T["pi"] % 2)
                mm(PS[b0][:], nTin[:], sp_[:, 0:512], start=False, stop=first, R=[nTinb, spb], A=[PB[b0]], sgc=True)
                if not first:
                    mm(PS[b0][:], nOnes[:], Rb_t[rbi][:], start=False, stop=True, R=[nOnesb, Rb_b[rbi]], A=[PB[b0]],
                       sgc=True)
                mm(PS[b1][:], nTin[:], sp_[:, 512:1024], start=False, stop=False, R=[nTinb, spb], A=[PB[b1]], sgc=True)
                mm(PS[b1][:], nOnes[:], sp_[:, 0:512], start=False, stop=first, R=[nOnesb, spb], A=[PB[b1]], sgc=True)
                if not first:
                    mm(PS[b1][:], nOnes[:], Rb_t[rbi][:], start=False, stop=True, R=[nOnesb, Rb_b[rbi]], A=[PB[b1]],
                       sgc=True)

            def st_G(T):
                if T["last"]:
                    return
                sp_ = sp_t[T["sl3"]]; spb = sp_b[T["sl3"]]
                R_ = R_t[T["rs"]]; Rbuf = R_b[T["rs"]]
                rbn = (T["chain"] % 2) * 2 + ((T["pi"] + 1) % 2)
                if T["pi"] == 0:
                    tt("dve", R_[:], sp_[:, 0:512], sp_[:, 512:1024], ALU.add, R=[spb], W=[Rbuf])
                else:
                    tt("dve", R_[:], R_[:], sp_[:, 0:512], ALU.add, R=[spb, Rbuf], A=[Rbuf])
                    tt("dve", R_[:], R_[:], sp_[:, 512:1024], ALU.add, R=[spb, Rbuf], A=[Rbuf])
                cp("dve", Rb_t[rbn][:], R_[:], R=[Rbuf], W=[Rb_b[rbn]])

            def st_A2(T):
                zap, zbufs = zpair(T)
                a_ = a_t[T["sl3"]]; ab = a_b[T["sl3"]]
                if bonly(T):
                    B.op("pool", lambda e: e.memset(a_[:], 0.0), writes=[ab])
                    act(bcols(a_[:]), bcols(zap), AF.Exp, R=zbufs, A=[ab])
                else:
                    act(a_[:], zap, AF.Exp, R=zbufs, W=[ab])

            def st_P3(T):
                c, g = T["c"], T["g"]
                a_ = a_t[T["sl3"]]; ab = a_b[T["sl3"]]
                ob = T["ob"]
                first = (T["pi"] == 0)
                mm(PS[ob][:], vt[c % 2][:, T["jh"], :], a_[:, 0:512], start=first, stop=False, R=[vtb[c % 2], ab],
                   **({"W": [PB[ob]]} if first else {"A": [PB[ob]]}))
                mm(PS[ob][:], vt[c % 2][:, T["jl"], :], a_[:, 512:1024], start=False, stop=T["last"],
                   R=[vtb[c % 2], ab], A=[PB[ob]])
                if T["last"]:
                    q0 = g * 256
                    for h in range(2):
                        cp("dve", sbo[h * 64:(h + 1) * 64, c, q0:q0 + 256],
                           PS[ob][h * 64:(h + 1) * 64, h * 256:(h + 1) * 256], R=[PB[ob]], A=[sbob])

            p3_load(0)
            n_t = len(pairs)
            for s_i in range(n_t + 2):
                if 0 <= s_i - 2 < n_t:
                    T2 = pairs[s_i - 2]
                    if T2["g"] == 0 and T2["pi"] == 0 and T2["c"] + 1 < 4:
                        p3_load(T2["c"] + 1)
                if s_i < n_t:
                    T = pairs[s_i]
                    st_P1(T)
                    st_A1a(T)
                    st_A1b(T)
                if 0 <= s_i - 1 < n_t:
                    T1 = pairs[s_i - 1]
                    st_P2(T1)
                    st_G(T1)
                    st_A2(T1)
                if 0 <= s_i - 2 < n_t:
                    st_P3(pairs[s_i - 2])
            B.barrier()
            if debug:
                B.dma("sp", dbg["sbo"], sbo[:], sbob, store=True, reads=[sbob])
                B.barrier()
            if stop_after == "P3":
                B.barrier()
                B.emit()
                return nc
            sbo_s = nc.dram_tensor("sbo_s", [128, 4, NQ], BF16).ap()
            sbosb = B.buf("sbo_s")
            B.dma("sp", sbo_s, sbo[:], sbob, store=True, reads=[sbob], writes=[sbosb])
            B.barrier()
        s13.close()

        with contextlib.ExitStack() as s4:
            xn = sbt(s4, "xn", [128, 8, NB, 144], BF16); xnb = B.buf("xn")
            B.dma("sp", xn[:].rearrange("p c b t -> p c (b t)"), xn_s, xnb, reads=[xnsb], writes=[xnb])
            NOUT = 3
            o_t = [sbt(s4, "o_t%d" % i, [128, TO], F32) for i in range(NOUT)]
            o_b = [B.buf("o_t%d" % i) for i in range(NOUT)]
            sg_t = [sbt(s4, "sg_t%d" % i, [128, TO], F32) for i in range(2)]
            sg_b = [B.buf("sg_t%d" % i) for i in range(2)]
            cnt4 = {"o": 0, "sg": 0}

            def gate_mm(pb, wg, wgb, dc, t):
                for k in range(8):
                    mm(PS[pb][:, 0:TO].rearrange("p (b t) -> p b t", b=BPT),
                       wg[:, k, dc * 128:(dc + 1) * 128], xn[:, k, t * BPT:(t + 1) * BPT, 16:144],
                       start=(k == 0), stop=(k == 7), R=[wgb, xnb],
                       **({"W": [PB[pb]]} if k == 0 else {"A": [PB[pb]]}))

            def gated(ypb, gpb):
                si = cnt4["sg"] % 2; cnt4["sg"] += 1
                oi = cnt4["o"] % NOUT; cnt4["o"] += 1
                act(sg_t[si][:], PS[gpb][:, 0:TO], AF.Sigmoid, R=[PB[gpb]], W=[sg_b[si]])
                tt("dve", o_t[oi][:], sg_t[si][:], PS[ypb][:, 0:TO], ALU.mult, R=[sg_b[si], PB[ypb]], W=[o_b[oi]])
                return o_t[oi], o_b[oi]

            c0sb = B.buf("c0_s")
            with contextlib.ExitStack() as sa:
                wpi = sbt(sa, "wpi", [128, 8, 512], BF16); wpib = B.buf("wpi")
                wmix = sbt(sa, "wmix", [128, 4, 128], BF16); wmixb = B.buf("wmix")
                wpo = sbt(sa, "wpo", [128, 4, D], BF16); wpob = B.buf("wpo")
                wg0 = sbt(sa, "wg0", [128, 8, D], BF16); wg0b = B.buf("wg0")
                new_stg(sa)
                load_w(wpi, wpib, w_in[:, 0:512], 8, 512, gcol=V_PRE)
                load_w(wmix, wmixb, w_pmix, 4, 128)
                load_w(wpo, wpob, w_po, 4, D)
                load_w(wg0, wg0b, w_in[:, O_G:O_G + D], 8, D, gcol=V_PRE)
                pa = [sbt(sa, "pa%d" % i, [128, BPT, 144], F32) for i in range(3)]
                pab = [B.buf("pa%d" % i) for i in range(3)]
                icn = sbt(sa, "icn", [128, 4, TO], F32); icnb = B.buf("icn")
                mixd = sbt(sa, "mixd", [128, 4, TO], BF16); mixdb = B.buf("mixd")
                pm = [sbt(sa, "pm%d" % i, [128, 4, TO], BF16) for i in range(2)]
                pmb = [B.buf("pm%d" % i) for i in range(2)]
                tmpa = sbt(sa, "tmpa", [128, BPT, 128], F32); tmpab = B.buf("tmpa")
                HB = TOH // 2
                HBK = BPT // 2

                def pa_pool(t):
                    for gi in range(4):
                        ts("dve", icn[:, gi, :], qpos[:, t * TO:(t + 1) * TO], 1.0, float(2 << gi), ALU.add, ALU.min,
                           R=[qposb], **({"W": [icnb]} if gi == 0 else {"A": [icnb]}))
                    B.op("dve", lambda e: e.reciprocal(out=icn[:], in_=icn[:]), reads=[icnb], writes=[icnb])
                    for gi in range(4):
                        p0, p0b = pa[0], pab[0]
                        for hf in range(2):
                            pb = hf
                            for k in range(8):
                                mm(PS[pb][:, 0:HB].rearrange("p (b t) -> p b t", b=HBK),
                                   wpi[:, k, gi * 128:(gi + 1) * 128],
                                   xn[:, k, t * BPT + hf * HBK:t * BPT + (hf + 1) * HBK, :],
                                   start=(k == 0), stop=(k == 7), R=[wpib, xnb],
                                   **({"W": [PB[pb]]} if k == 0 else {"A": [PB[pb]]}))
                            cp("act", p0[:, hf * HBK:(hf + 1) * HBK, :],
                               PS[pb][:, 0:HB].rearrange("p (b t) -> p b t", b=HBK), R=[PB[pb]],
                               **({"W": [p0b]} if hf == 0 else {"A": [p0b]}))
                        cur, curb = p0, p0b
                        pp = 1
                        dsh = 1
                        lo = 0
                        for step in range(gi + 1):
                            nxt, nxtb = pa[pp], pab[pp]
                            lo = lo + dsh
                            tt("dve" if step % 2 == 0 else "pool", nxt[:, :, lo:144], cur[:, :, lo:144],
                               cur[:, :, lo - dsh:144 - dsh], ALU.add, R=[curb], W=[nxtb])
                            cur, curb = nxt, nxtb
                            pp = 3 - pp
                            dsh *= 2
                        tt("dve", tmpa[:], cur[:, :, 16:144], icn[:, gi, :].rearrange("p (b t) -> p b t", b=BPT),
                           ALU.mult, R=[curb, icnb], W=[tmpab])
                        tt("dve", mixd[:, gi, :].rearrange("p (b t) -> p b t", b=BPT), tmpa[:], p0[:, :, 16:144],
                           ALU.subtract, R=[tmpab, p0b], **({"W": [mixdb]} if gi == 0 else {"A": [mixdb]}))
                        yield
                    for gi in range(4):
                        pb = 2 + (gi % 2)
                        mm(PS[pb][:, 0:TO], wmix[:, gi, :], mixd[:, gi, :], start=True, stop=True,
                           R=[wmixb, mixdb], W=[PB[pb]])
                        ts("dve", pm[t % 2][:, gi, :], PS[pb][:, 0:TO], vec[:, V_PSC + gi:V_PSC + gi + 1], None,
                           ALU.mult, None, R=[PB[pb], vecb], **({"W": [pmb[t % 2]]} if gi == 0 else {"A": [pmb[t % 2]]}))
                    yield

                def pa_proj(t):
                    for dc in range(8):
                        ypb = 4 + (dc % 2)
                        gpb = 6 + (dc % 2)
                        for gi in range(4):
                            mm(PS[ypb][:, 0:TO], wpo[:, gi, dc * 128:(dc + 1) * 128], pm[t % 2][:, gi, :],
                               start=(gi == 0), stop=(gi == 3), R=[wpob, pmb[t % 2]],
                               **({"W": [PB[ypb]]} if gi == 0 else {"A": [PB[ypb]]}))
                        gate_mm(gpb, wg0, wg0b, dc, t)
                        ot, otb = gated(ypb, gpb)
                        B.dma("sp", c0_s[dc * 128:(dc + 1) * 128, t * TO:(t + 1) * TO], ot[:], otb, store=True,
                              reads=[otb], acc=[c0sb])
                        yield

                run(pa_pool(0))
                for t in range(NT):
                    if t + 1 < NT:
                        interleave(pa_proj(t), pa_pool(t + 1))
                    else:
                        run(pa_proj(t))
                B.barrier()

            c2sb = B.buf("c2_s")
            with contextlib.ExitStack() as sc:
                wxq = sbt(sc, "wxq", [128, 8, 256], BF16); wxqb = B.buf("wxq")
                wkv = sbt(sc, "wkv", [128, 8, 512], BF16); wkvb = B.buf("wkv")
                wxo = sbt(sc, "wxo", [128, 2, D], BF16); wxob = B.buf("wxo")
                wg2 = sbt(sc, "wg2", [128, 8, D], BF16); wg2b = B.buf("wg2")
                new_stg(sc)
                load_w(wkv, wkvb, w_kv, 8, 512, gcol=V_MEM)
                load_w(wxq, wxqb, w_in[:, O_XQ:O_XQ + 256], 8, 256, gcol=V_PRE)
                load_w(wxo, wxob, w_xo, 2, D)
                load_w(wg2, wg2b, w_in[:, O_G + 2 * D:O_G + 3 * D], 8, D, gcol=V_PRE)
                m_t = sbt(sc, "m_t", [128, 8, 256], F32); m_b = B.buf("m_t")
                msq = sbt(sc, "msq", [128, 8, 256], BF16); msqb = B.buf("msq")
                mln = sbt(sc, "mln", [128, 256], F32); mlnb = B.buf("mln")
                mrs = sbt(sc, "mrs", [128, 256], F32); mrsb = B.buf("mrs")
                mn = sbt(sc, "mn", [128, 8, 256], BF16); mnb = B.buf("mn")
                mkT = sbt(sc, "mkT", [128, 2, 256], BF16); mkTb = B.buf("mkT")
                mvz = sbt(sc, "mvz", [128, 4, 2, 128], BF16); mvb = B.buf("mvz")
                xqz = sbt(sc, "xqz", [128, 4, TO], BF16); xqTb = B.buf("xqz")
                B.op("pool", lambda e: e.memset(mvz[:].rearrange("p a b c -> p (a b c)"), 0.0), writes=[mvb])
                B.op("pool", lambda e: e.memset(xqz[:].rearrange("p a b -> p (a b)"), 0.0), writes=[xqTb])
                nmx2 = [sbt(sc, "nmx%d" % i, [128, 4], F32) for i in range(2)]; nmxb2 = [B.buf("nmx") for i in range(2)]
                ssum2 = [sbt(sc, "ssum%d" % i, [128, 4], F32) for i in range(2)]; ssumb2 = [B.buf("ssum") for i in range(2)]
                rsm2_ = [sbt(sc, "rsx%d" % i, [128, 4], F32) for i in range(2)]; rsmb2 = [B.buf("rsm") for i in range(2)]
                P_t2 = [sbt(sc, "P_t%d" % i, [128, 4, 256], F32) for i in range(2)]; P_b2 = [B.buf("P_t") for i in range(2)]
                Pn2 = [sbt(sc, "Pn%d" % i, [128, 4, 256], BF16) for i in range(2)]; Pnb2 = [B.buf("Pn") for i in range(2)]
                PT2 = [sbt(sc, "PT%d" % i, [128, 8, 128], BF16) for i in range(2)]; PTb2 = [B.buf("PT") for i in range(2)]
                xoT = sbt(sc, "xoT", [128, 2, TO], BF16); xoTb = B.buf("xoT")

                B.dma("sp", m_t[:], memT.rearrange("(c p) n -> p c n", p=128), m_b, writes=[m_b])
                act(msq[:], m_t[:], AF.Square, R=[m_b], W=[msqb])
                for k in range(8):
                    mm(PS[0][:, 0:256], ones[:], msq[:, k, :], start=(k == 0), stop=(k == 7), R=[onesb, msqb],
                       **({"W": [PB[0]]} if k == 0 else {"A": [PB[0]]}))
                rstd_from(PS[0][:, 0:256], PB[0], 256, mln[:], mlnb, mrs[:], mrsb)
                for k in range(8):
                    stt(mn[:, k, :], m_t[:, k, :], vec[:, V_MEM + k:V_MEM + k + 1], mrs[:], R=[m_b, mrsb, vecb],
                        **({"W": [mnb]} if k == 0 else {"A": [mnb]}))
                for ch in range(2):
                    for k in range(8):
                        mm(PS[1][:, 0:256], wkv[:, k, ch * 128:(ch + 1) * 128], mn[:, k, :], start=(k == 0), stop=(k == 7),
                           R=[wkvb, mnb], **({"W": [PB[1]]} if k == 0 else {"A": [PB[1]]}))
                    cp("dve", mkT[:, ch, :], PS[1][:, 0:256], R=[PB[1]], **({"W": [mkTb]} if ch == 0 else {"A": [mkTb]}))
                for mc in range(2):
                    for k in range(8):
                        mm(PS[2][:, 0:256], mn[:, k, mc * 128:(mc + 1) * 128], wkv[:, k, 256:512], start=(k == 0), stop=(k == 7),
                           R=[wkvb, mnb], **({"W": [PB[2]]} if k == 0 else {"A": [PB[2]]}))
                    for h in range(4):
                        cp("dve", mvz[:, h, mc, (h % 2) * 64:(h % 2 + 1) * 64], PS[2][:, h * 64:(h + 1) * 64],
                           R=[PB[2]], A=[mvb])

                for t in range(NT):
                    for ch in range(2):
                        for k in range(8):
                            mm(PS[3][:, 0:TO].rearrange("p (b t) -> p b t", b=BPT),
                               wxq[:, k, ch * 128:(ch + 1) * 128], xn[:, k, t * BPT:(t + 1) * BPT, 16:144],
                               start=(k == 0), stop=(k == 7), R=[wxqb, xnb],
                               **({"W": [PB[3]]} if k == 0 else {"A": [PB[3]]}))
                        for hp in range(2):
                            act(xqz[hp * 64:(hp + 1) * 64, ch * 2 + hp, :], PS[3][hp * 64:(hp + 1) * 64, 0:TO], AF.Copy,
                                R=[PB[3]], A=[xqTb], scale=0.125)
                    for bk in range(BPT):
                        nmx, nmxb, ssum, ssumb = nmx2[bk % 2], nmxb2[bk % 2], ssum2[bk % 2], ssumb2[bk % 2]
                        rsm, rsmb, P_t, P_b = rsm2_[bk % 2], rsmb2[bk % 2], P_t2[bk % 2], P_b2[bk % 2]
                        Pn, Pnb, PT, PTb = Pn2[bk % 2], Pnb2[bk % 2], PT2[bk % 2], PTb2[bk % 2]
                        sb0 = 0 if bk % 2 == 0 else 4
                        for h in range(4):
                            pb = sb0 + h // 2
                            hp = h % 2
                            mm(PS[pb][:, hp * 256:(hp + 1) * 256],
                               xqz[:, h, bk * 128:(bk + 1) * 128], mkT[:, h // 2, :],
                               start=True, stop=True, R=[xqTb, mkTb],
                               **({"W": [PB[pb]]} if hp == 0 else {"A": [PB[pb]]}))
                        for pq in range(2):
                            B.op("dve", lambda e, pq=pq, nmx=nmx, sb0=sb0: e.reduce_max(
                                out=nmx[:, 2 * pq:2 * pq + 2], in_=PS[sb0 + pq][:].rearrange("p (h m) -> p h m", h=2),
                                axis=AX.X, negate=True), reads=[PB[sb0 + pq]],
                                **({"writes": [nmxb]} if pq == 0 else {"acc": [nmxb]}))
                        for h in range(4):
                            pb = sb0 + h // 2
                            hp = h % 2
                            act(P_t[:, h, :], PS[pb][:, hp * 256:(hp + 1) * 256], AF.Exp, R=[PB[pb], nmxb],
                                **({"W": [P_b, ssumb]} if h == 0 else {"A": [P_b, ssumb]}),
                                bias=nmx[:, h:h + 1], accum_out=ssum[:, h:h + 1])
                        B.op("dve", lambda e, rsm=rsm, ssum=ssum: e.reciprocal(out=rsm[:], in_=ssum[:]),
                             reads=[ssumb], writes=[rsmb])
                        for h in range(4):
                            ts("dve", Pn[:, h, :], P_t[:, h, :], rsm[:, h:h + 1], None, ALU.mult, None,
                               R=[P_b, rsmb], **({"W": [Pnb]} if h == 0 else {"A": [Pnb]}))
                        for h in range(4):
                            for mc in range(2):
                                i8 = h * 2 + mc
                                B.op("pe", lambda e, h=h, mc=mc, i8=i8, Pn=Pn: e.transpose(
                                    PST[:, i8 * 128:(i8 + 1) * 128], Pn[:, h, mc * 128:(mc + 1) * 128], ident[:]),
                                    reads=[Pnb, identb], **({"writes": [PSTB]} if i8 == 0 else {"acc": [PSTB]}))
                        cp("act", PT[:].rearrange("p a b -> p (a b)"), PST[:], R=[PSTB], W=[PTb])
                        for ch in range(2):
                            pb = 2 + ch
                            n4 = 0
                            for h in (2 * ch, 2 * ch + 1):
                                for mc in range(2):
                                    mm(PS[pb][:, bk * 128:(bk + 1) * 128], mvz[:, h, mc, :], PT[:, h * 2 + mc, :],
                                       start=(n4 == 0), stop=(n4 == 3), R=[mvb, PTb],
                                       **({"W": [PB[pb]]} if (bk == 0 and n4 == 0) else {"A": [PB[pb]]}))
                                    n4 += 1
                    for ch in range(2):
                        cp("dve", xoT[:, ch, :], PS[2 + ch][:, 0:TO], R=[PB[2 + ch]],
                           **({"W": [xoTb]} if ch == 0 else {"A": [xoTb]}))
                    for dc in range(8):
                        ypb = 4 + (dc % 2)
                        gpb = 6
                        for ch in range(2):
                            mm(PS[ypb][:, 0:TO], wxo[:, ch, dc * 128:(dc + 1) * 128], xoT[:, ch, :],
                               start=(ch == 0), stop=(ch == 1), R=[wxob, xoTb],
                               **({"W": [PB[ypb]]} if ch == 0 else {"A": [PB[ypb]]}))
                        gate_mm(gpb, wg2, wg2b, dc, t)
                        ot, otb = gated(ypb, gpb)
                        B.dma("pool", c2_s[dc * 128:(dc + 1) * 128, t * TO:(t + 1) * TO], ot[:], otb, store=True,
                              reads=[otb], acc=[c2sb])
                B.barrier()
            if debug:
                with contextlib.ExitStack() as sd:
                    dd = sbt(sd, "dd", [128, 8, NQ], F32); ddb = B.buf("dd")
                    for nm, src, srcb in (("c0", c0_s, c0sb), ("c2", c2_s, c2sb)):
                        B.dma("sp", dd[:], src.rearrange("(c p) n -> p c n", p=128), ddb, reads=[srcb], writes=[ddb])
                        B.dma("sp", dbg[nm].rearrange("(c p) n -> p c n", p=128), dd[:], ddb, store=True, reads=[ddb])
                    B.barrier()

            if stop_after == "P4c":
                B.barrier()
                B.emit()
                return nc
            h1sb = B.buf("h1_s"); n2sb = B.buf("n2_s")
            with contextlib.ExitStack() as sd4:
                wsbo = sbt(sd4, "wsbo", [128, 4, D], BF16); wsbob = B.buf("wsbo")
                wg1 = sbt(sd4, "wg1", [128, 8, D], BF16); wg1b = B.buf("wg1")
                wout = sbt(sd4, "wout", [128, 8, D], BF16); woutb = B.buf("wout")
                with contextlib.ExitStack() as sw:
                    new_stg(sw)
                    load_w(wsbo, wsbob, w_sbo, 4, D)
                    load_w(wg1, wg1b, w_in[:, O_G + D:O_G + 2 * D], 8, D, gcol=V_PRE)
                    load_w(wout, woutb, w_out, 8, D)
                    B.barrier()
                sbo4 = sbt(sd4, "sbo4", [128, 4, NQ], BF16); sbo4b = B.buf("sbo4")
                B.dma("sp", sbo4[:], sbo_s, sbo4b, reads=[sbosb], writes=[sbo4b])
                NCL = 3
                cl = [sbt(sd4, "cl%d" % i, [128, 2, TO], F32) for i in range(NCL)]
                clb = [B.buf("cl%d" % i) for i in range(NCL)]
                clc = {"i": 0}
                mg1 = sbt(sd4, "mg", [128, 8, TO], BF16)
                mg = [mg1, mg1]
                mgb1 = B.buf("mg")
                mgb = [mgb1, mgb1]
                mo = [sbt(sd4, "mo%d" % i, [128, 8, TO], F32) for i in range(2)]
                mob = [B.buf("mo%d" % i) for i in range(2)]
                sqm = [sbt(sd4, "sqm%d" % i, [128, TO], BF16) for i in range(2)]
                sqmb = [B.buf("sqm%d" % i) for i in range(2)]
                lnm = sbt(sd4, "lnm", [128, TO], F32); lnmb = B.buf("lnm")
                rsm1 = sbt(sd4, "rsm1", [128, TO], F32); rsm1b = B.buf("rsm1")
                rsm2 = sbt(sd4, "rsm2", [128, TO], F32); rsm2b = B.buf("rsm2")
                xr = sbt(sd4, "xr", [128, 8, BPT, 128], F32); xrb = B.buf("xr")
                n2t = sbt(sd4, "n2t", [128, 8, TO], BF16); n2tb = B.buf("n2t")
                xoh4 = xoh.rearrange("d (b t) -> d b t", t=144)

                def bd_S1(t):
                    for dc in range(8):
                        ci = clc["i"] % NCL
                        clc["i"] += 1
                        B.dma("sp", cl[ci][:, 0, :], c0_s[dc * 128:(dc + 1) * 128, t * TO:(t + 1) * TO], clb[ci],
                              reads=[c0sb], writes=[clb[ci]])
                        B.dma("sp", cl[ci][:, 1, :], c2_s[dc * 128:(dc + 1) * 128, t * TO:(t + 1) * TO], clb[ci],
                              reads=[c2sb], acc=[clb[ci]])
                        ypb = 4 + (dc % 2)
                        gpb = (dc % 2)
                        for c4 in range(4):
                            mm(PS[ypb][:, 0:TO], wsbo[:, c4, dc * 128:(dc + 1) * 128], sbo4[:, c4, t * TO:(t + 1) * TO],
                               start=(c4 == 0), stop=(c4 == 3), R=[wsbob, sbo4b],
                               **({"W": [PB[ypb]]} if c4 == 0 else {"A": [PB[ypb]]}))
                        gate_mm(gpb, wg1, wg1b, dc, t)
                        ot, otb = gated(ypb, gpb)
                        tt("pool", cl[ci][:, 0, :], cl[ci][:, 0, :], cl[ci][:, 1, :], ALU.add, R=[clb[ci]], A=[clb[ci]])
                        tt("dve", mg[t % 2][:, dc, :], ot[:], cl[ci][:, 0, :], ALU.add, R=[otb, clb[ci]],
                           **({"W": [mgb[t % 2]]} if dc == 0 else {"A": [mgb[t % 2]]}))
                        yield

                def bd_S2(t):
                    for dc in range(8):
                        B.dma("sp", xr[:, dc, :, :], xoh4[dc * 128:(dc + 1) * 128, t * BPT:(t + 1) * BPT, 16:144], xrb,
                              **({"writes": [xrb]} if dc == 0 else {"acc": [xrb]}))
                    m_, mb_ = mo[t % 2], mob[t % 2]
                    for dc in range(8):
                        pb = 2 + (dc % 2)
                        for k in range(8):
                            mm(PS[pb][:, 0:TO], wout[:, k, dc * 128:(dc + 1) * 128], mg[t % 2][:, k, :],
                               start=(k == 0), stop=(k == 7), R=[woutb, mgb[t % 2]],
                               **({"W": [PB[pb]]} if k == 0 else {"A": [PB[pb]]}))
                        if dc > 0:
                            d1 = dc - 1
                            mm(PS[6][:, 0:TO], ones[:], sqm[d1 % 2][:], start=(d1 == 0), stop=False,
                               R=[onesb, sqmb[d1 % 2]], **({"W": [PB[6]]} if d1 == 0 else {"A": [PB[6]]}))
                        cp("dve", m_[:, dc, :], PS[pb][:, 0:TO], R=[PB[pb]], **({"W": [mb_]} if dc == 0 else {"A": [mb_]}))
                        act(sqm[dc % 2][:], m_[:, dc, :], AF.Square, R=[mb_], W=[sqmb[dc % 2]])
                    mm(PS[6][:, 0:TO], ones[:], sqm[7 % 2][:], start=False, stop=True,
                       R=[onesb, sqmb[7 % 2]], A=[PB[6]])

                def bd_S3(t):
                    m_, mb_ = mo[t % 2], mob[t % 2]
                    rstd_from(PS[6][:, 0:TO], PB[6], TO, lnm[:], lnmb, rsm1[:], rsm1b)
                    for dc in range(8):
                        stt(m_[:, dc, :], m_[:, dc, :], vec[:, V_POST + dc:V_POST + dc + 1], rsm1[:],
                            R=[mb_, vecb, rsm1b], A=[mb_])
                        tt("pool", m_[:, dc, :], m_[:, dc, :], xr[:, dc, :, :].rearrange("p b t -> p (b t)"), ALU.add,
                           R=[mb_, xrb], A=[mb_])
                        act(sqm[dc % 2][:], m_[:, dc, :], AF.Square, R=[mb_], W=[sqmb[dc % 2]])
                        mm(PS[7][:, 0:TO], ones[:], sqm[dc % 2][:], start=(dc == 0), stop=(dc == 7),
                           R=[onesb, sqmb[dc % 2]], **({"W": [PB[7]]} if dc == 0 else {"A": [PB[7]]}))
                        yield
                    B.dma("sp", h1_s[:, t * TO:(t + 1) * TO].rearrange("(c p) n -> p c n", p=128), m_[:], mb_, store=True,
                          reads=[mb_], acc=[h1sb])
                    rstd_from(PS[7][:, 0:TO], PB[7], TO, lnm[:], lnmb, rsm2[:], rsm2b)
                    for dc in range(8):
                        stt(n2t[:, dc, :], m_[:, dc, :], vec[:, V_FPRE + dc:V_FPRE + dc + 1], rsm2[:],
                            R=[mb_, rsm2b, vecb], **({"W": [n2tb]} if dc == 0 else {"A": [n2tb]}))
                    B.dma("sp", n2_s[:, t * TO:(t + 1) * TO].rearrange("(c p) n -> p c n", p=128), n2t[:], n2tb, store=True,
                          reads=[n2tb], acc=[n2sb])

                run(bd_S1(0))
                for t in range(NT):
                    bd_S2(t)
                    if t + 1 < NT:
                        interleave(bd_S3(t), bd_S1(t + 1))
                    else:
                        run(bd_S3(t))
                B.barrier()
        if debug:
            with contextlib.ExitStack() as sd:
                dd = sbt(sd, "dd2", [128, 8, NQ], F32); ddb = B.buf("dd2")
                B.dma("sp", dd[:], h1_s.rearrange("(c p) n -> p c n", p=128), ddb, reads=[h1sb], writes=[ddb])
                B.dma("sp", dbg["h1"].rearrange("(c p) n -> p c n", p=128), dd[:], ddb, store=True, reads=[ddb])
                B.barrier()

        if stop_after == "P4":
            B.barrier()
            B.emit()
            return nc
        with contextlib.ExitStack() as s56:
            actT = sbt(s56, "actT", [128, NFC, NQ], BF16); actTb = B.buf("actT")
            wfo = sbt(s56, "wfo", [128, NFC, D], BF16); wfob = B.buf("wfo")
            with contextlib.ExitStack() as s5:
                n2 = sbt(s5, "n2", [128, 8, NQ], BF16); n2b = B.buf("n2")
                new_stg(s5)
                B.dma("sp", n2[:], n2_s.rearrange("(c p) n -> p c n", p=128), n2b, reads=[n2sb], writes=[n2b])
                wfi = [sbt(s5, "wfi%d" % i, [128, 8, 256], BF16) for i in range(2)]
                wfib = [B.buf("wfi%d" % i) for i in range(2)]
                sil = [sbt(s5, "sil%d" % i, [128, TO], F32) for i in range(2)]
                silb = [B.buf("sil%d" % i) for i in range(2)]

                def p5_loadw(f):
                    sl = f % 2
                    load_w(wfi[sl][:, :, 0:128], wfib[sl], w_fi[:, f * 128:(f + 1) * 128], 8, 128, gcol=V_FPRE, first=True)
                    load_w(wfi[sl][:, :, 128:256], wfib[sl], w_fi[:, DFF + f * 128:DFF + (f + 1) * 128], 8, 128,
                           gcol=V_FPRE, first=False)

                p5_loadw(0)
                cnt5 = 0
                for f in range(NFC):
                    if f + 1 < NFC:
                        p5_loadw(f + 1)
                    load_w(wfo[:, f:f + 1, :], wfob, w_fo[f * 128:(f + 1) * 128, :], 1, D, first=(f == 0))
                    sl = f % 2
                    for t in range(NT):
                        gp = (cnt5 % 2) * 2
                        up = gp + 1
                        si = cnt5 % 2
                        cnt5 += 1
                        for k in range(8):
                            mm(PS[gp][:, 0:TO], wfi[sl][:, k, 0:128], n2[:, k, t * TO:(t + 1) * TO],
                               start=(k == 0), stop=(k == 7), R=[wfib[sl], n2b],
                               **({"W": [PB[gp]]} if k == 0 else {"A": [PB[gp]]}))
                        for k in range(8):
                            mm(PS[up][:, 0:TO], wfi[sl][:, k, 128:256], n2[:, k, t * TO:(t + 1) * TO],
                               start=(k == 0), stop=(k == 7), R=[wfib[sl], n2b],
                               **({"W": [PB[up]]} if k == 0 else {"A": [PB[up]]}))
                        act(sil[si][:], PS[gp][:, 0:TO], AF.Silu, R=[PB[gp]], W=[silb[si]])
                        tt("dve", actT[:, f, t * TO:(t + 1) * TO], sil[si][:], PS[up][:, 0:TO], ALU.mult,
                           R=[silb[si], PB[up]], A=[actTb])
                B.barrier()
            with contextlib.ExitStack() as s6:
                ff = [sbt(s6, "ff%d" % i, [128, 8, TO], F32) for i in range(2)]
                ffb = [B.buf("ff%d" % i) for i in range(2)]
                sq6 = [sbt(s6, "sq6_%d" % i, [128, TO], BF16) for i in range(2)]
                sq6b = [B.buf("sq6_%d" % i) for i in range(2)]
                ln6 = sbt(s6, "ln6", [128, TO], F32); ln6b = B.buf("ln6")
                rs6 = sbt(s6, "rs6", [128, TO], F32); rs6b = B.buf("rs6")
                h1t = [sbt(s6, "h1t%d" % i, [128, TO], F32) for i in range(3)]
                h1tb = [B.buf("h1t%d" % i) for i in range(3)]
                outsb = B.buf("outT")
                cnt6 = {"h": 0}

                def p6_M(t):
                    f_, fb_ = ff[t % 2], ffb[t % 2]
                    ssb = 4 + (t % 2)
                    for dc in range(8):
                        pb = dc % 4
                        for f in range(NFC):
                            mm(PS[pb][:, 0:TO], wfo[:, f, dc * 128:(dc + 1) * 128], actT[:, f, t * TO:(t + 1) * TO],
                               start=(f == 0), stop=(f == NFC - 1), R=[wfob, actTb],
                               **({"W": [PB[pb]]} if f == 0 else {"A": [PB[pb]]}))
                        if dc > 0:
                            d1 = dc - 1
                            mm(PS[ssb][:, 0:TO], ones[:], sq6[d1 % 2][:], start=(d1 == 0), stop=False,
                               R=[onesb, sq6b[d1 % 2]], **({"W": [PB[ssb]]} if d1 == 0 else {"A": [PB[ssb]]}))
                        cp("dve", f_[:, dc, :], PS[pb][:, 0:TO], R=[PB[pb]], **({"W": [fb_]} if dc == 0 else {"A": [fb_]}))
                        act(sq6[dc % 2][:], f_[:, dc, :], AF.Square, R=[fb_], W=[sq6b[dc % 2]])
                        yield
                    mm(PS[ssb][:, 0:TO], ones[:], sq6[7 % 2][:], start=False, stop=True,
                       R=[onesb, sq6b[7 % 2]], A=[PB[ssb]])

                def p6_E(t):
                    f_, fb_ = ff[t % 2], ffb[t % 2]
                    ssb = 4 + (t % 2)
                    rstd_from(PS[ssb][:, 0:TO], PB[ssb], TO, ln6[:], ln6b, rs6[:], rs6b)
                    for dc in range(8):
                        hi = cnt6["h"] % 3
                        cnt6["h"] += 1
                        B.dma("sp", h1t[hi][:], h1_s[dc * 128:(dc + 1) * 128, t * TO:(t + 1) * TO], h1tb[hi],
                              reads=[h1sb], writes=[h1tb[hi]])
                        stt(f_[:, dc, :], f_[:, dc, :], vec[:, V_FPOST + dc:V_FPOST + dc + 1], rs6[:],
                            R=[fb_, vecb, rs6b], A=[fb_])
                        tt("pool", f_[:, dc, :], f_[:, dc, :], h1t[hi][:], ALU.add, R=[fb_, h1tb[hi]], A=[fb_])
                        yield
                    B.dma("sp", outT[:, t * TO:(t + 1) * TO].rearrange("(c p) n -> p c n", p=128), f_[:], fb_, store=True,
                          reads=[fb_], acc=[outsb])

                run(p6_M(0))
                for t in range(NT):
                    if t + 1 < NT:
                        interleave(p6_M(t + 1), p6_E(t))
                    else:
                        run(p6_E(t))
                B.barrier()
        B.barrier()
        B.emit()
    return nc


def _own_blocks(r, G):
    blks = []
    for g in range(G):
        blks.append(8 * g + r)
        blks.append(8 * g + 7 - r)
    return blks


def _pack_vecs(norm_mix_pre, norm_mem, norm_mix_post, norm_ffn_pre, norm_ffn_post, pool_scale):
    cols = []
    for v in (norm_mix_pre, norm_mem, norm_mix_post, norm_ffn_pre, norm_ffn_post):
        cols.append(np.asarray(v, np.float32).reshape(8, 128).T)
    cols.append(np.asarray(pool_scale, np.float32).reshape(4, 128).T)
    return np.ascontiguousarray(np.concatenate(cols, axis=1))


_PROG_CACHE = {}


def run_layer(x, mem, norm_mix_pre, w_in, w_pool_mix, pool_scale, w_pool_o, w_sb_o, norm_mem, w_mem_kv, w_x_o,
              w_out, norm_mix_post, norm_ffn_pre, w_ffn_in, w_ffn_out, norm_ffn_post, debug=False, stop_after=None):
    x = np.asarray(x, np.float32)
    mem = np.asarray(mem, np.float32)
    Bn, S, _ = x.shape
    G = S // 1024
    assert Bn == 2 and S == 1024 * G
    key = (G, debug, stop_after)
    if key not in _PROG_CACHE:
        _PROG_CACHE[key] = build_program(G, debug, stop_after)
    nc = _PROG_CACHE[key]
    f32 = lambda a: np.ascontiguousarray(np.asarray(a, np.float32))
    shared = {
        "w_in": f32(w_in[0]), "w_pmix": f32(np.asarray(w_pool_mix[0]).reshape(512, 128)), "w_po": f32(w_pool_o[0]),
        "w_sbo": f32(w_sb_o[0]), "w_kv": f32(w_mem_kv[0]), "w_xo": f32(w_x_o[0]), "w_out": f32(w_out[0]),
        "w_fi": f32(w_ffn_in[0]), "w_fo": f32(w_ffn_out[0]),
        "vecs": _pack_vecs(norm_mix_pre[0], norm_mem[0], norm_mix_post[0], norm_ffn_pre[0], norm_ffn_post[0],
                           pool_scale[0]),
    }
    xTb = [np.ascontiguousarray(x[b].T) for b in range(2)]
    memTb = [np.ascontiguousarray(mem[b].T) for b in range(2)]
    in_maps = []
    for core in range(8):
        b, r = core // 4, core % 4
        blks = _own_blocks(r, G)
        xoh = np.zeros((D, len(blks) * 144), np.float32)
        posv = np.zeros((len(blks) * 128,), np.float32)
        for n, blk in enumerate(blks):
            s0 = blk * 128
            lo = max(0, s0 - 16)
            xoh[:, n * 144 + 16 - (s0 - lo):(n + 1) * 144] = xTb[b][:, lo:s0 + 128]
            posv[n * 128:(n + 1) * 128] = np.arange(s0, s0 + 128, dtype=np.float32)
        m = dict(shared)
        m.update({"xT": xTb[b], "xoh": xoh, "pos": posv, "memT": memTb[b]})
        in_maps.append(m)
    res = run_bass_kernel_spmd(nc, in_maps, core_ids=list(range(8)))
    out = np.empty((2, S, D), np.float32)
    for core in range(8):
        b, r = core // 4, core % 4
        oT = np.asarray(res.results[core]["outT"])
        for n, blk in enumerate(_own_blocks(r, G)):
            out[b, blk * 128:(blk + 1) * 128, :] = oT[:, n * 128:(n + 1) * 128].T
    if debug:
        return out, res.results
    return out


def kernel(**inputs):
    return run_layer(**inputs)
```

```python
import contextlib
import numpy as np
import concourse.bass as bass
import concourse.mybir as mybir
from concourse.bass_utils import run_bass_kernel_spmd

F32 = mybir.dt.float32
BF16 = mybir.dt.bfloat16
I32 = mybir.dt.int32
AF = mybir.ActivationFunctionType
ALU = mybir.AluOpType
AX = mybir.AxisListType

ENGS = ("pe", "act", "dve", "pool", "sp")
SAME_ENG_RAW = True

D = 1024
DFF = 2816
NFC = DFF // 128
INW = 5376
O_Q, O_K, O_V, O_XQ, O_G = 512, 1024, 1536, 2048, 2304
EPS = 1e-6
NEG_BIG = -30000.0
V_PRE, V_MEM, V_POST, V_FPRE, V_FPOST, V_PSC, NVEC = 0, 8, 16, 24, 32, 40, 44


class Buf:
    __slots__ = ("name", "w", "r", "excl", "base")

    def __init__(self, name):
        self.name = name
        self.w = {}
        self.r = {}
        self.base = {}
        self.excl = False


class Builder:
    def __init__(self, nc, es):
        self.nc = nc
        self.es = es
        self.q = {e: [] for e in ENGS}
        self.cnt = {}
        self.seen = {e: {} for e in ENGS}
        self.semh = {}
        self.nbuf = 0
        for e in ENGS:
            if e != "sp":
                self._newsem(e)

    def _newsem(self, key):
        self.semh[key] = self.es.enter_context(self.nc.semaphore("s%d" % len(self.semh)))
        self.cnt[key] = 0

    def buf(self, name=None):
        self.nbuf += 1
        return Buf("%s#%d" % (name or "b", self.nbuf))

    def _wait(self, eng, key, val):
        if self.seen[eng].get(key, 0) >= val:
            return
        self.seen[eng][key] = val
        sem = self.semh[key]
        self.q[eng].append(lambda e, sem=sem, val=val: e.wait_ge(sem, val))

    def _deps(self, eng, reads, writes, acc):
        raw = {}
        oth = {}

        def add(dst, d):
            for k, v in d.items():
                if dst.get(k, 0) < v:
                    dst[k] = v
        for b in reads:
            add(raw, b.w)
            if b.excl:
                add(oth, b.r)
        for b in writes:
            add(oth, b.w)
            add(oth, b.r)
        for b in acc:
            add(oth, b.r)
            add(oth, b.base)
        for k, v in raw.items():
            if k == eng and (eng == "pe" or not SAME_ENG_RAW):
                continue
            self._wait(eng, k, v)
        for k, v in oth.items():
            if k == eng:
                continue
            self._wait(eng, k, v)

    def _record(self, key, val, reads, writes, acc):
        for b in reads:
            if b.r.get(key, 0) < val:
                b.r[key] = val
        for b in writes:
            b.w = {key: val}
            b.base = {key: val}
            b.r = {}
        for b in acc:
            b.w[key] = val
            b.r = {}

    def op(self, eng, fn, reads=(), writes=(), acc=()):
        self._deps(eng, reads, writes, acc)
        self.cnt[eng] += 1
        c = self.cnt[eng]
        sem = self.semh[eng]
        self.q[eng].append(lambda e, fn=fn, sem=sem: fn(e).then_inc(sem, 1))
        self._record(eng, c, reads, writes, acc)

    def dma(self, qeng, out_ap, in_ap, sb, store=False, reads=(), writes=(), acc=(), **kw):
        self._deps(qeng, reads, writes, acc)
        key = ("st" if store else "ld", sb.name, "sw" if qeng == "pool" else "hw")
        if key not in self.semh:
            self._newsem(key)
        self.cnt[key] += 16
        c = self.cnt[key]
        sem = self.semh[key]
        self.q[qeng].append(
            lambda e, o=out_ap, i=in_ap, sem=sem, kw=kw: e.dma_start(out=o, in_=i, **kw).then_inc(sem, 16))
        self._record(key, c, reads, writes, acc)

    def barrier(self):
        for e in ENGS:
            for k, c in self.cnt.items():
                if k == e or c == 0:
                    continue
                self._wait(e, k, c)

    def emit(self):
        q = self.q
        with self.nc.Block() as block:
            @block.tensor
            def _(e):
                for t in q["pe"]:
                    t(e)

            @block.scalar
            def _(e):
                for t in q["act"]:
                    t(e)

            @block.vector
            def _(e):
                for t in q["dve"]:
                    t(e)

            @block.gpsimd
            def _(e):
                for t in q["pool"]:
                    t(e)

            @block.sync
            def _(e):
                for t in q["sp"]:
                    t(e)


def build_program(G, debug=False, stop_after=None):
    S = 1024 * G
    NBLK = S // 128
    NB = 2 * G
    NQ = 128 * NB
    BPT = 4 if G >= 2 else 2
    TO = 128 * BPT
    TOH = 144 * BPT
    NT = NB // BPT
    ST = S // 512

    nc = bass.Bass("TRN2", target_bir_lowering=False)

    def din(name, shape, dt=F32):
        return nc.dram_tensor(name, list(shape), dt, kind="ExternalInput").ap()

    xT = din("xT", [D, S])
    xoh = din("xoh", [D, NB * 144])
    pos = din("pos", [NQ])
    memT = din("memT", [D, 256])
    w_in = din("w_in", [D, INW])
    w_pmix = din("w_pmix", [512, 128])
    w_po = din("w_po", [512, D])
    w_sbo = din("w_sbo", [512, D])
    w_kv = din("w_kv", [D, 512])
    w_xo = din("w_xo", [256, D])
    w_out = din("w_out", [D, D])
    w_fi = din("w_fi", [D, 2 * DFF])
    w_fo = din("w_fo", [DFF, D])
    vecs = din("vecs", [128, NVEC])
    outT = nc.dram_tensor("outT", [D, NQ], F32, kind="ExternalOutput").ap()
    kT_s = nc.dram_tensor("kT_s", [4, 128, S], BF16).ap()
    v_s = nc.dram_tensor("v_s", [4, 128, NBLK, 128], BF16).ap()
    c0_s = nc.dram_tensor("c0_s", [D, NQ], F32).ap()
    c2_s = nc.dram_tensor("c2_s", [D, NQ], F32).ap()
    h1_s = nc.dram_tensor("h1_s", [D, NQ], F32).ap()
    n2_s = nc.dram_tensor("n2_s", [D, NQ], BF16).ap()
    dbg = {}
    if debug:
        dbg["qT"] = nc.dram_tensor("dbg_qT", [128, 4, 2 * NQ], BF16, kind="ExternalOutput").ap()
        dbg["sbo"] = nc.dram_tensor("dbg_sbo", [128, 4, NQ], BF16, kind="ExternalOutput").ap()
        dbg["kT"] = nc.dram_tensor("dbg_kT", [4, 128, S], BF16, kind="ExternalOutput").ap()
        dbg["v"] = nc.dram_tensor("dbg_v", [4, 128, NBLK, 128], BF16, kind="ExternalOutput").ap()
        dbg["c0"] = nc.dram_tensor("dbg_c0", [D, NQ], F32, kind="ExternalOutput").ap()
        dbg["c2"] = nc.dram_tensor("dbg_c2", [D, NQ], F32, kind="ExternalOutput").ap()
        dbg["h1"] = nc.dram_tensor("dbg_h1", [D, NQ], F32, kind="ExternalOutput").ap()

    with contextlib.ExitStack() as es:
        B = Builder(nc, es)

        def sbt(scope, name, shape, dt):
            return scope.enter_context(nc.sbuf_tensor(name, list(shape), dt))

        def mm(out, lhsT, rhs, start, stop, R, W=(), A=(), sgc=False):
            B.op("pe", lambda e: e.matmul(out, lhsT, rhs, start=start, stop=stop, skip_group_check=sgc),
                 reads=R, writes=W, acc=A)

        def act(out, in_, func, R, W=(), A=(), **kw):
            B.op("act", lambda e: e.activation(out=out, in_=in_, func=func, **kw), reads=R, writes=W, acc=A)

        def tt(eng, out, in0, in1, op, R, W=(), A=()):
            B.op(eng, lambda e: e.tensor_tensor(out=out, in0=in0, in1=in1, op=op), reads=R, writes=W, acc=A)

        def ts(eng, out, in0, s1, s2, op0, op1, R, W=(), A=()):
            if op1 is None:
                B.op(eng, lambda e: e.tensor_scalar(out=out, in0=in0, scalar1=s1, scalar2=s2, op0=op0),
                     reads=R, writes=W, acc=A)
            else:
                B.op(eng, lambda e: e.tensor_scalar(out=out, in0=in0, scalar1=s1, scalar2=s2, op0=op0, op1=op1),
                     reads=R, writes=W, acc=A)

        def cp(eng, out, in_, R, W=(), A=()):
            if eng == "act":
                act(out, in_, AF.Copy, R, W, A)
            else:
                B.op(eng, lambda e: e.tensor_copy(out=out, in_=in_), reads=R, writes=W, acc=A)

        PSALL = es.enter_context(nc.psum_tensor("psall", [128, 4096], F32))
        PS = [PSALL[:, i * 512:(i + 1) * 512] for i in range(8)]
        PB = [B.buf("ps%d" % i) for i in range(8)]
        for b_ in PB:
            b_.excl = True
        PST = PS[7].bitcast(BF16)
        PSTB = PB[7]

        gs = es
        vec = sbt(gs, "vec", [128, NVEC], F32); vecb = B.buf("vec")
        ones = sbt(gs, "ones", [128, 128], BF16); onesb = B.buf("ones")
        nTin = sbt(gs, "nTin", [128, 128], BF16); nTinb = B.buf("nTin")
        nOnes = sbt(gs, "nOnes", [128, 128], BF16); nOnesb = B.buf("nOnes")
        nBigI = sbt(gs, "nBigI", [128, 128], BF16); nBigIb = B.buf("nBigI")
        ident = sbt(gs, "ident", [128, 128], BF16); identb = B.buf("ident")
        dif_i = sbt(gs, "dif_i", [128, 128], I32); difib = B.buf("dif_i")
        dif_f = sbt(gs, "dif_f", [128, 128], F32); diffb = B.buf("dif_f")
        kp8_i = sbt(gs, "kp8_i", [128, 8], I32); kp8ib = B.buf("kp8i")
        kp8 = sbt(gs, "kp8", [128, 8], F32); kp8b = B.buf("kp8")
        qpos = sbt(gs, "qpos", [128, NQ], F32); qposb = B.buf("qpos")
        notM = sbt(gs, "notM", [128, 8, 2, 256], BF16); notMb = B.buf("notM")
        NSTG = 2
        st_state = {"i": 0, "e": 0, "n": 0}

        def new_stg(scope):
            st_state["n"] += 1
            st_state["stg"] = [sbt(scope, "stg%d_%d" % (st_state["n"], i), [128, 1536], F32) for i in range(NSTG)]
            st_state["stgb"] = [B.buf("stg%d" % i) for i in range(NSTG)]

        B.dma("sp", vec[:], vecs, vecb, writes=[vecb])
        B.dma("sp", qpos[:], pos.partition_broadcast(128), qposb, writes=[qposb])
        B.op("dve", lambda e: e.memset(ones[:], 1.0), writes=[onesb])
        B.op("dve", lambda e: e.memset(nOnes[:], -1.0), writes=[nOnesb])
        B.op("pool", lambda e: e.iota(dif_i[:], pattern=[[-1, 128]], base=0, channel_multiplier=1), writes=[difib])
        cp("dve", dif_f[:], dif_i[:], R=[difib], W=[diffb])
        ts("dve", nTin[:], dif_f[:], 0.0, -1.0, ALU.is_ge, ALU.mult, R=[diffb], W=[nTinb])
        ts("dve", nBigI[:], dif_f[:], 0.0, NEG_BIG, ALU.is_equal, ALU.mult, R=[diffb], W=[nBigIb])
        ts("dve", ident[:], dif_f[:], 0.0, None, ALU.is_equal, None, R=[diffb], W=[identb])
        B.op("pool", lambda e: e.iota(kp8_i[:], pattern=[[128, 8]], base=0, channel_multiplier=1), writes=[kp8ib])
        cp("dve", kp8[:], kp8_i[:], R=[kp8ib], W=[kp8b])
        for jj in range(8):
            for h in range(2):
                ts("dve", notM[:, jj, h, :], qpos[:, 0:256], kp8[:, jj:jj + 1], None, ALU.is_le, None,
                   R=[qposb, kp8b], **({"W": [notMb]} if (jj == 0 and h == 0) else {"A": [notMb]}))

        def load_w(dst, dstb, src, kc, ncols, gcol=None, first=True):
            kg = max(1, min(kc, 1536 // ncols))
            k0 = 0
            while k0 < kc:
                kn = min(kg, kc - k0)
                i = st_state["i"] % NSTG
                st_state["i"] += 1
                stg, stgb = st_state["stg"], st_state["stgb"]
                sv = stg[i][:, 0:kn * ncols].rearrange("p (k n) -> p k n", k=kn)
                B.dma("sp", sv, src[k0 * 128:(k0 + kn) * 128, :].rearrange("(k p) n -> p k n", p=128), stgb[i],
                      writes=[stgb[i]])
                eng = "dve" if st_state["e"] % 2 == 0 else "act"
                st_state["e"] += 1
                kw = {"W": [dstb]} if (first and k0 == 0) else {"A": [dstb]}
                cp(eng, dst[:, k0:k0 + kn, :], sv, R=[stgb[i]], **kw)
                k0 += kn

        def run(gen):
            for _ in gen:
                pass

        def interleave(*gens):
            gens = list(gens)
            while gens:
                for g_ in list(gens):
                    try:
                        next(g_)
                    except StopIteration:
                        gens.remove(g_)

        def stt(out, in0, scal, in1, R, W=(), A=()):
            B.op("dve", lambda e: e.scalar_tensor_tensor(out=out, in0=in0, scalar=scal, in1=in1,
                                                         op0=ALU.mult, op1=ALU.mult), reads=R, writes=W, acc=A)

        def rstd_from(ss_ps, ss_b, ncols, ln_t, ln_b, out_t, out_b, out_is_write=True):
            act(ln_t, ss_ps, AF.Ln, R=[ss_b], W=[ln_b], scale=1.0 / D, bias=EPS)
            act(out_t, ln_t, AF.Exp, R=[ln_b], W=[out_b], scale=-0.5)

        s13 = contextlib.ExitStack()
        es.enter_context(s13)
        NGT = BPT // 2
        Qz = sbt(s13, "Qz", [128, 4, G, 2, 256], BF16); Qzb = B.buf("Qz")
        B.op("pool", lambda e: e.memset(Qz[:].rearrange("p a g h q -> p (a g h q)"), 0.0), writes=[Qzb])
        wqkv = sbt(s13, "wqkv", [128, 8, 1536], BF16); wqkvb = B.buf("wqkv")

        xn_s = nc.dram_tensor("xn_s", [128, 8, NB * 144], BF16).ap()
        xnsb = B.buf("xn_s")
        with contextlib.ExitStack() as s1:
            new_stg(s1)
            load_w(wqkv, wqkvb, w_in[:, O_Q:O_Q + 1536], 8, 1536)
            xo_t = [sbt(s1, "xo_t%d" % i, [128, 8, TOH], F32) for i in range(2)]
            xo_b = [B.buf("xo_t%d" % i) for i in range(2)]
            sq1 = sbt(s1, "sq1", [128, 8, TOH], BF16); sq1b = B.buf("sq1")
            ln1 = sbt(s1, "ln1", [128, TOH], F32); ln1b = B.buf("ln1")
            rs1 = sbt(s1, "rs1", [128, TOH], F32); rs1b = B.buf("rs1")
            xn1 = [sbt(s1, "xn1_%d" % i, [128, 8, BPT, 144], BF16) for i in range(2)]
            xn1b = [B.buf("xn1_%d" % i) for i in range(2)]
            HB = TOH // 2
            def p1_norm(t):
                sl = t % 2
                B.dma("sp", xo_t[sl][:], xoh[:, t * TOH:(t + 1) * TOH].rearrange("(c p) n -> p c n", p=128),
                      xo_b[sl], writes=[xo_b[sl]])
                act(sq1[:], xo_t[sl][:], AF.Square, R=[xo_b[sl]], W=[sq1b])
                for hf in range(2):
                    for k in range(8):
                        mm(PS[hf][:, 0:HB], ones[:], sq1[:, k, hf * HB:(hf + 1) * HB], start=(k == 0), stop=(k == 7),
                           R=[onesb, sq1b], **({"W": [PB[hf]]} if k == 0 else {"A": [PB[hf]]}))
                for hf in range(2):
                    act(ln1[:, hf * HB:(hf + 1) * HB], PS[hf][:, 0:HB], AF.Ln, R=[PB[hf]],
                        **({"W": [ln1b]} if hf == 0 else {"A": [ln1b]}), scale=1.0 / D, bias=EPS)
                act(rs1[:], ln1[:], AF.Exp, R=[ln1b], W=[rs1b], scale=-0.5)
                xnv = xn1[sl][:].rearrange("p c b t -> p c (b t)")
                for k in range(8):
                    stt(xnv[:, k, :], xo_t[sl][:, k, :], vec[:, V_PRE + k:V_PRE + k + 1], rs1[:],
                        R=[xo_b[sl], rs1b, vecb], **({"W": [xn1b[sl]]} if k == 0 else {"A": [xn1b[sl]]}))
                B.dma("pool", xn_s[:, :, t * TOH:(t + 1) * TOH], xnv, xn1b[sl], store=True,
                      reads=[xn1b[sl]], acc=[xnsb])

            def p1_q(t):
                sl = t % 2
                for c4 in range(4):
                    pb = 2 + (c4 % 2)
                    for k in range(8):
                        mm(PS[pb][:, 0:TO].rearrange("p (b t) -> p b t", b=BPT),
                           wqkv[:, k, c4 * 128:(c4 + 1) * 128], xn1[sl][:, k, :, 16:144],
                           start=(k == 0), stop=(k == 7), R=[wqkvb, xn1b[sl]],
                           **({"W": [PB[pb]]} if k == 0 else {"A": [PB[pb]]}))
                    for h in range(2):
                        act(Qz[h * 64:(h + 1) * 64, c4, t * NGT:(t + 1) * NGT, h, :],
                            PS[pb][h * 64:(h + 1) * 64, 0:TO].rearrange("p (g q) -> p g q", g=NGT), AF.Copy,
                            R=[PB[pb]], A=[Qzb], scale=0.125)

            p1_norm(0)
            for t in range(NT):
                if t + 1 < NT:
                    p1_norm(t + 1)
                p1_q(t)
        B.barrier()
        if debug:
            B.dma("sp", dbg["qT"], Qz[:].rearrange("p a g h q -> p a (g h q)"), Qzb, store=True, reads=[Qzb])
        if stop_after == "P1":
            B.barrier()
            B.emit()
            return nc

        kTsb = B.buf("kT_s"); vsb = B.buf("v_s")
        with contextlib.ExitStack() as s2:
            x_t = [sbt(s2, "x_t%d" % i, [128, 8, 512], F32) for i in range(3)]
            x_b = [B.buf("x_t%d" % i) for i in range(3)]
            sq2 = [sbt(s2, "sq2_%d" % i, [128, 8, 512], BF16) for i in range(2)]
            sq2b = [B.buf("sq2_%d" % i) for i in range(2)]
            ln2 = sbt(s2, "ln2", [128, 512], F32); ln2b = B.buf("ln2")
            rs2 = sbt(s2, "rs2", [128, 512], F32); rs2b = B.buf("rs2")
            xn2 = [sbt(s2, "xn2_%d" % i, [128, 8, 512], BF16) for i in range(2)]
            xn2b = [B.buf("xn2_%d" % i) for i in range(2)]
            kst = [sbt(s2, "kst%d" % i, [128, 4, 512], BF16) for i in range(2)]
            kstb = [B.buf("kst%d" % i) for i in range(2)]
            vst = [sbt(s2, "vst%d" % i, [128, 4, 512], BF16) for i in range(2)]
            vstb = [B.buf("vst%d" % i) for i in range(2)]

            def p2_load(t):
                B.dma("sp", x_t[t % 3][:], xT[:, t * 512:(t + 1) * 512].rearrange("(c p) n -> p c n", p=128),
                      x_b[t % 3], writes=[x_b[t % 3]])

            def p2_sq(t):
                sl = t % 2
                act(sq2[sl][:], x_t[t % 3][:], AF.Square, R=[x_b[t % 3]], W=[sq2b[sl]])

            def p2_norm(t):
                sl = t % 2
                for k in range(8):
                    mm(PS[0][:], ones[:], sq2[sl][:, k, :], start=(k == 0), stop=(k == 7), R=[onesb, sq2b[sl]],
                       **({"W": [PB[0]]} if k == 0 else {"A": [PB[0]]}))
                rstd_from(PS[0][:], PB[0], 512, ln2[:], ln2b, rs2[:], rs2b)
                for k in range(8):
                    stt(xn2[sl][:, k, :], x_t[t % 3][:, k, :], vec[:, V_PRE + k:V_PRE + k + 1], rs2[:],
                        R=[x_b[t % 3], rs2b, vecb], **({"W": [xn2b[sl]]} if k == 0 else {"A": [xn2b[sl]]}))

            def p2_kv(t):
                sl = t % 2
                for c4 in range(4):
                    pb = 1 + (c4 % 3)
                    for k in range(8):
                        mm(PS[pb][:], wqkv[:, k, 512 + c4 * 128:512 + (c4 + 1) * 128], xn2[sl][:, k, :],
                           start=(k == 0), stop=(k == 7), R=[wqkvb, xn2b[sl]],
                           **({"W": [PB[pb]]} if k == 0 else {"A": [PB[pb]]}))
                    cp("dve", kst[sl][:, c4, :], PS[pb][:], R=[PB[pb]],
                       **({"W": [kstb[sl]]} if c4 == 0 else {"A": [kstb[sl]]}))
                B.dma("pool", kT_s[:, :, t * 512:(t + 1) * 512].rearrange("c p s -> p c s"), kst[sl][:], kstb[sl],
                      store=True, reads=[kstb[sl]], acc=[kTsb])
                for bk in range(4):
                    pb = 4 + (bk % 3)
                    for k in range(8):
                        mm(PS[pb][:], xn2[sl][:, k, bk * 128:(bk + 1) * 128], wqkv[:, k, 1024:1536],
                           start=(k == 0), stop=(k == 7), R=[wqkvb, xn2b[sl]],
                           **({"W": [PB[pb]]} if k == 0 else {"A": [PB[pb]]}))
                    cp("dve", vst[sl][:, bk, :], PS[pb][:], R=[PB[pb]],
                       **({"W": [vstb[sl]]} if bk == 0 else {"A": [vstb[sl]]}))
                for c4 in range(4):
                    B.dma("pool", v_s[c4, :, 4 * t:4 * t + 4, :], vst[sl][:, :, c4 * 128:(c4 + 1) * 128], vstb[sl],
                          store=True, reads=[vstb[sl]], acc=[vsb])

            p2_load(0)
            p2_sq(0)
            p2_norm(0)
            if ST > 1:
                p2_load(1)
                p2_sq(1)
            for t in range(ST):
                if t + 2 < ST:
                    p2_load(t + 2)
                    p2_sq(t + 2)
                if t + 1 < ST:
                    p2_norm(t + 1)
                p2_kv(t)
        B.barrier()
        if debug:
            with contextlib.ExitStack() as sd:
                dk = sbt(sd, "dk", [128, S], BF16); dkb = B.buf("dk")
                dv = sbt(sd, "dv", [128, NBLK, 128], BF16); dvb = B.buf("dv")
                for c4 in range(4):
                    B.dma("sp", dk[:], kT_s[c4], dkb, reads=[kTsb], writes=[dkb])
                    B.dma("sp", dbg["kT"][c4], dk[:], dkb, store=True, reads=[dkb])
                    B.dma("sp", dv[:], v_s[c4], dvb, reads=[vsb], writes=[dvb])
                    B.dma("sp", dbg["v"][c4], dv[:], dvb, store=True, reads=[dvb])
                B.barrier()

        if stop_after == "P2":
            B.barrier()
            B.emit()
            return nc
        s34 = contextlib.ExitStack()
        with contextlib.ExitStack() as s3:
            sbo = sbt(s3, "sbo", [128, 4, NQ], BF16); sbob = B.buf("sbo")
            kt = [sbt(s3, "kt%d" % i, [128, S], BF16) for i in range(2)]
            ktb = [B.buf("kt%d" % i) for i in range(2)]
            vt = [sbt(s3, "vt%d" % i, [128, NBLK, 128], BF16) for i in range(2)]
            vtb = [B.buf("vt%d" % i) for i in range(2)]
            NE = 2
            e_t = [sbt(s3, "e_t%d" % i, [128, 1024], F32) for i in range(NE)]
            e_b = [B.buf("e_t%d" % i) for i in range(NE)]
            NSP = 3
            sp_t = [sbt(s3, "sp_t%d" % i, [128, 1024], BF16) for i in range(NSP)]
            sp_b = [B.buf("sp_t%d" % i) for i in range(NSP)]
            a_t = [sbt(s3, "a_t%d" % i, [128, 1024], BF16) for i in range(NSP)]
            a_b = [B.buf("a_t%d" % i) for i in range(NSP)]
            R_t = [sbt(s3, "R_t%d" % i, [128, 512], F32) for i in range(2)]
            R_b = [B.buf("R_t%d" % i) for i in range(2)]
            Rb_t = [sbt(s3, "Rb_t%d" % i, [128, 512], BF16) for i in range(4)]
            Rb_b = [B.buf("Rb_t%d" % i) for i in range(4)]
            NZP = 3

            def p3_load(c):
                B.dma("sp", kt[c % 2][:], kT_s[c], ktb[c % 2], reads=[kTsb], writes=[ktb[c % 2]])
                B.dma("sp", vt[c % 2][:], v_s[c], vtb[c % 2], reads=[vsb], writes=[vtb[c % 2]])

            pairs = []
            chain_id = 0
            for c in range(4):
                for g in range(G):
                    nj = 8 * g + 8
                    for pi in range(nj // 2):
                        jh = nj - 1 - 2 * pi
                        pairs.append(dict(c=c, g=g, jh=jh, jl=jh - 1, pi=pi, last=(jh == 1), chain=chain_id))
                    chain_id += 1
            for s_i, T in enumerate(pairs):
                T["zs"] = s_i % NZP
                T["sl3"] = s_i % NSP
                T["esl"] = s_i % NE
                T["ob"] = 6 + (T["chain"] % 2)
                T["rs"] = T["chain"] % 2

            def st_P1(T):
                c, g = T["c"], T["g"]
                qv = Qz[:, c, g, :, :].rearrange("p h q -> p (h q)")
                for i, j in enumerate((T["jh"], T["jl"])):
                    bk_ = 2 * T["zs"] + i
                    jj = j - 8 * g
                    mm(PS[bk_][:], kt[c % 2][:, j * 128:(j + 1) * 128], qv, start=True, stop=(jj < 0),
                       R=[ktb[c % 2], Qzb], W=[PB[bk_]])
                    if jj >= 0:
                        mm(PS[bk_][:], nBigI[:], notM[:, jj, :, :].rearrange("p h q -> p (h q)"),
                           start=False, stop=True, R=[nBigIb, notMb], A=[PB[bk_]])

            def zpair(T):
                zs = T["zs"]
                return PSALL[:, zs * 1024:(zs + 1) * 1024], [PB[2 * zs], PB[2 * zs + 1]]

            def bcols(ap):
                return ap.rearrange("p (j h q) -> p j h q", j=2, h=2)[:, :, :, 128:256]

            def bonly(T):
                return T["jl"] - 8 * T["g"] >= 4

            def st_A1a(T):
                zap, zbufs = zpair(T)
                e_ = e_t[T["esl"]]; eb = e_b[T["esl"]]
                if bonly(T):
                    act(bcols(e_[:]), bcols(zap), AF.Exp, R=zbufs, W=[eb])
                else:
                    act(e_[:], zap, AF.Exp, R=zbufs, W=[eb])

            def st_A1b(T):
                e_ = e_t[T["esl"]]; eb = e_b[T["esl"]]
                sp_ = sp_t[T["sl3"]]; spb = sp_b[T["sl3"]]
                if bonly(T):
                    B.op("pool", lambda e: e.memset(sp_[:], 0.0), writes=[spb])
                    act(bcols(sp_[:]), bcols(e_[:]), AF.Ln, R=[eb], A=[spb], bias=1.0)
                else:
                    act(sp_[:], e_[:], AF.Ln, R=[eb], W=[spb], bias=1.0)

            def st_P2(T):
                b0 = 2 * T["zs"]; b1 = b0 + 1
                sp_ = sp_t[T["sl3"]]; spb = sp_b[T["sl3"]]
                first = (T["pi"] == 0)
                rbi = (T["chain"] % 2) * 2 + (T["pi"] % 2)
                mm(PS[b0][:], nTin[:], sp_[:, 0:512], start=False, stop=first, R=[nTinb, spb], A=[PB[b0]], sgc=True)
                if not first:
                    mm(PS[b0][:], nOnes[:], Rb_t[rbi][:], start=False, stop=True, R=[nOnesb, Rb_b[rbi]], A=[PB[b0]],
                       sgc=True)
                mm(PS[b1][:], nTin[:], sp_[:, 512:1024], start=False, stop=False, R=[nTinb, spb], A=[PB[b1]], sgc=True)
                mm(PS[b1][:], nOnes[:], sp_[:, 0:512], start=False, stop=first, R=[nOnesb, spb], A=[PB[b1]], sgc=True)
                if not first:
                    mm(PS[b1][:], nOnes[:], Rb_t[rbi][:], start=False, stop=True, R=[nOnesb, Rb_b[rbi]], A=[PB[b1]],
                       sgc=True)

            def st_G(T):
                if T["last"]:
                    return
                sp_ = sp_t[T["sl3"]]; spb = sp_b[T["sl3"]]
                R_ = R_t[T["rs"]]; Rbuf = R_b[T["rs"]]
                rbn = (T["chain"] % 2) * 2 + ((T["pi"] + 1) % 2)
                if T["pi"] == 0:
                    tt("dve", R_[:], sp_[:, 0:512], sp_[:, 512:1024], ALU.add, R=[spb], W=[Rbuf])
                else:
                    tt("dve", R_[:], R_[:], sp_[:, 0:512], ALU.add, R=[spb, Rbuf], A=[Rbuf])
                    tt("dve", R_[:], R_[:], sp_[:, 512:1024], ALU.add, R=[spb, Rbuf], A=[Rbuf])
                cp("dve", Rb_t[rbn][:], R_[:], R=[Rbuf], W=[Rb_b[rbn]])

            def st_A2(T):
                zap, zbufs = zpair(T)
                a_ = a_t[T["sl3"]]; ab = a_b[T["sl3"]]
                if bonly(T):
                    B.op("pool", lambda e: e.memset(a_[:], 0.0), writes=[ab])
                    act(bcols(a_[:]), bcols(zap), AF.Exp, R=zbufs, A=[ab])
                else:
                    act(a_[:], zap, AF.Exp, R=zbufs, W=[ab])

            def st_P3(T):
                c, g = T["c"], T["g"]
                a_ = a_t[T["sl3"]]; ab = a_b[T["sl3"]]
                ob = T["ob"]
                first = (T["pi"] == 0)
                mm(PS[ob][:], vt[c % 2][:, T["jh"], :], a_[:, 0:512], start=first, stop=False, R=[vtb[c % 2], ab],
                   **({"W": [PB[ob]]} if first else {"A": [PB[ob]]}))
                mm(PS[ob][:], vt[c % 2][:, T["jl"], :], a_[:, 512:1024], start=False, stop=T["last"],
                   R=[vtb[c % 2], ab], A=[PB[ob]])
                if T["last"]:
                    q0 = g * 256
                    for h in range(2):
                        cp("dve", sbo[h * 64:(h + 1) * 64, c, q0:q0 + 256],
                           PS[ob][h * 64:(h + 1) * 64, h * 256:(h + 1) * 256], R=[PB[ob]], A=[sbob])

            p3_load(0)
            n_t = len(pairs)
            for s_i in range(n_t + 2):
                if 0 <= s_i - 2 < n_t:
                    T2 = pairs[s_i - 2]
                    if T2["g"] == 0 and T2["pi"] == 0 and T2["c"] + 1 < 4:
                        p3_load(T2["c"] + 1)
                if s_i < n_t:
                    T = pairs[s_i]
                    st_P1(T)
                    st_A1a(T)
                    st_A1b(T)
                if 0 <= s_i - 1 < n_t:
                    T1 = pairs[s_i - 1]
                    st_P2(T1)
                    st_G(T1)
                    st_A2(T1)
                if 0 <= s_i - 2 < n_t:
                    st_P3(pairs[s_i - 2])
            B.barrier()
            if debug:
                B.dma("sp", dbg["sbo"], sbo[:], sbob, store=True, reads=[sbob])
                B.barrier()
            if stop_after == "P3":
                B.barrier()
                B.emit()
                return nc
            sbo_s = nc.dram_tensor("sbo_s", [128, 4, NQ], BF16).ap()
            sbosb = B.buf("sbo_s")
            B.dma("sp", sbo_s, sbo[:], sbob, store=True, reads=[sbob], writes=[sbosb])
            B.barrier()
        s13.close()

        with contextlib.ExitStack() as s4:
            xn = sbt(s4, "xn", [128, 8, NB, 144], BF16); xnb = B.buf("xn")
            B.dma("sp", xn[:].rearrange("p c b t -> p c (b t)"), xn_s, xnb, reads=[xnsb], writes=[xnb])
            NOUT = 3
            o_t = [sbt(s4, "o_t%d" % i, [128, TO], F32) for i in range(NOUT)]
            o_b = [B.buf("o_t%d" % i) for i in range(NOUT)]
            sg_t = [sbt(s4, "sg_t%d" % i, [128, TO], F32) for i in range(2)]
            sg_b = [B.buf("sg_t%d" % i) for i in range(2)]
            cnt4 = {"o": 0, "sg": 0}

            def gate_mm(pb, wg, wgb, dc, t):
                for k in range(8):
                    mm(PS[pb][:, 0:TO].rearrange("p (b t) -> p b t", b=BPT),
                       wg[:, k, dc * 128:(dc + 1) * 128], xn[:, k, t * BPT:(t + 1) * BPT, 16:144],
                       start=(k == 0), stop=(k == 7), R=[wgb, xnb],
                       **({"W": [PB[pb]]} if k == 0 else {"A": [PB[pb]]}))

            def gated(ypb, gpb):
                si = cnt4["sg"] % 2; cnt4["sg"] += 1
                oi = cnt4["o"] % NOUT; cnt4["o"] += 1
                act(sg_t[si][:], PS[gpb][:, 0:TO], AF.Sigmoid, R=[PB[gpb]], W=[sg_b[si]])
                tt("dve", o_t[oi][:], sg_t[si][:], PS[ypb][:, 0:TO], ALU.mult, R=[sg_b[si], PB[ypb]], W=[o_b[oi]])
                return o_t[oi], o_b[oi]

            c0sb = B.buf("c0_s")
            with contextlib.ExitStack() as sa:
                wpi = sbt(sa, "wpi", [128, 8, 512], BF16); wpib = B.buf("wpi")
                wmix = sbt(sa, "wmix", [128, 4, 128], BF16); wmixb = B.buf("wmix")
                wpo = sbt(sa, "wpo", [128, 4, D], BF16); wpob = B.buf("wpo")
                wg0 = sbt(sa, "wg0", [128, 8, D], BF16); wg0b = B.buf("wg0")
                new_stg(sa)
                load_w(wpi, wpib, w_in[:, 0:512], 8, 512, gcol=V_PRE)
                load_w(wmix, wmixb, w_pmix, 4, 128)
                load_w(wpo, wpob, w_po, 4, D)
                load_w(wg0, wg0b, w_in[:, O_G:O_G + D], 8, D, gcol=V_PRE)
                pa = [sbt(sa, "pa%d" % i, [128, BPT, 144], F32) for i in range(3)]
                pab = [B.buf("pa%d" % i) for i in range(3)]
                icn = sbt(sa, "icn", [128, 4, TO], F32); icnb = B.buf("icn")
                mixd = sbt(sa, "mixd", [128, 4, TO], BF16); mixdb = B.buf("mixd")
                pm = [sbt(sa, "pm%d" % i, [128, 4, TO], BF16) for i in range(2)]
                pmb = [B.buf("pm%d" % i) for i in range(2)]
                tmpa = sbt(sa, "tmpa", [128, BPT, 128], F32); tmpab = B.buf("tmpa")
                HB = TOH // 2
                HBK = BPT // 2

                def pa_pool(t):
                    for gi in range(4):
                        ts("dve", icn[:, gi, :], qpos[:, t * TO:(t + 1) * TO], 1.0, float(2 << gi), ALU.add, ALU.min,
                           R=[qposb], **({"W": [icnb]} if gi == 0 else {"A": [icnb]}))
                    B.op("dve", lambda e: e.reciprocal(out=icn[:], in_=icn[:]), reads=[icnb], writes=[icnb])
                    for gi in range(4):
                        p0, p0b = pa[0], pab[0]
                        for hf in range(2):
                            pb = hf
                            for k in range(8):
                                mm(PS[pb][:, 0:HB].rearrange("p (b t) -> p b t", b=HBK),
                                   wpi[:, k, gi * 128:(gi + 1) * 128],
                                   xn[:, k, t * BPT + hf * HBK:t * BPT + (hf + 1) * HBK, :],
                                   start=(k == 0), stop=(k == 7), R=[wpib, xnb],
                                   **({"W": [PB[pb]]} if k == 0 else {"A": [PB[pb]]}))
                            cp("act", p0[:, hf * HBK:(hf + 1) * HBK, :],
                               PS[pb][:, 0:HB].rearrange("p (b t) -> p b t", b=HBK), R=[PB[pb]],
                               **({"W": [p0b]} if hf == 0 else {"A": [p0b]}))
                        cur, curb = p0, p0b
                        pp = 1
                        dsh = 1
                        lo = 0
                        for step in range(gi + 1):
                            nxt, nxtb = pa[pp], pab[pp]
                            lo = lo + dsh
                            tt("dve" if step % 2 == 0 else "pool", nxt[:, :, lo:144], cur[:, :, lo:144],
                               cur[:, :, lo - dsh:144 - dsh], ALU.add, R=[curb], W=[nxtb])
                            cur, curb = nxt, nxtb
                            pp = 3 - pp
                            dsh *= 2
                        tt("dve", tmpa[:], cur[:, :, 16:144], icn[:, gi, :].rearrange("p (b t) -> p b t", b=BPT),
                           ALU.mult, R=[curb, icnb], W=[tmpab])
                        tt("dve", mixd[:, gi, :].rearrange("p (b t) -> p b t", b=BPT), tmpa[:], p0[:, :, 16:144],
                           ALU.subtract, R=[tmpab, p0b], **({"W": [mixdb]} if gi == 0 else {"A": [mixdb]}))
                        yield
                    for gi in range(4):
                        pb = 2 + (gi % 2)
                        mm(PS[pb][:, 0:TO], wmix[:, gi, :], mixd[:, gi, :], start=True, stop=True,
                           R=[wmixb, mixdb], W=[PB[pb]])
                        ts("dve", pm[t % 2][:, gi, :], PS[pb][:, 0:TO], vec[:, V_PSC + gi:V_PSC + gi + 1], None,
                           ALU.mult, None, R=[PB[pb], vecb], **({"W": [pmb[t % 2]]} if gi == 0 else {"A": [pmb[t % 2]]}))
                    yield

                def pa_proj(t):
                    def fin(dc, ypb, gpb):
                        ot, otb = gated(ypb, gpb)
                        B.dma("sp", c0_s[dc * 128:(dc + 1) * 128, t * TO:(t + 1) * TO], ot[:], otb, store=True,
                              reads=[otb], acc=[c0sb])
                    pend = None
                    for dc in range(8):
                        ypb = 4 + (dc % 2)
                        gpb = 6 + (dc % 2)
                        for gi in range(4):
                            mm(PS[ypb][:, 0:TO], wpo[:, gi, dc * 128:(dc + 1) * 128], pm[t % 2][:, gi, :],
                               start=(gi == 0), stop=(gi == 3), R=[wpob, pmb[t % 2]],
                               **({"W": [PB[ypb]]} if gi == 0 else {"A": [PB[ypb]]}))
                        gate_mm(gpb, wg0, wg0b, dc, t)
                        if pend is not None:
                            fin(*pend)
                        pend = (dc, ypb, gpb)
                        yield
                    fin(*pend)

                run(pa_pool(0))
                for t in range(NT):
                    if t + 1 < NT:
                        interleave(pa_proj(t), pa_pool(t + 1))
                    else:
                        run(pa_proj(t))
                B.barrier()

            c2sb = B.buf("c2_s")
            with contextlib.ExitStack() as sc:
                wxq = sbt(sc, "wxq", [128, 8, 256], BF16); wxqb = B.buf("wxq")
                wkv = sbt(sc, "wkv", [128, 8, 512], BF16); wkvb = B.buf("wkv")
                wxo = sbt(sc, "wxo", [128, 2, D], BF16); wxob = B.buf("wxo")
                wg2 = sbt(sc, "wg2", [128, 8, D], BF16); wg2b = B.buf("wg2")
                new_stg(sc)
                load_w(wkv, wkvb, w_kv, 8, 512, gcol=V_MEM)
                load_w(wxq, wxqb, w_in[:, O_XQ:O_XQ + 256], 8, 256, gcol=V_PRE)
                load_w(wxo, wxob, w_xo, 2, D)
                load_w(wg2, wg2b, w_in[:, O_G + 2 * D:O_G + 3 * D], 8, D, gcol=V_PRE)
                m_t = sbt(sc, "m_t", [128, 8, 256], F32); m_b = B.buf("m_t")
                msq = sbt(sc, "msq", [128, 8, 256], BF16); msqb = B.buf("msq")
                mln = sbt(sc, "mln", [128, 256], F32); mlnb = B.buf("mln")
                mrs = sbt(sc, "mrs", [128, 256], F32); mrsb = B.buf("mrs")
                mn = sbt(sc, "mn", [128, 8, 256], BF16); mnb = B.buf("mn")
                mkT = sbt(sc, "mkT", [128, 2, 256], BF16); mkTb = B.buf("mkT")
                mvz = sbt(sc, "mvz", [128, 4, 2, 128], BF16); mvb = B.buf("mvz")
                xqz = sbt(sc, "xqz", [128, 4, TO], BF16); xqTb = B.buf("xqz")
                B.op("pool", lambda e: e.memset(mvz[:].rearrange("p a b c -> p (a b c)"), 0.0), writes=[mvb])
                B.op("pool", lambda e: e.memset(xqz[:].rearrange("p a b -> p (a b)"), 0.0), writes=[xqTb])
                nmx2 = [sbt(sc, "nmx%d" % i, [128, 4], F32) for i in range(2)]; nmxb2 = [B.buf("nmx") for i in range(2)]
                ssum2 = [sbt(sc, "ssum%d" % i, [128, 4], F32) for i in range(2)]; ssumb2 = [B.buf("ssum") for i in range(2)]
                rsm2_ = [sbt(sc, "rsx%d" % i, [128, 4], F32) for i in range(2)]; rsmb2 = [B.buf("rsm") for i in range(2)]
                P_t2 = [sbt(sc, "P_t%d" % i, [128, 4, 256], F32) for i in range(2)]; P_b2 = [B.buf("P_t") for i in range(2)]
                Pn2 = [sbt(sc, "Pn%d" % i, [128, 4, 256], BF16) for i in range(2)]; Pnb2 = [B.buf("Pn") for i in range(2)]
                PT2 = [sbt(sc, "PT%d" % i, [128, 8, 128], BF16) for i in range(2)]; PTb2 = [B.buf("PT") for i in range(2)]
                xoT = sbt(sc, "xoT", [128, 2, TO], BF16); xoTb = B.buf("xoT")

                B.dma("sp", m_t[:], memT.rearrange("(c p) n -> p c n", p=128), m_b, writes=[m_b])
                act(msq[:], m_t[:], AF.Square, R=[m_b], W=[msqb])
                for k in range(8):
                    mm(PS[0][:, 0:256], ones[:], msq[:, k, :], start=(k == 0), stop=(k == 7), R=[onesb, msqb],
                       **({"W": [PB[0]]} if k == 0 else {"A": [PB[0]]}))
                rstd_from(PS[0][:, 0:256], PB[0], 256, mln[:], mlnb, mrs[:], mrsb)
                for k in range(8):
                    stt(mn[:, k, :], m_t[:, k, :], vec[:, V_MEM + k:V_MEM + k + 1], mrs[:], R=[m_b, mrsb, vecb],
                        **({"W": [mnb]} if k == 0 else {"A": [mnb]}))
                for ch in range(2):
                    for k in range(8):
                        mm(PS[1][:, 0:256], wkv[:, k, ch * 128:(ch + 1) * 128], mn[:, k, :], start=(k == 0), stop=(k == 7),
                           R=[wkvb, mnb], **({"W": [PB[1]]} if k == 0 else {"A": [PB[1]]}))
                    cp("dve", mkT[:, ch, :], PS[1][:, 0:256], R=[PB[1]], **({"W": [mkTb]} if ch == 0 else {"A": [mkTb]}))
                for mc in range(2):
                    for k in range(8):
                        mm(PS[2][:, 0:256], mn[:, k, mc * 128:(mc + 1) * 128], wkv[:, k, 256:512], start=(k == 0), stop=(k == 7),
                           R=[wkvb, mnb], **({"W": [PB[2]]} if k == 0 else {"A": [PB[2]]}))
                    for h in range(4):
                        cp("dve", mvz[:, h, mc, (h % 2) * 64:(h % 2 + 1) * 64], PS[2][:, h * 64:(h + 1) * 64],
                           R=[PB[2]], A=[mvb])

                for t in range(NT):
                    for ch in range(2):
                        for k in range(8):
                            mm(PS[3][:, 0:TO].rearrange("p (b t) -> p b t", b=BPT),
                               wxq[:, k, ch * 128:(ch + 1) * 128], xn[:, k, t * BPT:(t + 1) * BPT, 16:144],
                               start=(k == 0), stop=(k == 7), R=[wxqb, xnb],
                               **({"W": [PB[3]]} if k == 0 else {"A": [PB[3]]}))
                        for hp in range(2):
                            act(xqz[hp * 64:(hp + 1) * 64, ch * 2 + hp, :], PS[3][hp * 64:(hp + 1) * 64, 0:TO], AF.Copy,
                                R=[PB[3]], A=[xqTb], scale=0.125)
                    for bk in range(BPT):
                        nmx, nmxb, ssum, ssumb = nmx2[bk % 2], nmxb2[bk % 2], ssum2[bk % 2], ssumb2[bk % 2]
                        rsm, rsmb, P_t, P_b = rsm2_[bk % 2], rsmb2[bk % 2], P_t2[bk % 2], P_b2[bk % 2]
                        Pn, Pnb, PT, PTb = Pn2[bk % 2], Pnb2[bk % 2], PT2[bk % 2], PTb2[bk % 2]
                        sb0 = 0 if bk % 2 == 0 else 4
                        for h in range(4):
                            pb = sb0 + h // 2
                            hp = h % 2
                            mm(PS[pb][:, hp * 256:(hp + 1) * 256],
                               xqz[:, h, bk * 128:(bk + 1) * 128], mkT[:, h // 2, :],
                               start=True, stop=True, R=[xqTb, mkTb],
                               **({"W": [PB[pb]]} if hp == 0 else {"A": [PB[pb]]}))
                        for pq in range(2):
                            B.op("dve", lambda e, pq=pq, nmx=nmx, sb0=sb0: e.reduce_max(
                                out=nmx[:, 2 * pq:2 * pq + 2], in_=PS[sb0 + pq][:].rearrange("p (h m) -> p h m", h=2),
                                axis=AX.X, negate=True), reads=[PB[sb0 + pq]],
                                **({"writes": [nmxb]} if pq == 0 else {"acc": [nmxb]}))
                        for h in range(4):
                            pb = sb0 + h // 2
                            hp = h % 2
                            act(P_t[:, h, :], PS[pb][:, hp * 256:(hp + 1) * 256], AF.Exp, R=[PB[pb], nmxb],
                                **({"W": [P_b, ssumb]} if h == 0 else {"A": [P_b, ssumb]}),
                                bias=nmx[:, h:h + 1], accum_out=ssum[:, h:h + 1])
                        B.op("dve", lambda e, rsm=rsm, ssum=ssum: e.reciprocal(out=rsm[:], in_=ssum[:]),
                             reads=[ssumb], writes=[rsmb])
                        for h in range(4):
                            ts("dve", Pn[:, h, :], P_t[:, h, :], rsm[:, h:h + 1], None, ALU.mult, None,
                               R=[P_b, rsmb], **({"W": [Pnb]} if h == 0 else {"A": [Pnb]}))
                        for h in range(4):
                            for mc in range(2):
                                i8 = h * 2 + mc
                                B.op("pe", lambda e, h=h, mc=mc, i8=i8, Pn=Pn: e.transpose(
                                    PST[:, i8 * 128:(i8 + 1) * 128], Pn[:, h, mc * 128:(mc + 1) * 128], ident[:]),
                                    reads=[Pnb, identb], **({"writes": [PSTB]} if i8 == 0 else {"acc": [PSTB]}))
                        cp("act", PT[:].rearrange("p a b -> p (a b)"), PST[:], R=[PSTB], W=[PTb])
                        for ch in range(2):
                            pb = 2 + ch
                            n4 = 0
                            for h in (2 * ch, 2 * ch + 1):
                                for mc in range(2):
                                    mm(PS[pb][:, bk * 128:(bk + 1) * 128], mvz[:, h, mc, :], PT[:, h * 2 + mc, :],
                                       start=(n4 == 0), stop=(n4 == 3), R=[mvb, PTb],
                                       **({"W": [PB[pb]]} if (bk == 0 and n4 == 0) else {"A": [PB[pb]]}))
                                    n4 += 1
                    for ch in range(2):
                        cp("dve", xoT[:, ch, :], PS[2 + ch][:, 0:TO], R=[PB[2 + ch]],
                           **({"W": [xoTb]} if ch == 0 else {"A": [xoTb]}))
                    def fin_c(dc, ypb, gpb, t=t):
                        ot, otb = gated(ypb, gpb)
                        B.dma("pool", c2_s[dc * 128:(dc + 1) * 128, t * TO:(t + 1) * TO], ot[:], otb, store=True,
                              reads=[otb], acc=[c2sb])
                    pend = None
                    for dc in range(8):
                        ypb = 4 + (dc % 2)
                        gpb = 6 + (dc % 2)
                        for ch in range(2):
                            mm(PS[ypb][:, 0:TO], wxo[:, ch, dc * 128:(dc + 1) * 128], xoT[:, ch, :],
                               start=(ch == 0), stop=(ch == 1), R=[wxob, xoTb],
                               **({"W": [PB[ypb]]} if ch == 0 else {"A": [PB[ypb]]}))
                        gate_mm(gpb, wg2, wg2b, dc, t)
                        if pend is not None:
                            fin_c(*pend)
                        pend = (dc, ypb, gpb)
                    fin_c(*pend)
                B.barrier()
            if debug:
                with contextlib.ExitStack() as sd:
                    dd = sbt(sd, "dd", [128, 8, NQ], F32); ddb = B.buf("dd")
                    for nm, src, srcb in (("c0", c0_s, c0sb), ("c2", c2_s, c2sb)):
                        B.dma("sp", dd[:], src.rearrange("(c p) n -> p c n", p=128), ddb, reads=[srcb], writes=[ddb])
                        B.dma("sp", dbg[nm].rearrange("(c p) n -> p c n", p=128), dd[:], ddb, store=True, reads=[ddb])
                    B.barrier()

            if stop_after == "P4c":
                B.barrier()
                B.emit()
                return nc
            h1sb = B.buf("h1_s"); n2sb = B.buf("n2_s")
            with contextlib.ExitStack() as sd4:
                wsbo = sbt(sd4, "wsbo", [128, 4, D], BF16); wsbob = B.buf("wsbo")
                wg1 = sbt(sd4, "wg1", [128, 8, D], BF16); wg1b = B.buf("wg1")
                wout = sbt(sd4, "wout", [128, 8, D], BF16); woutb = B.buf("wout")
                with contextlib.ExitStack() as sw:
                    new_stg(sw)
                    load_w(wsbo, wsbob, w_sbo, 4, D)
                    load_w(wg1, wg1b, w_in[:, O_G + D:O_G + 2 * D], 8, D, gcol=V_PRE)
                    load_w(wout, woutb, w_out, 8, D)
                    B.barrier()
                sbo4 = sbt(sd4, "sbo4", [128, 4, NQ], BF16); sbo4b = B.buf("sbo4")
                B.dma("sp", sbo4[:], sbo_s, sbo4b, reads=[sbosb], writes=[sbo4b])
                NCL = 3
                cl = [sbt(sd4, "cl%d" % i, [128, 2, TO], F32) for i in range(NCL)]
                clb = [B.buf("cl%d" % i) for i in range(NCL)]
                clc = {"i": 0}
                mg1 = sbt(sd4, "mg", [128, 8, TO], BF16)
                mg = [mg1, mg1]
                mgb1 = B.buf("mg")
                mgb = [mgb1, mgb1]
                mo = [sbt(sd4, "mo%d" % i, [128, 8, TO], F32) for i in range(2)]
                mob = [B.buf("mo%d" % i) for i in range(2)]
                sqm = [sbt(sd4, "sqm%d" % i, [128, TO], BF16) for i in range(2)]
                sqmb = [B.buf("sqm%d" % i) for i in range(2)]
                lnm = sbt(sd4, "lnm", [128, TO], F32); lnmb = B.buf("lnm")
                rsm1 = sbt(sd4, "rsm1", [128, TO], F32); rsm1b = B.buf("rsm1")
                rsm2 = sbt(sd4, "rsm2", [128, TO], F32); rsm2b = B.buf("rsm2")
                xr = sbt(sd4, "xr", [128, 8, BPT, 128], F32); xrb = B.buf("xr")
                n2t = sbt(sd4, "n2t", [128, 8, TO], BF16); n2tb = B.buf("n2t")
                xoh4 = xoh.rearrange("d (b t) -> d b t", t=144)

                def bd_S1(t):
                    def fin_b(dc, ci, ypb, gpb):
                        ot, otb = gated(ypb, gpb)
                        tt("pool", cl[ci][:, 0, :], cl[ci][:, 0, :], cl[ci][:, 1, :], ALU.add, R=[clb[ci]], A=[clb[ci]])
                        tt("dve", mg[t % 2][:, dc, :], ot[:], cl[ci][:, 0, :], ALU.add, R=[otb, clb[ci]],
                           **({"W": [mgb[t % 2]]} if dc == 0 else {"A": [mgb[t % 2]]}))
                    pend = None
                    for dc in range(8):
                        ci = clc["i"] % NCL
                        clc["i"] += 1
                        B.dma("sp", cl[ci][:, 0, :], c0_s[dc * 128:(dc + 1) * 128, t * TO:(t + 1) * TO], clb[ci],
                              reads=[c0sb], writes=[clb[ci]])
                        B.dma("sp", cl[ci][:, 1, :], c2_s[dc * 128:(dc + 1) * 128, t * TO:(t + 1) * TO], clb[ci],
                              reads=[c2sb], acc=[clb[ci]])
                        ypb = 4 + (dc % 2)
                        gpb = (dc % 2)
                        for c4 in range(4):
                            mm(PS[ypb][:, 0:TO], wsbo[:, c4, dc * 128:(dc + 1) * 128], sbo4[:, c4, t * TO:(t + 1) * TO],
                               start=(c4 == 0), stop=(c4 == 3), R=[wsbob, sbo4b],
                               **({"W": [PB[ypb]]} if c4 == 0 else {"A": [PB[ypb]]}))
                        gate_mm(gpb, wg1, wg1b, dc, t)
                        if pend is not None:
                            fin_b(*pend)
                        pend = (dc, ci, ypb, gpb)
                        yield
                    fin_b(*pend)

                def bd_S2(t):
                    for dc in range(8):
                        B.dma("sp", xr[:, dc, :, :], xoh4[dc * 128:(dc + 1) * 128, t * BPT:(t + 1) * BPT, 16:144], xrb,
                              **({"writes": [xrb]} if dc == 0 else {"acc": [xrb]}))
                    m_, mb_ = mo[t % 2], mob[t % 2]
                    for dc in range(8):
                        pb = 2 + (dc % 2)
                        for k in range(8):
                            mm(PS[pb][:, 0:TO], wout[:, k, dc * 128:(dc + 1) * 128], mg[t % 2][:, k, :],
                               start=(k == 0), stop=(k == 7), R=[woutb, mgb[t % 2]],
                               **({"W": [PB[pb]]} if k == 0 else {"A": [PB[pb]]}))
                        if dc > 0:
                            d1 = dc - 1
                            mm(PS[6][:, 0:TO], ones[:], sqm[d1 % 2][:], start=(d1 == 0), stop=False,
                               R=[onesb, sqmb[d1 % 2]], **({"W": [PB[6]]} if d1 == 0 else {"A": [PB[6]]}))
                        cp("dve", m_[:, dc, :], PS[pb][:, 0:TO], R=[PB[pb]], **({"W": [mb_]} if dc == 0 else {"A": [mb_]}))
                        act(sqm[dc % 2][:], m_[:, dc, :], AF.Square, R=[mb_], W=[sqmb[dc % 2]])
                    mm(PS[6][:, 0:TO], ones[:], sqm[7 % 2][:], start=False, stop=True,
                       R=[onesb, sqmb[7 % 2]], A=[PB[6]])

                def bd_S3(t):
                    m_, mb_ = mo[t % 2], mob[t % 2]
                    rstd_from(PS[6][:, 0:TO], PB[6], TO, lnm[:], lnmb, rsm1[:], rsm1b)
                    for dc in range(8):
                        stt(m_[:, dc, :], m_[:, dc, :], vec[:, V_POST + dc:V_POST + dc + 1], rsm1[:],
                            R=[mb_, vecb, rsm1b], A=[mb_])
                        tt("pool", m_[:, dc, :], m_[:, dc, :], xr[:, dc, :, :].rearrange("p b t -> p (b t)"), ALU.add,
                           R=[mb_, xrb], A=[mb_])
                        act(sqm[dc % 2][:], m_[:, dc, :], AF.Square, R=[mb_], W=[sqmb[dc % 2]])
                        if dc > 0:
                            d1 = dc - 1
                            mm(PS[7][:, 0:TO], ones[:], sqm[d1 % 2][:], start=(d1 == 0), stop=False,
                               R=[onesb, sqmb[d1 % 2]], **({"W": [PB[7]]} if d1 == 0 else {"A": [PB[7]]}))
                        yield
                    mm(PS[7][:, 0:TO], ones[:], sqm[7 % 2][:], start=False, stop=True,
                       R=[onesb, sqmb[7 % 2]], A=[PB[7]])
                    B.dma("sp", h1_s[:, t * TO:(t + 1) * TO].rearrange("(c p) n -> p c n", p=128), m_[:], mb_, store=True,
                          reads=[mb_], acc=[h1sb])
                    rstd_from(PS[7][:, 0:TO], PB[7], TO, lnm[:], lnmb, rsm2[:], rsm2b)
                    for dc in range(8):
                        stt(n2t[:, dc, :], m_[:, dc, :], vec[:, V_FPRE + dc:V_FPRE + dc + 1], rsm2[:],
                            R=[mb_, rsm2b, vecb], **({"W": [n2tb]} if dc == 0 else {"A": [n2tb]}))
                    B.dma("sp", n2_s[:, t * TO:(t + 1) * TO].rearrange("(c p) n -> p c n", p=128), n2t[:], n2tb, store=True,
                          reads=[n2tb], acc=[n2sb])

                run(bd_S1(0))
                for t in range(NT):
                    bd_S2(t)
                    if t + 1 < NT:
                        interleave(bd_S3(t), bd_S1(t + 1))
                    else:
                        run(bd_S3(t))
                B.barrier()
        if debug:
            with contextlib.ExitStack() as sd:
                dd = sbt(sd, "dd2", [128, 8, NQ], F32); ddb = B.buf("dd2")
                B.dma("sp", dd[:], h1_s.rearrange("(c p) n -> p c n", p=128), ddb, reads=[h1sb], writes=[ddb])
                B.dma("sp", dbg["h1"].rearrange("(c p) n -> p c n", p=128), dd[:], ddb, store=True, reads=[ddb])
                B.barrier()

        if stop_after == "P4":
            B.barrier()
            B.emit()
            return nc
        with contextlib.ExitStack() as s56:
            actT = sbt(s56, "actT", [128, NFC, NQ], BF16); actTb = B.buf("actT")
            wfo = sbt(s56, "wfo", [128, NFC, D], BF16); wfob = B.buf("wfo")
            with contextlib.ExitStack() as s5:
                n2 = sbt(s5, "n2", [128, 8, NQ], BF16); n2b = B.buf("n2")
                new_stg(s5)
                B.dma("sp", n2[:], n2_s.rearrange("(c p) n -> p c n", p=128), n2b, reads=[n2sb], writes=[n2b])
                wfi = [sbt(s5, "wfi%d" % i, [128, 8, 256], BF16) for i in range(2)]
                wfib = [B.buf("wfi%d" % i) for i in range(2)]
                sil = [sbt(s5, "sil%d" % i, [128, TO], F32) for i in range(2)]
                silb = [B.buf("sil%d" % i) for i in range(2)]

                def p5_loadw(f):
                    sl = f % 2
                    load_w(wfi[sl][:, :, 0:128], wfib[sl], w_fi[:, f * 128:(f + 1) * 128], 8, 128, gcol=V_FPRE, first=True)
                    load_w(wfi[sl][:, :, 128:256], wfib[sl], w_fi[:, DFF + f * 128:DFF + (f + 1) * 128], 8, 128,
                           gcol=V_FPRE, first=False)

                p5_loadw(0)
                cnt5 = 0
                for f in range(NFC):
                    if f + 1 < NFC:
                        p5_loadw(f + 1)
                    load_w(wfo[:, f:f + 1, :], wfob, w_fo[f * 128:(f + 1) * 128, :], 1, D, first=(f == 0))
                    sl = f % 2
                    for t in range(NT):
                        gp = (cnt5 % 2) * 2
                        up = gp + 1
                        si = cnt5 % 2
                        cnt5 += 1
                        for k in range(8):
                            mm(PS[gp][:, 0:TO], wfi[sl][:, k, 0:128], n2[:, k, t * TO:(t + 1) * TO],
                               start=(k == 0), stop=(k == 7), R=[wfib[sl], n2b],
                               **({"W": [PB[gp]]} if k == 0 else {"A": [PB[gp]]}))
                        for k in range(8):
                            mm(PS[up][:, 0:TO], wfi[sl][:, k, 128:256], n2[:, k, t * TO:(t + 1) * TO],
                               start=(k == 0), stop=(k == 7), R=[wfib[sl], n2b],
                               **({"W": [PB[up]]} if k == 0 else {"A": [PB[up]]}))
                        act(sil[si][:], PS[gp][:, 0:TO], AF.Silu, R=[PB[gp]], W=[silb[si]])
                        tt("dve", actT[:, f, t * TO:(t + 1) * TO], sil[si][:], PS[up][:, 0:TO], ALU.mult,
                           R=[silb[si], PB[up]], A=[actTb])
                B.barrier()
            with contextlib.ExitStack() as s6:
                ff = [sbt(s6, "ff%d" % i, [128, 8, TO], F32) for i in range(2)]
                ffb = [B.buf("ff%d" % i) for i in range(2)]
                sq6 = [sbt(s6, "sq6_%d" % i, [128, TO], BF16) for i in range(2)]
                sq6b = [B.buf("sq6_%d" % i) for i in range(2)]
                ln6 = sbt(s6, "ln6", [128, TO], F32); ln6b = B.buf("ln6")
                rs6 = sbt(s6, "rs6", [128, TO], F32); rs6b = B.buf("rs6")
                h1t = [sbt(s6, "h1t%d" % i, [128, TO], F32) for i in range(3)]
                h1tb = [B.buf("h1t%d" % i) for i in range(3)]
                outsb = B.buf("outT")
                cnt6 = {"h": 0}

                def p6_M(t):
                    f_, fb_ = ff[t % 2], ffb[t % 2]
                    ssb = 4 + (t % 2)
                    for dc in range(8):
                        pb = dc % 4
                        for f in range(NFC):
                            mm(PS[pb][:, 0:TO], wfo[:, f, dc * 128:(dc + 1) * 128], actT[:, f, t * TO:(t + 1) * TO],
                               start=(f == 0), stop=(f == NFC - 1), R=[wfob, actTb],
                               **({"W": [PB[pb]]} if f == 0 else {"A": [PB[pb]]}))
                        if dc > 0:
                            d1 = dc - 1
                            mm(PS[ssb][:, 0:TO], ones[:], sq6[d1 % 2][:], start=(d1 == 0), stop=False,
                               R=[onesb, sq6b[d1 % 2]], **({"W": [PB[ssb]]} if d1 == 0 else {"A": [PB[ssb]]}))
                        cp("dve", f_[:, dc, :], PS[pb][:, 0:TO], R=[PB[pb]], **({"W": [fb_]} if dc == 0 else {"A": [fb_]}))
                        act(sq6[dc % 2][:], f_[:, dc, :], AF.Square, R=[fb_], W=[sq6b[dc % 2]])
                        yield
                    mm(PS[ssb][:, 0:TO], ones[:], sq6[7 % 2][:], start=False, stop=True,
                       R=[onesb, sq6b[7 % 2]], A=[PB[ssb]])

                def p6_E(t):
                    f_, fb_ = ff[t % 2], ffb[t % 2]
                    ssb = 4 + (t % 2)
                    rstd_from(PS[ssb][:, 0:TO], PB[ssb], TO, ln6[:], ln6b, rs6[:], rs6b)
                    for dc in range(8):
                        hi = cnt6["h"] % 3
                        cnt6["h"] += 1
                        B.dma("sp", h1t[hi][:], h1_s[dc * 128:(dc + 1) * 128, t * TO:(t + 1) * TO], h1tb[hi],
                              reads=[h1sb], writes=[h1tb[hi]])
                        stt(f_[:, dc, :], f_[:, dc, :], vec[:, V_FPOST + dc:V_FPOST + dc + 1], rs6[:],
                            R=[fb_, vecb, rs6b], A=[fb_])
                        tt("pool", f_[:, dc, :], f_[:, dc, :], h1t[hi][:], ALU.add, R=[fb_, h1tb[hi]], A=[fb_])
                        yield
                    B.dma("sp", outT[:, t * TO:(t + 1) * TO].rearrange("(c p) n -> p c n", p=128), f_[:], fb_, store=True,
                          reads=[fb_], acc=[outsb])

                run(p6_M(0))
                for t in range(NT):
                    if t + 1 < NT:
                        interleave(p6_M(t + 1), p6_E(t))
                    else:
                        run(p6_E(t))
                B.barrier()
        B.barrier()
        B.emit()
    return nc


def _own_blocks(r, G):
    blks = []
    for g in range(G):
        blks.append(8 * g + r)
        blks.append(8 * g + 7 - r)
    return blks


def _pack_vecs(norm_mix_pre, norm_mem, norm_mix_post, norm_ffn_pre, norm_ffn_post, pool_scale):
    cols = []
    for v in (norm_mix_pre, norm_mem, norm_mix_post, norm_ffn_pre, norm_ffn_post):
        cols.append(np.asarray(v, np.float32).reshape(8, 128).T)
    cols.append(np.asarray(pool_scale, np.float32).reshape(4, 128).T)
    return np.ascontiguousarray(np.concatenate(cols, axis=1))


_PROG_CACHE = {}


def run_layer(x, mem, norm_mix_pre, w_in, w_pool_mix, pool_scale, w_pool_o, w_sb_o, norm_mem, w_mem_kv, w_x_o,
              w_out, norm_mix_post, norm_ffn_pre, w_ffn_in, w_ffn_out, norm_ffn_post, debug=False, stop_after=None):
    x = np.asarray(x, np.float32)
    mem = np.asarray(mem, np.float32)
    Bn, S, _ = x.shape
    G = S // 1024
    assert Bn == 2 and S == 1024 * G
    key = (G, debug, stop_after)
    if key not in _PROG_CACHE:
        _PROG_CACHE[key] = build_program(G, debug, stop_after)
    nc = _PROG_CACHE[key]
    f32 = lambda a: np.ascontiguousarray(np.asarray(a, np.float32))
    shared = {
        "w_in": f32(w_in[0]), "w_pmix": f32(np.asarray(w_pool_mix[0]).reshape(512, 128)), "w_po": f32(w_pool_o[0]),
        "w_sbo": f32(w_sb_o[0]), "w_kv": f32(w_mem_kv[0]), "w_xo": f32(w_x_o[0]), "w_out": f32(w_out[0]),
        "w_fi": f32(w_ffn_in[0]), "w_fo": f32(w_ffn_out[0]),
        "vecs": _pack_vecs(norm_mix_pre[0], norm_mem[0], norm_mix_post[0], norm_ffn_pre[0], norm_ffn_post[0],
                           pool_scale[0]),
    }
    xTb = [np.ascontiguousarray(x[b].T) for b in range(2)]
    memTb = [np.ascontiguousarray(mem[b].T) for b in range(2)]
    in_maps = []
    for core in range(8):
        b, r = core // 4, core % 4
        blks = _own_blocks(r, G)
        xoh = np.zeros((D, len(blks) * 144), np.float32)
        posv = np.zeros((len(blks) * 128,), np.float32)
        for n, blk in enumerate(blks):
            s0 = blk * 128
            lo = max(0, s0 - 16)
            xoh[:, n * 144 + 16 - (s0 - lo):(n + 1) * 144] = xTb[b][:, lo:s0 + 128]
            posv[n * 128:(n + 1) * 128] = np.arange(s0, s0 + 128, dtype=np.float32)
        m = dict(shared)
        m.update({"xT": xTb[b], "xoh": xoh, "pos": posv, "memT": memTb[b]})
        in_maps.append(m)
    res = run_bass_kernel_spmd(nc, in_maps, core_ids=list(range(8)))
    out = np.empty((2, S, D), np.float32)
    for core in range(8):
        b, r = core // 4, core % 4
        oT = np.asarray(res.results[core]["outT"])
        for n, blk in enumerate(_own_blocks(r, G)):
            out[b, blk * 128:(blk + 1) * 128, :] = oT[:, n * 128:(n + 1) * 128].T
    if debug:
        return out, res.results
    return out


def kernel(**inputs):
    return run_layer(**inputs)
```

```python
import contextlib
import numpy as np
import concourse.bass as bass
import concourse.mybir as mybir
from concourse.bass_utils import run_bass_kernel_spmd

F32 = mybir.dt.float32
BF16 = mybir.dt.bfloat16
I32 = mybir.dt.int32
AF = mybir.ActivationFunctionType
ALU = mybir.AluOpType
AX = mybir.AxisListType

ENGS = ("pe", "act", "dve", "pool", "sp")
SAME_ENG_RAW = True

D = 1024
DFF = 2816
NFC = DFF // 128
INW = 5376
O_Q, O_K, O_V, O_XQ, O_G = 512, 1024, 1536, 2048, 2304
EPS = 1e-6
NEG_BIG = -30000.0
V_PRE, V_MEM, V_POST, V_FPRE, V_FPOST, V_PSC, NVEC = 0, 8, 16, 24, 32, 40, 44


class Buf:
    __slots__ = ("name", "w", "r", "excl", "base")

    def __init__(self, name):
        self.name = name
        self.w = {}
        self.r = {}
        self.base = {}
        self.excl = False


class Builder:
    def __init__(self, nc, es):
        self.nc = nc
        self.es = es
        self.q = {e: [] for e in ENGS}
        self.cnt = {}
        self.seen = {e: {} for e in ENGS}
        self.semh = {}
        self.nbuf = 0
        for e in ENGS:
            if e != "sp":
                self._newsem(e)

    def _newsem(self, key):
        self.semh[key] = self.es.enter_context(self.nc.semaphore("s%d" % len(self.semh)))
        self.cnt[key] = 0

    def buf(self, name=None):
        self.nbuf += 1
        return Buf("%s#%d" % (name or "b", self.nbuf))

    def _wait(self, eng, key, val):
        if self.seen[eng].get(key, 0) >= val:
            return
        self.seen[eng][key] = val
        sem = self.semh[key]
        self.q[eng].append(lambda e, sem=sem, val=val: e.wait_ge(sem, val))

    def _deps(self, eng, reads, writes, acc):
        raw = {}
        oth = {}

        def add(dst, d):
            for k, v in d.items():
                if dst.get(k, 0) < v:
                    dst[k] = v
        for b in reads:
            add(raw, b.w)
            if b.excl:
                add(oth, b.r)
        for b in writes:
            add(oth, b.w)
            add(oth, b.r)
        for b in acc:
            add(oth, b.r)
            add(oth, b.base)
        for k, v in raw.items():
            if k == eng and (eng == "pe" or not SAME_ENG_RAW):
                continue
            self._wait(eng, k, v)
        for k, v in oth.items():
            if k == eng:
                continue
            self._wait(eng, k, v)

    def _record(self, key, val, reads, writes, acc):
        for b in reads:
            if b.r.get(key, 0) < val:
                b.r[key] = val
        for b in writes:
            b.w = {key: val}
            b.base = {key: val}
            b.r = {}
        for b in acc:
            b.w[key] = val
            b.r = {}

    def op(self, eng, fn, reads=(), writes=(), acc=()):
        self._deps(eng, reads, writes, acc)
        self.cnt[eng] += 1
        c = self.cnt[eng]
        sem = self.semh[eng]
        self.q[eng].append(lambda e, fn=fn, sem=sem: fn(e).then_inc(sem, 1))
        self._record(eng, c, reads, writes, acc)

    def dma(self, qeng, out_ap, in_ap, sb, store=False, reads=(), writes=(), acc=(), **kw):
        self._deps(qeng, reads, writes, acc)
        key = ("st" if store else "ld", sb.name, "sw" if qeng == "pool" else "hw")
        if key not in self.semh:
            self._newsem(key)
        self.cnt[key] += 16
        c = self.cnt[key]
        sem = self.semh[key]
        self.q[qeng].append(
            lambda e, o=out_ap, i=in_ap, sem=sem, kw=kw: e.dma_start(out=o, in_=i, **kw).then_inc(sem, 16))
        self._record(key, c, reads, writes, acc)

    def barrier(self):
        for e in ENGS:
            for k, c in self.cnt.items():
                if k == e or c == 0:
                    continue
                self._wait(e, k, c)

    def emit(self):
        q = self.q
        with self.nc.Block() as block:
            @block.tensor
            def _(e):
                for t in q["pe"]:
                    t(e)

            @block.scalar
            def _(e):
                for t in q["act"]:
                    t(e)

            @block.vector
            def _(e):
                for t in q["dve"]:
                    t(e)

            @block.gpsimd
            def _(e):
                for t in q["pool"]:
                    t(e)

            @block.sync
            def _(e):
                for t in q["sp"]:
                    t(e)


def build_program(G, debug=False, stop_after=None):
    S = 1024 * G
    NBLK = S // 128
    NB = 2 * G
    NQ = 128 * NB
    BPT = 4 if G >= 2 else 2
    TO = 128 * BPT
    TOH = 144 * BPT
    NT = NB // BPT
    ST = S // 512

    nc = bass.Bass("TRN2", target_bir_lowering=False)

    def din(name, shape, dt=F32):
        return nc.dram_tensor(name, list(shape), dt, kind="ExternalInput").ap()

    xT = din("xT", [D, S])
    xoh = din("xoh", [D, NB * 144])
    pos = din("pos", [NQ])
    memT = din("memT", [D, 256])
    w_in = din("w_in", [D, INW])
    w_pmix = din("w_pmix", [512, 128])
    w_po = din("w_po", [512, D])
    w_sbo = din("w_sbo", [512, D])
    w_kv = din("w_kv", [D, 512])
    w_xo = din("w_xo", [256, D])
    w_out = din("w_out", [D, D])
    w_fi = din("w_fi", [D, 2 * DFF])
    w_fo = din("w_fo", [DFF, D])
    vecs = din("vecs", [128, NVEC])
    outT = nc.dram_tensor("outT", [D, NQ], F32, kind="ExternalOutput").ap()
    kT_s = nc.dram_tensor("kT_s", [4, 128, S], BF16).ap()
    v_s = nc.dram_tensor("v_s", [4, 128, NBLK, 128], BF16).ap()
    c0_s = nc.dram_tensor("c0_s", [D, NQ], F32).ap()
    c2_s = nc.dram_tensor("c2_s", [D, NQ], F32).ap()
    h1_s = nc.dram_tensor("h1_s", [D, NQ], F32).ap()
    n2_s = nc.dram_tensor("n2_s", [D, NQ], BF16).ap()
    dbg = {}
    if debug:
        dbg["qT"] = nc.dram_tensor("dbg_qT", [128, 4, 2 * NQ], BF16, kind="ExternalOutput").ap()
        dbg["sbo"] = nc.dram_tensor("dbg_sbo", [128, 4, NQ], BF16, kind="ExternalOutput").ap()
        dbg["kT"] = nc.dram_tensor("dbg_kT", [4, 128, S], BF16, kind="ExternalOutput").ap()
        dbg["v"] = nc.dram_tensor("dbg_v", [4, 128, NBLK, 128], BF16, kind="ExternalOutput").ap()
        dbg["c0"] = nc.dram_tensor("dbg_c0", [D, NQ], F32, kind="ExternalOutput").ap()
        dbg["c2"] = nc.dram_tensor("dbg_c2", [D, NQ], F32, kind="ExternalOutput").ap()
        dbg["h1"] = nc.dram_tensor("dbg_h1", [D, NQ], F32, kind="ExternalOutput").ap()

    with contextlib.ExitStack() as es:
        B = Builder(nc, es)

        def sbt(scope, name, shape, dt):
            return scope.enter_context(nc.sbuf_tensor(name, list(shape), dt))

        def mm(out, lhsT, rhs, start, stop, R, W=(), A=(), sgc=False):
            B.op("pe", lambda e: e.matmul(out, lhsT, rhs, start=start, stop=stop, skip_group_check=sgc),
                 reads=R, writes=W, acc=A)

        def act(out, in_, func, R, W=(), A=(), **kw):
            B.op("act", lambda e: e.activation(out=out, in_=in_, func=func, **kw), reads=R, writes=W, acc=A)

        def tt(eng, out, in0, in1, op, R, W=(), A=()):
            B.op(eng, lambda e: e.tensor_tensor(out=out, in0=in0, in1=in1, op=op), reads=R, writes=W, acc=A)

        def ts(eng, out, in0, s1, s2, op0, op1, R, W=(), A=()):
            if op1 is None:
                B.op(eng, lambda e: e.tensor_scalar(out=out, in0=in0, scalar1=s1, scalar2=s2, op0=op0),
                     reads=R, writes=W, acc=A)
            else:
                B.op(eng, lambda e: e.tensor_scalar(out=out, in0=in0, scalar1=s1, scalar2=s2, op0=op0, op1=op1),
                     reads=R, writes=W, acc=A)

        def cp(eng, out, in_, R, W=(), A=()):
            if eng == "act":
                act(out, in_, AF.Copy, R, W, A)
            else:
                B.op(eng, lambda e: e.tensor_copy(out=out, in_=in_), reads=R, writes=W, acc=A)

        PSALL = es.enter_context(nc.psum_tensor("psall", [128, 4096], F32))
        PS = [PSALL[:, i * 512:(i + 1) * 512] for i in range(8)]
        PB = [B.buf("ps%d" % i) for i in range(8)]
        for b_ in PB:
            b_.excl = True
        PST = PS[7].bitcast(BF16)
        PSTB = PB[7]

        gs = es
        vec = sbt(gs, "vec", [128, NVEC], F32); vecb = B.buf("vec")
        ones = sbt(gs, "ones", [128, 128], BF16); onesb = B.buf("ones")
        nTin = sbt(gs, "nTin", [128, 128], BF16); nTinb = B.buf("nTin")
        nOnes = sbt(gs, "nOnes", [128, 128], BF16); nOnesb = B.buf("nOnes")
        nBigI = sbt(gs, "nBigI", [128, 128], BF16); nBigIb = B.buf("nBigI")
        ident = sbt(gs, "ident", [128, 128], BF16); identb = B.buf("ident")
        dif_i = sbt(gs, "dif_i", [128, 128], I32); difib = B.buf("dif_i")
        dif_f = sbt(gs, "dif_f", [128, 128], F32); diffb = B.buf("dif_f")
        kp8_i = sbt(gs, "kp8_i", [128, 8], I32); kp8ib = B.buf("kp8i")
        kp8 = sbt(gs, "kp8", [128, 8], F32); kp8b = B.buf("kp8")
        qpos = sbt(gs, "qpos", [128, NQ], F32); qposb = B.buf("qpos")
        notM = sbt(gs, "notM", [128, 8, 2, 256], BF16); notMb = B.buf("notM")
        NSTG = 2
        st_state = {"i": 0, "e": 0, "n": 0}

        def new_stg(scope):
            st_state["n"] += 1
            st_state["stg"] = [sbt(scope, "stg%d_%d" % (st_state["n"], i), [128, 1536], F32) for i in range(NSTG)]
            st_state["stgb"] = [B.buf("stg%d" % i) for i in range(NSTG)]

        B.dma("sp", vec[:], vecs, vecb, writes=[vecb])
        B.dma("sp", qpos[:], pos.partition_broadcast(128), qposb, writes=[qposb])
        B.op("dve", lambda e: e.memset(ones[:], 1.0), writes=[onesb])
        B.op("dve", lambda e: e.memset(nOnes[:], -1.0), writes=[nOnesb])
        B.op("pool", lambda e: e.iota(dif_i[:], pattern=[[-1, 128]], base=0, channel_multiplier=1), writes=[difib])
        cp("dve", dif_f[:], dif_i[:], R=[difib], W=[diffb])
        ts("dve", nTin[:], dif_f[:], 0.0, -1.0, ALU.is_ge, ALU.mult, R=[diffb], W=[nTinb])
        ts("dve", nBigI[:], dif_f[:], 0.0, NEG_BIG, ALU.is_equal, ALU.mult, R=[diffb], W=[nBigIb])
        ts("dve", ident[:], dif_f[:], 0.0, None, ALU.is_equal, None, R=[diffb], W=[identb])
        B.op("pool", lambda e: e.iota(kp8_i[:], pattern=[[128, 8]], base=0, channel_multiplier=1), writes=[kp8ib])
        cp("dve", kp8[:], kp8_i[:], R=[kp8ib], W=[kp8b])
        for jj in range(8):
            for h in range(2):
                ts("dve", notM[:, jj, h, :], qpos[:, 0:256], kp8[:, jj:jj + 1], None, ALU.is_le, None,
                   R=[qposb, kp8b], **({"W": [notMb]} if (jj == 0 and h == 0) else {"A": [notMb]}))

        def load_w(dst, dstb, src, kc, ncols, gcol=None, first=True):
            kg = max(1, min(kc, 1536 // ncols))
            k0 = 0
            while k0 < kc:
                kn = min(kg, kc - k0)
                i = st_state["i"] % NSTG
                st_state["i"] += 1
                stg, stgb = st_state["stg"], st_state["stgb"]
                sv = stg[i][:, 0:kn * ncols].rearrange("p (k n) -> p k n", k=kn)
                B.dma("sp", sv, src[k0 * 128:(k0 + kn) * 128, :].rearrange("(k p) n -> p k n", p=128), stgb[i],
                      writes=[stgb[i]])
                eng = "dve" if st_state["e"] % 2 == 0 else "act"
                st_state["e"] += 1
                kw = {"W": [dstb]} if (first and k0 == 0) else {"A": [dstb]}
                cp(eng, dst[:, k0:k0 + kn, :], sv, R=[stgb[i]], **kw)
                k0 += kn

        def run(gen):
            for _ in gen:
                pass

        def interleave(*gens):
            gens = list(gens)
            while gens:
                for g_ in list(gens):
                    try:
                        next(g_)
                    except StopIteration:
                        gens.remove(g_)

        def stt(out, in0, scal, in1, R, W=(), A=()):
            B.op("dve", lambda e: e.scalar_tensor_tensor(out=out, in0=in0, scalar=scal, in1=in1,
                                                         op0=ALU.mult, op1=ALU.mult), reads=R, writes=W, acc=A)

        def rstd_from(ss_ps, ss_b, ncols, ln_t, ln_b, out_t, out_b, out_is_write=True):
            act(ln_t, ss_ps, AF.Ln, R=[ss_b], W=[ln_b], scale=1.0 / D, bias=EPS)
            act(out_t, ln_t, AF.Exp, R=[ln_b], W=[out_b], scale=-0.5)

        s13 = contextlib.ExitStack()
        es.enter_context(s13)
        NGT = BPT // 2
        Qz = sbt(s13, "Qz", [128, 4, G, 2, 256], BF16); Qzb = B.buf("Qz")
        B.op("pool", lambda e: e.memset(Qz[:].rearrange("p a g h q -> p (a g h q)"), 0.0), writes=[Qzb])
        wqkv = sbt(s13, "wqkv", [128, 8, 1536], BF16); wqkvb = B.buf("wqkv")

        xn_s = nc.dram_tensor("xn_s", [128, 8, NB * 144], BF16).ap()
        xnsb = B.buf("xn_s")
        with contextlib.ExitStack() as s1:
            new_stg(s1)
            load_w(wqkv, wqkvb, w_in[:, O_Q:O_Q + 1536], 8, 1536)
            xo_t = [sbt(s1, "xo_t%d" % i, [128, 8, TOH], F32) for i in range(2)]
            xo_b = [B.buf("xo_t%d" % i) for i in range(2)]
            sq1 = sbt(s1, "sq1", [128, 8, TOH], BF16); sq1b = B.buf("sq1")
            ln1 = sbt(s1, "ln1", [128, TOH], F32); ln1b = B.buf("ln1")
            rs1 = sbt(s1, "rs1", [128, TOH], F32); rs1b = B.buf("rs1")
            xn1 = [sbt(s1, "xn1_%d" % i, [128, 8, BPT, 144], BF16) for i in range(2)]
            xn1b = [B.buf("xn1_%d" % i) for i in range(2)]
            HB = TOH // 2
            def p1_norm(t):
                sl = t % 2
                B.dma("sp", xo_t[sl][:], xoh[:, t * TOH:(t + 1) * TOH].rearrange("(c p) n -> p c n", p=128),
                      xo_b[sl], writes=[xo_b[sl]])
                act(sq1[:], xo_t[sl][:], AF.Square, R=[xo_b[sl]], W=[sq1b])
                for hf in range(2):
                    for k in range(8):
                        mm(PS[hf][:, 0:HB], ones[:], sq1[:, k, hf * HB:(hf + 1) * HB], start=(k == 0), stop=(k == 7),
                           R=[onesb, sq1b], **({"W": [PB[hf]]} if k == 0 else {"A": [PB[hf]]}))
                for hf in range(2):
                    act(ln1[:, hf * HB:(hf + 1) * HB], PS[hf][:, 0:HB], AF.Ln, R=[PB[hf]],
                        **({"W": [ln1b]} if hf == 0 else {"A": [ln1b]}), scale=1.0 / D, bias=EPS)
                act(rs1[:], ln1[:], AF.Exp, R=[ln1b], W=[rs1b], scale=-0.5)
                xnv = xn1[sl][:].rearrange("p c b t -> p c (b t)")
                for k in range(8):
                    stt(xnv[:, k, :], xo_t[sl][:, k, :], vec[:, V_PRE + k:V_PRE + k + 1], rs1[:],
                        R=[xo_b[sl], rs1b, vecb], **({"W": [xn1b[sl]]} if k == 0 else {"A": [xn1b[sl]]}))
                B.dma("pool", xn_s[:, :, t * TOH:(t + 1) * TOH], xnv, xn1b[sl], store=True,
                      reads=[xn1b[sl]], acc=[xnsb])

            def p1_q(t):
                sl = t % 2
                for c4 in range(4):
                    pb = 2 + (c4 % 2)
                    for k in range(8):
                        mm(PS[pb][:, 0:TO].rearrange("p (b t) -> p b t", b=BPT),
                           wqkv[:, k, c4 * 128:(c4 + 1) * 128], xn1[sl][:, k, :, 16:144],
                           start=(k == 0), stop=(k == 7), R=[wqkvb, xn1b[sl]],
                           **({"W": [PB[pb]]} if k == 0 else {"A": [PB[pb]]}))
                    for h in range(2):
                        act(Qz[h * 64:(h + 1) * 64, c4, t * NGT:(t + 1) * NGT, h, :],
                            PS[pb][h * 64:(h + 1) * 64, 0:TO].rearrange("p (g q) -> p g q", g=NGT), AF.Copy,
                            R=[PB[pb]], A=[Qzb], scale=0.125)

            p1_norm(0)
            for t in range(NT):
                if t + 1 < NT:
                    p1_norm(t + 1)
                p1_q(t)
        B.barrier()
        if debug:
            B.dma("sp", dbg["qT"], Qz[:].rearrange("p a g h q -> p a (g h q)"), Qzb, store=True, reads=[Qzb])
        if stop_after == "P1":
            B.barrier()
            B.emit()
            return nc

        kTsb = B.buf("kT_s"); vsb = B.buf("v_s")
        with contextlib.ExitStack() as s2:
            x_t = [sbt(s2, "x_t%d" % i, [128, 8, 512], F32) for i in range(3)]
            x_b = [B.buf("x_t%d" % i) for i in range(3)]
            sq2 = [sbt(s2, "sq2_%d" % i, [128, 8, 512], BF16) for i in range(2)]
            sq2b = [B.buf("sq2_%d" % i) for i in range(2)]
            ln2 = sbt(s2, "ln2", [128, 512], F32); ln2b = B.buf("ln2")
            rs2 = sbt(s2, "rs2", [128, 512], F32); rs2b = B.buf("rs2")
            xn2 = [sbt(s2, "xn2_%d" % i, [128, 8, 512], BF16) for i in range(2)]
            xn2b = [B.buf("xn2_%d" % i) for i in range(2)]
            kst = [sbt(s2, "kst%d" % i, [128, 4, 512], BF16) for i in range(2)]
            kstb = [B.buf("kst%d" % i) for i in range(2)]
            vst = [sbt(s2, "vst%d" % i, [128, 4, 512], BF16) for i in range(2)]
            vstb = [B.buf("vst%d" % i) for i in range(2)]

            def p2_load(t):
                B.dma("sp", x_t[t % 3][:], xT[:, t * 512:(t + 1) * 512].rearrange("(c p) n -> p c n", p=128),
                      x_b[t % 3], writes=[x_b[t % 3]])

            def p2_sq(t):
                sl = t % 2
                act(sq2[sl][:], x_t[t % 3][:], AF.Square, R=[x_b[t % 3]], W=[sq2b[sl]])

            def p2_norm(t):
                sl = t % 2
                for k in range(8):
                    mm(PS[0][:], ones[:], sq2[sl][:, k, :], start=(k == 0), stop=(k == 7), R=[onesb, sq2b[sl]],
                       **({"W": [PB[0]]} if k == 0 else {"A": [PB[0]]}))
                rstd_from(PS[0][:], PB[0], 512, ln2[:], ln2b, rs2[:], rs2b)
                for k in range(8):
                    stt(xn2[sl][:, k, :], x_t[t % 3][:, k, :], vec[:, V_PRE + k:V_PRE + k + 1], rs2[:],
                        R=[x_b[t % 3], rs2b, vecb], **({"W": [xn2b[sl]]} if k == 0 else {"A": [xn2b[sl]]}))

            def p2_kv(t):
                sl = t % 2
                for c4 in range(4):
                    pb = 1 + (c4 % 3)
                    for k in range(8):
                        mm(PS[pb][:], wqkv[:, k, 512 + c4 * 128:512 + (c4 + 1) * 128], xn2[sl][:, k, :],
                           start=(k == 0), stop=(k == 7), R=[wqkvb, xn2b[sl]],
                           **({"W": [PB[pb]]} if k == 0 else {"A": [PB[pb]]}))
                    cp("dve", kst[sl][:, c4, :], PS[pb][:], R=[PB[pb]],
                       **({"W": [kstb[sl]]} if c4 == 0 else {"A": [kstb[sl]]}))
                B.dma("pool", kT_s[:, :, t * 512:(t + 1) * 512].rearrange("c p s -> p c s"), kst[sl][:], kstb[sl],
                      store=True, reads=[kstb[sl]], acc=[kTsb])
                for bk in range(4):
                    pb = 4 + (bk % 3)
                    for k in range(8):
                        mm(PS[pb][:], xn2[sl][:, k, bk * 128:(bk + 1) * 128], wqkv[:, k, 1024:1536],
                           start=(k == 0), stop=(k == 7), R=[wqkvb, xn2b[sl]],
                           **({"W": [PB[pb]]} if k == 0 else {"A": [PB[pb]]}))
                    cp("dve", vst[sl][:, bk, :], PS[pb][:], R=[PB[pb]],
                       **({"W": [vstb[sl]]} if bk == 0 else {"A": [vstb[sl]]}))
                for c4 in range(4):
                    B.dma("pool", v_s[c4, :, 4 * t:4 * t + 4, :], vst[sl][:, :, c4 * 128:(c4 + 1) * 128], vstb[sl],
                          store=True, reads=[vstb[sl]], acc=[vsb])

            p2_load(0)
            p2_sq(0)
            p2_norm(0)
            if ST > 1:
                p2_load(1)
                p2_sq(1)
            for t in range(ST):
                if t + 2 < ST:
                    p2_load(t + 2)
                    p2_sq(t + 2)
                if t + 1 < ST:
                    p2_norm(t + 1)
                p2_kv(t)
        B.barrier()
        if debug:
            with contextlib.ExitStack() as sd:
                dk = sbt(sd, "dk", [128, S], BF16); dkb = B.buf("dk")
                dv = sbt(sd, "dv", [128, NBLK, 128], BF16); dvb = B.buf("dv")
                for c4 in range(4):
                    B.dma("sp", dk[:], kT_s[c4], dkb, reads=[kTsb], writes=[dkb])
                    B.dma("sp", dbg["kT"][c4], dk[:], dkb, store=True, reads=[dkb])
                    B.dma("sp", dv[:], v_s[c4], dvb, reads=[vsb], writes=[dvb])
                    B.dma("sp", dbg["v"][c4], dv[:], dvb, store=True, reads=[dvb])
                B.barrier()

        if stop_after == "P2":
            B.barrier()
            B.emit()
            return nc
        s34 = contextlib.ExitStack()
        with contextlib.ExitStack() as s3:
            sbo = sbt(s3, "sbo", [128, 4, NQ], BF16); sbob = B.buf("sbo")
            kt = [sbt(s3, "kt%d" % i, [128, S], BF16) for i in range(2)]
            ktb = [B.buf("kt%d" % i) for i in range(2)]
            vt = [sbt(s3, "vt%d" % i, [128, NBLK, 128], BF16) for i in range(2)]
            vtb = [B.buf("vt%d" % i) for i in range(2)]
            NE = 2
            e_t = [sbt(s3, "e_t%d" % i, [128, 1024], F32) for i in range(NE)]
            e_b = [B.buf("e_t%d" % i) for i in range(NE)]
            NSP = 3
            sp_t = [sbt(s3, "sp_t%d" % i, [128, 1024], BF16) for i in range(NSP)]
            sp_b = [B.buf("sp_t%d" % i) for i in range(NSP)]
            a_t = [sbt(s3, "a_t%d" % i, [128, 1024], BF16) for i in range(NSP)]
            a_b = [B.buf("a_t%d" % i) for i in range(NSP)]
            R_t = [sbt(s3, "R_t%d" % i, [128, 512], F32) for i in range(2)]
            R_b = [B.buf("R_t%d" % i) for i in range(2)]
            Rb_t = [sbt(s3, "Rb_t%d" % i, [128, 512], BF16) for i in range(4)]
            Rb_b = [B.buf("Rb_t%d" % i) for i in range(4)]
            NZP = 3

            def p3_load(c):
                B.dma("sp", kt[c % 2][:], kT_s[c], ktb[c % 2], reads=[kTsb], writes=[ktb[c % 2]])
                B.dma("sp", vt[c % 2][:], v_s[c], vtb[c % 2], reads=[vsb], writes=[vtb[c % 2]])

            pairs = []
            chain_id = 0
            for c in range(4):
                for g in range(G):
                    nj = 8 * g + 8
                    for pi in range(nj // 2):
                        jh = nj - 1 - 2 * pi
                        pairs.append(dict(c=c, g=g, jh=jh, jl=jh - 1, pi=pi, last=(jh == 1), chain=chain_id))
                    chain_id += 1
            for s_i, T in enumerate(pairs):
                T["zs"] = s_i % NZP
                T["sl3"] = s_i % NSP
                T["esl"] = s_i % NE
                T["ob"] = 6 + (T["chain"] % 2)
                T["rs"] = T["chain"] % 2

            def st_P1(T):
                c, g = T["c"], T["g"]
                qv = Qz[:, c, g, :, :].rearrange("p h q -> p (h q)")
                for i, j in enumerate((T["jh"], T["jl"])):
                    bk_ = 2 * T["zs"] + i
                    jj = j - 8 * g
                    mm(PS[bk_][:], kt[c % 2][:, j * 128:(j + 1) * 128], qv, start=True, stop=(jj < 0),
                       R=[ktb[c % 2], Qzb], W=[PB[bk_]])
                    if jj >= 0:
                        mm(PS[bk_][:], nBigI[:], notM[:, jj, :, :].rearrange("p h q -> p (h q)"),
                           start=False, stop=True, R=[nBigIb, notMb], A=[PB[bk_]])

            def zpair(T):
                zs = T["zs"]
                return PSALL[:, zs * 1024:(zs + 1) * 1024], [PB[2 * zs], PB[2 * zs + 1]]

            def bcols(ap):
                return ap.rearrange("p (j h q) -> p j h q", j=2, h=2)[:, :, :, 128:256]

            def bonly(T):
                return T["jl"] - 8 * T["g"] >= 4

            def st_A1a(T):
                zap, zbufs = zpair(T)
                e_ = e_t[T["esl"]]; eb = e_b[T["esl"]]
                if bonly(T):
                    act(bcols(e_[:]), bcols(zap), AF.Exp, R=zbufs, W=[eb])
                else:
                    act(e_[:], zap, AF.Exp, R=zbufs, W=[eb])

            def st_A1b(T):
                e_ = e_t[T["esl"]]; eb = e_b[T["esl"]]
                sp_ = sp_t[T["sl3"]]; spb = sp_b[T["sl3"]]
                if bonly(T):
                    B.op("pool", lambda e: e.memset(sp_[:], 0.0), writes=[spb])
                    act(bcols(sp_[:]), bcols(e_[:]), AF.Ln, R=[eb], A=[spb], bias=1.0)
                else:
                    act(sp_[:], e_[:], AF.Ln, R=[eb], W=[spb], bias=1.0)

            def st_P2(T):
                b0 = 2 * T["zs"]; b1 = b0 + 1
                sp_ = sp_t[T["sl3"]]; spb = sp_b[T["sl3"]]
                first = (T["pi"] == 0)
                rbi = (T["chain"] % 2) * 2 + (T["pi"] % 2)
                mm(PS[b0][:], nTin[:], sp_[:, 0:512], start=False, stop=first, R=[nTinb, spb], A=[PB[b0]], sgc=True)
                if not first:
                    mm(PS[b0][:], nOnes[:], Rb_t[rbi][:], start=False, stop=True, R=[nOnesb, Rb_b[rbi]], A=[PB[b0]],
                       sgc=True)
                mm(PS[b1][:], nTin[:], sp_[:, 512:1024], start=False, stop=False, R=[nTinb, spb], A=[PB[b1]], sgc=True)
                mm(PS[b1][:], nOnes[:], sp_[:, 0:512], start=False, stop=first, R=[nOnesb, spb], A=[PB[b1]], sgc=True)
                if not first:
                    mm(PS[b1][:], nOnes[:], Rb_t[rbi][:], start=False, stop=True, R=[nOnesb, Rb_b[rbi]], A=[PB[b1]],
                       sgc=True)

            def st_G(T):
                if T["last"]:
                    return
                sp_ = sp_t[T["sl3"]]; spb = sp_b[T["sl3"]]
                R_ = R_t[T["rs"]]; Rbuf = R_b[T["rs"]]
                rbn = (T["chain"] % 2) * 2 + ((T["pi"] + 1) % 2)
                if T["pi"] == 0:
                    tt("dve", R_[:], sp_[:, 0:512], sp_[:, 512:1024], ALU.add, R=[spb], W=[Rbuf])
                else:
                    tt("dve", R_[:], R_[:], sp_[:, 0:512], ALU.add, R=[spb, Rbuf], A=[Rbuf])
                    tt("dve", R_[:], R_[:], sp_[:, 512:1024], ALU.add, R=[spb, Rbuf], A=[Rbuf])
                cp("dve", Rb_t[rbn][:], R_[:], R=[Rbuf], W=[Rb_b[rbn]])

            def st_A2(T):
                zap, zbufs = zpair(T)
                a_ = a_t[T["sl3"]]; ab = a_b[T["sl3"]]
                if bonly(T):
                    B.op("pool", lambda e: e.memset(a_[:], 0.0), writes=[ab])
                    act(bcols(a_[:]), bcols(zap), AF.Exp, R=zbufs, A=[ab])
                else:
                    act(a_[:], zap, AF.Exp, R=zbufs, W=[ab])

            def st_P3(T):
                c, g = T["c"], T["g"]
                a_ = a_t[T["sl3"]]; ab = a_b[T["sl3"]]
                ob = T["ob"]
                first = (T["pi"] == 0)
                mm(PS[ob][:], vt[c % 2][:, T["jh"], :], a_[:, 0:512], start=first, stop=False, R=[vtb[c % 2], ab],
                   **({"W": [PB[ob]]} if first else {"A": [PB[ob]]}))
                mm(PS[ob][:], vt[c % 2][:, T["jl"], :], a_[:, 512:1024], start=False, stop=T["last"],
                   R=[vtb[c % 2], ab], A=[PB[ob]])
                if T["last"]:
                    q0 = g * 256
                    for h in range(2):
                        cp("dve", sbo[h * 64:(h + 1) * 64, c, q0:q0 + 256],
                           PS[ob][h * 64:(h + 1) * 64, h * 256:(h + 1) * 256], R=[PB[ob]], A=[sbob])

            p3_load(0)
            n_t = len(pairs)
            for s_i in range(n_t + 2):
                if 0 <= s_i - 2 < n_t:
                    T2 = pairs[s_i - 2]
                    if T2["g"] == 0 and T2["pi"] == 0 and T2["c"] + 1 < 4:
                        p3_load(T2["c"] + 1)
                if s_i < n_t:
                    T = pairs[s_i]
                    st_P1(T)
                    st_A1a(T)
                    st_A1b(T)
                if 0 <= s_i - 1 < n_t:
                    T1 = pairs[s_i - 1]
                    st_P2(T1)
                    st_G(T1)
                    st_A2(T1)
                if 0 <= s_i - 2 < n_t:
                    st_P3(pairs[s_i - 2])
            B.barrier()
            if debug:
                B.dma("sp", dbg["sbo"], sbo[:], sbob, store=True, reads=[sbob])
                B.barrier()
            if stop_after == "P3":
                B.barrier()
                B.emit()
                return nc
            sbo_s = nc.dram_tensor("sbo_s", [128, 4, NQ], BF16).ap()
            sbosb = B.buf("sbo_s")
            B.dma("sp", sbo_s, sbo[:], sbob, store=True, reads=[sbob], writes=[sbosb])
            B.barrier()
        s13.close()

        with contextlib.ExitStack() as s4:
            xn = sbt(s4, "xn", [128, 8, NB, 144], BF16); xnb = B.buf("xn")
            B.dma("sp", xn[:].rearrange("p c b t -> p c (b t)"), xn_s, xnb, reads=[xnsb], writes=[xnb])
            NOUT = 3
            o_t = [sbt(s4, "o_t%d" % i, [128, TO], F32) for i in range(NOUT)]
            o_b = [B.buf("o_t%d" % i) for i in range(NOUT)]
            sg_t = [sbt(s4, "sg_t%d" % i, [128, TO], F32) for i in range(2)]
            sg_b = [B.buf("sg_t%d" % i) for i in range(2)]
            cnt4 = {"o": 0, "sg": 0}

            def gate_mm(pb, wg, wgb, dc, t):
                for k in range(8):
                    mm(PS[pb][:, 0:TO].rearrange("p (b t) -> p b t", b=BPT),
                       wg[:, k, dc * 128:(dc + 1) * 128], xn[:, k, t * BPT:(t + 1) * BPT, 16:144],
                       start=(k == 0), stop=(k == 7), R=[wgb, xnb],
                       **({"W": [PB[pb]]} if k == 0 else {"A": [PB[pb]]}))

            def gated(ypb, gpb):
                si = cnt4["sg"] % 2; cnt4["sg"] += 1
                oi = cnt4["o"] % NOUT; cnt4["o"] += 1
                act(sg_t[si][:], PS[gpb][:, 0:TO], AF.Sigmoid, R=[PB[gpb]], W=[sg_b[si]])
                tt("dve", o_t[oi][:], sg_t[si][:], PS[ypb][:, 0:TO], ALU.mult, R=[sg_b[si], PB[ypb]], W=[o_b[oi]])
                return o_t[oi], o_b[oi]

            c0sb = B.buf("c0_s")
            with contextlib.ExitStack() as sa:
                wpi = sbt(sa, "wpi", [128, 8, 512], BF16); wpib = B.buf("wpi")
                wmix = sbt(sa, "wmix", [128, 4, 128], BF16); wmixb = B.buf("wmix")
                wpo = sbt(sa, "wpo", [128, 4, D], BF16); wpob = B.buf("wpo")
                wg0 = sbt(sa, "wg0", [128, 8, D], BF16); wg0b = B.buf("wg0")
                new_stg(sa)
                load_w(wpi, wpib, w_in[:, 0:512], 8, 512, gcol=V_PRE)
                load_w(wmix, wmixb, w_pmix, 4, 128)
                load_w(wpo, wpob, w_po, 4, D)
                load_w(wg0, wg0b, w_in[:, O_G:O_G + D], 8, D, gcol=V_PRE)
                pa = [sbt(sa, "pa%d" % i, [128, BPT, 144], F32) for i in range(3)]
                pab = [B.buf("pa%d" % i) for i in range(3)]
                icn = sbt(sa, "icn", [128, 4, TO], F32); icnb = B.buf("icn")
                mixd = sbt(sa, "mixd", [128, 4, TO], BF16); mixdb = B.buf("mixd")
                pm = [sbt(sa, "pm%d" % i, [128, 4, TO], BF16) for i in range(2)]
                pmb = [B.buf("pm%d" % i) for i in range(2)]
                tmpa = sbt(sa, "tmpa", [128, BPT, 128], F32); tmpab = B.buf("tmpa")
                HB = TOH // 2
                HBK = BPT // 2

                def pa_pool(t):
                    for gi in range(4):
                        ts("dve", icn[:, gi, :], qpos[:, t * TO:(t + 1) * TO], 1.0, float(2 << gi), ALU.add, ALU.min,
                           R=[qposb], **({"W": [icnb]} if gi == 0 else {"A": [icnb]}))
                    B.op("dve", lambda e: e.reciprocal(out=icn[:], in_=icn[:]), reads=[icnb], writes=[icnb])
                    for gi in range(4):
                        p0, p0b = pa[0], pab[0]
                        for hf in range(2):
                            pb = hf
                            for k in range(8):
                                mm(PS[pb][:, 0:HB].rearrange("p (b t) -> p b t", b=HBK),
                                   wpi[:, k, gi * 128:(gi + 1) * 128],
                                   xn[:, k, t * BPT + hf * HBK:t * BPT + (hf + 1) * HBK, :],
                                   start=(k == 0), stop=(k == 7), R=[wpib, xnb],
                                   **({"W": [PB[pb]]} if k == 0 else {"A": [PB[pb]]}))
                            cp("act", p0[:, hf * HBK:(hf + 1) * HBK, :],
                               PS[pb][:, 0:HB].rearrange("p (b t) -> p b t", b=HBK), R=[PB[pb]],
                               **({"W": [p0b]} if hf == 0 else {"A": [p0b]}))
                        cur, curb = p0, p0b
                        pp = 1
                        dsh = 1
                        lo = 0
                        for step in range(gi + 1):
                            nxt, nxtb = pa[pp], pab[pp]
                            lo = lo + dsh
                            tt("dve" if step % 2 == 0 else "pool", nxt[:, :, lo:144], cur[:, :, lo:144],
                               cur[:, :, lo - dsh:144 - dsh], ALU.add, R=[curb], W=[nxtb])
                            cur, curb = nxt, nxtb
                            pp = 3 - pp
                            dsh *= 2
                        tt("dve", tmpa[:], cur[:, :, 16:144], icn[:, gi, :].rearrange("p (b t) -> p b t", b=BPT),
                           ALU.mult, R=[curb, icnb], W=[tmpab])
                        tt("dve", mixd[:, gi, :].rearrange("p (b t) -> p b t", b=BPT), tmpa[:], p0[:, :, 16:144],
                           ALU.subtract, R=[tmpab, p0b], **({"W": [mixdb]} if gi == 0 else {"A": [mixdb]}))
                        yield
                    for gi in range(4):
                        pb = 2 + (gi % 2)
                        mm(PS[pb][:, 0:TO], wmix[:, gi, :], mixd[:, gi, :], start=True, stop=True,
                           R=[wmixb, mixdb], W=[PB[pb]])
                        ts("dve", pm[t % 2][:, gi, :], PS[pb][:, 0:TO], vec[:, V_PSC + gi:V_PSC + gi + 1], None,
                           ALU.mult, None, R=[PB[pb], vecb], **({"W": [pmb[t % 2]]} if gi == 0 else {"A": [pmb[t % 2]]}))
                    yield

                def pa_proj(t):
                    def fin(dc, ypb, gpb):
                        ot, otb = gated(ypb, gpb)
                        B.dma("sp", c0_s[dc * 128:(dc + 1) * 128, t * TO:(t + 1) * TO], ot[:], otb, store=True,
                              reads=[otb], acc=[c0sb])
                    pend = None
                    for dc in range(8):
                        ypb = 4 + (dc % 2)
                        gpb = 6 + (dc % 2)
                        for gi in range(4):
                            mm(PS[ypb][:, 0:TO], wpo[:, gi, dc * 128:(dc + 1) * 128], pm[t % 2][:, gi, :],
                               start=(gi == 0), stop=(gi == 3), R=[wpob, pmb[t % 2]],
                               **({"W": [PB[ypb]]} if gi == 0 else {"A": [PB[ypb]]}))
                        gate_mm(gpb, wg0, wg0b, dc, t)
                        if pend is not None:
                            fin(*pend)
                        pend = (dc, ypb, gpb)
                        yield
                    fin(*pend)

                run(pa_pool(0))
                for t in range(NT):
                    if t + 1 < NT:
                        interleave(pa_proj(t), pa_pool(t + 1))
                    else:
                        run(pa_proj(t))
                B.barrier()

            wsbo = sbt(s4, "wsbo", [128, 4, D], BF16); wsbob = B.buf("wsbo")
            wg1 = sbt(s4, "wg1", [128, 8, D], BF16); wg1b = B.buf("wg1")
            wout = sbt(s4, "wout", [128, 8, D], BF16); woutb = B.buf("wout")

            c2sb = B.buf("c2_s")
            with contextlib.ExitStack() as sc:
                wxq = sbt(sc, "wxq", [128, 8, 256], BF16); wxqb = B.buf("wxq")
                wkv = sbt(sc, "wkv", [128, 8, 512], BF16); wkvb = B.buf("wkv")
                wxo = sbt(sc, "wxo", [128, 2, D], BF16); wxob = B.buf("wxo")
                wg2 = sbt(sc, "wg2", [128, 8, D], BF16); wg2b = B.buf("wg2")
                new_stg(sc)
                load_w(wkv, wkvb, w_kv, 8, 512, gcol=V_MEM)
                load_w(wxq, wxqb, w_in[:, O_XQ:O_XQ + 256], 8, 256, gcol=V_PRE)
                load_w(wxo, wxob, w_xo, 2, D)
                load_w(wg2, wg2b, w_in[:, O_G + 2 * D:O_G + 3 * D], 8, D, gcol=V_PRE)
                m_t = sbt(sc, "m_t", [128, 8, 256], F32); m_b = B.buf("m_t")
                msq = sbt(sc, "msq", [128, 8, 256], BF16); msqb = B.buf("msq")
                mln = sbt(sc, "mln", [128, 256], F32); mlnb = B.buf("mln")
                mrs = sbt(sc, "mrs", [128, 256], F32); mrsb = B.buf("mrs")
                mn = sbt(sc, "mn", [128, 8, 256], BF16); mnb = B.buf("mn")
                mkT = sbt(sc, "mkT", [128, 2, 256], BF16); mkTb = B.buf("mkT")
                mvz = sbt(sc, "mvz", [128, 4, 2, 128], BF16); mvb = B.buf("mvz")
                xqz = sbt(sc, "xqz", [128, 4, TO], BF16); xqTb = B.buf("xqz")
                B.op("pool", lambda e: e.memset(mvz[:].rearrange("p a b c -> p (a b c)"), 0.0), writes=[mvb])
                B.op("pool", lambda e: e.memset(xqz[:].rearrange("p a b -> p (a b)"), 0.0), writes=[xqTb])
                nmx2 = [sbt(sc, "nmx%d" % i, [128, 4], F32) for i in range(2)]; nmxb2 = [B.buf("nmx") for i in range(2)]
                ssum2 = [sbt(sc, "ssum%d" % i, [128, 4], F32) for i in range(2)]; ssumb2 = [B.buf("ssum") for i in range(2)]
                rsm2_ = [sbt(sc, "rsx%d" % i, [128, 4], F32) for i in range(2)]; rsmb2 = [B.buf("rsm") for i in range(2)]
                P_t2 = [sbt(sc, "P_t%d" % i, [128, 4, 256], F32) for i in range(2)]; P_b2 = [B.buf("P_t") for i in range(2)]
                Pn2 = [sbt(sc, "Pn%d" % i, [128, 4, 256], BF16) for i in range(2)]; Pnb2 = [B.buf("Pn") for i in range(2)]
                PT2 = [sbt(sc, "PT%d" % i, [128, 8, 128], BF16) for i in range(2)]; PTb2 = [B.buf("PT") for i in range(2)]
                xoT = sbt(sc, "xoT", [128, 2, TO], BF16); xoTb = B.buf("xoT")

                B.dma("sp", m_t[:], memT.rearrange("(c p) n -> p c n", p=128), m_b, writes=[m_b])
                act(msq[:], m_t[:], AF.Square, R=[m_b], W=[msqb])
                for k in range(8):
                    mm(PS[0][:, 0:256], ones[:], msq[:, k, :], start=(k == 0), stop=(k == 7), R=[onesb, msqb],
                       **({"W": [PB[0]]} if k == 0 else {"A": [PB[0]]}))
                rstd_from(PS[0][:, 0:256], PB[0], 256, mln[:], mlnb, mrs[:], mrsb)
                for k in range(8):
                    stt(mn[:, k, :], m_t[:, k, :], vec[:, V_MEM + k:V_MEM + k + 1], mrs[:], R=[m_b, mrsb, vecb],
                        **({"W": [mnb]} if k == 0 else {"A": [mnb]}))
                for ch in range(2):
                    for k in range(8):
                        mm(PS[1][:, 0:256], wkv[:, k, ch * 128:(ch + 1) * 128], mn[:, k, :], start=(k == 0), stop=(k == 7),
                           R=[wkvb, mnb], **({"W": [PB[1]]} if k == 0 else {"A": [PB[1]]}))
                    cp("dve", mkT[:, ch, :], PS[1][:, 0:256], R=[PB[1]], **({"W": [mkTb]} if ch == 0 else {"A": [mkTb]}))
                for mc in range(2):
                    for k in range(8):
                        mm(PS[2][:, 0:256], mn[:, k, mc * 128:(mc + 1) * 128], wkv[:, k, 256:512], start=(k == 0), stop=(k == 7),
                           R=[wkvb, mnb], **({"W": [PB[2]]} if k == 0 else {"A": [PB[2]]}))
                    for h in range(4):
                        cp("dve", mvz[:, h, mc, (h % 2) * 64:(h % 2 + 1) * 64], PS[2][:, h * 64:(h + 1) * 64],
                           R=[PB[2]], A=[mvb])

                for t in range(NT):
                    for ch in range(2):
                        for k in range(8):
                            mm(PS[3][:, 0:TO].rearrange("p (b t) -> p b t", b=BPT),
                               wxq[:, k, ch * 128:(ch + 1) * 128], xn[:, k, t * BPT:(t + 1) * BPT, 16:144],
                               start=(k == 0), stop=(k == 7), R=[wxqb, xnb],
                               **({"W": [PB[3]]} if k == 0 else {"A": [PB[3]]}))
                        for hp in range(2):
                            act(xqz[hp * 64:(hp + 1) * 64, ch * 2 + hp, :], PS[3][hp * 64:(hp + 1) * 64, 0:TO], AF.Copy,
                                R=[PB[3]], A=[xqTb], scale=0.125)
                    for bk in range(BPT):
                        nmx, nmxb, ssum, ssumb = nmx2[bk % 2], nmxb2[bk % 2], ssum2[bk % 2], ssumb2[bk % 2]
                        rsm, rsmb, P_t, P_b = rsm2_[bk % 2], rsmb2[bk % 2], P_t2[bk % 2], P_b2[bk % 2]
                        Pn, Pnb, PT, PTb = Pn2[bk % 2], Pnb2[bk % 2], PT2[bk % 2], PTb2[bk % 2]
                        sb0 = 0 if bk % 2 == 0 else 4
                        for h in range(4):
                            pb = sb0 + h // 2
                            hp = h % 2
                            mm(PS[pb][:, hp * 256:(hp + 1) * 256],
                               xqz[:, h, bk * 128:(bk + 1) * 128], mkT[:, h // 2, :],
                               start=True, stop=True, R=[xqTb, mkTb],
                               **({"W": [PB[pb]]} if hp == 0 else {"A": [PB[pb]]}))
                        for pq in range(2):
                            B.op("dve", lambda e, pq=pq, nmx=nmx, sb0=sb0: e.reduce_max(
                                out=nmx[:, 2 * pq:2 * pq + 2], in_=PS[sb0 + pq][:].rearrange("p (h m) -> p h m", h=2),
                                axis=AX.X, negate=True), reads=[PB[sb0 + pq]],
                                **({"writes": [nmxb]} if pq == 0 else {"acc": [nmxb]}))
                        for h in range(4):
                            pb = sb0 + h // 2
                            hp = h % 2
                            act(P_t[:, h, :], PS[pb][:, hp * 256:(hp + 1) * 256], AF.Exp, R=[PB[pb], nmxb],
                                **({"W": [P_b, ssumb]} if h == 0 else {"A": [P_b, ssumb]}),
                                bias=nmx[:, h:h + 1], accum_out=ssum[:, h:h + 1])
                        B.op("dve", lambda e, rsm=rsm, ssum=ssum: e.reciprocal(out=rsm[:], in_=ssum[:]),
                             reads=[ssumb], writes=[rsmb])
                        for h in range(4):
                            ts("dve", Pn[:, h, :], P_t[:, h, :], rsm[:, h:h + 1], None, ALU.mult, None,
                               R=[P_b, rsmb], **({"W": [Pnb]} if h == 0 else {"A": [Pnb]}))
                        for h in range(4):
                            for mc in range(2):
                                i8 = h * 2 + mc
                                B.op("pe", lambda e, h=h, mc=mc, i8=i8, Pn=Pn: e.transpose(
                                    PST[:, i8 * 128:(i8 + 1) * 128], Pn[:, h, mc * 128:(mc + 1) * 128], ident[:]),
                                    reads=[Pnb, identb], **({"writes": [PSTB]} if i8 == 0 else {"acc": [PSTB]}))
                        cp("act", PT[:].rearrange("p a b -> p (a b)"), PST[:], R=[PSTB], W=[PTb])
                        for ch in range(2):
                            pb = 2 + ch
                            n4 = 0
                            for h in (2 * ch, 2 * ch + 1):
                                for mc in range(2):
                                    mm(PS[pb][:, bk * 128:(bk + 1) * 128], mvz[:, h, mc, :], PT[:, h * 2 + mc, :],
                                       start=(n4 == 0), stop=(n4 == 3), R=[mvb, PTb],
                                       **({"W": [PB[pb]]} if (bk == 0 and n4 == 0) else {"A": [PB[pb]]}))
                                    n4 += 1
                    for ch in range(2):
                        cp("dve", xoT[:, ch, :], PS[2 + ch][:, 0:TO], R=[PB[2 + ch]],
                           **({"W": [xoTb]} if ch == 0 else {"A": [xoTb]}))
                    def fin_c(dc, ypb, gpb, t=t):
                        ot, otb = gated(ypb, gpb)
                        B.dma("pool", c2_s[dc * 128:(dc + 1) * 128, t * TO:(t + 1) * TO], ot[:], otb, store=True,
                              reads=[otb], acc=[c2sb])
                    if t == 0:
                        load_w(wsbo, wsbob, w_sbo, 4, D)
                        load_w(wg1, wg1b, w_in[:, O_G + D:O_G + 2 * D], 8, D)
                        load_w(wout, woutb, w_out, 8, D)
                    pend = None
                    for dc in range(8):
                        ypb = 4 + (dc % 2)
                        gpb = 6 + (dc % 2)
                        for ch in range(2):
                            mm(PS[ypb][:, 0:TO], wxo[:, ch, dc * 128:(dc + 1) * 128], xoT[:, ch, :],
                               start=(ch == 0), stop=(ch == 1), R=[wxob, xoTb],
                               **({"W": [PB[ypb]]} if ch == 0 else {"A": [PB[ypb]]}))
                        gate_mm(gpb, wg2, wg2b, dc, t)
                        if pend is not None:
                            fin_c(*pend)
                        pend = (dc, ypb, gpb)
                    fin_c(*pend)
                B.barrier()
            if debug:
                with contextlib.ExitStack() as sd:
                    dd = sbt(sd, "dd", [128, 8, NQ], F32); ddb = B.buf("dd")
                    for nm, src, srcb in (("c0", c0_s, c0sb), ("c2", c2_s, c2sb)):
                        B.dma("sp", dd[:], src.rearrange("(c p) n -> p c n", p=128), ddb, reads=[srcb], writes=[ddb])
                        B.dma("sp", dbg[nm].rearrange("(c p) n -> p c n", p=128), dd[:], ddb, store=True, reads=[ddb])
                    B.barrier()

            if stop_after == "P4c":
                B.barrier()
                B.emit()
                return nc
            h1sb = B.buf("h1_s"); n2sb = B.buf("n2_s")
            with contextlib.ExitStack() as sd4:
                sbo4 = sbt(sd4, "sbo4", [128, 4, NQ], BF16); sbo4b = B.buf("sbo4")
                B.dma("sp", sbo4[:], sbo_s, sbo4b, reads=[sbosb], writes=[sbo4b])
                NCL = 3
                cl = [sbt(sd4, "cl%d" % i, [128, 2, TO], F32) for i in range(NCL)]
                clb = [B.buf("cl%d" % i) for i in range(NCL)]
                clc = {"i": 0}
                mg1 = sbt(sd4, "mg", [128, 8, TO], BF16)
                mg = [mg1, mg1]
                mgb1 = B.buf("mg")
                mgb = [mgb1, mgb1]
                mo = [sbt(sd4, "mo%d" % i, [128, 8, TO], F32) for i in range(2)]
                mob = [B.buf("mo%d" % i) for i in range(2)]
                sqm = [sbt(sd4, "sqm%d" % i, [128, TO], BF16) for i in range(2)]
                sqmb = [B.buf("sqm%d" % i) for i in range(2)]
                lnm = sbt(sd4, "lnm", [128, TO], F32); lnmb = B.buf("lnm")
                rsm1 = sbt(sd4, "rsm1", [128, TO], F32); rsm1b = B.buf("rsm1")
                rsm2 = sbt(sd4, "rsm2", [128, TO], F32); rsm2b = B.buf("rsm2")
                xr = sbt(sd4, "xr", [128, 8, BPT, 128], F32); xrb = B.buf("xr")
                n2t = sbt(sd4, "n2t", [128, 8, TO], BF16); n2tb = B.buf("n2t")
                xoh4 = xoh.rearrange("d (b t) -> d b t", t=144)

                def bd_S1(t):
                    def fin_b(dc, ci, ypb, gpb):
                        ot, otb = gated(ypb, gpb)
                        tt("pool", cl[ci][:, 0, :], cl[ci][:, 0, :], cl[ci][:, 1, :], ALU.add, R=[clb[ci]], A=[clb[ci]])
                        tt("dve", mg[t % 2][:, dc, :], ot[:], cl[ci][:, 0, :], ALU.add, R=[otb, clb[ci]],
                           **({"W": [mgb[t % 2]]} if dc == 0 else {"A": [mgb[t % 2]]}))
                    pend = None
                    for dc in range(8):
                        ci = clc["i"] % NCL
                        clc["i"] += 1
                        B.dma("sp", cl[ci][:, 0, :], c0_s[dc * 128:(dc + 1) * 128, t * TO:(t + 1) * TO], clb[ci],
                              reads=[c0sb], writes=[clb[ci]])
                        B.dma("sp", cl[ci][:, 1, :], c2_s[dc * 128:(dc + 1) * 128, t * TO:(t + 1) * TO], clb[ci],
                              reads=[c2sb], acc=[clb[ci]])
                        ypb = 4 + (dc % 2)
                        gpb = (dc % 2)
                        for c4 in range(4):
                            mm(PS[ypb][:, 0:TO], wsbo[:, c4, dc * 128:(dc + 1) * 128], sbo4[:, c4, t * TO:(t + 1) * TO],
                               start=(c4 == 0), stop=(c4 == 3), R=[wsbob, sbo4b],
                               **({"W": [PB[ypb]]} if c4 == 0 else {"A": [PB[ypb]]}))
                        gate_mm(gpb, wg1, wg1b, dc, t)
                        if pend is not None:
                            fin_b(*pend)
                        pend = (dc, ci, ypb, gpb)
                        yield
                    fin_b(*pend)

                def bd_S2(t):
                    for dc in range(8):
                        B.dma("sp", xr[:, dc, :, :], xoh4[dc * 128:(dc + 1) * 128, t * BPT:(t + 1) * BPT, 16:144], xrb,
                              **({"writes": [xrb]} if dc == 0 else {"acc": [xrb]}))
                    m_, mb_ = mo[t % 2], mob[t % 2]
                    for dc in range(8):
                        pb = 2 + (dc % 2)
                        for k in range(8):
                            mm(PS[pb][:, 0:TO], wout[:, k, dc * 128:(dc + 1) * 128], mg[t % 2][:, k, :],
                               start=(k == 0), stop=(k == 7), R=[woutb, mgb[t % 2]],
                               **({"W": [PB[pb]]} if k == 0 else {"A": [PB[pb]]}))
                        if dc > 0:
                            d1 = dc - 1
                            mm(PS[6][:, 0:TO], ones[:], sqm[d1 % 2][:], start=(d1 == 0), stop=False,
                               R=[onesb, sqmb[d1 % 2]], **({"W": [PB[6]]} if d1 == 0 else {"A": [PB[6]]}))
                        cp("dve", m_[:, dc, :], PS[pb][:, 0:TO], R=[PB[pb]], **({"W": [mb_]} if dc == 0 else {"A": [mb_]}))
                        act(sqm[dc % 2][:], m_[:, dc, :], AF.Square, R=[mb_], W=[sqmb[dc % 2]])
                    mm(PS[6][:, 0:TO], ones[:], sqm[7 % 2][:], start=False, stop=True,
                       R=[onesb, sqmb[7 % 2]], A=[PB[6]])

                def bd_S3(t):
                    m_, mb_ = mo[t % 2], mob[t % 2]
                    rstd_from(PS[6][:, 0:TO], PB[6], TO, lnm[:], lnmb, rsm1[:], rsm1b)
                    for dc in range(8):
                        stt(m_[:, dc, :], m_[:, dc, :], vec[:, V_POST + dc:V_POST + dc + 1], rsm1[:],
                            R=[mb_, vecb, rsm1b], A=[mb_])
                        tt("pool", m_[:, dc, :], m_[:, dc, :], xr[:, dc, :, :].rearrange("p b t -> p (b t)"), ALU.add,
                           R=[mb_, xrb], A=[mb_])
                        act(sqm[dc % 2][:], m_[:, dc, :], AF.Square, R=[mb_], W=[sqmb[dc % 2]])
                        if dc > 0:
                            d1 = dc - 1
                            mm(PS[7][:, 0:TO], ones[:], sqm[d1 % 2][:], start=(d1 == 0), stop=False,
                               R=[onesb, sqmb[d1 % 2]], **({"W": [PB[7]]} if d1 == 0 else {"A": [PB[7]]}))
                        yield
                    mm(PS[7][:, 0:TO], ones[:], sqm[7 % 2][:], start=False, stop=True,
                       R=[onesb, sqmb[7 % 2]], A=[PB[7]])
                    B.dma("sp", h1_s[:, t * TO:(t + 1) * TO].rearrange("(c p) n -> p c n", p=128), m_[:], mb_, store=True,
                          reads=[mb_], acc=[h1sb])
                    rstd_from(PS[7][:, 0:TO], PB[7], TO, lnm[:], lnmb, rsm2[:], rsm2b)
                    for dc in range(8):
                        stt(n2t[:, dc, :], m_[:, dc, :], vec[:, V_FPRE + dc:V_FPRE + dc + 1], rsm2[:],
                            R=[mb_, rsm2b, vecb], **({"W": [n2tb]} if dc == 0 else {"A": [n2tb]}))
                    B.dma("sp", n2_s[:, t * TO:(t + 1) * TO].rearrange("(c p) n -> p c n", p=128), n2t[:], n2tb, store=True,
                          reads=[n2tb], acc=[n2sb])

                run(bd_S1(0))
                for t in range(NT):
                    bd_S2(t)
                    if t + 1 < NT:
                        interleave(bd_S3(t), bd_S1(t + 1))
                    else:
                        run(bd_S3(t))
                B.barrier()
        if debug:
            with contextlib.ExitStack() as sd:
                dd = sbt(sd, "dd2", [128, 8, NQ], F32); ddb = B.buf("dd2")
                B.dma("sp", dd[:], h1_s.rearrange("(c p) n -> p c n", p=128), ddb, reads=[h1sb], writes=[ddb])
                B.dma("sp", dbg["h1"].rearrange("(c p) n -> p c n", p=128), dd[:], ddb, store=True, reads=[ddb])
                B.barrier()

        if stop_after == "P4":
            B.barrier()
            B.emit()
            return nc
        with contextlib.ExitStack() as s56:
            actT = sbt(s56, "actT", [128, NFC, NQ], BF16); actTb = B.buf("actT")
            wfo = sbt(s56, "wfo", [128, NFC, D], BF16); wfob = B.buf("wfo")
            with contextlib.ExitStack() as s5:
                n2 = sbt(s5, "n2", [128, 8, NQ], BF16); n2b = B.buf("n2")
                new_stg(s5)
                B.dma("sp", n2[:], n2_s.rearrange("(c p) n -> p c n", p=128), n2b, reads=[n2sb], writes=[n2b])
                wfi = [sbt(s5, "wfi%d" % i, [128, 8, 256], BF16) for i in range(2)]
                wfib = [B.buf("wfi%d" % i) for i in range(2)]
                sil = [sbt(s5, "sil%d" % i, [128, TO], F32) for i in range(2)]
                silb = [B.buf("sil%d" % i) for i in range(2)]

                def p5_loadw(f):
                    sl = f % 2
                    load_w(wfi[sl][:, :, 0:128], wfib[sl], w_fi[:, f * 128:(f + 1) * 128], 8, 128, gcol=V_FPRE, first=True)
                    load_w(wfi[sl][:, :, 128:256], wfib[sl], w_fi[:, DFF + f * 128:DFF + (f + 1) * 128], 8, 128,
                           gcol=V_FPRE, first=False)

                p5_loadw(0)
                cnt5 = 0
                for f in range(NFC):
                    if f + 1 < NFC:
                        p5_loadw(f + 1)
                    load_w(wfo[:, f:f + 1, :], wfob, w_fo[f * 128:(f + 1) * 128, :], 1, D, first=(f == 0))
                    sl = f % 2
                    for t in range(NT):
                        gp = (cnt5 % 2) * 2
                        up = gp + 1
                        si = cnt5 % 2
                        cnt5 += 1
                        for k in range(8):
                            mm(PS[gp][:, 0:TO], wfi[sl][:, k, 0:128], n2[:, k, t * TO:(t + 1) * TO],
                               start=(k == 0), stop=(k == 7), R=[wfib[sl], n2b],
                               **({"W": [PB[gp]]} if k == 0 else {"A": [PB[gp]]}))
                        for k in range(8):
                            mm(PS[up][:, 0:TO], wfi[sl][:, k, 128:256], n2[:, k, t * TO:(t + 1) * TO],
                               start=(k == 0), stop=(k == 7), R=[wfib[sl], n2b],
                               **({"W": [PB[up]]} if k == 0 else {"A": [PB[up]]}))
                        act(sil[si][:], PS[gp][:, 0:TO], AF.Silu, R=[PB[gp]], W=[silb[si]])
                        tt("dve", actT[:, f, t * TO:(t + 1) * TO], sil[si][:], PS[up][:, 0:TO], ALU.mult,
                           R=[silb[si], PB[up]], A=[actTb])
                B.barrier()
            with contextlib.ExitStack() as s6:
                ff = [sbt(s6, "ff%d" % i, [128, 8, TO], F32) for i in range(2)]
                ffb = [B.buf("ff%d" % i) for i in range(2)]
                sq6 = [sbt(s6, "sq6_%d" % i, [128, TO], BF16) for i in range(2)]
                sq6b = [B.buf("sq6_%d" % i) for i in range(2)]
                ln6 = sbt(s6, "ln6", [128, TO], F32); ln6b = B.buf("ln6")
                rs6 = sbt(s6, "rs6", [128, TO], F32); rs6b = B.buf("rs6")
                h1t = [sbt(s6, "h1t%d" % i, [128, TO], F32) for i in range(3)]
                h1tb = [B.buf("h1t%d" % i) for i in range(3)]
                outsb = B.buf("outT")
                cnt6 = {"h": 0}

                def p6_M(t):
                    f_, fb_ = ff[t % 2], ffb[t % 2]
                    ssb = 4 + (t % 2)
                    for dc in range(8):
                        pb = dc % 4
                        for f in range(NFC):
                            mm(PS[pb][:, 0:TO], wfo[:, f, dc * 128:(dc + 1) * 128], actT[:, f, t * TO:(t + 1) * TO],
                               start=(f == 0), stop=(f == NFC - 1), R=[wfob, actTb],
                               **({"W": [PB[pb]]} if f == 0 else {"A": [PB[pb]]}))
                        if dc > 0:
                            d1 = dc - 1
                            mm(PS[ssb][:, 0:TO], ones[:], sq6[d1 % 2][:], start=(d1 == 0), stop=False,
                               R=[onesb, sq6b[d1 % 2]], **({"W": [PB[ssb]]} if d1 == 0 else {"A": [PB[ssb]]}))
                        cp("dve", f_[:, dc, :], PS[pb][:, 0:TO], R=[PB[pb]], **({"W": [fb_]} if dc == 0 else {"A": [fb_]}))
                        act(sq6[dc % 2][:], f_[:, dc, :], AF.Square, R=[fb_], W=[sq6b[dc % 2]])
                        yield
                    mm(PS[ssb][:, 0:TO], ones[:], sq6[7 % 2][:], start=False, stop=True,
                       R=[onesb, sq6b[7 % 2]], A=[PB[ssb]])

                def p6_E(t):
                    f_, fb_ = ff[t % 2], ffb[t % 2]
                    ssb = 4 + (t % 2)
                    rstd_from(PS[ssb][:, 0:TO], PB[ssb], TO, ln6[:], ln6b, rs6[:], rs6b)
                    for dc in range(8):
                        hi = cnt6["h"] % 3
                        cnt6["h"] += 1
                        B.dma("sp", h1t[hi][:], h1_s[dc * 128:(dc + 1) * 128, t * TO:(t + 1) * TO], h1tb[hi],
                              reads=[h1sb], writes=[h1tb[hi]])
                        stt(f_[:, dc, :], f_[:, dc, :], vec[:, V_FPOST + dc:V_FPOST + dc + 1], rs6[:],
                            R=[fb_, vecb, rs6b], A=[fb_])
                        tt("pool", f_[:, dc, :], f_[:, dc, :], h1t[hi][:], ALU.add, R=[fb_, h1tb[hi]], A=[fb_])
                        yield
                    B.dma("sp", outT[:, t * TO:(t + 1) * TO].rearrange("(c p) n -> p c n", p=128), f_[:], fb_, store=True,
                          reads=[fb_], acc=[outsb])

                run(p6_M(0))
                for t in range(NT):
                    if t + 1 < NT:
                        interleave(p6_M(t + 1), p6_E(t))
                    else:
                        run(p6_E(t))
                B.barrier()
        B.barrier()
        B.emit()
    return nc


def _own_blocks(r, G):
    blks = []
    for g in range(G):
        blks.append(8 * g + r)
        blks.append(8 * g + 7 - r)
    return blks


def _pack_vecs(norm_mix_pre, norm_mem, norm_mix_post, norm_ffn_pre, norm_ffn_post, pool_scale):
    cols = []
    for v in (norm_mix_pre, norm_mem, norm_mix_post, norm_ffn_pre, norm_ffn_post):
        cols.append(np.asarray(v, np.float32).reshape(8, 128).T)
    cols.append(np.asarray(pool_scale, np.float32).reshape(4, 128).T)
    return np.ascontiguousarray(np.concatenate(cols, axis=1))


_PROG_CACHE = {}


def run_layer(x, mem, norm_mix_pre, w_in, w_pool_mix, pool_scale, w_pool_o, w_sb_o, norm_mem, w_mem_kv, w_x_o,
              w_out, norm_mix_post, norm_ffn_pre, w_ffn_in, w_ffn_out, norm_ffn_post, debug=False, stop_after=None):
    x = np.asarray(x, np.float32)
    mem = np.asarray(mem, np.float32)
    Bn, S, _ = x.shape
    G = S // 1024
    assert Bn == 2 and S == 1024 * G
    key = (G, debug, stop_after)
    if key not in _PROG_CACHE:
        _PROG_CACHE[key] = build_program(G, debug, stop_after)
    nc = _PROG_CACHE[key]
    f32 = lambda a: np.ascontiguousarray(np.asarray(a, np.float32))
    shared = {
        "w_in": f32(w_in[0]), "w_pmix": f32(np.asarray(w_pool_mix[0]).reshape(512, 128)), "w_po": f32(w_pool_o[0]),
        "w_sbo": f32(w_sb_o[0]), "w_kv": f32(w_mem_kv[0]), "w_xo": f32(w_x_o[0]), "w_out": f32(w_out[0]),
        "w_fi": f32(w_ffn_in[0]), "w_fo": f32(w_ffn_out[0]),
        "vecs": _pack_vecs(norm_mix_pre[0], norm_mem[0], norm_mix_post[0], norm_ffn_pre[0], norm_ffn_post[0],
                           pool_scale[0]),
    }
    xTb = [np.ascontiguousarray(x[b].T) for b in range(2)]
    memTb = [np.ascontiguousarray(mem[b].T) for b in range(2)]
    in_maps = []
    for core in range(8):
        b, r = core // 4, core % 4
        blks = _own_blocks(r, G)
        xoh = np.zeros((D, len(blks) * 144), np.float32)
        posv = np.zeros((len(blks) * 128,), np.float32)
        for n, blk in enumerate(blks):
            s0 = blk * 128
            lo = max(0, s0 - 16)
            xoh[:, n * 144 + 16 - (s0 - lo):(n + 1) * 144] = xTb[b][:, lo:s0 + 128]
            posv[n * 128:(n + 1) * 128] = np.arange(s0, s0 + 128, dtype=np.float32)
        m = dict(shared)
        m.update({"xT": xTb[b], "xoh": xoh, "pos": posv, "memT": memTb[b]})
        in_maps.append(m)
    res = run_bass_kernel_spmd(nc, in_maps, core_ids=list(range(8)))
    out = np.empty((2, S, D), np.float32)
    for core in range(8):
        b, r = core // 4, core % 4
        oT = np.asarray(res.results[core]["outT"])
        for n, blk in enumerate(_own_blocks(r, G)):
            out[b, blk * 128:(blk + 1) * 128, :] = oT[:, n * 128:(n + 1) * 128].T
    if debug:
        return out, res.results
    return out


def kernel(**inputs):
    return run_layer(**inputs)
```

```python
import contextlib
import numpy as np
import concourse.bass as bass
import concourse.mybir as mybir
from concourse.bass_utils import run_bass_kernel_spmd

F32 = mybir.dt.float32
BF16 = mybir.dt.bfloat16
I32 = mybir.dt.int32
AF = mybir.ActivationFunctionType
ALU = mybir.AluOpType
AX = mybir.AxisListType

ENGS = ("pe", "act", "dve", "pool", "sp")
SAME_ENG_RAW = True

D = 1024
DFF = 2816
NFC = DFF // 128
INW = 5376
O_Q, O_K, O_V, O_XQ, O_G = 512, 1024, 1536, 2048, 2304
EPS = 1e-6
NEG_BIG = -30000.0
V_PRE, V_MEM, V_POST, V_FPRE, V_FPOST, V_PSC, NVEC = 0, 8, 16, 24, 32, 40, 44


class Buf:
    __slots__ = ("name", "w", "r", "excl", "base")

    def __init__(self, name):
        self.name = name
        self.w = {}
        self.r = {}
        self.base = {}
        self.excl = False


class Builder:
    def __init__(self, nc, es):
        self.nc = nc
        self.es = es
        self.q = {e: [] for e in ENGS}
        self.cnt = {}
        self.seen = {e: {} for e in ENGS}
        self.semh = {}
        self.nbuf = 0
        for e in ENGS:
            if e != "sp":
                self._newsem(e)

    def _newsem(self, key):
        self.semh[key] = self.es.enter_context(self.nc.semaphore("s%d" % len(self.semh)))
        self.cnt[key] = 0

    def buf(self, name=None):
        self.nbuf += 1
        return Buf("%s#%d" % (name or "b", self.nbuf))

    def _wait(self, eng, key, val):
        if self.seen[eng].get(key, 0) >= val:
            return
        self.seen[eng][key] = val
        sem = self.semh[key]
        self.q[eng].append(lambda e, sem=sem, val=val: e.wait_ge(sem, val))

    def _deps(self, eng, reads, writes, acc):
        raw = {}
        oth = {}

        def add(dst, d):
            for k, v in d.items():
                if dst.get(k, 0) < v:
                    dst[k] = v
        for b in reads:
            add(raw, b.w)
            if b.excl:
                add(oth, b.r)
        for b in writes:
            add(oth, b.w)
            add(oth, b.r)
        for b in acc:
            add(oth, b.r)
            add(oth, b.base)
        for k, v in raw.items():
            if k == eng and (eng == "pe" or not SAME_ENG_RAW):
                continue
            self._wait(eng, k, v)
        for k, v in oth.items():
            if k == eng:
                continue
            self._wait(eng, k, v)

    def _record(self, key, val, reads, writes, acc):
        for b in reads:
            if b.r.get(key, 0) < val:
                b.r[key] = val
        for b in writes:
            b.w = {key: val}
            b.base = {key: val}
            b.r = {}
        for b in acc:
            b.w[key] = val
            b.r = {}

    def op(self, eng, fn, reads=(), writes=(), acc=()):
        self._deps(eng, reads, writes, acc)
        self.cnt[eng] += 1
        c = self.cnt[eng]
        sem = self.semh[eng]
        self.q[eng].append(lambda e, fn=fn, sem=sem: fn(e).then_inc(sem, 1))
        self._record(eng, c, reads, writes, acc)

    def dma(self, qeng, out_ap, in_ap, sb, store=False, reads=(), writes=(), acc=(), **kw):
        self._deps(qeng, reads, writes, acc)
        key = ("st" if store else "ld", sb.name, "sw" if qeng == "pool" else "hw")
        if key not in self.semh:
            self._newsem(key)
        self.cnt[key] += 16
        c = self.cnt[key]
        sem = self.semh[key]
        self.q[qeng].append(
            lambda e, o=out_ap, i=in_ap, sem=sem, kw=kw: e.dma_start(out=o, in_=i, **kw).then_inc(sem, 16))
        self._record(key, c, reads, writes, acc)

    def barrier(self):
        for e in ENGS:
            for k, c in self.cnt.items():
                if k == e or c == 0:
                    continue
                self._wait(e, k, c)

    def emit(self):
        q = self.q
        with self.nc.Block() as block:
            @block.tensor
            def _(e):
                for t in q["pe"]:
                    t(e)

            @block.scalar
            def _(e):
                for t in q["act"]:
                    t(e)

            @block.vector
            def _(e):
                for t in q["dve"]:
                    t(e)

            @block.gpsimd
            def _(e):
                for t in q["pool"]:
                    t(e)

            @block.sync
            def _(e):
                for t in q["sp"]:
                    t(e)


def build_program(G, debug=False, stop_after=None):
    S = 1024 * G
    NBLK = S // 128
    NB = 2 * G
    NQ = 128 * NB
    BPT = 4 if G >= 2 else 2
    TO = 128 * BPT
    TOH = 144 * BPT
    NT = NB // BPT
    ST = S // 512

    nc = bass.Bass("TRN2", target_bir_lowering=False)

    def din(name, shape, dt=F32):
        return nc.dram_tensor(name, list(shape), dt, kind="ExternalInput").ap()

    xT = din("xT", [D, S])
    xoh = din("xoh", [D, NB * 144])
    pos = din("pos", [NQ])
    memT = din("memT", [D, 256])
    w_in = din("w_in", [D, INW])
    w_pmix = din("w_pmix", [512, 128])
    w_po = din("w_po", [512, D])
    w_sbo = din("w_sbo", [512, D])
    w_kv = din("w_kv", [D, 512])
    w_xo = din("w_xo", [256, D])
    w_out = din("w_out", [D, D])
    w_fi = din("w_fi", [D, 2 * DFF])
    w_fo = din("w_fo", [DFF, D])
    vecs = din("vecs", [128, NVEC])
    outT = nc.dram_tensor("outT", [D, NQ], F32, kind="ExternalOutput").ap()
    kT_s = nc.dram_tensor("kT_s", [4, 128, S], BF16).ap()
    v_s = nc.dram_tensor("v_s", [4, 128, NBLK, 128], BF16).ap()
    c0_s = nc.dram_tensor("c0_s", [D, NQ], F32).ap()
    c2_s = nc.dram_tensor("c2_s", [D, NQ], F32).ap()
    h1_s = nc.dram_tensor("h1_s", [D, NQ], F32).ap()
    n2_s = nc.dram_tensor("n2_s", [D, NQ], BF16).ap()
    dbg = {}
    if debug:
        dbg["qT"] = nc.dram_tensor("dbg_qT", [128, 4, 2 * NQ], BF16, kind="ExternalOutput").ap()
        dbg["sbo"] = nc.dram_tensor("dbg_sbo", [128, 4, NQ], BF16, kind="ExternalOutput").ap()
        dbg["kT"] = nc.dram_tensor("dbg_kT", [4, 128, S], BF16, kind="ExternalOutput").ap()
        dbg["v"] = nc.dram_tensor("dbg_v", [4, 128, NBLK, 128], BF16, kind="ExternalOutput").ap()
        dbg["c0"] = nc.dram_tensor("dbg_c0", [D, NQ], F32, kind="ExternalOutput").ap()
        dbg["c2"] = nc.dram_tensor("dbg_c2", [D, NQ], F32, kind="ExternalOutput").ap()
        dbg["h1"] = nc.dram_tensor("dbg_h1", [D, NQ], F32, kind="ExternalOutput").ap()

    with contextlib.ExitStack() as es:
        B = Builder(nc, es)

        def sbt(scope, name, shape, dt):
            return scope.enter_context(nc.sbuf_tensor(name, list(shape), dt))

        def mm(out, lhsT, rhs, start, stop, R, W=(), A=(), sgc=False):
            B.op("pe", lambda e: e.matmul(out, lhsT, rhs, start=start, stop=stop, skip_group_check=sgc),
                 reads=R, writes=W, acc=A)

        def act(out, in_, func, R, W=(), A=(), **kw):
            B.op("act", lambda e: e.activation(out=out, in_=in_, func=func, **kw), reads=R, writes=W, acc=A)

        def tt(eng, out, in0, in1, op, R, W=(), A=()):
            B.op(eng, lambda e: e.tensor_tensor(out=out, in0=in0, in1=in1, op=op), reads=R, writes=W, acc=A)

        def ts(eng, out, in0, s1, s2, op0, op1, R, W=(), A=()):
            if op1 is None:
                B.op(eng, lambda e: e.tensor_scalar(out=out, in0=in0, scalar1=s1, scalar2=s2, op0=op0),
                     reads=R, writes=W, acc=A)
            else:
                B.op(eng, lambda e: e.tensor_scalar(out=out, in0=in0, scalar1=s1, scalar2=s2, op0=op0, op1=op1),
                     reads=R, writes=W, acc=A)

        def cp(eng, out, in_, R, W=(), A=()):
            if eng == "act":
                act(out, in_, AF.Copy, R, W, A)
            else:
                B.op(eng, lambda e: e.tensor_copy(out=out, in_=in_), reads=R, writes=W, acc=A)

        PSALL = es.enter_context(nc.psum_tensor("psall", [128, 4096], F32))
        PS = [PSALL[:, i * 512:(i + 1) * 512] for i in range(8)]
        PB = [B.buf("ps%d" % i) for i in range(8)]
        for b_ in PB:
            b_.excl = True
        PST = PS[7].bitcast(BF16)
        PSTB = PB[7]

        gs = es
        vec = sbt(gs, "vec", [128, NVEC], F32); vecb = B.buf("vec")
        ones = sbt(gs, "ones", [128, 128], BF16); onesb = B.buf("ones")
        nTin = sbt(gs, "nTin", [128, 128], BF16); nTinb = B.buf("nTin")
        nOnes = sbt(gs, "nOnes", [128, 128], BF16); nOnesb = B.buf("nOnes")
        nBigI = sbt(gs, "nBigI", [128, 128], BF16); nBigIb = B.buf("nBigI")
        ident = sbt(gs, "ident", [128, 128], BF16); identb = B.buf("ident")
        dif_i = sbt(gs, "dif_i", [128, 128], I32); difib = B.buf("dif_i")
        dif_f = sbt(gs, "dif_f", [128, 128], F32); diffb = B.buf("dif_f")
        kp8_i = sbt(gs, "kp8_i", [128, 8], I32); kp8ib = B.buf("kp8i")
        kp8 = sbt(gs, "kp8", [128, 8], F32); kp8b = B.buf("kp8")
        qpos = sbt(gs, "qpos", [128, NQ], F32); qposb = B.buf("qpos")
        notM = sbt(gs, "notM", [128, 8, 2, 256], BF16); notMb = B.buf("notM")
        NSTG = 2
        st_state = {"i": 0, "e": 0, "n": 0}

        def new_stg(scope, n=NSTG):
            st_state["n"] += 1
            st_state["stg"] = [sbt(scope, "stg%d_%d" % (st_state["n"], i), [128, 1536], F32) for i in range(n)]
            st_state["stgb"] = [B.buf("stg%d" % i) for i in range(n)]

        B.dma("sp", vec[:], vecs, vecb, writes=[vecb])
        B.dma("sp", qpos[:], pos.partition_broadcast(128), qposb, writes=[qposb])
        B.op("dve", lambda e: e.memset(ones[:], 1.0), writes=[onesb])
        B.op("dve", lambda e: e.memset(nOnes[:], -1.0), writes=[nOnesb])
        B.op("pool", lambda e: e.iota(dif_i[:], pattern=[[-1, 128]], base=0, channel_multiplier=1), writes=[difib])
        cp("dve", dif_f[:], dif_i[:], R=[difib], W=[diffb])
        ts("dve", nTin[:], dif_f[:], 0.0, -1.0, ALU.is_ge, ALU.mult, R=[diffb], W=[nTinb])
        ts("dve", nBigI[:], dif_f[:], 0.0, NEG_BIG, ALU.is_equal, ALU.mult, R=[diffb], W=[nBigIb])
        ts("dve", ident[:], dif_f[:], 0.0, None, ALU.is_equal, None, R=[diffb], W=[identb])
        B.op("pool", lambda e: e.iota(kp8_i[:], pattern=[[128, 8]], base=0, channel_multiplier=1), writes=[kp8ib])
        cp("dve", kp8[:], kp8_i[:], R=[kp8ib], W=[kp8b])
        for jj in range(8):
            for h in range(2):
                ts("dve", notM[:, jj, h, :], qpos[:, 0:256], kp8[:, jj:jj + 1], None, ALU.is_le, None,
                   R=[qposb, kp8b], **({"W": [notMb]} if (jj == 0 and h == 0) else {"A": [notMb]}))

        def load_w(dst, dstb, src, kc, ncols, gcol=None, first=True):
            kg = max(1, min(kc, 1536 // ncols))
            k0 = 0
            while k0 < kc:
                kn = min(kg, kc - k0)
                stg, stgb = st_state["stg"], st_state["stgb"]
                i = st_state["i"] % len(stg)
                st_state["i"] += 1
                sv = stg[i][:, 0:kn * ncols].rearrange("p (k n) -> p k n", k=kn)
                B.dma("sp", sv, src[k0 * 128:(k0 + kn) * 128, :].rearrange("(k p) n -> p k n", p=128), stgb[i],
                      writes=[stgb[i]])
                eng = "dve" if st_state["e"] % 2 == 0 else "act"
                st_state["e"] += 1
                kw = {"W": [dstb]} if (first and k0 == 0) else {"A": [dstb]}
                cp(eng, dst[:, k0:k0 + kn, :], sv, R=[stgb[i]], **kw)
                k0 += kn

        def run(gen):
            for _ in gen:
                pass

        def interleave(*gens):
            gens = list(gens)
            while gens:
                for g_ in list(gens):
                    try:
                        next(g_)
                    except StopIteration:
                        gens.remove(g_)

        def stt(out, in0, scal, in1, R, W=(), A=()):
            B.op("dve", lambda e: e.scalar_tensor_tensor(out=out, in0=in0, scalar=scal, in1=in1,
                                                         op0=ALU.mult, op1=ALU.mult), reads=R, writes=W, acc=A)

        def rstd_from(ss_ps, ss_b, ncols, ln_t, ln_b, out_t, out_b, out_is_write=True):
            act(ln_t, ss_ps, AF.Ln, R=[ss_b], W=[ln_b], scale=1.0 / D, bias=EPS)
            act(out_t, ln_t, AF.Exp, R=[ln_b], W=[out_b], scale=-0.5)

        s13 = contextlib.ExitStack()
        es.enter_context(s13)
        NGT = BPT // 2
        Qz = sbt(s13, "Qz", [128, 4, G, 2, 256], BF16); Qzb = B.buf("Qz")
        B.op("pool", lambda e: e.memset(Qz[:].rearrange("p a g h q -> p (a g h q)"), 0.0), writes=[Qzb])
        wqkv = sbt(s13, "wqkv", [128, 8, 1536], BF16); wqkvb = B.buf("wqkv")

        xn_s = nc.dram_tensor("xn_s", [128, 8, NB * 144], BF16).ap()
        xnsb = B.buf("xn_s")
        with contextlib.ExitStack() as s1:
            new_stg(s1, 4)
            load_w(wqkv, wqkvb, w_in[:, O_Q:O_Q + 1536], 8, 1536)
            xo_t = [sbt(s1, "xo_t%d" % i, [128, 8, TOH], F32) for i in range(2)]
            xo_b = [B.buf("xo_t%d" % i) for i in range(2)]
            sq1 = sbt(s1, "sq1", [128, 8, TOH], BF16); sq1b = B.buf("sq1")
            ln1 = sbt(s1, "ln1", [128, TOH], F32); ln1b = B.buf("ln1")
            rs1 = sbt(s1, "rs1", [128, TOH], F32); rs1b = B.buf("rs1")
            xn1 = [sbt(s1, "xn1_%d" % i, [128, 8, BPT, 144], BF16) for i in range(2)]
            xn1b = [B.buf("xn1_%d" % i) for i in range(2)]
            HB = TOH // 2
            def p1_norm(t):
                sl = t % 2
                B.dma("sp", xo_t[sl][:], xoh[:, t * TOH:(t + 1) * TOH].rearrange("(c p) n -> p c n", p=128),
                      xo_b[sl], writes=[xo_b[sl]])
                act(sq1[:], xo_t[sl][:], AF.Square, R=[xo_b[sl]], W=[sq1b])
                for hf in range(2):
                    for k in range(8):
                        mm(PS[hf][:, 0:HB], ones[:], sq1[:, k, hf * HB:(hf + 1) * HB], start=(k == 0), stop=(k == 7),
                           R=[onesb, sq1b], **({"W": [PB[hf]]} if k == 0 else {"A": [PB[hf]]}))
                for hf in range(2):
                    act(ln1[:, hf * HB:(hf + 1) * HB], PS[hf][:, 0:HB], AF.Ln, R=[PB[hf]],
                        **({"W": [ln1b]} if hf == 0 else {"A": [ln1b]}), scale=1.0 / D, bias=EPS)
                act(rs1[:], ln1[:], AF.Exp, R=[ln1b], W=[rs1b], scale=-0.5)
                xnv = xn1[sl][:].rearrange("p c b t -> p c (b t)")
                for k in range(8):
                    stt(xnv[:, k, :], xo_t[sl][:, k, :], vec[:, V_PRE + k:V_PRE + k + 1], rs1[:],
                        R=[xo_b[sl], rs1b, vecb], **({"W": [xn1b[sl]]} if k == 0 else {"A": [xn1b[sl]]}))
                B.dma("pool", xn_s[:, :, t * TOH:(t + 1) * TOH], xnv, xn1b[sl], store=True,
                      reads=[xn1b[sl]], acc=[xnsb])

            def p1_q(t):
                sl = t % 2
                for c4 in range(4):
                    pb = 2 + (c4 % 2)
                    for k in range(8):
                        mm(PS[pb][:, 0:TO].rearrange("p (b t) -> p b t", b=BPT),
                           wqkv[:, k, c4 * 128:(c4 + 1) * 128], xn1[sl][:, k, :, 16:144],
                           start=(k == 0), stop=(k == 7), R=[wqkvb, xn1b[sl]],
                           **({"W": [PB[pb]]} if k == 0 else {"A": [PB[pb]]}))
                    for h in range(2):
                        act(Qz[h * 64:(h + 1) * 64, c4, t * NGT:(t + 1) * NGT, h, :],
                            PS[pb][h * 64:(h + 1) * 64, 0:TO].rearrange("p (g q) -> p g q", g=NGT), AF.Copy,
                            R=[PB[pb]], A=[Qzb], scale=0.125)

            p1_norm(0)
            for t in range(NT):
                if t + 1 < NT:
                    p1_norm(t + 1)
                p1_q(t)
        B.barrier()
        if debug:
            B.dma("sp", dbg["qT"], Qz[:].rearrange("p a g h q -> p a (g h q)"), Qzb, store=True, reads=[Qzb])
        if stop_after == "P1":
            B.barrier()
            B.emit()
            return nc

        kTsb = B.buf("kT_s"); vsb = B.buf("v_s")
        with contextlib.ExitStack() as s2:
            x_t = [sbt(s2, "x_t%d" % i, [128, 8, 512], F32) for i in range(3)]
            x_b = [B.buf("x_t%d" % i) for i in range(3)]
            sq2 = [sbt(s2, "sq2_%d" % i, [128, 8, 512], BF16) for i in range(2)]
            sq2b = [B.buf("sq2_%d" % i) for i in range(2)]
            ln2 = sbt(s2, "ln2", [128, 512], F32); ln2b = B.buf("ln2")
            rs2 = sbt(s2, "rs2", [128, 512], F32); rs2b = B.buf("rs2")
            xn2 = [sbt(s2, "xn2_%d" % i, [128, 8, 512], BF16) for i in range(2)]
            xn2b = [B.buf("xn2_%d" % i) for i in range(2)]
            kst = [sbt(s2, "kst%d" % i, [128, 4, 512], BF16) for i in range(2)]
            kstb = [B.buf("kst%d" % i) for i in range(2)]
            vst = [sbt(s2, "vst%d" % i, [128, 4, 512], BF16) for i in range(2)]
            vstb = [B.buf("vst%d" % i) for i in range(2)]

            def p2_load(t):
                B.dma("sp", x_t[t % 3][:], xT[:, t * 512:(t + 1) * 512].rearrange("(c p) n -> p c n", p=128),
                      x_b[t % 3], writes=[x_b[t % 3]])

            def p2_sq(t):
                sl = t % 2
                act(sq2[sl][:], x_t[t % 3][:], AF.Square, R=[x_b[t % 3]], W=[sq2b[sl]])

            def p2_norm(t):
                sl = t % 2
                for k in range(8):
                    mm(PS[0][:], ones[:], sq2[sl][:, k, :], start=(k == 0), stop=(k == 7), R=[onesb, sq2b[sl]],
                       **({"W": [PB[0]]} if k == 0 else {"A": [PB[0]]}))
                rstd_from(PS[0][:], PB[0], 512, ln2[:], ln2b, rs2[:], rs2b)
                for k in range(8):
                    stt(xn2[sl][:, k, :], x_t[t % 3][:, k, :], vec[:, V_PRE + k:V_PRE + k + 1], rs2[:],
                        R=[x_b[t % 3], rs2b, vecb], **({"W": [xn2b[sl]]} if k == 0 else {"A": [xn2b[sl]]}))

            def p2_kv(t):
                sl = t % 2
                for c4 in range(4):
                    pb = 1 + (c4 % 3)
                    for k in range(8):
                        mm(PS[pb][:], wqkv[:, k, 512 + c4 * 128:512 + (c4 + 1) * 128], xn2[sl][:, k, :],
                           start=(k == 0), stop=(k == 7), R=[wqkvb, xn2b[sl]],
                           **({"W": [PB[pb]]} if k == 0 else {"A": [PB[pb]]}))
                    cp("dve", kst[sl][:, c4, :], PS[pb][:], R=[PB[pb]],
                       **({"W": [kstb[sl]]} if c4 == 0 else {"A": [kstb[sl]]}))
                B.dma("pool", kT_s[:, :, t * 512:(t + 1) * 512].rearrange("c p s -> p c s"), kst[sl][:], kstb[sl],
                      store=True, reads=[kstb[sl]], acc=[kTsb])
                for bk in range(4):
                    pb = 4 + (bk % 3)
                    for k in range(8):
                        mm(PS[pb][:], xn2[sl][:, k, bk * 128:(bk + 1) * 128], wqkv[:, k, 1024:1536],
                           start=(k == 0), stop=(k == 7), R=[wqkvb, xn2b[sl]],
                           **({"W": [PB[pb]]} if k == 0 else {"A": [PB[pb]]}))
                    cp("dve", vst[sl][:, bk, :], PS[pb][:], R=[PB[pb]],
                       **({"W": [vstb[sl]]} if bk == 0 else {"A": [vstb[sl]]}))
                for c4 in range(4):
                    B.dma("pool", v_s[c4, :, 4 * t:4 * t + 4, :], vst[sl][:, :, c4 * 128:(c4 + 1) * 128], vstb[sl],
                          store=True, reads=[vstb[sl]], acc=[vsb])

            p2_load(0)
            p2_sq(0)
            p2_norm(0)
            if ST > 1:
                p2_load(1)
                p2_sq(1)
            for t in range(ST):
                if t + 2 < ST:
                    p2_load(t + 2)
                    p2_sq(t + 2)
                if t + 1 < ST:
                    p2_norm(t + 1)
                p2_kv(t)
        B.barrier()
        if debug:
            with contextlib.ExitStack() as sd:
                dk = sbt(sd, "dk", [128, S], BF16); dkb = B.buf("dk")
                dv = sbt(sd, "dv", [128, NBLK, 128], BF16); dvb = B.buf("dv")
                for c4 in range(4):
                    B.dma("sp", dk[:], kT_s[c4], dkb, reads=[kTsb], writes=[dkb])
                    B.dma("sp", dbg["kT"][c4], dk[:], dkb, store=True, reads=[dkb])
                    B.dma("sp", dv[:], v_s[c4], dvb, reads=[vsb], writes=[dvb])
                    B.dma("sp", dbg["v"][c4], dv[:], dvb, store=True, reads=[dvb])
                B.barrier()

        if stop_after == "P2":
            B.barrier()
            B.emit()
            return nc
        s34 = contextlib.ExitStack()
        with contextlib.ExitStack() as s3:
            sbo = sbt(s3, "sbo", [128, 4, NQ], BF16); sbob = B.buf("sbo")
            kt = [sbt(s3, "kt%d" % i, [128, S], BF16) for i in range(2)]
            ktb = [B.buf("kt%d" % i) for i in range(2)]
            vt = [sbt(s3, "vt%d" % i, [128, NBLK, 128], BF16) for i in range(2)]
            vtb = [B.buf("vt%d" % i) for i in range(2)]
            NE = 2
            e_t = [sbt(s3, "e_t%d" % i, [128, 1024], F32) for i in range(NE)]
            e_b = [B.buf("e_t%d" % i) for i in range(NE)]
            NSP = 3
            sp_t = [sbt(s3, "sp_t%d" % i, [128, 1024], BF16) for i in range(NSP)]
            sp_b = [B.buf("sp_t%d" % i) for i in range(NSP)]
            a_t = [sbt(s3, "a_t%d" % i, [128, 1024], BF16) for i in range(NSP)]
            a_b = [B.buf("a_t%d" % i) for i in range(NSP)]
            R_t = [sbt(s3, "R_t%d" % i, [128, 512], F32) for i in range(2)]
            R_b = [B.buf("R_t%d" % i) for i in range(2)]
            Rb_t = [sbt(s3, "Rb_t%d" % i, [128, 512], BF16) for i in range(4)]
            Rb_b = [B.buf("Rb_t%d" % i) for i in range(4)]
            NZP = 3

            def p3_load(c):
                B.dma("sp", kt[c % 2][:], kT_s[c], ktb[c % 2], reads=[kTsb], writes=[ktb[c % 2]])
                B.dma("sp", vt[c % 2][:], v_s[c], vtb[c % 2], reads=[vsb], writes=[vtb[c % 2]])

            pairs = []
            chain_id = 0
            for c in range(4):
                for g in range(G):
                    nj = 8 * g + 8
                    for pi in range(nj // 2):
                        jh = nj - 1 - 2 * pi
                        pairs.append(dict(c=c, g=g, jh=jh, jl=jh - 1, pi=pi, last=(jh == 1), chain=chain_id))
                    chain_id += 1
            for s_i, T in enumerate(pairs):
                T["zs"] = s_i % NZP
                T["sl3"] = s_i % NSP
                T["esl"] = s_i % NE
                T["ob"] = 6 + (T["chain"] % 2)
                T["rs"] = T["chain"] % 2

            def st_P1(T):
                c, g = T["c"], T["g"]
                qv = Qz[:, c, g, :, :].rearrange("p h q -> p (h q)")
                for i, j in enumerate((T["jh"], T["jl"])):
                    bk_ = 2 * T["zs"] + i
                    jj = j - 8 * g
                    mm(PS[bk_][:], kt[c % 2][:, j * 128:(j + 1) * 128], qv, start=True, stop=(jj < 0),
                       R=[ktb[c % 2], Qzb], W=[PB[bk_]])
                    if jj >= 0:
                        mm(PS[bk_][:], nBigI[:], notM[:, jj, :, :].rearrange("p h q -> p (h q)"),
                           start=False, stop=True, R=[nBigIb, notMb], A=[PB[bk_]])

            def zpair(T):
                zs = T["zs"]
                return PSALL[:, zs * 1024:(zs + 1) * 1024], [PB[2 * zs], PB[2 * zs + 1]]

            def bcols(ap):
                return ap.rearrange("p (j h q) -> p j h q", j=2, h=2)[:, :, :, 128:256]

            def bonly(T):
                return T["jl"] - 8 * T["g"] >= 4

            def st_A1a(T):
                zap, zbufs = zpair(T)
                e_ = e_t[T["esl"]]; eb = e_b[T["esl"]]
                if bonly(T):
                    act(bcols(e_[:]), bcols(zap), AF.Exp, R=zbufs, W=[eb])
                else:
                    act(e_[:], zap, AF.Exp, R=zbufs, W=[eb])

            def st_A1b(T):
                e_ = e_t[T["esl"]]; eb = e_b[T["esl"]]
                sp_ = sp_t[T["sl3"]]; spb = sp_b[T["sl3"]]
                if bonly(T):
                    B.op("pool", lambda e: e.memset(sp_[:], 0.0), writes=[spb])
                    act(bcols(sp_[:]), bcols(e_[:]), AF.Ln, R=[eb], A=[spb], bias=1.0)
                else:
                    act(sp_[:], e_[:], AF.Ln, R=[eb], W=[spb], bias=1.0)

            def st_P2(T):
                b0 = 2 * T["zs"]; b1 = b0 + 1
                sp_ = sp_t[T["sl3"]]; spb = sp_b[T["sl3"]]
                first = (T["pi"] == 0)
                rbi = (T["chain"] % 2) * 2 + (T["pi"] % 2)
                mm(PS[b0][:], nTin[:], sp_[:, 0:512], start=False, stop=first, R=[nTinb, spb], A=[PB[b0]], sgc=True)
                if not first:
                    mm(PS[b0][:], nOnes[:], Rb_t[rbi][:], start=False, stop=True, R=[nOnesb, Rb_b[rbi]], A=[PB[b0]],
                       sgc=True)
                mm(PS[b1][:], nTin[:], sp_[:, 512:1024], start=False, stop=False, R=[nTinb, spb], A=[PB[b1]], sgc=True)
                mm(PS[b1][:], nOnes[:], sp_[:, 0:512], start=False, stop=first, R=[nOnesb, spb], A=[PB[b1]], sgc=True)
                if not first:
                    mm(PS[b1][:], nOnes[:], Rb_t[rbi][:], start=False, stop=True, R=[nOnesb, Rb_b[rbi]], A=[PB[b1]],
                       sgc=True)

            def st_G(T):
                if T["last"]:
                    return
                sp_ = sp_t[T["sl3"]]; spb = sp_b[T["sl3"]]
                R_ = R_t[T["rs"]]; Rbuf = R_b[T["rs"]]
                rbn = (T["chain"] % 2) * 2 + ((T["pi"] + 1) % 2)
                if T["pi"] == 0:
                    tt("dve", R_[:], sp_[:, 0:512], sp_[:, 512:1024], ALU.add, R=[spb], W=[Rbuf])
                else:
                    tt("dve", R_[:], R_[:], sp_[:, 0:512], ALU.add, R=[spb, Rbuf], A=[Rbuf])
                    tt("dve", R_[:], R_[:], sp_[:, 512:1024], ALU.add, R=[spb, Rbuf], A=[Rbuf])
                cp("dve", Rb_t[rbn][:], R_[:], R=[Rbuf], W=[Rb_b[rbn]])

            def st_A2(T):
                zap, zbufs = zpair(T)
                a_ = a_t[T["sl3"]]; ab = a_b[T["sl3"]]
                if bonly(T):
                    B.op("pool", lambda e: e.memset(a_[:], 0.0), writes=[ab])
                    act(bcols(a_[:]), bcols(zap), AF.Exp, R=zbufs, A=[ab])
                else:
                    act(a_[:], zap, AF.Exp, R=zbufs, W=[ab])

            def st_P3(T):
                c, g = T["c"], T["g"]
                a_ = a_t[T["sl3"]]; ab = a_b[T["sl3"]]
                ob = T["ob"]
                first = (T["pi"] == 0)
                mm(PS[ob][:], vt[c % 2][:, T["jh"], :], a_[:, 0:512], start=first, stop=False, R=[vtb[c % 2], ab],
                   **({"W": [PB[ob]]} if first else {"A": [PB[ob]]}))
                mm(PS[ob][:], vt[c % 2][:, T["jl"], :], a_[:, 512:1024], start=False, stop=T["last"],
                   R=[vtb[c % 2], ab], A=[PB[ob]])
                if T["last"]:
                    q0 = g * 256
                    for h in range(2):
                        cp("dve", sbo[h * 64:(h + 1) * 64, c, q0:q0 + 256],
                           PS[ob][h * 64:(h + 1) * 64, h * 256:(h + 1) * 256], R=[PB[ob]], A=[sbob])

            p3_load(0)
            n_t = len(pairs)
            for s_i in range(n_t + 2):
                if 0 <= s_i - 2 < n_t:
                    T2 = pairs[s_i - 2]
                    if T2["g"] == 0 and T2["pi"] == 0 and T2["c"] + 1 < 4:
                        p3_load(T2["c"] + 1)
                if s_i < n_t:
                    T = pairs[s_i]
                    st_P1(T)
                    st_A1a(T)
                    st_A1b(T)
                if 0 <= s_i - 1 < n_t:
                    T1 = pairs[s_i - 1]
                    st_P2(T1)
                    st_G(T1)
                    st_A2(T1)
                if 0 <= s_i - 2 < n_t:
                    st_P3(pairs[s_i - 2])
            B.barrier()
            if debug:
                B.dma("sp", dbg["sbo"], sbo[:], sbob, store=True, reads=[sbob])
                B.barrier()
            if stop_after == "P3":
                B.barrier()
                B.emit()
                return nc
            sbo_s = nc.dram_tensor("sbo_s", [128, 4, NQ], BF16).ap()
            sbosb = B.buf("sbo_s")
            B.dma("sp", sbo_s, sbo[:], sbob, store=True, reads=[sbob], writes=[sbosb])
            B.barrier()
        s13.close()

        with contextlib.ExitStack() as s4:
            xn = sbt(s4, "xn", [128, 8, NB, 144], BF16); xnb = B.buf("xn")
            B.dma("sp", xn[:].rearrange("p c b t -> p c (b t)"), xn_s, xnb, reads=[xnsb], writes=[xnb])
            NOUT = 3
            o_t = [sbt(s4, "o_t%d" % i, [128, TO], F32) for i in range(NOUT)]
            o_b = [B.buf("o_t%d" % i) for i in range(NOUT)]
            sg_t = [sbt(s4, "sg_t%d" % i, [128, TO], F32) for i in range(2)]
            sg_b = [B.buf("sg_t%d" % i) for i in range(2)]
            cnt4 = {"o": 0, "sg": 0}

            def gate_mm(pb, wg, wgb, dc, t):
                for k in range(8):
                    mm(PS[pb][:, 0:TO].rearrange("p (b t) -> p b t", b=BPT),
                       wg[:, k, dc * 128:(dc + 1) * 128], xn[:, k, t * BPT:(t + 1) * BPT, 16:144],
                       start=(k == 0), stop=(k == 7), R=[wgb, xnb],
                       **({"W": [PB[pb]]} if k == 0 else {"A": [PB[pb]]}))

            def gated(ypb, gpb):
                si = cnt4["sg"] % 2; cnt4["sg"] += 1
                oi = cnt4["o"] % NOUT; cnt4["o"] += 1
                act(sg_t[si][:], PS[gpb][:, 0:TO], AF.Sigmoid, R=[PB[gpb]], W=[sg_b[si]])
                tt("dve", o_t[oi][:], sg_t[si][:], PS[ypb][:, 0:TO], ALU.mult, R=[sg_b[si], PB[ypb]], W=[o_b[oi]])
                return o_t[oi], o_b[oi]

            c0sb = B.buf("c0_s")
            with contextlib.ExitStack() as sa:
                wpi = sbt(sa, "wpi", [128, 8, 512], BF16); wpib = B.buf("wpi")
                wmix = sbt(sa, "wmix", [128, 4, 128], BF16); wmixb = B.buf("wmix")
                wpo = sbt(sa, "wpo", [128, 4, D], BF16); wpob = B.buf("wpo")
                wg0 = sbt(sa, "wg0", [128, 8, D], BF16); wg0b = B.buf("wg0")
                new_stg(sa, 4)
                load_w(wpi, wpib, w_in[:, 0:512], 8, 512, gcol=V_PRE)
                load_w(wmix, wmixb, w_pmix, 4, 128)
                load_w(wpo, wpob, w_po, 4, D)
                load_w(wg0, wg0b, w_in[:, O_G:O_G + D], 8, D, gcol=V_PRE)
                pa = [sbt(sa, "pa%d" % i, [128, BPT, 144], F32) for i in range(3)]
                pab = [B.buf("pa%d" % i) for i in range(3)]
                icn = sbt(sa, "icn", [128, 4, TO], F32); icnb = B.buf("icn")
                mixd = sbt(sa, "mixd", [128, 4, TO], BF16); mixdb = B.buf("mixd")
                pm = [sbt(sa, "pm%d" % i, [128, 4, TO], BF16) for i in range(2)]
                pmb = [B.buf("pm%d" % i) for i in range(2)]
                tmpa = sbt(sa, "tmpa", [128, BPT, 128], F32); tmpab = B.buf("tmpa")
                HB = TOH // 2
                HBK = BPT // 2

                def pa_pool(t):
                    for gi in range(4):
                        ts("dve", icn[:, gi, :], qpos[:, t * TO:(t + 1) * TO], 1.0, float(2 << gi), ALU.add, ALU.min,
                           R=[qposb], **({"W": [icnb]} if gi == 0 else {"A": [icnb]}))
                    B.op("dve", lambda e: e.reciprocal(out=icn[:], in_=icn[:]), reads=[icnb], writes=[icnb])
                    for gi in range(4):
                        p0, p0b = pa[0], pab[0]
                        for hf in range(2):
                            pb = hf
                            for k in range(8):
                                mm(PS[pb][:, 0:HB].rearrange("p (b t) -> p b t", b=HBK),
                                   wpi[:, k, gi * 128:(gi + 1) * 128],
                                   xn[:, k, t * BPT + hf * HBK:t * BPT + (hf + 1) * HBK, :],
                                   start=(k == 0), stop=(k == 7), R=[wpib, xnb],
                                   **({"W": [PB[pb]]} if k == 0 else {"A": [PB[pb]]}))
                            cp("act", p0[:, hf * HBK:(hf + 1) * HBK, :],
                               PS[pb][:, 0:HB].rearrange("p (b t) -> p b t", b=HBK), R=[PB[pb]],
                               **({"W": [p0b]} if hf == 0 else {"A": [p0b]}))
                        cur, curb = p0, p0b
                        pp = 1
                        dsh = 1
                        lo = 0
                        for step in range(gi + 1):
                            nxt, nxtb = pa[pp], pab[pp]
                            lo = lo + dsh
                            tt("dve" if step % 2 == 0 else "pool", nxt[:, :, lo:144], cur[:, :, lo:144],
                               cur[:, :, lo - dsh:144 - dsh], ALU.add, R=[curb], W=[nxtb])
                            cur, curb = nxt, nxtb
                            pp = 3 - pp
                            dsh *= 2
                        tt("dve", tmpa[:], cur[:, :, 16:144], icn[:, gi, :].rearrange("p (b t) -> p b t", b=BPT),
                           ALU.mult, R=[curb, icnb], W=[tmpab])
                        tt("dve", mixd[:, gi, :].rearrange("p (b t) -> p b t", b=BPT), tmpa[:], p0[:, :, 16:144],
                           ALU.subtract, R=[tmpab, p0b], **({"W": [mixdb]} if gi == 0 else {"A": [mixdb]}))
                        yield
                    for gi in range(4):
                        pb = 2 + (gi % 2)
                        mm(PS[pb][:, 0:TO], wmix[:, gi, :], mixd[:, gi, :], start=True, stop=True,
                           R=[wmixb, mixdb], W=[PB[pb]])
                        ts("dve", pm[t % 2][:, gi, :], PS[pb][:, 0:TO], vec[:, V_PSC + gi:V_PSC + gi + 1], None,
                           ALU.mult, None, R=[PB[pb], vecb], **({"W": [pmb[t % 2]]} if gi == 0 else {"A": [pmb[t % 2]]}))
                    yield

                def pa_proj(t):
                    def fin(dc, ypb, gpb):
                        ot, otb = gated(ypb, gpb)
                        B.dma("sp", c0_s[dc * 128:(dc + 1) * 128, t * TO:(t + 1) * TO], ot[:], otb, store=True,
                              reads=[otb], acc=[c0sb])
                    pend = None
                    for dc in range(8):
                        ypb = 4 + (dc % 2)
                        gpb = 6 + (dc % 2)
                        for gi in range(4):
                            mm(PS[ypb][:, 0:TO], wpo[:, gi, dc * 128:(dc + 1) * 128], pm[t % 2][:, gi, :],
                               start=(gi == 0), stop=(gi == 3), R=[wpob, pmb[t % 2]],
                               **({"W": [PB[ypb]]} if gi == 0 else {"A": [PB[ypb]]}))
                        gate_mm(gpb, wg0, wg0b, dc, t)
                        if pend is not None:
                            fin(*pend)
                        pend = (dc, ypb, gpb)
                        yield
                    fin(*pend)

                run(pa_pool(0))
                for t in range(NT):
                    if t + 1 < NT:
                        interleave(pa_proj(t), pa_pool(t + 1))
                    else:
                        run(pa_proj(t))
                B.barrier()

            wsbo = sbt(s4, "wsbo", [128, 4, D], BF16); wsbob = B.buf("wsbo")
            wg1 = sbt(s4, "wg1", [128, 8, D], BF16); wg1b = B.buf("wg1")
            wout = sbt(s4, "wout", [128, 8, D], BF16); woutb = B.buf("wout")

            c2sb = B.buf("c2_s")
            with contextlib.ExitStack() as sc:
                wxq = sbt(sc, "wxq", [128, 8, 256], BF16); wxqb = B.buf("wxq")
                wkv = sbt(sc, "wkv", [128, 8, 512], BF16); wkvb = B.buf("wkv")
                wxo = sbt(sc, "wxo", [128, 2, D], BF16); wxob = B.buf("wxo")
                wg2 = sbt(sc, "wg2", [128, 8, D], BF16); wg2b = B.buf("wg2")
                new_stg(sc)
                load_w(wkv, wkvb, w_kv, 8, 512, gcol=V_MEM)
                load_w(wxq, wxqb, w_in[:, O_XQ:O_XQ + 256], 8, 256, gcol=V_PRE)
                load_w(wxo, wxob, w_xo, 2, D)
                load_w(wg2, wg2b, w_in[:, O_G + 2 * D:O_G + 3 * D], 8, D, gcol=V_PRE)
                m_t = sbt(sc, "m_t", [128, 8, 256], F32); m_b = B.buf("m_t")
                msq = sbt(sc, "msq", [128, 8, 256], BF16); msqb = B.buf("msq")
                mln = sbt(sc, "mln", [128, 256], F32); mlnb = B.buf("mln")
                mrs = sbt(sc, "mrs", [128, 256], F32); mrsb = B.buf("mrs")
                mn = sbt(sc, "mn", [128, 8, 256], BF16); mnb = B.buf("mn")
                mkT = sbt(sc, "mkT", [128, 2, 256], BF16); mkTb = B.buf("mkT")
                mvz = sbt(sc, "mvz", [128, 4, 2, 128], BF16); mvb = B.buf("mvz")
                xqz = sbt(sc, "xqz", [128, 4, TO], BF16); xqTb = B.buf("xqz")
                B.op("pool", lambda e: e.memset(mvz[:].rearrange("p a b c -> p (a b c)"), 0.0), writes=[mvb])
                B.op("pool", lambda e: e.memset(xqz[:].rearrange("p a b -> p (a b)"), 0.0), writes=[xqTb])
                nmx2 = [sbt(sc, "nmx%d" % i, [128, 4], F32) for i in range(2)]; nmxb2 = [B.buf("nmx") for i in range(2)]
                ssum2 = [sbt(sc, "ssum%d" % i, [128, 4], F32) for i in range(2)]; ssumb2 = [B.buf("ssum") for i in range(2)]
                rsm2_ = [sbt(sc, "rsx%d" % i, [128, 4], F32) for i in range(2)]; rsmb2 = [B.buf("rsm") for i in range(2)]
                P_t2 = [sbt(sc, "P_t%d" % i, [128, 4, 256], F32) for i in range(2)]; P_b2 = [B.buf("P_t") for i in range(2)]
                Pn2 = [sbt(sc, "Pn%d" % i, [128, 4, 256], BF16) for i in range(2)]; Pnb2 = [B.buf("Pn") for i in range(2)]
                PT2 = [sbt(sc, "PT%d" % i, [128, 8, 128], BF16) for i in range(2)]; PTb2 = [B.buf("PT") for i in range(2)]
                xoT = sbt(sc, "xoT", [128, 2, TO], BF16); xoTb = B.buf("xoT")

                B.dma("sp", m_t[:], memT.rearrange("(c p) n -> p c n", p=128), m_b, writes=[m_b])
                act(msq[:], m_t[:], AF.Square, R=[m_b], W=[msqb])
                for k in range(8):
                    mm(PS[0][:, 0:256], ones[:], msq[:, k, :], start=(k == 0), stop=(k == 7), R=[onesb, msqb],
                       **({"W": [PB[0]]} if k == 0 else {"A": [PB[0]]}))
                rstd_from(PS[0][:, 0:256], PB[0], 256, mln[:], mlnb, mrs[:], mrsb)
                for k in range(8):
                    stt(mn[:, k, :], m_t[:, k, :], vec[:, V_MEM + k:V_MEM + k + 1], mrs[:], R=[m_b, mrsb, vecb],
                        **({"W": [mnb]} if k == 0 else {"A": [mnb]}))
                for ch in range(2):
                    for k in range(8):
                        mm(PS[1][:, 0:256], wkv[:, k, ch * 128:(ch + 1) * 128], mn[:, k, :], start=(k == 0), stop=(k == 7),
                           R=[wkvb, mnb], **({"W": [PB[1]]} if k == 0 else {"A": [PB[1]]}))
                    cp("dve", mkT[:, ch, :], PS[1][:, 0:256], R=[PB[1]], **({"W": [mkTb]} if ch == 0 else {"A": [mkTb]}))
                for mc in range(2):
                    for k in range(8):
                        mm(PS[2][:, 0:256], mn[:, k, mc * 128:(mc + 1) * 128], wkv[:, k, 256:512], start=(k == 0), stop=(k == 7),
                           R=[wkvb, mnb], **({"W": [PB[2]]} if k == 0 else {"A": [PB[2]]}))
                    for h in range(4):
                        cp("dve", mvz[:, h, mc, (h % 2) * 64:(h % 2 + 1) * 64], PS[2][:, h * 64:(h + 1) * 64],
                           R=[PB[2]], A=[mvb])

                for t in range(NT):
                    for ch in range(2):
                        for k in range(8):
                            mm(PS[3][:, 0:TO].rearrange("p (b t) -> p b t", b=BPT),
                               wxq[:, k, ch * 128:(ch + 1) * 128], xn[:, k, t * BPT:(t + 1) * BPT, 16:144],
                               start=(k == 0), stop=(k == 7), R=[wxqb, xnb],
                               **({"W": [PB[3]]} if k == 0 else {"A": [PB[3]]}))
                        for hp in range(2):
                            act(xqz[hp * 64:(hp + 1) * 64, ch * 2 + hp, :], PS[3][hp * 64:(hp + 1) * 64, 0:TO], AF.Copy,
                                R=[PB[3]], A=[xqTb], scale=0.125)
                    for bk in range(BPT):
                        nmx, nmxb, ssum, ssumb = nmx2[bk % 2], nmxb2[bk % 2], ssum2[bk % 2], ssumb2[bk % 2]
                        rsm, rsmb, P_t, P_b = rsm2_[bk % 2], rsmb2[bk % 2], P_t2[bk % 2], P_b2[bk % 2]
                        Pn, Pnb, PT, PTb = Pn2[bk % 2], Pnb2[bk % 2], PT2[bk % 2], PTb2[bk % 2]
                        sb0 = 0 if bk % 2 == 0 else 4
                        for h in range(4):
                            pb = sb0 + h // 2
                            hp = h % 2
                            mm(PS[pb][:, hp * 256:(hp + 1) * 256],
                               xqz[:, h, bk * 128:(bk + 1) * 128], mkT[:, h // 2, :],
                               start=True, stop=True, R=[xqTb, mkTb],
                               **({"W": [PB[pb]]} if hp == 0 else {"A": [PB[pb]]}))
                        for pq in range(2):
                            B.op("dve", lambda e, pq=pq, nmx=nmx, sb0=sb0: e.reduce_max(
                                out=nmx[:, 2 * pq:2 * pq + 2], in_=PS[sb0 + pq][:].rearrange("p (h m) -> p h m", h=2),
                                axis=AX.X, negate=True), reads=[PB[sb0 + pq]],
                                **({"writes": [nmxb]} if pq == 0 else {"acc": [nmxb]}))
                        for h in range(4):
                            pb = sb0 + h // 2
                            hp = h % 2
                            act(P_t[:, h, :], PS[pb][:, hp * 256:(hp + 1) * 256], AF.Exp, R=[PB[pb], nmxb],
                                **({"W": [P_b, ssumb]} if h == 0 else {"A": [P_b, ssumb]}),
                                bias=nmx[:, h:h + 1], accum_out=ssum[:, h:h + 1])
                        B.op("dve", lambda e, rsm=rsm, ssum=ssum: e.reciprocal(out=rsm[:], in_=ssum[:]),
                             reads=[ssumb], writes=[rsmb])
                        for h in range(4):
                            ts("dve", Pn[:, h, :], P_t[:, h, :], rsm[:, h:h + 1], None, ALU.mult, None,
                               R=[P_b, rsmb], **({"W": [Pnb]} if h == 0 else {"A": [Pnb]}))
                        for h in range(4):
                            for mc in range(2):
                                i8 = h * 2 + mc
                                B.op("pe", lambda e, h=h, mc=mc, i8=i8, Pn=Pn: e.transpose(
                                    PST[:, i8 * 128:(i8 + 1) * 128], Pn[:, h, mc * 128:(mc + 1) * 128], ident[:]),
                                    reads=[Pnb, identb], **({"writes": [PSTB]} if i8 == 0 else {"acc": [PSTB]}))
                        cp("act", PT[:].rearrange("p a b -> p (a b)"), PST[:], R=[PSTB], W=[PTb])
                        for ch in range(2):
                            pb = 2 + ch
                            n4 = 0
                            for h in (2 * ch, 2 * ch + 1):
                                for mc in range(2):
                                    mm(PS[pb][:, bk * 128:(bk + 1) * 128], mvz[:, h, mc, :], PT[:, h * 2 + mc, :],
                                       start=(n4 == 0), stop=(n4 == 3), R=[mvb, PTb],
                                       **({"W": [PB[pb]]} if (bk == 0 and n4 == 0) else {"A": [PB[pb]]}))
                                    n4 += 1
                    for ch in range(2):
                        cp("dve", xoT[:, ch, :], PS[2 + ch][:, 0:TO], R=[PB[2 + ch]],
                           **({"W": [xoTb]} if ch == 0 else {"A": [xoTb]}))
                    def fin_c(dc, ypb, gpb, t=t):
                        ot, otb = gated(ypb, gpb)
                        B.dma("pool", c2_s[dc * 128:(dc + 1) * 128, t * TO:(t + 1) * TO], ot[:], otb, store=True,
                              reads=[otb], acc=[c2sb])
                    if t == 0:
                        load_w(wsbo, wsbob, w_sbo, 4, D)
                        load_w(wg1, wg1b, w_in[:, O_G + D:O_G + 2 * D], 8, D)
                        load_w(wout, woutb, w_out, 8, D)
                    pend = None
                    for dc in range(8):
                        ypb = 4 + (dc % 2)
                        gpb = 6 + (dc % 2)
                        for ch in range(2):
                            mm(PS[ypb][:, 0:TO], wxo[:, ch, dc * 128:(dc + 1) * 128], xoT[:, ch, :],
                               start=(ch == 0), stop=(ch == 1), R=[wxob, xoTb],
                               **({"W": [PB[ypb]]} if ch == 0 else {"A": [PB[ypb]]}))
                        gate_mm(gpb, wg2, wg2b, dc, t)
                        if pend is not None:
                            fin_c(*pend)
                        pend = (dc, ypb, gpb)
                    fin_c(*pend)
                B.barrier()
            if debug:
                with contextlib.ExitStack() as sd:
                    dd = sbt(sd, "dd", [128, 8, NQ], F32); ddb = B.buf("dd")
                    for nm, src, srcb in (("c0", c0_s, c0sb), ("c2", c2_s, c2sb)):
                        B.dma("sp", dd[:], src.rearrange("(c p) n -> p c n", p=128), ddb, reads=[srcb], writes=[ddb])
                        B.dma("sp", dbg[nm].rearrange("(c p) n -> p c n", p=128), dd[:], ddb, store=True, reads=[ddb])
                    B.barrier()

            if stop_after == "P4c":
                B.barrier()
                B.emit()
                return nc
            h1sb = B.buf("h1_s"); n2sb = B.buf("n2_s")
            with contextlib.ExitStack() as sd4:
                sbo4 = sbt(sd4, "sbo4", [128, 4, NQ], BF16); sbo4b = B.buf("sbo4")
                B.dma("sp", sbo4[:], sbo_s, sbo4b, reads=[sbosb], writes=[sbo4b])
                NCL = 3
                cl = [sbt(sd4, "cl%d" % i, [128, 2, TO], F32) for i in range(NCL)]
                clb = [B.buf("cl%d" % i) for i in range(NCL)]
                clc = {"i": 0}
                mg1 = sbt(sd4, "mg", [128, 8, TO], BF16)
                mg = [mg1, mg1]
                mgb1 = B.buf("mg")
                mgb = [mgb1, mgb1]
                mo = [sbt(sd4, "mo%d" % i, [128, 8, TO], F32) for i in range(2)]
                mob = [B.buf("mo%d" % i) for i in range(2)]
                sqm = [sbt(sd4, "sqm%d" % i, [128, TO], BF16) for i in range(2)]
                sqmb = [B.buf("sqm%d" % i) for i in range(2)]
                lnm = sbt(sd4, "lnm", [128, TO], F32); lnmb = B.buf("lnm")
                rsm1 = sbt(sd4, "rsm1", [128, TO], F32); rsm1b = B.buf("rsm1")
                rsm2 = sbt(sd4, "rsm2", [128, TO], F32); rsm2b = B.buf("rsm2")
                xr = sbt(sd4, "xr", [128, 8, BPT, 128], F32); xrb = B.buf("xr")
                n2t = sbt(sd4, "n2t", [128, 8, TO], BF16); n2tb = B.buf("n2t")
                xoh4 = xoh.rearrange("d (b t) -> d b t", t=144)

                def bd_S1(t):
                    def fin_b(dc, ci, ypb, gpb):
                        ot, otb = gated(ypb, gpb)
                        tt("pool", cl[ci][:, 0, :], cl[ci][:, 0, :], cl[ci][:, 1, :], ALU.add, R=[clb[ci]], A=[clb[ci]])
                        tt("dve", mg[t % 2][:, dc, :], ot[:], cl[ci][:, 0, :], ALU.add, R=[otb, clb[ci]],
                           **({"W": [mgb[t % 2]]} if dc == 0 else {"A": [mgb[t % 2]]}))
                    pend = None
                    for dc in range(8):
                        ci = clc["i"] % NCL
                        clc["i"] += 1
                        B.dma("sp", cl[ci][:, 0, :], c0_s[dc * 128:(dc + 1) * 128, t * TO:(t + 1) * TO], clb[ci],
                              reads=[c0sb], writes=[clb[ci]])
                        B.dma("sp", cl[ci][:, 1, :], c2_s[dc * 128:(dc + 1) * 128, t * TO:(t + 1) * TO], clb[ci],
                              reads=[c2sb], acc=[clb[ci]])
                        ypb = 4 + (dc % 2)
                        gpb = (dc % 2)
                        for c4 in range(4):
                            mm(PS[ypb][:, 0:TO], wsbo[:, c4, dc * 128:(dc + 1) * 128], sbo4[:, c4, t * TO:(t + 1) * TO],
                               start=(c4 == 0), stop=(c4 == 3), R=[wsbob, sbo4b],
                               **({"W": [PB[ypb]]} if c4 == 0 else {"A": [PB[ypb]]}))
                        gate_mm(gpb, wg1, wg1b, dc, t)
                        if pend is not None:
                            fin_b(*pend)
                        pend = (dc, ci, ypb, gpb)
                        yield
                    fin_b(*pend)

                def bd_S2(t):
                    for dc in range(8):
                        B.dma("sp", xr[:, dc, :, :], xoh4[dc * 128:(dc + 1) * 128, t * BPT:(t + 1) * BPT, 16:144], xrb,
                              **({"writes": [xrb]} if dc == 0 else {"acc": [xrb]}))
                    m_, mb_ = mo[t % 2], mob[t % 2]
                    for dc in range(8):
                        pb = 2 + (dc % 2)
                        for k in range(8):
                            mm(PS[pb][:, 0:TO], wout[:, k, dc * 128:(dc + 1) * 128], mg[t % 2][:, k, :],
                               start=(k == 0), stop=(k == 7), R=[woutb, mgb[t % 2]],
                               **({"W": [PB[pb]]} if k == 0 else {"A": [PB[pb]]}))
                        if dc > 0:
                            d1 = dc - 1
                            mm(PS[6][:, 0:TO], ones[:], sqm[d1 % 2][:], start=(d1 == 0), stop=False,
                               R=[onesb, sqmb[d1 % 2]], **({"W": [PB[6]]} if d1 == 0 else {"A": [PB[6]]}))
                        cp("dve", m_[:, dc, :], PS[pb][:, 0:TO], R=[PB[pb]], **({"W": [mb_]} if dc == 0 else {"A": [mb_]}))
                        act(sqm[dc % 2][:], m_[:, dc, :], AF.Square, R=[mb_], W=[sqmb[dc % 2]])
                    mm(PS[6][:, 0:TO], ones[:], sqm[7 % 2][:], start=False, stop=True,
                       R=[onesb, sqmb[7 % 2]], A=[PB[6]])

                def bd_S3(t):
                    m_, mb_ = mo[t % 2], mob[t % 2]
                    rstd_from(PS[6][:, 0:TO], PB[6], TO, lnm[:], lnmb, rsm1[:], rsm1b)
                    for dc in range(8):
                        stt(m_[:, dc, :], m_[:, dc, :], vec[:, V_POST + dc:V_POST + dc + 1], rsm1[:],
                            R=[mb_, vecb, rsm1b], A=[mb_])
                        tt("pool", m_[:, dc, :], m_[:, dc, :], xr[:, dc, :, :].rearrange("p b t -> p (b t)"), ALU.add,
                           R=[mb_, xrb], A=[mb_])
                        act(sqm[dc % 2][:], m_[:, dc, :], AF.Square, R=[mb_], W=[sqmb[dc % 2]])
                        if dc > 0:
                            d1 = dc - 1
                            mm(PS[7][:, 0:TO], ones[:], sqm[d1 % 2][:], start=(d1 == 0), stop=False,
                               R=[onesb, sqmb[d1 % 2]], **({"W": [PB[7]]} if d1 == 0 else {"A": [PB[7]]}))
                        yield
                    mm(PS[7][:, 0:TO], ones[:], sqm[7 % 2][:], start=False, stop=True,
                       R=[onesb, sqmb[7 % 2]], A=[PB[7]])
                    B.dma("sp", h1_s[:, t * TO:(t + 1) * TO].rearrange("(c p) n -> p c n", p=128), m_[:], mb_, store=True,
                          reads=[mb_], acc=[h1sb])
                    rstd_from(PS[7][:, 0:TO], PB[7], TO, lnm[:], lnmb, rsm2[:], rsm2b)
                    for dc in range(8):
                        stt(n2t[:, dc, :], m_[:, dc, :], vec[:, V_FPRE + dc:V_FPRE + dc + 1], rsm2[:],
                            R=[mb_, rsm2b, vecb], **({"W": [n2tb]} if dc == 0 else {"A": [n2tb]}))
                    B.dma("sp", n2_s[:, t * TO:(t + 1) * TO].rearrange("(c p) n -> p c n", p=128), n2t[:], n2tb, store=True,
                          reads=[n2tb], acc=[n2sb])

                run(bd_S1(0))
                for t in range(NT):
                    bd_S2(t)
                    if t + 1 < NT:
                        interleave(bd_S3(t), bd_S1(t + 1))
                    else:
                        run(bd_S3(t))
                B.barrier()
        if debug:
            with contextlib.ExitStack() as sd:
                dd = sbt(sd, "dd2", [128, 8, NQ], F32); ddb = B.buf("dd2")
                B.dma("sp", dd[:], h1_s.rearrange("(c p) n -> p c n", p=128), ddb, reads=[h1sb], writes=[ddb])
                B.dma("sp", dbg["h1"].rearrange("(c p) n -> p c n", p=128), dd[:], ddb, store=True, reads=[ddb])
                B.barrier()

        if stop_after == "P4":
            B.barrier()
            B.emit()
            return nc
        with contextlib.ExitStack() as s56:
            actT = sbt(s56, "actT", [128, NFC, NQ], BF16); actTb = B.buf("actT")
            wfo = sbt(s56, "wfo", [128, NFC, D], BF16); wfob = B.buf("wfo")
            with contextlib.ExitStack() as s5:
                n2 = sbt(s5, "n2", [128, 8, NQ], BF16); n2b = B.buf("n2")
                new_stg(s5)
                B.dma("sp", n2[:], n2_s.rearrange("(c p) n -> p c n", p=128), n2b, reads=[n2sb], writes=[n2b])
                wfi = [sbt(s5, "wfi%d" % i, [128, 8, 256], BF16) for i in range(2)]
                wfib = [B.buf("wfi%d" % i) for i in range(2)]
                sil = [sbt(s5, "sil%d" % i, [128, TO], F32) for i in range(2)]
                silb = [B.buf("sil%d" % i) for i in range(2)]

                def p5_loadw(f):
                    sl = f % 2
                    load_w(wfi[sl][:, :, 0:128], wfib[sl], w_fi[:, f * 128:(f + 1) * 128], 8, 128, gcol=V_FPRE, first=True)
                    load_w(wfi[sl][:, :, 128:256], wfib[sl], w_fi[:, DFF + f * 128:DFF + (f + 1) * 128], 8, 128,
                           gcol=V_FPRE, first=False)

                p5_loadw(0)
                cnt5 = 0
                for f in range(NFC):
                    if f + 1 < NFC:
                        p5_loadw(f + 1)
                    load_w(wfo[:, f:f + 1, :], wfob, w_fo[f * 128:(f + 1) * 128, :], 1, D, first=(f == 0))
                    sl = f % 2
                    for t in range(NT):
                        gp = (cnt5 % 2) * 2
                        up = gp + 1
                        si = cnt5 % 2
                        cnt5 += 1
                        for k in range(8):
                            mm(PS[gp][:, 0:TO], wfi[sl][:, k, 0:128], n2[:, k, t * TO:(t + 1) * TO],
                               start=(k == 0), stop=(k == 7), R=[wfib[sl], n2b],
                               **({"W": [PB[gp]]} if k == 0 else {"A": [PB[gp]]}))
                        for k in range(8):
                            mm(PS[up][:, 0:TO], wfi[sl][:, k, 128:256], n2[:, k, t * TO:(t + 1) * TO],
                               start=(k == 0), stop=(k == 7), R=[wfib[sl], n2b],
                               **({"W": [PB[up]]} if k == 0 else {"A": [PB[up]]}))
                        act(sil[si][:], PS[gp][:, 0:TO], AF.Silu, R=[PB[gp]], W=[silb[si]])
                        tt("dve", actT[:, f, t * TO:(t + 1) * TO], sil[si][:], PS[up][:, 0:TO], ALU.mult,
                           R=[silb[si], PB[up]], A=[actTb])
                B.barrier()
            with contextlib.ExitStack() as s6:
                ff = [sbt(s6, "ff%d" % i, [128, 8, TO], F32) for i in range(2)]
                ffb = [B.buf("ff%d" % i) for i in range(2)]
                sq6 = [sbt(s6, "sq6_%d" % i, [128, TO], BF16) for i in range(2)]
                sq6b = [B.buf("sq6_%d" % i) for i in range(2)]
                ln6 = sbt(s6, "ln6", [128, TO], F32); ln6b = B.buf("ln6")
                rs6 = sbt(s6, "rs6", [128, TO], F32); rs6b = B.buf("rs6")
                h1t = [sbt(s6, "h1t%d" % i, [128, TO], F32) for i in range(3)]
                h1tb = [B.buf("h1t%d" % i) for i in range(3)]
                outsb = B.buf("outT")
                cnt6 = {"h": 0}

                def p6_M(t):
                    f_, fb_ = ff[t % 2], ffb[t % 2]
                    ssb = 4 + (t % 2)
                    for dc in range(8):
                        pb = dc % 4
                        for f in range(NFC):
                            mm(PS[pb][:, 0:TO], wfo[:, f, dc * 128:(dc + 1) * 128], actT[:, f, t * TO:(t + 1) * TO],
                               start=(f == 0), stop=(f == NFC - 1), R=[wfob, actTb],
                               **({"W": [PB[pb]]} if f == 0 else {"A": [PB[pb]]}))
                        if dc > 0:
                            d1 = dc - 1
                            mm(PS[ssb][:, 0:TO], ones[:], sq6[d1 % 2][:], start=(d1 == 0), stop=False,
                               R=[onesb, sq6b[d1 % 2]], **({"W": [PB[ssb]]} if d1 == 0 else {"A": [PB[ssb]]}))
                        cp("dve", f_[:, dc, :], PS[pb][:, 0:TO], R=[PB[pb]], **({"W": [fb_]} if dc == 0 else {"A": [fb_]}))
                        act(sq6[dc % 2][:], f_[:, dc, :], AF.Square, R=[fb_], W=[sq6b[dc % 2]])
                        yield
                    mm(PS[ssb][:, 0:TO], ones[:], sq6[7 % 2][:], start=False, stop=True,
                       R=[onesb, sq6b[7 % 2]], A=[PB[ssb]])

                def p6_E(t):
                    f_, fb_ = ff[t % 2], ffb[t % 2]
                    ssb = 4 + (t % 2)
                    rstd_from(PS[ssb][:, 0:TO], PB[ssb], TO, ln6[:], ln6b, rs6[:], rs6b)
                    for dc in range(8):
                        hi = cnt6["h"] % 3
                        cnt6["h"] += 1
                        B.dma("sp", h1t[hi][:], h1_s[dc * 128:(dc + 1) * 128, t * TO:(t + 1) * TO], h1tb[hi],
                              reads=[h1sb], writes=[h1tb[hi]])
                        stt(f_[:, dc, :], f_[:, dc, :], vec[:, V_FPOST + dc:V_FPOST + dc + 1], rs6[:],
                            R=[fb_, vecb, rs6b], A=[fb_])
                        tt("pool", f_[:, dc, :], f_[:, dc, :], h1t[hi][:], ALU.add, R=[fb_, h1tb[hi]], A=[fb_])
                        yield
                    B.dma("sp", outT[:, t * TO:(t + 1) * TO].rearrange("(c p) n -> p c n", p=128), f_[:], fb_, store=True,
                          reads=[fb_], acc=[outsb])

                run(p6_M(0))
                for t in range(NT):
                    if t + 1 < NT:
                        interleave(p6_M(t + 1), p6_E(t))
                    else:
                        run(p6_E(t))
                B.barrier()
        B.barrier()
        B.emit()
    return nc


def _own_blocks(r, G):
    blks = []
    for g in range(G):
        blks.append(8 * g + r)
        blks.append(8 * g + 7 - r)
    return blks


def _pack_vecs(norm_mix_pre, norm_mem, norm_mix_post, norm_ffn_pre, norm_ffn_post, pool_scale):
    cols = []
    for v in (norm_mix_pre, norm_mem, norm_mix_post, norm_ffn_pre, norm_ffn_post):
        cols.append(np.asarray(v, np.float32).reshape(8, 128).T)
    cols.append(np.asarray(pool_scale, np.float32).reshape(4, 128).T)
    return np.ascontiguousarray(np.concatenate(cols, axis=1))


_PROG_CACHE = {}


def run_layer(x, mem, norm_mix_pre, w_in, w_pool_mix, pool_scale, w_pool_o, w_sb_o, norm_mem, w_mem_kv, w_x_o,
              w_out, norm_mix_post, norm_ffn_pre, w_ffn_in, w_ffn_out, norm_ffn_post, debug=False, stop_after=None):
    x = np.asarray(x, np.float32)
    mem = np.asarray(mem, np.float32)
    Bn, S, _ = x.shape
    G = S // 1024
    assert Bn == 2 and S == 1024 * G
    key = (G, debug, stop_after)
    if key not in _PROG_CACHE:
        _PROG_CACHE[key] = build_program(G, debug, stop_after)
    nc = _PROG_CACHE[key]
    f32 = lambda a: np.ascontiguousarray(np.asarray(a, np.float32))
    shared = {
        "w_in": f32(w_in[0]), "w_pmix": f32(np.asarray(w_pool_mix[0]).reshape(512, 128)), "w_po": f32(w_pool_o[0]),
        "w_sbo": f32(w_sb_o[0]), "w_kv": f32(w_mem_kv[0]), "w_xo": f32(w_x_o[0]), "w_out": f32(w_out[0]),
        "w_fi": f32(w_ffn_in[0]), "w_fo": f32(w_ffn_out[0]),
        "vecs": _pack_vecs(norm_mix_pre[0], norm_mem[0], norm_mix_post[0], norm_ffn_pre[0], norm_ffn_post[0],
                           pool_scale[0]),
    }
    xTb = [np.ascontiguousarray(x[b].T) for b in range(2)]
    memTb = [np.ascontiguousarray(mem[b].T) for b in range(2)]
    in_maps = []
    for core in range(8):
        b, r = core // 4, core % 4
        blks = _own_blocks(r, G)
        xoh = np.zeros((D, len(blks) * 144), np.float32)
        posv = np.zeros((len(blks) * 128,), np.float32)
        for n, blk in enumerate(blks):
            s0 = blk * 128
            lo = max(0, s0 - 16)
            xoh[:, n * 144 + 16 - (s0 - lo):(n + 1) * 144] = xTb[b][:, lo:s0 + 128]
            posv[n * 128:(n + 1) * 128] = np.arange(s0, s0 + 128, dtype=np.float32)
        m = dict(shared)
        m.update({"xT": xTb[b], "xoh": xoh, "pos": posv, "memT": memTb[b]})
        in_maps.append(m)
    res = run_bass_kernel_spmd(nc, in_maps, core_ids=list(range(8)))
    out = np.empty((2, S, D), np.float32)
    for core in range(8):
        b, r = core // 4, core % 4
        oT = np.asarray(res.results[core]["outT"])
        for n, blk in enumerate(_own_blocks(r, G)):
            out[b, blk * 128:(blk + 1) * 128, :] = oT[:, n * 128:(n + 1) * 128].T
    if debug:
        return out, res.results
    return out


def kernel(**inputs):
    return run_layer(**inputs)
```
